# Optimizing a Trainium2 kernel written in Bass

```python
import math
import jax
import jax.numpy as jnp
from jax import lax
import numpy as np

D_MODEL = 1024
BATCH = 8
SEQ = 4096
DEPTH = 4

N_META = 16
CHUNK = 64
N_EVEN = (DEPTH + 1) // 2
N_ODD = DEPTH // 2
RMS_EPS = 1e-6

S5_WIDTH = 256
S5_GROUP = 16
S5_GROUPS = S5_WIDTH // S5_GROUP
S5_STATE = 64

SSD_HEADDIM = 64
SSD_INNER = 768
SSD_HEADS = SSD_INNER // SSD_HEADDIM
SSD_GROUPS = 2
SSD_STATE = 128
SSD_CONV = 4
SSD_XBC = SSD_INNER + 2 * SSD_GROUPS * SSD_STATE
EVEN_IN = S5_WIDTH + SSD_INNER + SSD_XBC + SSD_HEADS
MIX_WIDTH = S5_WIDTH + SSD_INNER

RWKV_WIDTH = 512
RWKV_HEADDIM = 64
RWKV_HEADS = RWKV_WIDTH // RWKV_HEADDIM
LORA_W = 64
LORA_A = 64
LORA_V = 32
LORA_G = 128
GN_EPS = 64e-5
RWKV_IN = 3 * RWKV_WIDTH + LORA_W + LORA_A + LORA_G

HGRN_WIDTH = 512
HGRN_HEADS = 4
HGRN_HEADDIM = HGRN_WIDTH // HGRN_HEADS
HGRN_IN = 4 * HGRN_WIDTH
ODD_IN = RWKV_IN + HGRN_IN

D_FF = 2816
FFN_CONV = 3

kernel_name = "hybrid_s5_ssd_rwkv7_hgrn2_convffn"


def rms_norm(x, g, eps=RMS_EPS):
    xf = x.astype(jnp.float32)
    y = xf * lax.rsqrt(jnp.mean(xf * xf, axis=-1, keepdims=True) + eps)
    return (y * g.astype(jnp.float32)).astype(x.dtype)


def causal_dwconv(x, w, b):
    width, ch = w.shape
    y = lax.conv_general_dilated(
        x, w[:, None, :].astype(x.dtype), window_strides=(1,), padding=[(width - 1, 0)],
        dimension_numbers=("NWC", "WIO", "NWC"), feature_group_count=ch)
    return y + b.astype(x.dtype)


def front_pad(t, n):
    return jnp.pad(t, [(0, 0), (n, 0)] + [(0, 0)] * (t.ndim - 2))


def token_shift(p, mu):
    prev = jnp.pad(p, ((0, 0), (1, 0), (0, 0)))[:, :-1]
    return p + (prev - p) * mu.astype(p.dtype)


def _complex_affine_combine(e1, e2):
    a1r, a1i, b1r, b1i = e1
    a2r, a2i, b2r, b2i = e2
    return (a2r * a1r - a2i * a1i,
            a2r * a1i + a2i * a1r,
            a2r * b1r - a2i * b1i + b2r,
            a2r * b1i + a2i * b1r + b2i)


def s5_mixer(u, lam_re, lam_im, log_dt, b_re, b_im, c_re, c_im, d_skip, w_glu, b_glu):
    f32 = jnp.float32
    bsz, length, _ = u.shape
    ug = u.astype(f32).reshape(bsz, length, S5_GROUPS, S5_GROUP)
    lr, li = lam_re.astype(f32), lam_im.astype(f32)
    dt = jnp.exp(log_dt.astype(f32))[:, None]
    mag = jnp.exp(lr * dt)
    ab_re, ab_im = mag * jnp.cos(li * dt), mag * jnp.sin(li * dt)
    den = lr * lr + li * li
    zr, zi = ab_re - 1.0, ab_im
    f_re = (zr * lr + zi * li) / den
    f_im = (zi * lr - zr * li) / den
    br, bi = b_re.astype(f32), b_im.astype(f32)
    bb_re = f_re[..., None] * br - f_im[..., None] * bi
    bb_im = f_re[..., None] * bi + f_im[..., None] * br
    bu_re = jnp.einsum("blgc,gpc->blgp", ug, bb_re)
    bu_im = jnp.einsum("blgc,gpc->blgp", ug, bb_im)
    a_re = jnp.broadcast_to(ab_re, (1, length) + ab_re.shape)
    a_im = jnp.broadcast_to(ab_im, (1, length) + ab_im.shape)
    _, _, h_re, h_im = lax.associative_scan(
        _complex_affine_combine, (a_re, a_im, bu_re, bu_im), axis=1)
    y = (jnp.einsum("blgp,gcp->blgc", h_re, c_re.astype(f32))
         - jnp.einsum("blgp,gcp->blgc", h_im, c_im.astype(f32))
         + d_skip.astype(f32).reshape(S5_GROUPS, S5_GROUP) * ug)
    y = jax.nn.gelu(y.reshape(bsz, length, S5_WIDTH))
    out = y * jax.nn.sigmoid(y @ w_glu.astype(f32) + b_glu.astype(f32))
    return out.astype(u.dtype)


def ssd_mixer(z, xbc, dt_raw, conv_w, conv_b, dt_bias, a_log, d_skip, norm_w):
    f32 = jnp.float32
    bsz, length, _ = z.shape
    pad = CHUNK - N_META
    n_chunks = (length + pad) // CHUNK
    hpg = SSD_HEADS // SSD_GROUPS
    xbc = jax.nn.silu(causal_dwconv(xbc, conv_w, conv_b)).astype(f32)
    dt = jax.nn.softplus(dt_raw.astype(f32) + dt_bias.astype(f32))
    xbc, dt = front_pad(xbc, pad), front_pad(dt, pad)
    gn = SSD_GROUPS * SSD_STATE
    xs = xbc[..., :SSD_INNER].reshape(bsz, n_chunks, CHUNK, SSD_GROUPS, hpg, SSD_HEADDIM)
    bmat = xbc[..., SSD_INNER:SSD_INNER + gn].reshape(bsz, n_chunks, CHUNK, SSD_GROUPS, SSD_STATE)
    cmat = xbc[..., SSD_INNER + gn:].reshape(bsz, n_chunks, CHUNK, SSD_GROUPS, SSD_STATE)
    dt = dt.reshape(bsz, n_chunks, CHUNK, SSD_GROUPS, hpg)
    a = -jnp.exp(a_log.astype(f32)).reshape(SSD_GROUPS, hpg)
    xdt = xs * dt[..., None]
    acum = jnp.cumsum(dt * a, axis=2)
    causal = jnp.tril(jnp.ones((CHUNK, CHUNK), dtype=bool))
    seg = acum[:, :, :, None] - acum[:, :, None, :]
    seg = jnp.exp(jnp.where(causal[:, :, None, None], seg, -jnp.inf))
    cb = jnp.einsum("bclgn,bcsgn->bclsg", cmat, bmat)
    y_diag = jnp.einsum("bclsg,bclsgj,bcsgjp->bclgjp", cb, seg, xdt)
    decay_to_end = jnp.exp(acum[:, :, -1:] - acum)
    chunk_states = jnp.einsum("bclgn,bclgj,bclgjp->bcgjpn", bmat, decay_to_end, xdt)
    chunk_decay = jnp.exp(acum[:, :, -1])

    def step(state, inp):
        st, dec = inp
        return state * dec[..., None, None] + st, state

    init = jnp.zeros((bsz, SSD_GROUPS, hpg, SSD_HEADDIM, SSD_STATE), f32)
    _, prev = lax.scan(step, init, (jnp.moveaxis(chunk_states, 1, 0), jnp.moveaxis(chunk_decay, 1, 0)))
    prev = jnp.moveaxis(prev, 0, 1)
    y_off = jnp.einsum("bclgn,bcgjpn,bclgj->bclgjp", cmat, prev, jnp.exp(acum))
    y = y_diag + y_off + xs * d_skip.astype(f32).reshape(SSD_GROUPS, hpg)[:, :, None]
    y = y.reshape(bsz, n_chunks * CHUNK, SSD_INNER)[:, pad:]
    y = y * jax.nn.silu(z.astype(f32))
    y = rms_norm(y.reshape(bsz, length, SSD_GROUPS, SSD_INNER // SSD_GROUPS),
                 norm_w.reshape(SSD_GROUPS, SSD_INNER // SSD_GROUPS))
    return y.reshape(bsz, length, SSD_INNER).astype(z.dtype)


def rwkv7_mixer(p, w0, w2, a0, a2, g2, k_k, k_a, r_k, ln_w, ln_b, v_first, v0=None, v2=None):
    f32 = jnp.float32
    p = p.astype(f32)
    bsz, length, _ = p.shape
    nh, hd, wd = RWKV_HEADS, RWKV_HEADDIM, RWKV_WIDTH
    r, k, v = p[..., :wd], p[..., wd:2 * wd], p[..., 2 * wd:3 * wd]
    o0 = 3 * wd
    pw = p[..., o0:o0 + LORA_W]
    pa = p[..., o0 + LORA_W:o0 + LORA_W + LORA_A]
    pg = p[..., o0 + LORA_W + LORA_A:RWKV_IN]
    w_log = -jax.nn.softplus(-(w0 + jnp.tanh(pw) @ w2)) - 0.5
    decay = jnp.exp(-jnp.exp(w_log))
    a = jax.nn.sigmoid(a0 + pa @ a2)
    if v_first is None:
        v_first = v
    else:
        pv = p[..., RWKV_IN:]
        v = v + (v_first - v) * jax.nn.sigmoid(v0 + pv @ v2)
    g = jax.nn.sigmoid(pg) @ g2

    def heads(t):
        return t.reshape(bsz, length, nh, hd)

    kk = heads(k * k_k)
    kk = kk * lax.rsqrt(jnp.maximum(jnp.sum(kk * kk, axis=-1, keepdims=True), 1e-24))
    k = k * (1.0 + (a - 1.0) * k_a)
    rh, wh, kh, vh, ah = heads(r), heads(decay), heads(k), heads(v), heads(a)

    def step(state, inp):
        r_t, w_t, k_t, v_t, kk_t, a_t = inp
        sa = jnp.einsum("bhvk,bhk->bhv", state, kk_t)
        state = (state * w_t[:, :, None, :]
                 - sa[..., None] * (kk_t * a_t)[:, :, None, :]
                 + v_t[..., None] * k_t[:, :, None, :])
        return state, jnp.einsum("bhvk,bhk->bhv", state, r_t)

    init = jnp.zeros((bsz, nh, hd, hd), f32)
    xs = tuple(jnp.moveaxis(t, 1, 0) for t in (rh, wh, kh, vh, kk, ah))
    _, o = lax.scan(step, init, xs)
    o = jnp.moveaxis(o, 0, 1)
    mean = jnp.mean(o, axis=-1, keepdims=True)
    var = jnp.mean(jnp.square(o - mean), axis=-1, keepdims=True)
    o = (o - mean) * lax.rsqrt(var + GN_EPS) * ln_w.reshape(nh, hd) + ln_b.reshape(nh, hd)
    o = o + jnp.sum(rh * kh * r_k.reshape(nh, hd), axis=-1, keepdims=True) * vh
    return o.reshape(bsz, length, wd) * g, v_first


def hgrn2_mixer(p, lb, norm_w):
    f32 = jnp.float32
    bsz, length, _ = p.shape
    nh, hd = HGRN_HEADS, HGRN_HEADDIM
    q, f, i, og = jnp.split(p.astype(f32), 4, axis=-1)
    q = jax.nn.silu(q)
    forget = lb + (1.0 - lb) * jax.nn.sigmoid(f)
    log_f = jnp.log(forget)
    k = 1.0 - forget
    pad = CHUNK - N_META
    n_chunks = (length + pad) // CHUNK

    def chunked(t):
        t = front_pad(t, pad).reshape(bsz, n_chunks, CHUNK, nh, hd)
        return t.transpose(1, 0, 3, 2, 4)

    causal = jnp.tril(jnp.ones((CHUNK, CHUNK), dtype=bool))

    def step(state, inp):
        qc, kc, vc, gc = inp
        gcum = jnp.cumsum(gc, axis=2)
        rel = gcum[:, :, :, None, :] - gcum[:, :, None, :, :]
        rel = jnp.exp(jnp.where(causal[:, :, None], rel, -jnp.inf))
        att = jnp.einsum("bhlk,bhsk,bhlsk->bhls", qc, kc, rel)
        out = att @ vc + jnp.einsum("bhlk,bhkv->bhlv", qc * jnp.exp(gcum), state)
        g_end = gcum[:, :, -1:]
        state = (jnp.exp(g_end[:, :, 0])[..., None] * state
                 + jnp.einsum("bhsk,bhsv->bhkv", kc * jnp.exp(g_end - gcum), vc))
        return state, out

    init = jnp.zeros((bsz, nh, hd, hd), f32)
    _, o = lax.scan(step, init, (chunked(q), chunked(k), chunked(i), chunked(log_f)))
    o = o.transpose(1, 0, 3, 2, 4).reshape(bsz, n_chunks * CHUNK, nh, hd)[:, pad:]
    o = rms_norm(o, norm_w.reshape(nh, hd))
    return o.reshape(bsz, length, HGRN_WIDTH) * jax.nn.silu(og)


def conv_ffn(x, w_up, conv_w, conv_b, w_down):
    h = causal_dwconv(x @ w_up, conv_w, conv_b)
    gate, val = jnp.split(h, 2, axis=-1)
    return (jax.nn.gelu(gate, approximate=True) * val) @ w_down


def setup_inputs(seed: int = 0) -> dict:
    key = jax.random.key(seed)
    keys = iter(jax.random.split(key, 96))
    f32 = jnp.float32

    def nrm(shape, scale):
        return jax.random.normal(next(keys), shape, f32) * scale

    def unif(shape, lo, hi):
        return jax.random.uniform(next(keys), shape, f32, lo, hi)

    def gain(shape):
        return 1.0 + nrm(shape, 0.02)

    dt0 = jnp.exp(unif((N_EVEN, SSD_HEADS), math.log(1e-3), math.log(1e-1)))
    ramp = (jnp.arange(RWKV_WIDTH, dtype=f32) / (RWKV_WIDTH - 1)) ** 0.85
    return {
        "x": nrm((BATCH, SEQ, D_MODEL), 1.0),
        "meta": nrm((N_META, D_MODEL), 1.0),
        "norm_mix_pre": gain((DEPTH, D_MODEL)),
        "norm_mix_post": gain((DEPTH, D_MODEL)),
        "norm_ffn_pre": gain((DEPTH, D_MODEL)),
        "norm_ffn_post": gain((DEPTH, D_MODEL)),
        "mix_w_out": nrm((DEPTH, MIX_WIDTH, D_MODEL), MIX_WIDTH ** -0.5),
        "ffn_w_up": nrm((DEPTH, D_MODEL, 2 * D_FF), D_MODEL ** -0.5),
        "ffn_conv_w": nrm((DEPTH, FFN_CONV, 2 * D_FF), FFN_CONV ** -0.5),
        "ffn_conv_b": nrm((DEPTH, 2 * D_FF), 0.02),
        "ffn_w_down": nrm((DEPTH, D_FF, D_MODEL), D_FF ** -0.5),
        "ev_w_in": nrm((N_EVEN, D_MODEL, EVEN_IN), D_MODEL ** -0.5),
        "s5_lam_re": -0.5 + nrm((N_EVEN, S5_GROUPS, S5_STATE), 0.01),
        "s5_lam_im": math.pi * jnp.arange(S5_STATE, dtype=f32) + nrm((N_EVEN, S5_GROUPS, S5_STATE), 0.01),
        "s5_log_dt": unif((N_EVEN, S5_GROUPS), math.log(1e-3), math.log(1e-1)),
        "s5_b_re": nrm((N_EVEN, S5_GROUPS, S5_STATE, S5_GROUP), (2 * S5_GROUP) ** -0.5),
        "s5_b_im": nrm((N_EVEN, S5_GROUPS, S5_STATE, S5_GROUP), (2 * S5_GROUP) ** -0.5),
        "s5_c_re": nrm((N_EVEN, S5_GROUPS, S5_GROUP, S5_STATE), (2 * S5_STATE) ** -0.5),
        "s5_c_im": nrm((N_EVEN, S5_GROUPS, S5_GROUP, S5_STATE), (2 * S5_STATE) ** -0.5),
        "s5_d": nrm((N_EVEN, S5_WIDTH), 1.0),
        "s5_w_glu": nrm((N_EVEN, S5_WIDTH, S5_WIDTH), S5_WIDTH ** -0.5),
        "s5_b_glu": nrm((N_EVEN, S5_WIDTH), 0.02),
        "ssd_conv_w": nrm((N_EVEN, SSD_CONV, SSD_XBC), SSD_CONV ** -0.5),
        "ssd_conv_b": nrm((N_EVEN, SSD_XBC), 0.02),
        "ssd_dt_bias": dt0 + jnp.log(-jnp.expm1(-dt0)),
        "ssd_a_log": jnp.log(unif((N_EVEN, SSD_HEADS), 1.0, 16.0)),
        "ssd_d": 1.0 + nrm((N_EVEN, SSD_HEADS), 0.1),
        "ssd_norm": gain((N_EVEN, SSD_INNER)),
        "od_w_in": nrm((N_ODD, D_MODEL, ODD_IN), D_MODEL ** -0.5),
        "rw_mu": unif((N_ODD, RWKV_IN), 0.0, 1.0),
        "rw_w0": ramp * 5.0 - 6.5 + nrm((N_ODD, RWKV_WIDTH), 0.1),
        "rw_w2": nrm((N_ODD, LORA_W, RWKV_WIDTH), 0.1),
        "rw_a0": nrm((N_ODD, RWKV_WIDTH), 0.1),
        "rw_a2": nrm((N_ODD, LORA_A, RWKV_WIDTH), 0.1),
        "rw_g2": nrm((N_ODD, LORA_G, RWKV_WIDTH), LORA_G ** -0.5),
        "rw_k_k": 0.85 + nrm((N_ODD, RWKV_WIDTH), 0.02),
        "rw_k_a": 1.0 + nrm((N_ODD, RWKV_WIDTH), 0.02),
        "rw_r_k": -0.04 + nrm((N_ODD, RWKV_WIDTH), 0.1),
        "rw_ln_w": gain((N_ODD, RWKV_WIDTH)),
        "rw_ln_b": nrm((N_ODD, RWKV_WIDTH), 0.02),
        "rw_w_vin": nrm((N_ODD - 1, D_MODEL, LORA_V), D_MODEL ** -0.5),
        "rw_mu_v": unif((N_ODD - 1, LORA_V), 0.0, 1.0),
        "rw_v0": 1.0 + nrm((N_ODD - 1, RWKV_WIDTH), 0.1),
        "rw_v2": nrm((N_ODD - 1, LORA_V, RWKV_WIDTH), 0.1),
        "hg_lb_raw": nrm((N_ODD, HGRN_WIDTH), 0.1),
        "hg_norm": gain((N_ODD, HGRN_WIDTH)),
    }


def reference(x, meta, norm_mix_pre, norm_mix_post, norm_ffn_pre, norm_ffn_post, mix_w_out,
              ffn_w_up, ffn_conv_w, ffn_conv_b, ffn_w_down, ev_w_in,
              s5_lam_re, s5_lam_im, s5_log_dt, s5_b_re, s5_b_im, s5_c_re, s5_c_im, s5_d, s5_w_glu, s5_b_glu,
              ssd_conv_w, ssd_conv_b, ssd_dt_bias, ssd_a_log, ssd_d, ssd_norm,
              od_w_in, rw_mu, rw_w0, rw_w2, rw_a0, rw_a2, rw_g2, rw_k_k, rw_k_a, rw_r_k, rw_ln_w, rw_ln_b,
              rw_w_vin, rw_mu_v, rw_v0, rw_v2, hg_lb_raw, hg_norm):
    bsz = x.shape[0]
    h = jnp.concatenate(
        [jnp.broadcast_to(meta.astype(x.dtype)[None], (bsz, N_META, D_MODEL)), x], axis=1)
    lb_w = jax.nn.softmax(hg_lb_raw.astype(jnp.float32), axis=0)
    lb_table = jnp.cumsum(lb_w, axis=0) - lb_w[0]
    v_first = None
    s1 = S5_WIDTH
    s2 = s1 + SSD_INNER
    s3 = s2 + SSD_XBC
    for layer in range(DEPTH):
        hn = rms_norm(h, norm_mix_pre[layer])
        if layer % 2 == 0:
            e = layer // 2
            p = hn @ ev_w_in[e]
            y_a = s5_mixer(p[..., :s1], s5_lam_re[e], s5_lam_im[e], s5_log_dt[e], s5_b_re[e], s5_b_im[e],
                           s5_c_re[e], s5_c_im[e], s5_d[e], s5_w_glu[e], s5_b_glu[e])
            y_b = ssd_mixer(p[..., s1:s2], p[..., s2:s3], p[..., s3:], ssd_conv_w[e], ssd_conv_b[e],
                            ssd_dt_bias[e], ssd_a_log[e], ssd_d[e], ssd_norm[e])
            y = jnp.concatenate([y_a.astype(h.dtype), y_b.astype(h.dtype)], axis=-1)
        else:
            o = layer // 2
            if o == 0:
                p = hn @ od_w_in[o]
                p_rw = token_shift(p[..., :RWKV_IN], rw_mu[o])
                y_c, v_first = rwkv7_mixer(p_rw, rw_w0[o], rw_w2[o], rw_a0[o], rw_a2[o], rw_g2[o], rw_k_k[o],
                                           rw_k_a[o], rw_r_k[o], rw_ln_w[o], rw_ln_b[o], None)
            else:
                w_in = jnp.concatenate([od_w_in[o], rw_w_vin[o - 1]], axis=1)
                p = hn @ w_in
                p_rw = token_shift(jnp.concatenate([p[..., :RWKV_IN], p[..., ODD_IN:]], axis=-1),
                                   jnp.concatenate([rw_mu[o], rw_mu_v[o - 1]]))
                y_c, v_first = rwkv7_mixer(p_rw, rw_w0[o], rw_w2[o], rw_a0[o], rw_a2[o], rw_g2[o], rw_k_k[o],
                                           rw_k_a[o], rw_r_k[o], rw_ln_w[o], rw_ln_b[o], v_first,
                                           rw_v0[o - 1], rw_v2[o - 1])
            y_d = hgrn2_mixer(p[..., RWKV_IN:ODD_IN], lb_table[o], hg_norm[o])
            y = jnp.concatenate([y_c.astype(h.dtype), y_d.astype(h.dtype)], axis=-1)
        h = h + rms_norm(y @ mix_w_out[layer], norm_mix_post[layer])
        hn = rms_norm(h, norm_ffn_pre[layer])
        h = h + rms_norm(conv_ffn(hn, ffn_w_up[layer], ffn_conv_w[layer], ffn_conv_b[layer], ffn_w_down[layer]),
                         norm_ffn_post[layer])
    return h[:, N_META:]
```

```python
import math
import numpy as np
import concourse.bass as bass
import concourse.mybir as mybir
from concourse.bass_utils import run_bass_kernel_spmd

F32 = mybir.dt.float32
BF16 = mybir.dt.bfloat16
I32 = mybir.dt.int32
U32 = mybir.dt.uint32
AF = mybir.ActivationFunctionType
ALU = mybir.AluOpType
AX = mybir.AxisListType

D = 1024
NB = 8
SEQ = 4096
NMETA = 16
TREAL = SEQ + NMETA
CH = 128
CPT = 3
NQ = 2 * CPT
NT = CH * CPT
NTILES = (TREAL + NT - 1) // NT
TPAD = NTILES * NT
DEPTH = 4
DFF = 2816
NJ = DFF // 128
EPS = 1e-6
GN_EPS = 64e-5
SAME_SYNC = True


class StopBuild(Exception):
    pass


class Reg:
    __slots__ = ("w", "r")

    def __init__(self):
        self.w = None
        self.r = {}


class View:
    __slots__ = ("ap", "regs", "excl")

    def __init__(self, ap, regs, excl=False):
        self.ap = ap
        self.regs = regs
        self.excl = excl


class Buf:
    def __init__(self, S, name, shape, dtype, nreg=1, space="sbuf"):
        nc = S.nc
        if space == "sbuf":
            self.t = nc.alloc_sbuf_tensor(name, list(shape), dtype, align_bytes=64)
            S.sbuf_bytes += int(np.prod(shape[1:])) * (2 if dtype == BF16 else 4)
        else:
            self.t = nc.alloc_psum_tensor(name, list(shape), dtype)
        self.regs = [Reg() for _ in range(nreg)]

    def __call__(self, ap=None, r=None):
        if ap is None:
            ap = self.t[:]
        if r is None:
            regs = self.regs
        elif isinstance(r, int):
            regs = [self.regs[r]]
        else:
            regs = [self.regs[i] for i in r]
        return View(ap, regs)


def dram(ap):
    return View(ap, [])


class Sched:
    def __init__(self, nc):
        self.nc = nc
        self.E = {"pe": nc.tensor, "dve": nc.vector, "act": nc.scalar, "pool": nc.gpsimd, "sp": nc.sync}
        self.sem = {k: nc.alloc_semaphore("sem_" + k) for k in ("pe", "dve", "act", "pool")}
        self.cnt = {k: 0 for k in self.sem}
        self.seen = {k: {} for k in self.E}
        self.dq = {}
        self.sbuf_bytes = 0
        self.ninst = 0
        self.out_toks = []
        self.nops = 0
        self.max_ops = None

    def _deps(self, outs, ins):
        deps = {}

        def need(tok):
            if tok is None:
                return
            cur = deps.get(tok[0])
            if cur is None or cur[1] < tok[1]:
                deps[tok[0]] = tok

        for v in ins:
            for rg in v.regs:
                need(rg.w)
        for v in outs:
            for rg in v.regs:
                need(rg.w)
                for tok in rg.r.values():
                    need(tok)
        return deps

    def _wait(self, eng, deps):
        E = self.E[eng]
        own = self.sem[eng].num if eng in self.sem else None
        for sid, tok in deps.items():
            if sid == own and (eng == "pe" or not SAME_SYNC):
                continue
            if self.seen[eng].get(sid, 0) < tok[1]:
                if self.max_ops is not None and self.nops >= self.max_ops - 3:
                    print("  WAIT", eng, "on", tok[2].name, tok[1], "cnts", self.cnt)
                E.wait_ge(tok[2], tok[1])
                self.seen[eng][sid] = tok[1]
                self.ninst += 1

    def _mark(self, tok, outs, ins):
        for v in ins:
            for rg in v.regs:
                cur = rg.r.get(tok[0])
                if cur is None or cur[1] < tok[1]:
                    rg.r[tok[0]] = tok
        for v in outs:
            for rg in v.regs:
                rg.w = tok
                rg.r = {}

    def op(self, eng, fn, outs, ins, signal=True):
        xs = [v for v in ins if v.excl]
        if xs:
            outs = list(outs) + xs
            ins = [v for v in ins if not v.excl]
        self._wait(eng, self._deps(outs, ins))
        inst = fn(self.E[eng])
        if self.max_ops is not None and self.nops >= self.max_ops - 3:
            try:
                print("  INST", eng, inst.concise())
            except Exception as ex:
                print("  INST?", ex, inst.ins)
        self.ninst += 1
        sem = self.sem[eng]
        if signal:
            self.cnt[eng] += 1
            inst.then_inc(sem, 1)
            tok = (sem.num, self.cnt[eng], sem)
        else:
            tok = (sem.num, self.cnt[eng] + 1, sem)
        self._mark(tok, outs, ins)
        self.nops += 1
        if self.max_ops is not None and self.nops >= self.max_ops:
            self.max_ops = None
            print("STOP at op", self.nops, eng)
            raise StopBuild()

    def dma(self, q, out, in_, is_output=False):
        if q not in self.dq:
            nr = 32 if q == "pool" else 8
            self.dq[q] = {"ring": [self.nc.alloc_semaphore("dsem_%s_%d" % (q, i)) for i in range(nr)], "n": 0, "toks": [None] * nr, "nr": nr}
        Q = self.dq[q]
        i = Q["n"]
        nr = Q["nr"]
        slot = i % nr
        deps = self._deps([out], [in_])
        if Q["toks"][slot] is not None:
            t = Q["toks"][slot]
            if t[0] not in deps or deps[t[0]][1] < t[1]:
                deps[t[0]] = t
        self._wait(q, deps)
        sem = Q["ring"][slot]
        val = 16 * (i // nr + 1)
        self.E[q].dma_start(out=out.ap, in_=in_.ap).then_inc(sem, 16)
        self.ninst += 1
        tok = (sem.num, val, sem)
        Q["toks"][slot] = tok
        Q["n"] += 1
        self._mark(tok, [out], [in_])
        if is_output:
            self.out_toks.append(tok)

    def finish(self):
        deps = {}
        for tok in self.out_toks:
            if tok[0] not in deps or deps[tok[0]][1] < tok[1]:
                deps[tok[0]] = tok
        for f in ("pe", "dve", "act", "pool"):
            if self.cnt[f]:
                deps[self.sem[f].num] = (self.sem[f].num, self.cnt[f], self.sem[f])
        for q, Q in self.dq.items():
            for t in Q["toks"]:
                if t is not None and (t[0] not in deps or deps[t[0]][1] < t[1]):
                    deps[t[0]] = t
        self._wait("sp", deps)

    def barrier(self):
        for eng in ("pe", "dve", "act", "pool"):
            deps = {}
            for f in ("pe", "dve", "act", "pool"):
                if (f == eng and eng == "pe") or self.cnt[f] == 0:
                    continue
                sem = self.sem[f]
                deps[sem.num] = (sem.num, self.cnt[f], sem)
            self._wait(eng, deps)

    def mm(self, out, lhsT, rhs, start=True, stop=True, signal=True):
        signal = True
        self.op("pe", lambda e: e.matmul(out.ap, lhsT=lhsT.ap, rhs=rhs.ap, start=start, stop=stop), [out], [lhsT, rhs], signal)

    def tr(self, out, in_, ident):
        self.op("pe", lambda e: e.transpose(out.ap, in_.ap, ident.ap), [out], [in_, ident])

    def tt(self, eng, out, a, b, op):
        self.op(eng, lambda e: e.tensor_tensor(out=out.ap, in0=a.ap, in1=b.ap, op=op), [out], [a, b])

    def ts(self, eng, out, a, s1, op0, s2=None, op1=None):
        ins = [a] + [s for s in (s1, s2) if isinstance(s, View)]
        a1 = s1.ap if isinstance(s1, View) else s1
        a2 = s2.ap if isinstance(s2, View) else s2
        if op1 is None:
            self.op(eng, lambda e: e.tensor_scalar(out=out.ap, in0=a.ap, scalar1=a1, scalar2=None, op0=op0), [out], ins)
        else:
            self.op(eng, lambda e: e.tensor_scalar(out=out.ap, in0=a.ap, scalar1=a1, scalar2=a2, op0=op0, op1=op1), [out], ins)

    def stt(self, eng, out, a, s, b, op0, op1):
        ins = [a, b] + ([s] if isinstance(s, View) else [])
        sa = s.ap if isinstance(s, View) else s
        self.op(eng, lambda e: e.scalar_tensor_tensor(out=out.ap, in0=a.ap, scalar=sa, in1=b.ap, op0=op0, op1=op1), [out], ins)

    def act(self, out, in_, func, bias=None, scale=None):
        ins = [in_] + [s for s in (bias, scale) if isinstance(s, View)]
        kw = {}
        if bias is not None:
            kw["bias"] = bias.ap if isinstance(bias, View) else bias
        if scale is not None:
            kw["scale"] = scale.ap if isinstance(scale, View) else scale
        self.op("act", lambda e: e.activation(out=out.ap, in_=in_.ap, func=func, **kw), [out], ins)

    def cp(self, eng, out, in_):
        if eng == "act":
            self.op("act", lambda e: e.copy(out=out.ap, in_=in_.ap), [out], [in_])
        else:
            self.op(eng, lambda e: e.tensor_copy(out=out.ap, in_=in_.ap), [out], [in_])

    def scan(self, out, d0, d1, init, op0, op1):
        ins = [d0, d1] + ([init] if isinstance(init, View) else [])
        ia = init.ap if isinstance(init, View) else init
        self.op("dve", lambda e: e.tensor_tensor_scan(out=out.ap, data0=d0.ap, data1=d1.ap, initial=ia, op0=op0, op1=op1), [out], ins)

    def recip(self, out, in_):
        self.op("dve", lambda e: e.reciprocal(out=out.ap, in_=in_.ap), [out], [in_])

    def memset(self, eng, out, val):
        self.op(eng, lambda e: e.memset(out.ap, val), [out], [])

    def cpred(self, out, mask, data):
        self.op("dve", lambda e: e.copy_predicated(out=out.ap, mask=mask.ap, data=data.ap), [out], [mask, data])


class Packer:
    def __init__(self):
        self.cols = []
        self.off = {}
        self.n = 0

    def add(self, name, arr):
        arr = np.asarray(arr, dtype=np.float32)
        assert arr.shape[0] == 128, (name, arr.shape)
        arr = arr.reshape(128, -1)
        self.off[name] = (self.n, arr.shape[1])
        self.cols.append(arr)
        self.n += arr.shape[1]

    def pack(self):
        return np.ascontiguousarray(np.concatenate(self.cols, axis=1))


def feat(v):
    v = np.asarray(v, dtype=np.float32)
    return np.ascontiguousarray(v.reshape(-1, 128).T)


def rep(v):
    v = np.asarray(v, dtype=np.float32).reshape(1, -1)
    return np.ascontiguousarray(np.broadcast_to(v, (128, v.shape[1])))


def wtiles(W):
    K, N = W.shape
    Np = ((N + 127) // 128) * 128
    Kp = ((K + 1023) // 1024) * 1024
    Wp = np.zeros((Kp, Np), np.float32)
    Wp[:K, :N] = W
    out = []
    for kg in range(Kp // 1024):
        blk = Wp[kg * 1024:(kg + 1) * 1024]
        out.append(blk.reshape(8, 128, Np // 128, 128).transpose(2, 1, 0, 3))
    return out


def consts_host():
    P = Packer()
    idx = np.arange(128)
    P.add("ident", np.eye(128))
    P.add("ones", np.ones((128, 128)))
    le = (idx[:, None] <= idx[None, :]).astype(np.float32)
    lt = (idx[:, None] < idx[None, :]).astype(np.float32)
    P.add("le", le)
    P.add("lt", lt)
    P.add("nle", -le)
    P.add("nlt", -lt)
    P.add("ngt", -(idx[:, None] > idx[None, :]).astype(np.float32))
    blk = (idx[:, None] // 64 == idx[None, :] // 64).astype(np.float32)
    P.add("blk64", blk)
    P.add("le64", le * blk)
    P.add("ramp", rep(np.arange(1, 129)))
    return P


ODD_ORDER0 = [12, 13, 0, 4, 8, 1, 5, 9, 2, 6, 10, 3, 7, 11] + [14 + h + 4 * s for h in range(4) for s in range(4)]
ODD_ORDER1 = [12, 13, 30, 0, 4, 8, 1, 5, 9, 2, 6, 10, 3, 7, 11] + [14 + h + 4 * s for h in range(4) for s in range(4)]


def layer_host(inp, L):
    P = Packer()
    B = Packer()
    g = lambda n: np.asarray(inp[n], dtype=np.float32)
    P.add("n_mix_pre", feat(g("norm_mix_pre")[L]))
    P.add("n_mix_post", feat(g("norm_mix_post")[L]))
    P.add("n_ffn_pre", feat(g("norm_ffn_pre")[L]))
    P.add("n_ffn_post", feat(g("norm_ffn_post")[L]))
    cw = g("ffn_conv_w")[L]
    cb = g("ffn_conv_b")[L]
    order = np.concatenate([np.concatenate([np.arange(j * 128, (j + 1) * 128), DFF + np.arange(j * 128, (j + 1) * 128)]) for j in range(NJ)])
    P.add("ffn_cw", np.stack([feat(cw[k][order]) for k in range(3)], axis=2))
    P.add("ffn_cb", feat(cb[order]))
    tiles = []
    if L % 2 == 0:
        e = L // 2
        tiles.append(wtiles(g("ev_w_in")[e])[0])
        lr, li, ldt = g("s5_lam_re")[e], g("s5_lam_im")[e], g("s5_log_dt")[e]
        st = lambda a: np.ascontiguousarray(a.reshape(8, 2, 64).transpose(1, 2, 0).reshape(128, 8))
        P.add("s5_lr_s", st(lr))
        P.add("s5_li_s", st(li))
        P.add("s5_ldt_s", st(np.repeat(ldt[:, None], 64, axis=1)))
        B.add("s5_lr_b", rep(lr.reshape(-1)))
        B.add("s5_li_b", rep(li.reshape(-1)))
        B.add("s5_ldt_b", rep(np.repeat(ldt, 64)))
        br, bi = g("s5_b_re")[e], g("s5_b_im")[e]

        def bd(b):
            o = np.zeros((128, 2, 8, 64), np.float32)
            for kt in range(2):
                for gl in range(8):
                    o[gl * 16:(gl + 1) * 16, kt, gl, :] = b[8 * kt + gl].T
            return o.reshape(128, 1024)
        B.add("s5_br_bd", bd(br))
        B.add("s5_bi_bd", bd(bi))
        cr, ci = g("s5_c_re")[e], g("s5_c_im")[e]

        def cpad(c):
            o = np.zeros((128, 8, 128), np.float32)
            for j in range(8):
                for gp in range(2):
                    gg = 2 * j + gp
                    col = (gg % 8) * 16
                    o[gp * 64:(gp + 1) * 64, j, col:col + 16] = c[gg].T
            return o.reshape(128, 1024)
        B.add("s5_cr_pad", cpad(cr))
        B.add("s5_ci_pad", cpad(ci))
        B.add("s5_wglu", g("s5_w_glu")[e].reshape(2, 128, 256).transpose(1, 0, 2))
        P.add("s5_d", feat(g("s5_d")[e]))
        P.add("s5_bglu", feat(g("s5_b_glu")[e]))
        scw = g("ssd_conv_w")[e]
        P.add("ssd_cw", np.stack([feat(scw[k]) for k in range(4)], axis=2))
        P.add("ssd_cb", feat(g("ssd_conv_b")[e]))
        dtb = np.zeros((128, 1), np.float32)
        dtb[:12, 0] = g("ssd_dt_bias")[e]
        P.add("ssd_dtb", dtb)
        P.add("ssd_alog", rep(g("ssd_a_log")[e]))
        P.add("ssd_d", feat(np.repeat(g("ssd_d")[e], 64)))
        P.add("ssd_norm", feat(g("ssd_norm")[e]))
    else:
        o = L // 2
        w_in = g("od_w_in")[o]
        if o > 0:
            w_in = np.concatenate([w_in, g("rw_w_vin")[o - 1]], axis=1)
        wt = wtiles(w_in)[0]
        tiles.append(np.stack([wt[i] for i in (ODD_ORDER1 if o > 0 else ODD_ORDER0)]))
        mu = g("rw_mu")[o]
        if o > 0:
            mu = np.concatenate([mu, g("rw_mu_v")[o - 1], np.zeros(96, np.float32)])
        else:
            mu = np.concatenate([mu, np.zeros(128, np.float32)])
        P.add("rw_mu", feat(mu))
        P.add("rw_w0", feat(g("rw_w0")[o]))
        P.add("rw_a0", feat(g("rw_a0")[o]))
        B.add("rw_wa2", np.concatenate([g("rw_w2")[o], g("rw_a2")[o]], axis=0))
        B.add("rw_g2", g("rw_g2")[o])
        v2 = np.zeros((128, 512), np.float32)
        if o > 0:
            v2[:32] = g("rw_v2")[o - 1]
            P.add("rw_v0", feat(g("rw_v0")[o - 1]))
        B.add("rw_v2", v2)
        P.add("rw_k_k", feat(g("rw_k_k")[o]))
        P.add("rw_k_a", feat(g("rw_k_a")[o]))
        P.add("rw_r_k", feat(g("rw_r_k")[o]))
        P.add("rw_ln_w", feat(g("rw_ln_w")[o]))
        P.add("rw_ln_b", feat(g("rw_ln_b")[o]))
        P.add("hg_lb_raw0", feat(g("hg_lb_raw")[0]))
        P.add("hg_lb_raw1", feat(g("hg_lb_raw")[1]))
        P.add("hg_norm", feat(g("hg_norm")[o]))
    tiles.append(wtiles(g("mix_w_out")[L])[0])
    up = wtiles(g("ffn_w_up")[L])[0]
    tiles.append(np.stack([up[j + (0 if s == 0 else NJ)] for j in range(NJ) for s in (0, 1)]))
    dn = wtiles(g("ffn_w_down")[L])
    tiles.append(np.stack([dn[kg][m] for m in range(8) for kg in range(3)]))
    ws = np.ascontiguousarray(np.concatenate(tiles, axis=0).reshape(-1, 128, 1024))
    return P, B, ws
def build_program(pk_consts, pk_layers, pk_big, n_ws, n_layers=DEPTH, n_tiles=NTILES, dbg=None):
    nc = bass.Bass("TRN2", target_bir_lowering=False)
    S = Sched(nc)
    if dbg is not None and dbg[0] == "nops":
        S.max_ops = dbg[1]
    xT = nc.dram_tensor("xT", [8, 128, TPAD], F32, kind="ExternalInput").ap()
    oT = nc.dram_tensor("oT", [8, 128, TPAD], F32, kind="ExternalOutput").ap()
    cst_d = nc.dram_tensor("consts", [128, pk_consts.n], F32, kind="ExternalInput").ap()
    pv_d = [nc.dram_tensor("pv%d" % L, [128, pk_layers[L].n], F32, kind="ExternalInput").ap() for L in range(DEPTH)]
    pb_d = [nc.dram_tensor("pb%d" % L, [128, pk_big[L].n], F32, kind="ExternalInput").ap() for L in range(DEPTH)]
    ws_d = [nc.dram_tensor("ws%d" % L, [n_ws[L], 128, 1024], F32, kind="ExternalInput").ap() for L in range(DEPTH)]
    wsb_d = [nc.dram_tensor("wsb%d" % L, [n_ws[L], 128, 1024], BF16, kind="Internal").ap() for L in range(DEPTH)]
    WCH = 4
    wsb_regs = [[Reg() for _ in range((n_ws[L] + WCH - 1) // WCH)] for L in range(DEPTH)]
    dbg_d = None
    if dbg is not None:
        dbg_d = nc.dram_tensor("dbg", [8, 128, NT], F32, kind="ExternalOutput").ap()

    uid = [0]

    def mk(name, shape, dtype, nreg=1):
        uid[0] += 1
        return Buf(S, "%s_%d" % (name, uid[0]), shape, dtype, nreg)

    CST = mk("cst", [128, pk_consts.n], F32)
    PV = [mk("pvs", [128, pk_layers[L].n], F32) for L in range(DEPTH)]
    cst_bf = mk("cst_bf", [128, 3, 128], BF16)
    PSQ = [Reg() for _ in range(32)]
    PSt = nc.alloc_psum_tensor("psum_all", [128, 8, 512], F32)

    def C(name):
        o, n = pk_consts.off[name]
        return CST(CST.t[:, o:o + n])

    def pvv(L, name, a=None, b=None, p0=0, p1=128):
        o, n = pk_layers[L].off[name]
        if a is None:
            a, b = 0, n
        return PV[L](PV[L].t[p0:p1, o + a:o + b])

    def psr(bank, a, b):
        return [PSQ[bank]]

    def ps(bank, a=0, b=512, p0=0, p1=128):
        return View(PSt[p0:p1, bank, a:b], psr(bank, a, b), True)

    H = mk("h", [128, 8, NT], F32)
    HN = mk("hn", [128, 8, NT], BF16)
    SQ = mk("sq", [128, 2, NT], BF16, nreg=2)
    RSTD = mk("rstd", [128, NT], F32)
    TMP8 = mk("tmp8", [128, 8, NT], F32, nreg=8)
    YMIX = mk("ymix", [128, 8, NT], BF16, nreg=8)
    NWS = 6
    WRING = mk("wring", [128, NWS, 1024], BF16, nreg=NWS)
    FHALO = [mk("fhalo", [128, 2 * NJ, 2], F32) for L in range(DEPTH)]
    VFIRST = mk("vfirst", [128, 4, NT], F32, nreg=4)
    EPSB = mk("epsb", [128, 2], F32)
    wctr = [0]
    sqc = [0]

    ident = C("ident")
    ones_f = C("ones")
    ident_bf = cst_bf(cst_bf.t[:, 0, :])
    ones_bf = cst_bf(cst_bf.t[:, 1, :])
    blk64_bf = cst_bf(cst_bf.t[:, 2, :])

    def fl(v, pat, **kw):
        return View(v.ap.rearrange(pat, **kw), v.regs)

    def bc(v, ap):
        return View(ap, v.regs)

    LS = {}
    for L in range(n_layers):
        st = {}
        if L % 2 == 0:
            st["bb_re"] = mk("bbre", [128, 1024], BF16)
            st["bb_im"] = mk("bbim", [128, 1024], BF16)
            st["c_re"] = mk("cre", [128, 8, 128], F32)
            st["c_imn"] = mk("cimn", [128, 8, 128], F32)
            st["wglu"] = mk("wglu", [128, 2, 256], BF16)
            st["cos"] = mk("cos", [128, 8, CH], F32)
            st["sin"] = mk("sin", [128, 8, CH], F32)
            st["m"] = mk("m", [128, 8], F32)
            st["car_re"] = mk("carre", [128, 8], F32)
            st["car_im"] = mk("carim", [128, 8], F32)
            st["a_neg"] = mk("aneg", [128, 12], F32)
            st["Sst"] = mk("sst", [128, 12, 64], F32)
            st["prevT"] = mk("prevT", [128, 12, 128], BF16)
            st["xhalo"] = mk("xhalo", [128, 10, 3], F32)
        else:
            st["wa2"] = mk("wa2", [128, 512], BF16)
            st["g2"] = mk("g2", [128, 512], BF16)
            st["v2"] = mk("v2", [128, 512], BF16)
            st["lb"] = mk("lb", [128, 4], F32)
            st["oml"] = mk("oml", [128, 4], F32)
            st["T"] = mk("rwT", [128, 4, 64], F32, nreg=4)
            st["Tbf"] = mk("rwTbf", [128, 4, 64], BF16, nreg=4)
            st["HS"] = mk("hgS", [128, 4, 128], F32, nreg=4)
            st["HSbf"] = mk("hgSbf", [128, 4, 128], BF16, nreg=4)
            st["rwhalo"] = mk("rwhalo", [128, 15], F32)
        LS[L] = st

    arena_snap = (nc.sbuf_base, nc.sbuf_top)
    persist_bytes = S.sbuf_bytes

    def stop(stage):
        if dbg is not None and dbg[0] == "stop" and dbg[1] == stage:
            print("STOP stage", stage, "nops", S.nops)
            raise StopBuild()

    def arena_reset():
        S.barrier()
        nc.sbuf_base, nc.sbuf_top = arena_snap

    for L in range(n_layers):
        for ci in range(len(wsb_regs[L])):
            a, b = ci * WCH, min(n_ws[L], (ci + 1) * WCH)
            import os
            if os.environ.get("MAXCAST") and ci >= int(os.environ["MAXCAST"]):
                continue
            S.dma("pool", View(wsb_d[L][a:b], [wsb_regs[L][ci]]), dram(ws_d[L][a:b]))
    S.dma("sp", CST(), dram(cst_d))
    S.dma("sp", H(), dram(xT[:, :, 0:NT].rearrange("d p t -> p d t")))
    for L in range(n_layers):
        S.dma("sp", PV[L](), dram(pv_d[L]))
    for i, nm in enumerate(["ident", "ones", "blk64"]):
        S.cp("dve", cst_bf(cst_bf.t[:, i, :]), C(nm))
    for L in range(n_layers):
        S.memset("pool", FHALO[L](), 0.0)
    S.memset("dve", EPSB(EPSB.t[:, 0:1]), EPS)
    S.memset("dve", EPSB(EPSB.t[:, 1:2]), GN_EPS)
    eps_v = EPSB(EPSB.t[:, 0:1])
    gneps_v = EPSB(EPSB.t[:, 1:2])

    def sincos(x, sin_out, cos_out, tmp):
        ti = View(tmp.t[:, 3, :].bitcast(I32), tmp.regs)
        y, k, f = tmp(tmp.t[:, 0, :]), tmp(tmp.t[:, 1, :]), tmp(tmp.t[:, 2, :])
        for which, outv in ((0, sin_out), (1, cos_out)):
            S.ts("dve", y, x, 1.0 / (2 * math.pi), ALU.mult, 0.5 + 0.25 * which, ALU.add)
            S.cp("dve", ti, y)
            S.cp("dve", k, ti)
            S.tt("dve", f, y, k, ALU.subtract)
            S.ts("dve", k, f, 0.0, ALU.is_lt)
            S.tt("dve", f, f, k, ALU.add)
            S.ts("dve", f, f, 1.0, ALU.min)
            S.ts("dve", f, f, 2 * math.pi, ALU.mult, -math.pi, ALU.add)
            S.act(outv, f, AF.Sin)

    PVB = mk("pvb", [128, 8, 512], F32, nreg=8)
    TB = mk("s5tmpb", [128, 12, 512], F32, nreg=12)
    TS4 = mk("s5tmp4", [128, 4, 512], F32)

    def tb(i):
        return TB(TB.t[:, i, :], i)

    for L in range(n_layers):
        st = LS[L]

        def bgload(slot, name, a, b):
            o, n = pk_big[L].off[name]
            v = PVB(PVB.t[:, slot, 0:b - a], slot)
            S.dma("sp", v, dram(pb_d[L][:, o + a:o + b]))
            return v
        if L % 2 == 0:
            for half in range(2):
                hs = slice(half * 512, (half + 1) * 512)
                a_, b_ = half * 512, (half + 1) * 512
                lr, li, ldt = bgload(0, "s5_lr_b", a_, b_), bgload(1, "s5_li_b", a_, b_), bgload(2, "s5_ldt_b", a_, b_)
                br, bi = bgload(3, "s5_br_bd", a_, b_), bgload(4, "s5_bi_bd", a_, b_)
                crp, cip = bgload(5, "s5_cr_pad", a_, b_), bgload(6, "s5_ci_pad", a_, b_)
                dt, lrdt, lidt, mag, sn, cs, abr, abi, den, fre, fim, t1 = [tb(i) for i in range(12)]
                S.act(dt, ldt, AF.Exp)
                S.tt("dve", lrdt, lr, dt, ALU.mult)
                S.tt("dve", lidt, li, dt, ALU.mult)
                S.act(mag, lrdt, AF.Exp)
                sincos(lidt, sn, cs, TS4)
                S.tt("dve", abr, mag, cs, ALU.mult)
                S.tt("dve", abi, mag, sn, ALU.mult)
                S.tt("dve", den, lr, lr, ALU.mult)
                S.tt("dve", t1, li, li, ALU.mult)
                S.tt("dve", den, den, t1, ALU.add)
                S.recip(den, den)
                S.ts("dve", abr, abr, -1.0, ALU.add)
                S.tt("dve", fre, abr, lr, ALU.mult)
                S.tt("dve", t1, abi, li, ALU.mult)
                S.tt("dve", fre, fre, t1, ALU.add)
                S.tt("dve", fre, fre, den, ALU.mult)
                S.tt("dve", fim, abi, lr, ALU.mult)
                S.tt("dve", t1, abr, li, ALU.mult)
                S.tt("dve", fim, fim, t1, ALU.subtract)
                S.tt("dve", fim, fim, den, ALU.mult)
                S.tt("dve", t1, fre, br, ALU.mult)
                S.tt("dve", dt, fim, bi, ALU.mult)
                S.tt("dve", st["bb_re"](st["bb_re"].t[:, hs]), t1, dt, ALU.subtract)
                S.tt("dve", t1, fre, bi, ALU.mult)
                S.tt("dve", dt, fim, br, ALU.mult)
                S.tt("dve", st["bb_im"](st["bb_im"].t[:, hs]), t1, dt, ALU.add)
                js = slice(half * 4, half * 4 + 4)
                S.cp("dve", st["c_re"](st["c_re"].t[:, js, :].rearrange("p a b -> p (a b)")), crp)
                S.ts("dve", st["c_imn"](st["c_imn"].t[:, js, :].rearrange("p a b -> p (a b)")), cip, -1.0, ALU.mult)
                if half == 0:
                    wg = bgload(7, "s5_wglu", 0, 512)
                    S.cp("dve", fl(st["wglu"](), "p a b -> p (a b)"), wg)
                    dts, th = TS4(TS4.t[:, 0, 0:8]), TS4(TS4.t[:, 0, 8:16])
                    S.act(dts, pvv(L, "s5_ldt_s"), AF.Exp)
                    S.tt("dve", th, pvv(L, "s5_li_s"), dts, ALU.mult)
                    S.tt("dve", dts, pvv(L, "s5_lr_s"), dts, ALU.mult)
                    S.act(st["m"](), dts, AF.Exp)
                    S.cp("dve", st["car_re"](), th)
                ang = TB(TB.t[:, 1, :].rearrange("p (a b) -> p a b", a=4), 1)
                ramp = C("ramp")
                thv = st["car_re"](st["car_re"].t[:, js])
                S.tt("dve", ang, bc(ramp, ramp.ap[:, None, :].broadcast_to([128, 4, 128])),
                     bc(thv, thv.ap[:, :, None].broadcast_to([128, 4, 128])), ALU.mult)
                sincos(tb(1), st["sin"](st["sin"].t[:, js, :].rearrange("p a b -> p (a b)")),
                       st["cos"](st["cos"].t[:, js, :].rearrange("p a b -> p (a b)")), TS4)
            S.memset("dve", st["car_re"](), 0.0)
            S.memset("dve", st["car_im"](), 0.0)
            S.act(st["a_neg"](), pvv(L, "ssd_alog"), AF.Exp)
            S.ts("dve", st["a_neg"](), st["a_neg"](), -1.0, ALU.mult)
            S.memset("pool", st["Sst"](), 0.0)
            S.memset("pool", st["prevT"](), 0.0)
            S.memset("pool", st["xhalo"](), 0.0)
        else:
            S.cp("dve", st["wa2"](), bgload(0, "rw_wa2", 0, 512))
            S.cp("dve", st["g2"](), bgload(1, "rw_g2", 0, 512))
            S.cp("dve", st["v2"](), bgload(2, "rw_v2", 0, 512))
            if L // 2 == 0:
                S.memset("dve", st["lb"](), 0.0)
            else:
                S.tt("dve", st["lb"](), pvv(L, "hg_lb_raw1"), pvv(L, "hg_lb_raw0"), ALU.subtract)
                S.act(st["lb"](), st["lb"](), AF.Sigmoid)
            S.ts("dve", st["oml"](), st["lb"](), -1.0, ALU.mult, 1.0, ALU.add)
            S.memset("pool", st["T"](), 0.0)
            S.memset("pool", st["Tbf"](), 0.0)
            S.memset("pool", st["HS"](), 0.0)
            S.memset("pool", st["HSbf"](), 0.0)
            S.memset("pool", st["rwhalo"](), 0.0)
    arena_reset()
    if dbg is not None and dbg[0] == "stop" and dbg[1] == "prologue":
        S.dma("sp", dram(oT[:, :, 0:NT].rearrange("d p t -> p d t")), H(), is_output=True)
        S.finish()
        return nc

    def wload(L, idx):
        slot = wctr[0] % NWS
        wctr[0] += 1
        S.dma("sp", WRING(WRING.t[:, slot, :], slot), View(wsb_d[L][idx], [wsb_regs[L][idx // WCH]]))
        return slot

    def wk(slot, k, m0=0, m1=128, p0=0, p1=128):
        return WRING(WRING.t[p0:p1, slot, k * 128 + m0:k * 128 + m1], slot)

    psrot = [0]

    def next_bank():
        b = psrot[0] % 4
        psrot[0] += 1
        return b

    def norm_stats(src_views, nfeat, ones=None, epsv=None, n=NT):
        m = len(src_views)
        for i, v in enumerate(src_views):
            q = sqc[0] % 2
            sqc[0] += 1
            S.act(SQ(SQ.t[:, q, 0:n], q), v, AF.Square)
            S.mm(ps(7, 0, n), ones if ones is not None else ones_bf, SQ(SQ.t[:, q, 0:n], q), start=(i == 0), stop=(i == m - 1), signal=(i == m - 1))
        r = RSTD(RSTD.t[:, 0:n])
        S.act(r, ps(7, 0, n), AF.Sqrt, bias=epsv if epsv is not None else eps_v, scale=1.0 / nfeat)
        S.recip(r, r)
        return r

    def pre_norm(L, gname):
        r = norm_stats([H(H.t[:, d, :]) for d in range(8)], D)
        for d in range(8):
            S.stt("dve", HN(HN.t[:, d, :]), H(H.t[:, d, :]), pvv(L, gname, d, d + 1), r, ALU.mult, ALU.mult)

    def post_norm_add(L, gname):
        r = norm_stats([TMP8(TMP8.t[:, d, :], d) for d in range(8)], D)
        for d in range(8):
            S.stt("dve", TMP8(TMP8.t[:, d, :], d), TMP8(TMP8.t[:, d, :], d), pvv(L, gname, d, d + 1), r, ALU.mult, ALU.mult)
            S.tt("pool", H(H.t[:, d, :]), H(H.t[:, d, :]), TMP8(TMP8.t[:, d, :], d), ALU.add)

    def inproj(L, widx):
        slot = wload(L, widx)
        b = next_bank()
        for k in range(8):
            S.mm(ps(b, 0, NT), wk(slot, k), HN(HN.t[:, k, :]), start=(k == 0), stop=(k == 7), signal=(k == 7))
        return b

    def proj8(L, widx0, rhs_buf, nk):
        nkg = (nk + 7) // 8
        wi = widx0
        for m in range(8):
            b = next_bank()
            kk = 0
            for kg in range(nkg):
                slot = wload(L, wi)
                wi += 1
                for k in range(min(8, nk - kg * 8)):
                    S.mm(ps(b, 0, NT), wk(slot, k), rhs_buf(rhs_buf.t[:, kk, :], kk), start=(kk == 0), stop=(kk == nk - 1), signal=(kk == nk - 1))
                    kk += 1
            S.cp("act", TMP8(TMP8.t[:, m, :], m), ps(b, 0, NT))
        return wi

    def ffn(L, widx):
        arena_reset()
        ABUF = mk("abuf", [128, NJ, NT], BF16, nreg=NJ)
        FRAW = [mk("fraw", [128, 2, 2 + NT], F32) for i in range(2)]
        FACC = [mk("facc", [128, 2, NT], F32) for i in range(2)]
        pre_norm(L, "n_ffn_pre")
        o, _ = pk_layers[L].off["ffn_cw"]
        ob, _ = pk_layers[L].off["ffn_cb"]
        for j in range(NJ):
            fr, fa = FRAW[j % 2], FACC[j % 2]
            S.cp("pool", fr(fr.t[:, :, 0:2]), FHALO[L](FHALO[L].t[:, 2 * j:2 * j + 2, :]))
            for s in range(2):
                b = inproj(L, widx)
                widx += 1
                S.cp("act", fr(fr.t[:, s, 2:2 + NT]), ps(b, 0, NT))
                ti = 2 * j + s
                cw = lambda kk: PV[L](PV[L].t[:, o + ti * 3 + kk:o + ti * 3 + kk + 1])
                cbv = PV[L](PV[L].t[:, ob + ti:ob + ti + 1])
                acc = fa(fa.t[:, s, :])
                S.ts("dve", acc, fr(fr.t[:, s, 0:NT]), cw(0), ALU.mult, cbv, ALU.add)
                S.stt("dve", acc, fr(fr.t[:, s, 1:1 + NT]), cw(1), acc, ALU.mult, ALU.add)
                S.stt("dve", acc, fr(fr.t[:, s, 2:2 + NT]), cw(2), acc, ALU.mult, ALU.add)
            S.cp("pool", FHALO[L](FHALO[L].t[:, 2 * j:2 * j + 2, :]), fr(fr.t[:, :, NT:NT + 2]))
            S.act(fa(fa.t[:, 0, :]), fa(fa.t[:, 0, :]), AF.Gelu_apprx_tanh)
            S.tt("pool", ABUF(ABUF.t[:, j, :], j), fa(fa.t[:, 0, :]), fa(fa.t[:, 1, :]), ALU.mult)
        widx = proj8(L, widx, ABUF, NJ)
        post_norm_add(L, "n_ffn_post")
        return widx

    def even_layer(L):
        st = LS[L]
        arena_reset()
        pre_norm(L, "n_mix_pre")
        stop("prenorm")
        UF = mk("uf", [128, 2, NT], F32)
        UB = mk("ub", [128, 2, NT], BF16)
        S5T = [mk("s5t", [128, 6, CH], F32) for i in range(2)]
        S5X = [mk("s5x", [128, 2, NT], F32) for i in range(2)]
        S5Y = mk("s5y", [128, 2, NT], F32)
        S5YB = mk("s5yb", [128, 2, NT], BF16)
        for m in range(2):
            b = inproj(L, m)
            S.cp("act", UF(UF.t[:, m, :]), ps(b, 0, NT))
            S.cp("dve", UB(UB.t[:, m, :]), ps(b, 0, NT))
        stop("inproj")
        for j in range(8):
            kt, jj = j // 4, j % 4
            bre, bim = next_bank(), next_bank()
            c0 = kt * 512 + jj * 128
            S.mm(ps(bre, 0, NT), st["bb_re"](st["bb_re"].t[:, c0:c0 + 128]), UB(UB.t[:, kt, :]))
            S.mm(ps(bim, 0, NT), st["bb_im"](st["bb_im"].t[:, c0:c0 + 128]), UB(UB.t[:, kt, :]))
            X = S5X[j % 2]
            cs, sn = st["cos"](st["cos"].t[:, j, :]), st["sin"](st["sin"].t[:, j, :])
            mb = bc(st["m"](), st["m"].t[:, j:j + 1].to_broadcast([128, CH]))
            for c in range(CPT):
                T = S5T[c % 2]
                pr, pi = ps(bre, c * CH, (c + 1) * CH), ps(bim, c * CH, (c + 1) * CH)
                t = lambda i: T(T.t[:, i, :])
                S.tt("dve", t(0), pr, cs, ALU.mult)
                S.tt("dve", t(1), pi, sn, ALU.mult)
                S.tt("pool", t(0), t(0), t(1), ALU.add)
                S.tt("dve", t(2), pi, cs, ALU.mult)
                S.tt("dve", t(3), pr, sn, ALU.mult)
                S.tt("pool", t(2), t(2), t(3), ALU.subtract)
                if c == 0:
                    ir, ii = st["car_re"](st["car_re"].t[:, j:j + 1]), st["car_im"](st["car_im"].t[:, j:j + 1])
                else:
                    ir, ii = X(X.t[:, 0, c * CH - 1:c * CH]), X(X.t[:, 1, c * CH - 1:c * CH])
                S.scan(t(4), mb, t(0), ir, ALU.mult, ALU.add)
                S.scan(t(5), mb, t(2), ii, ALU.mult, ALU.add)
                xr, xi = X(X.t[:, 0, c * CH:(c + 1) * CH]), X(X.t[:, 1, c * CH:(c + 1) * CH])
                S.tt("dve", t(0), t(4), cs, ALU.mult)
                S.tt("pool", t(1), t(5), sn, ALU.mult)
                S.tt("dve", xr, t(0), t(1), ALU.subtract)
                S.tt("dve", t(2), t(5), cs, ALU.mult)
                S.tt("pool", t(3), t(4), sn, ALU.mult)
                S.tt("dve", xi, t(2), t(3), ALU.add)
            S.cp("pool", st["car_re"](st["car_re"].t[:, j:j + 1]), X(X.t[:, 0, NT - 1:NT]))
            S.cp("pool", st["car_im"](st["car_im"].t[:, j:j + 1]), X(X.t[:, 1, NT - 1:NT]))
            yb = 4 + j // 4
            S.mm(ps(yb, 0, NT), st["c_re"](st["c_re"].t[:, j, :]), X(X.t[:, 0, :]), start=(jj == 0), stop=False, signal=False)
            S.mm(ps(yb, 0, NT), st["c_imn"](st["c_imn"].t[:, j, :]), X(X.t[:, 1, :]), start=False, stop=(jj == 3), signal=True)
        stop("s5scan")
        for ot in range(2):
            yv = S5Y(S5Y.t[:, ot, :])
            S.stt("dve", yv, UF(UF.t[:, ot, :]), pvv(L, "s5_d", ot, ot + 1), ps(4 + ot, 0, NT), ALU.mult, ALU.add)
            S.act(yv, yv, AF.Gelu_apprx_tanh)
            S.cp("pool", S5YB(S5YB.t[:, ot, :]), yv)
        for ot in range(2):
            b = next_bank()
            for k in range(2):
                S.mm(ps(b, 0, NT), st["wglu"](st["wglu"].t[:, k, ot * 128:(ot + 1) * 128]), S5YB(S5YB.t[:, k, :]), start=(k == 0), stop=(k == 1), signal=(k == 1))
            sg = S5X[0](S5X[0].t[:, 0, :])
            S.act(sg, ps(b, 0, NT), AF.Sigmoid, bias=pvv(L, "s5_bglu", ot, ot + 1))
            S.tt("dve", YMIX(YMIX.t[:, ot, :], ot), S5Y(S5Y.t[:, ot, :]), sg, ALU.mult)
        stop("s5")
        arena_reset()
        ZS = mk("zs", [128, 6, NT], F32)
        XA = mk("xa", [128, 10, NT], F32, nreg=10)
        XAB = mk("xab", [128, 4, NT], BF16)
        DTT = mk("dtt", [128, NT], F32)
        YS = mk("ys", [128, 6, NT], F32)
        RAW = [mk("xraw", [128, 3 + NT], F32) for i in range(2)]
        XDP = mk("xdp", [128, 12, 128], BF16)
        XDD = mk("xdd", [128, 12, 64], BF16)
        XST = mk("xst", [128, 12, 64], F32)
        BTK = mk("btk", [128, 2, 128], BF16)
        SM = mk("ssdsmall", [128, 8, 12], F32, nreg=8)
        SEG = mk("seg", [128, 12, 128], F32)
        EAR = mk("ear", [128, 12, 128], F32)
        SCT = mk("sct", [128, 12, 128], BF16)
        CEX = mk("cex", [128, 12, 128], BF16)
        CBM = mk("cbm", [128, 2, 128], F32)
        S.memset("pool", XDP(), 0.0)
        ocw, _ = pk_layers[L].off["ssd_cw"]
        ocb, _ = pk_layers[L].off["ssd_cb"]
        for m in range(2, 19):
            b = inproj(L, m)
            p = ps(b, 0, NT)
            if m < 8:
                S.act(ZS(ZS.t[:, m - 2, :]), p, AF.Silu)
            elif m < 18:
                i = m - 8
                rw = RAW[i % 2]
                S.cp("pool", rw(rw.t[:, 0:3]), st["xhalo"](st["xhalo"].t[:, i, :]))
                S.cp("act", rw(rw.t[:, 3:3 + NT]), p)
                cw = lambda kk: PV[L](PV[L].t[:, ocw + i * 4 + kk:ocw + i * 4 + kk + 1])
                acc = XA(XA.t[:, i, :], i)
                S.ts("dve", acc, rw(rw.t[:, 0:NT]), cw(0), ALU.mult, PV[L](PV[L].t[:, ocb + i:ocb + i + 1]), ALU.add)
                for kk in range(1, 4):
                    S.stt("dve", acc, rw(rw.t[:, kk:kk + NT]), cw(kk), acc, ALU.mult, ALU.add)
                S.cp("pool", st["xhalo"](st["xhalo"].t[:, i, :]), rw(rw.t[:, NT:NT + 3]))
                S.act(acc, acc, AF.Silu)
                if i >= 6:
                    S.cp("pool", XAB(XAB.t[:, i - 6, :]), acc)
            else:
                S.act(DTT(DTT.t[0:12, :]), ps(b, 0, NT, 0, 12), AF.Exp, bias=pvv(L, "ssd_dtb", 0, 1, 0, 12))
                S.act(DTT(DTT.t[0:12, :]), DTT(DTT.t[0:12, :]), AF.Ln, bias=pvv(L, "ssd_dtb", 0, 1, 0, 12) if False else 1.0)
        stop("ssdproj")
        a_neg = st["a_neg"]()
        le = C("le")
        sm = lambda i: SM(SM.t[:, i, :], i)
        arow_regs = psr(1, 0, 512) + psr(2, 0, 512) + psr(3, 0, 512)
        for c in range(CPT):
            cc = slice(c * CH, (c + 1) * CH)
            for i in range(6):
                bk, off = (4, i * 128) if i < 4 else (5, (i - 4) * 128)
                S.tr(ps(bk, off, off + 128), XA(XA.t[:, i, cc], i), ident)
            for g2 in range(2):
                S.tr(ps(6, g2 * 128, (g2 + 1) * 128), XA(XA.t[:, 6 + g2, cc], 6 + g2), ident)
            S.tr(ps(7, 256, 268), DTT(DTT.t[0:12, cc]), bc(ident, ident.ap[0:12, 0:12]))
            S.cp("act", XST(XST.t[:, 0:8, :].rearrange("p a b -> p (a b)")), ps(4, 0, 512))
            S.cp("act", XST(XST.t[:, 8:12, :].rearrange("p a b -> p (a b)")), ps(5, 0, 256))
            S.cp("act", fl(BTK(), "p a b -> p (a b)"), ps(6, 0, 256))
            dtk, da, acol, aend, dte, w2, edec = [sm(i) for i in range(7)]
            S.cp("dve", dtk, ps(7, 256, 268))
            S.tt("dve", da, dtk, a_neg, ALU.mult)
            S.mm(ps(7, 272, 284), le, da)
            S.cp("act", acol, ps(7, 272, 284))
            S.tt("dve", SEG(), bc(da, da.ap[:, :, None].broadcast_to([128, 12, 128])),
                 bc(le, le.ap[:, None, :].broadcast_to([128, 12, 128])), ALU.mult)
            for q in range(3):
                S.mm(ps(1 + q, 0, 512), ones_f, SEG(SEG.t[:, 4 * q:4 * q + 4, :].rearrange("p a b -> p (a b)")))
            arow = View(PSt[:, 1:4, :].rearrange("p a (b c) -> p (a b) c", c=128), arow_regs, True)
            S.act(EAR(), arow, AF.Exp)
            S.cp("dve", aend, View(PSt[:, 1:4, :].rearrange("p a (b c) -> p (a b) c", c=128)[:, :, 127], arow_regs, True))
            S.tt("dve", SEG(), arow, bc(acol, acol.ap[:, :, None].broadcast_to([128, 12, 128])), ALU.subtract)
            S.ts("pool", SEG(), SEG(), 0.0, ALU.min)
            S.act(SEG(), SEG(), AF.Exp)
            for g2 in range(2):
                S.mm(ps(6, 256 + g2 * 128, 256 + (g2 + 1) * 128), XAB(XAB.t[:, g2, cc]), XAB(XAB.t[:, 2 + g2, cc]))
            S.tt("dve", CBM(), View(PSt[:, 6, 256:512].rearrange("p (a b) -> p a b", a=2), psr(6, 256, 512), True),
                 bc(le, le.ap[:, None, :].broadcast_to([128, 2, 128])), ALU.mult)
            for g2 in range(2):
                hs = slice(6 * g2, 6 * g2 + 6)
                S.tt("dve", SCT(SCT.t[:, hs, :]), SEG(SEG.t[:, hs, :]), CBM(CBM.t[:, g2:g2 + 1, :].broadcast_to([128, 6, 128])), ALU.mult)
                S.tt("pool", CEX(CEX.t[:, hs, :]), EAR(EAR.t[:, hs, :]),
                     XA(XA.t[:, 8 + g2:9 + g2, cc].broadcast_to([128, 6, 128]), 8 + g2), ALU.mult)
            xdp5 = XDP.t[:].rearrange("p (a two) (h c) -> p a two h c", two=2, h=2)
            xst4 = XST.t[:].rearrange("p (a two) c -> p a two c", two=2)
            dtk4 = dtk.ap.rearrange("p (a two) -> p a two", two=2)
            for par in range(2):
                S.tt("dve", XDP(xdp5[:, :, par, par, :]), XST(xst4[:, :, par, :]),
                     bc(dtk, dtk4[:, :, par:par + 1].broadcast_to([128, 6, 64])), ALU.mult)
            for pj in range(6):
                yo = ps(0, pj * 128, (pj + 1) * 128) if pj < 4 else ps(7, (pj - 4) * 128, (pj - 3) * 128)
                for hh in range(2):
                    hI = 2 * pj + hh
                    S.mm(yo, XDP(XDP.t[:, hI, :]), SCT(SCT.t[:, hI, :]), start=(hh == 0), stop=False, signal=False)
                for hh in range(2):
                    hI = 2 * pj + hh
                    S.mm(yo, st["prevT"](st["prevT"].t[:, hI, :]), CEX(CEX.t[:, hI, :]), start=False, stop=(hh == 1), signal=(hh == 1))
                S.stt("dve", YS(YS.t[:, pj, cc]), XA(XA.t[:, pj, cc], pj), pvv(L, "ssd_d", pj, pj + 1), yo, ALU.mult, ALU.add)
            S.tt("dve", dte, aend, acol, ALU.subtract)
            S.act(dte, dte, AF.Exp)
            S.tt("dve", w2, dtk, dte, ALU.mult)
            S.act(edec, aend, AF.Exp)
            S.tt("dve", XDD(), XST(), bc(w2, w2.ap[:, :, None].broadcast_to([128, 12, 64])), ALU.mult)
            for g2 in range(2):
                S.mm(ps(4 + g2, 0, 384), BTK(BTK.t[:, g2, :]), XDD(XDD.t[:, 6 * g2:6 * g2 + 6, :].rearrange("p a b -> p (a b)")))
            Sst = st["Sst"]
            S.tt("dve", Sst(), Sst(), bc(edec, edec.ap[:, :, None].broadcast_to([128, 12, 64])), ALU.mult)
            for g2 in range(2):
                S.tt("dve", Sst(Sst.t[:, 6 * g2:6 * g2 + 6, :]), Sst(Sst.t[:, 6 * g2:6 * g2 + 6, :]),
                     View(PSt[:, 4 + g2, 0:384].rearrange("p (a b) -> p a b", a=6), psr(4 + g2, 0, 384), True), ALU.add)
            pt5 = st["prevT"].t[:].rearrange("p (a two) (h c) -> p a two h c", two=2, h=2)
            ss4 = Sst.t[:].rearrange("p (a two) c -> p a two c", two=2)
            for par in range(2):
                S.cp("pool", st["prevT"](pt5[:, :, par, par, :]), Sst(ss4[:, :, par, :]))
        S.tt("dve", YS(), YS(), ZS(), ALU.mult)
        for g2 in range(2):
            r = norm_stats([YS(YS.t[:, 3 * g2 + i, :]) for i in range(3)], 384)
            for i in range(3):
                S.stt("dve", YMIX(YMIX.t[:, 2 + 3 * g2 + i, :], 2 + 3 * g2 + i), YS(YS.t[:, 3 * g2 + i, :]),
                      pvv(L, "ssd_norm", 3 * g2 + i, 3 * g2 + i + 1), r, ALU.mult, ALU.mult)
        return 19

    C0 = math.exp(-0.5)

    def odd_layer(L):
        st = LS[L]
        o = L // 2
        arena_reset()
        pre_norm(L, "n_mix_pre")
        widx = 0
        RH = mk("rh", [128, 4, NT], BF16, nreg=4)
        KH = mk("kh", [128, 4, NT], BF16, nreg=4)
        KHH = mk("khh", [128, 4, NT], BF16, nreg=4)
        BH = mk("bh", [128, 4, NT], BF16, nreg=4)
        VB = mk("vb", [128, 4, NT], BF16, nreg=4)
        BV = mk("bv", [128, 4, NT], F32, nreg=4)
        SG = mk("sg", [128, NT], BF16)
        GL = mk("gl", [128, 4, CPT], F32, nreg=4)
        r1_snap = (nc.sbuf_base, nc.sbuf_top)
        RAW = [mk("rraw", [128, 1 + NT], F32) for i in range(2)]
        TW = mk("tw", [128, NT], BF16)
        PVb = mk("pvb16", [128, NT], BF16)
        Dt = mk("dtmp", [128, NT], F32)
        Rr, Kk, Vv, Aa, SGW, KAP, KT_, Bb, E_, LGS, VS = [mk("r0t", [128, NT], F32) for _ in range(11)]
        RK = mk("rk", [128, NT], BF16)

        def shifted(idx, dst):
            nonlocal widx
            b = inproj(L, widx)
            widx += 1
            rw = RAW[widx % 2]
            S.cp("pool", rw(rw.t[:, 0:1]), st["rwhalo"](st["rwhalo"].t[:, idx:idx + 1]))
            S.cp("act", rw(rw.t[:, 1:1 + NT]), ps(b, 0, NT))
            S.tt("pool", Dt(), rw(rw.t[:, 0:NT]), rw(rw.t[:, 1:1 + NT]), ALU.subtract)
            S.stt("dve", dst, Dt(), pvv(L, "rw_mu", idx, idx + 1), rw(rw.t[:, 1:1 + NT]), ALU.mult, ALU.add)
            S.cp("pool", st["rwhalo"](st["rwhalo"].t[:, idx:idx + 1]), rw(rw.t[:, NT:NT + 1]))

        shifted(12, Rr())
        S.act(TW(TW.t[0:64, :]), Rr(Rr.t[0:64, :]), AF.Tanh)
        S.cp("dve", TW(TW.t[64:128, :]), Rr(Rr.t[64:128, :]))
        shifted(13, Kk())
        S.act(SG(), Kk(), AF.Sigmoid)
        if o > 0:
            shifted(14, Vv())
            S.cp("dve", PVb(PVb.t[0:32, :]), Vv(Vv.t[0:32, :]))
        for i in range(4):
            ic = slice(i * 128, (i + 1) * 128)
            shifted(i, Rr())
            shifted(4 + i, Kk())
            shifted(8 + i, Vv())
            bw, ba = next_bank(), next_bank()
            S.mm(ps(bw, 0, NT), st["wa2"](st["wa2"].t[0:64, ic]), TW(TW.t[0:64, :]))
            S.mm(ps(ba, 0, NT), st["wa2"](st["wa2"].t[64:128, ic]), TW(TW.t[64:128, :]))
            S.act(SGW(), ps(bw, 0, NT), AF.Sigmoid, bias=pvv(L, "rw_w0", i, i + 1))
            S.act(Aa(), ps(ba, 0, NT), AF.Sigmoid, bias=pvv(L, "rw_a0", i, i + 1))
            if o > 0:
                bv_ = next_bank()
                S.mm(ps(bv_, 0, NT), st["v2"](st["v2"].t[0:32, ic]), PVb(PVb.t[0:32, :]))
                S.act(VS(), ps(bv_, 0, NT), AF.Sigmoid, bias=pvv(L, "rw_v0", i, i + 1))
                S.tt("pool", Dt(), VFIRST(VFIRST.t[:, i, :], i), Vv(), ALU.subtract)
                S.tt("dve", Dt(), Dt(), VS(), ALU.mult)
                S.tt("dve", Vv(), Vv(), Dt(), ALU.add)
            else:
                S.cp("pool", VFIRST(VFIRST.t[:, i, :], i), Vv())
            S.cp("pool", VB(VB.t[:, i, :], i), Vv())
            S.ts("dve", KAP(), Kk(), pvv(L, "rw_k_k", i, i + 1), ALU.mult)
            q = sqc[0] % 2
            sqc[0] += 1
            S.act(SQ(SQ.t[:, q, :], q), KAP(), AF.Square)
            S.mm(ps(7, 0, NT), blk64_bf, SQ(SQ.t[:, q, :], q))
            S.act(E_(), ps(7, 0, NT), AF.Sqrt)
            S.ts("dve", E_(), E_(), 1e-12, ALU.max)
            S.recip(E_(), E_())
            S.tt("dve", KAP(), KAP(), E_(), ALU.mult)
            S.ts("dve", KT_(), Aa(), -1.0, ALU.add, pvv(L, "rw_k_a", i, i + 1), ALU.mult)
            S.stt("dve", KT_(), KT_(), 1.0, Kk(), ALU.add, ALU.mult)
            S.tt("pool", Bb(), KAP(), Aa(), ALU.mult)
            S.stt("dve", RK(), Rr(), pvv(L, "rw_r_k", i, i + 1), KT_(), ALU.mult, ALU.mult)
            bb_ = next_bank()
            S.mm(ps(bb_, 0, NT), blk64_bf, RK())
            S.tt("dve", BV(BV.t[:, i, :], i), ps(bb_, 0, NT), Vv(), ALU.mult)
            for c in range(CPT):
                cc = slice(c * CH, (c + 1) * CH)
                S.scan(LGS(LGS.t[:, cc]), ones_f, SGW(SGW.t[:, cc]), 0.0, ALU.mult, ALU.add)
            S.act(E_(), LGS(), AF.Exp, scale=-C0)
            S.tt("dve", RH(RH.t[:, i, :], i), Rr(), E_(), ALU.mult)
            S.cp("pool", GL(GL.t[:, i, :], i), bc(E_(), E_.t[:, :].rearrange("p (c t) -> p c t", c=CPT)[:, :, CH - 1]))
            S.tt("pool", Dt(), LGS(), SGW(), ALU.subtract)
            S.act(Dt(), Dt(), AF.Exp, scale=-C0)
            S.tt("dve", KH(KH.t[:, i, :], i), KAP(), Dt(), ALU.mult)
            S.act(E_(), LGS(), AF.Exp, scale=C0)
            S.tt("dve", KHH(KHH.t[:, i, :], i), KT_(), E_(), ALU.mult)
            S.tt("pool", BH(BH.t[:, i, :], i), Bb(), E_(), ALU.mult)
        S.barrier()
        nc.sbuf_base, nc.sbuf_top = r1_snap
        G = mk("g", [128, 4, NT], F32, nreg=4)
        VT = [mk("vt", [128, 128], BF16) for _ in range(2)]
        KTK = [mk("ktk", [128, 128], BF16) for _ in range(2)]
        NBT = [mk("nbt", [128, 128], BF16) for _ in range(2)]
        UT = [mk("ut", [128, 128], BF16) for _ in range(2)]
        NM = [mk("nm", [128, 128], F32) for _ in range(2)]
        NMT = [mk("nmt", [128, 128], F32) for _ in range(2)]
        PP = [[mk("pp", [128, 128], F32) for _ in range(2)] for _ in range(2)]
        PPT = [[mk("ppt", [128, 128], F32) for _ in range(2)] for _ in range(2)]
        YY = [[mk("yy", [128, 128], F32) for _ in range(2)] for _ in range(2)]
        ARK = [mk("ark", [128, 128], BF16) for _ in range(2)]
        ARB = [mk("arb", [128, 128], BF16) for _ in range(2)]
        AKK = [mk("akk", [128, 128], BF16) for _ in range(2)]
        RR = [mk("rr", [128, 64], F32) for _ in range(2)]
        OS = mk("os", [128, 8, 64], F32)
        OQ = mk("oq", [128, 8, 64], F32)
        ST8 = mk("st8", [128, 4, 8], F32, nreg=4)
        OF = mk("of", [128, NT], F32)
        for i in range(4):
            b = next_bank()
            S.mm(ps(b, 0, NT), st["g2"](st["g2"].t[:, i * 128:(i + 1) * 128]), SG())
            S.cp("act", G(G.t[:, i, :], i), ps(b, 0, NT))
        lt, le, nle, nlt, ngt = C("lt"), C("le"), C("nle"), C("nlt"), C("ngt")
        T, Tbf = st["T"], st["Tbf"]
        for c in range(CPT):
            cc = slice(c * CH, (c + 1) * CH)
            for i in range(4):
                pp = i % 2
                S.mm(ps(0, 0, 128), VB(VB.t[:, i, cc], i), ident_bf)
                S.mm(ps(0, 128, 256), KHH(KHH.t[:, i, cc], i), ident_bf)
                S.mm(ps(0, 256, 384), BH(BH.t[:, i, cc], i), ident_bf)
                S.cp("act", VT[pp](), ps(0, 0, 128))
                S.cp("act", KTK[pp](), ps(0, 128, 256))
                S.ts("dve", NBT[pp](), ps(0, 256, 384), -1.0, ALU.mult)
                hv = []
                for hh in range(2):
                    pb = 64 * hh
                    rh = RH(RH.t[pb:pb + 64, i, cc], i)
                    kh = KH(KH.t[pb:pb + 64, i, cc], i)
                    khh = KHH(KHH.t[pb:pb + 64, i, cc], i)
                    bh = BH(BH.t[pb:pb + 64, i, cc], i)
                    t0v = Tbf(Tbf.t[pb:pb + 64, i, :], i)
                    hv.append((pb, rh, kh, khh, bh, t0v))
                for hh in range(2):
                    pb, rh, kh, khh, bh, t0v = hv[hh]
                    bN = 1 + hh
                    S.mm(ps(bN, 0, 128), bh, kh)
                    S.mm(ps(bN, 128, 256), kh, bh)
                    S.mm(ps(bN, 256, 384), khh, rh)
                    S.mm(ps(bN, 384, 512), bh, rh)
                    S.mm(ps(3, hh * 128, hh * 128 + 128), khh, kh)
                    S.tt("dve", NM[hh](), ps(bN, 0, 128), nlt, ALU.mult)
                    S.tt("dve", NMT[hh](), ps(bN, 128, 256), ngt, ALU.mult)
                    S.tt("dve", ARK[hh](), ps(bN, 256, 384), le, ALU.mult)
                    S.tt("dve", ARB[hh](), ps(bN, 384, 512), nle, ALU.mult)
                    S.tt("dve", AKK[hh](), ps(3, hh * 128, hh * 128 + 128), lt, ALU.mult)
                    S.tt("pool", YY[hh][0](), NM[hh](), ident, ALU.add)
                for hh in range(2):
                    pb, rh, kh, khh, bh, t0v = hv[hh]
                    S.mm(ps(3, 256 + hh * 64, 320 + hh * 64), kh, t0v, start=True, stop=False, signal=False)
                    S.mm(ps(3, 256 + hh * 64, 320 + hh * 64), AKK[hh](), VT[pp](VT[pp].t[:, pb:pb + 64]), start=False, stop=True)
                    S.cp("act", RR[hh](), ps(3, 256 + hh * 64, 320 + hh * 64))
                cur = [(NM[0], NMT[0]), (NM[1], NMT[1])]
                for lvl in range(1, 7):
                    a = lvl % 2
                    for hh in range(2):
                        Pm, PTm = cur[hh]
                        bC = 5 + hh
                        if lvl < 6:
                            S.mm(ps(bC, 0, 128), PTm(), Pm())
                        S.mm(ps(bC, 128, 256), Pm(), PTm())
                        if lvl < 6:
                            S.cp("act", PP[hh][a](), ps(bC, 0, 128))
                        S.cp("act", PPT[hh][a](), ps(bC, 128, 256))
                    for hh in range(2):
                        bC = 5 + hh
                        yprev, ynew = YY[hh][(lvl - 1) % 2], YY[hh][lvl % 2]
                        S.mm(ps(bC, 256, 384), PPT[hh][a](), yprev())
                        S.tt("dve", ynew(), yprev(), ps(bC, 256, 384), ALU.add)
                        cur[hh] = (PP[hh][a], PPT[hh][a])
                yfin = [YY[0][0], YY[1][0]]
                for hh in range(2):
                    pb, rh, kh, khh, bh, t0v = hv[hh]
                    S.mm(ps(3, 384 + hh * 64, 448 + hh * 64), yfin[hh](), RR[hh]())
                    S.cp("act", UT[pp](UT[pp].t[:, pb:pb + 64]), ps(3, 384 + hh * 64, 448 + hh * 64))
                for hh in range(2):
                    pb, rh, kh, khh, bh, t0v = hv[hh]
                    oc = (2 * i + hh) * 64
                    ov = ps(4, oc, oc + 64)
                    S.mm(ov, rh, t0v, start=True, stop=False, signal=False)
                    S.mm(ov, ARK[hh](), VT[pp](VT[pp].t[:, pb:pb + 64]), start=False, stop=False, signal=False)
                    S.mm(ov, ARB[hh](), UT[pp](UT[pp].t[:, pb:pb + 64]), start=False, stop=True)
                S.mm(ps(0, 384, 512), KTK[pp](), VT[pp](), start=True, stop=False, signal=False)
                S.mm(ps(0, 384, 512), NBT[pp](), UT[pp](), start=False, stop=True)
                for hh in range(2):
                    pb = 64 * hh
                    tv = T(T.t[pb:pb + 64, i, :], i)
                    S.tt("dve", tv, tv, ps(0, 384 + pb, 448 + pb, pb, pb + 64), ALU.add)
                    S.ts("dve", tv, tv, GL(GL.t[pb:pb + 64, i, c:c + 1], i), ALU.mult)
                    S.cp("pool", Tbf(Tbf.t[pb:pb + 64, i, :], i), tv)
            osf = fl(OS(), "p a b -> p (a b)")
            S.cp("act", osf, ps(4, 0, 512))
            S.act(fl(OQ(), "p a b -> p (a b)"), ps(4, 0, 512), AF.Square)
            s1, s2, s3, s4 = [ST8(ST8.t[:, k, :], k) for k in range(4)]
            S.op("dve", lambda e: e.tensor_reduce(out=s1.ap, in_=OS.t[:], axis=AX.X, op=ALU.add), [s1], [OS()])
            S.op("dve", lambda e: e.tensor_reduce(out=s2.ap, in_=OQ.t[:], axis=AX.X, op=ALU.add), [s2], [OQ()])
            S.ts("dve", s1, s1, 1.0 / 64, ALU.mult)
            S.tt("dve", s3, s1, s1, ALU.mult)
            S.stt("dve", s2, s2, 1.0 / 64, s3, ALU.mult, ALU.subtract)
            S.act(s2, s2, AF.Sqrt, bias=gneps_v)
            S.recip(s2, s2)
            S.tt("dve", OS(), OS(), bc(s1, s1.ap[:, :, None].broadcast_to([128, 8, 64])), ALU.subtract)
            S.tt("dve", OS(), OS(), bc(s2, s2.ap[:, :, None].broadcast_to([128, 8, 64])), ALU.mult)
            for i in range(4):
                S.tr(ps(5 + i % 2, 384, 512), bc(OS(), osf.ap[:, i * 128:(i + 1) * 128]), ident)
                ofv = OF(OF.t[:, cc])
                S.ts("dve", ofv, ps(5 + i % 2, 384, 512), pvv(L, "rw_ln_w", i, i + 1), ALU.mult, pvv(L, "rw_ln_b", i, i + 1), ALU.add)
                S.tt("pool", ofv, ofv, BV(BV.t[:, i, cc], i), ALU.add)
                S.tt("dve", YMIX(YMIX.t[:, i, cc], i), ofv, G(G.t[:, i, cc], i), ALU.mult)
        arena_reset()
        le64 = View(C("le64").ap.bitcast(U32), CST.regs)
        HS, HSbf = st["HS"], st["HSbf"]
        hb = [dict() for _ in range(2)]
        for k in range(2):
            for nm in ("Q", "LF", "K1", "I", "OG", "GC", "NG", "EC", "EX", "KD", "O"):
                hb[k][nm] = mk("hg" + nm, [128, NT], F32)
            for nm in ("QT", "KT", "QG"):
                hb[k][nm] = mk("hg" + nm, [128, NT], BF16)
            hb[k]["ITK"] = mk("hgitk", [128, 128], BF16)
            hb[k]["KDT"] = mk("hgkdt", [128, 128], BF16)
            hb[k]["ATM"] = mk("hgatm", [128, 128], BF16)
            S.memset("pool", hb[k]["ATM"](), 0.0)
        for hd in range(4):
            Bf = hb[hd % 2]
            Q, LF, K1, I_, OG, GC, NG, EC, EX, KD, O_ = [Bf[n] for n in ("Q", "LF", "K1", "I", "OG", "GC", "NG", "EC", "EX", "KD", "O")]
            QT, KT, QG, ITK, KDT, ATM = [Bf[n] for n in ("QT", "KT", "QG", "ITK", "KDT", "ATM")]
            b = inproj(L, widx); widx += 1
            S.act(Q(), ps(b, 0, NT), AF.Silu)
            b = inproj(L, widx); widx += 1
            S.act(LF(), ps(b, 0, NT), AF.Sigmoid)
            S.ts("dve", LF(), LF(), st["oml"](st["oml"].t[:, hd:hd + 1]), ALU.mult, st["lb"](st["lb"].t[:, hd:hd + 1]), ALU.add)
            S.ts("pool", K1(), LF(), -1.0, ALU.mult, 1.0, ALU.add)
            S.act(LF(), LF(), AF.Ln)
            b = inproj(L, widx); widx += 1
            S.cp("act", I_(), ps(b, 0, NT))
            b = inproj(L, widx); widx += 1
            S.act(OG(), ps(b, 0, NT), AF.Silu)
            for q in range(NQ):
                cq = slice(q * 64, (q + 1) * 64)
                S.scan(GC(GC.t[:, cq]), bc(ones_f, ones_f.ap[:, 0:64]), LF(LF.t[:, cq]), 0.0, ALU.mult, ALU.add)
            S.ts("pool", NG(), GC(), -1.0, ALU.mult)
            S.act(EC(), GC(), AF.Exp)
            S.tt("dve", QG(), Q(), EC(), ALU.mult)
            for q in range(NQ):
                cq = slice(q * 64, (q + 1) * 64)
                mid = q * 64 + 31
                end = q * 64 + 63
                S.act(EX(EX.t[:, cq]), GC(GC.t[:, cq]), AF.Exp, bias=NG(NG.t[:, mid:mid + 1]))
                S.tt("dve", QT(QT.t[:, cq]), Q(Q.t[:, cq]), EX(EX.t[:, cq]), ALU.mult)
            for q in range(NQ):
                cq = slice(q * 64, (q + 1) * 64)
                mid = q * 64 + 31
                S.act(EX(EX.t[:, cq]), NG(NG.t[:, cq]), AF.Exp, bias=GC(GC.t[:, mid:mid + 1]))
                S.tt("dve", KT(KT.t[:, cq]), K1(K1.t[:, cq]), EX(EX.t[:, cq]), ALU.mult)
            for q in range(NQ):
                cq = slice(q * 64, (q + 1) * 64)
                end = q * 64 + 63
                S.act(EX(EX.t[:, cq]), NG(NG.t[:, cq]), AF.Exp, bias=GC(GC.t[:, end:end + 1]))
                S.tt("pool", KD(KD.t[:, cq]), K1(K1.t[:, cq]), EX(EX.t[:, cq]), ALU.mult)
            for blk in range(CPT):
                cb_ = slice(blk * 128, (blk + 1) * 128)
                S.tr(ps(0, 0, 128), I_(I_.t[:, cb_]), ident)
                S.tr(ps(0, 128, 256), KD(KD.t[:, cb_]), ident)
                S.cp("act", ITK(), ps(0, 0, 128))
                S.cp("act", KDT(), ps(0, 128, 256))
                S.mm(ps(1, 0, 128), KT(KT.t[:, cb_]), QT(QT.t[:, cb_]))
                S.cpred(ATM(), le64, ps(1, 0, 128))
                ob_ = 2 + blk % 2
                S.mm(ps(ob_, 0, 128), ITK(), ATM(), start=True, stop=False, signal=False)
                for qq in range(2):
                    q = 2 * blk + qq
                    cq = slice(q * 64, (q + 1) * 64)
                    end = q * 64 + 63
                    S.mm(ps(ob_, qq * 64, qq * 64 + 64), HSbf(HSbf.t[:, hd, :], hd), QG(QG.t[:, cq]), start=False, stop=(qq == 1), signal=True)
                    S.mm(ps(4 + qq, 0, 128), KDT(KDT.t[qq * 64:qq * 64 + 64, :]), ITK(ITK.t[qq * 64:qq * 64 + 64, :]))
                    hs = HS(HS.t[:, hd, :], hd)
                    S.stt("dve", hs, hs, EC(EC.t[:, end:end + 1]), ps(4 + qq, 0, 128), ALU.mult, ALU.add)
                    S.cp("pool", HSbf(HSbf.t[:, hd, :], hd), hs)
                S.cp("act", O_(O_.t[:, cb_]), ps(ob_, 0, 128))
            r = norm_stats([O_()], 128)
            S.stt("dve", O_(), O_(), pvv(L, "hg_norm", hd, hd + 1), r, ALU.mult, ALU.mult)
            S.tt("dve", YMIX(YMIX.t[:, 4 + hd, :], 4 + hd), O_(), OG(), ALU.mult)
        return widx

    def mix_out(L, widx):
        widx = proj8(L, widx, YMIX, 8)
        post_norm_add(L, "n_mix_post")
        return widx

    try:
      for ti in range(n_tiles):
        t0 = ti * NT
        S.dma("sp", H(), dram(xT[:, :, t0:t0 + NT].rearrange("d p t -> p d t")))
        for L in range(n_layers):
            stop("start")
            widx = even_layer(L) if L % 2 == 0 else odd_layer(L)
            stop("mixer")
            if dbg is not None and dbg[0] == "ymix" and dbg[1] == L and ti == 0:
                for d in range(8):
                    S.cp("dve", TMP8(TMP8.t[:, d, :], d), YMIX(YMIX.t[:, d, :], d))
                S.dma("sp", dram(dbg_d.rearrange("d p t -> p d t")), TMP8(), is_output=True)
            widx = mix_out(L, widx)
            stop("mixout")
            widx = ffn(L, widx)
            assert widx == n_ws[L], (widx, n_ws[L])
        S.dma("sp", dram(oT[:, :, t0:t0 + NT].rearrange("d p t -> p d t")), H(), is_output=True)
    except StopBuild:
        S.dma("sp", dram(oT[:, :, 0:NT].rearrange("d p t -> p d t")), H(), is_output=True)
    S.finish()
    print("program: ninst=%d persist_bytes/partition=%d" % (S.ninst, persist_bytes))
    return nc


def host_prep(inputs):
    pkc = consts_host()
    pls, pbs, wss = [], [], []
    for L in range(DEPTH):
        P, B, ws = layer_host(inputs, L)
        pls.append(P)
        pbs.append(B)
        wss.append(ws)
    return pkc, pls, pbs, wss


def make_xT(inputs):
    x = np.asarray(inputs["x"], dtype=np.float32)
    meta = np.asarray(inputs["meta"], dtype=np.float32)
    xs = []
    for b in range(NB):
        full = np.zeros((TPAD, D), np.float32)
        full[:NMETA] = meta
        full[NMETA:TREAL] = x[b]
        xs.append(np.ascontiguousarray(full.T).reshape(8, 128, TPAD))
    return xs


def make_shared(pkc, pls, pbs, wss):
    shared = {"consts": pkc.pack()}
    for L in range(DEPTH):
        shared["pv%d" % L] = pls[L].pack()
        shared["pb%d" % L] = pbs[L].pack()
        shared["ws%d" % L] = wss[L]
    return shared


def kernel(**inputs):
    pkc, pls, pbs, wss = host_prep(inputs)
    nc = build_program(pkc, pls, pbs, [w.shape[0] for w in wss])
    xs = make_xT(inputs)
    shared = make_shared(pkc, pls, pbs, wss)
    in_maps = [dict(shared, xT=xs[b]) for b in range(NB)]
    res = run_bass_kernel_spmd(nc, in_maps, core_ids=list(range(NB)))
    out = np.empty((NB, SEQ, D), np.float32)
    for b in range(NB):
        o = np.asarray(res.results[b]["oT"]).reshape(D, TPAD)
        out[b] = o[:, NMETA:TREAL].T
    return out
```

```python
import math
import numpy as np
import concourse.bass as bass
import concourse.mybir as mybir
from concourse.bass_utils import run_bass_kernel_spmd

F32 = mybir.dt.float32
BF16 = mybir.dt.bfloat16
I32 = mybir.dt.int32
U32 = mybir.dt.uint32
AF = mybir.ActivationFunctionType
ALU = mybir.AluOpType
AX = mybir.AxisListType

D = 1024
NB = 8
SEQ = 4096
NMETA = 16
TREAL = SEQ + NMETA
CH = 128
CPT = 3
NQ = 2 * CPT
NT = CH * CPT
NTILES = (TREAL + NT - 1) // NT
TPAD = NTILES * NT
DEPTH = 4
DFF = 2816
NJ = DFF // 128
EPS = 1e-6
GN_EPS = 64e-5
import os
SAME_SYNC = True
NOSYNC_ENG = tuple(os.environ.get('NOSYNC', '').split(',')) if os.environ.get('NOSYNC') else ()


class StopBuild(Exception):
    pass


class Reg:
    __slots__ = ("w", "r")

    def __init__(self):
        self.w = None
        self.r = {}


class View:
    __slots__ = ("ap", "regs", "excl")

    def __init__(self, ap, regs, excl=False):
        self.ap = ap
        self.regs = regs
        self.excl = excl


class Buf:
    def __init__(self, S, name, shape, dtype, nreg=1, space="sbuf"):
        nc = S.nc
        if space == "sbuf":
            self.t = nc.alloc_sbuf_tensor(name, list(shape), dtype, align_bytes=64)
            S.sbuf_bytes += int(np.prod(shape[1:])) * (2 if dtype == BF16 else 4)
        else:
            self.t = nc.alloc_psum_tensor(name, list(shape), dtype)
        self.regs = [Reg() for _ in range(nreg)]

    def __call__(self, ap=None, r=None):
        if ap is None:
            ap = self.t[:]
        if r is None:
            regs = self.regs
        elif isinstance(r, int):
            regs = [self.regs[r]]
        else:
            regs = [self.regs[i] for i in r]
        return View(ap, regs)


def dram(ap):
    return View(ap, [])


class Sched:
    def __init__(self, nc):
        self.nc = nc
        self.E = {"pe": nc.tensor, "dve": nc.vector, "act": nc.scalar, "pool": nc.gpsimd, "sp": nc.sync}
        self.sem = {k: nc.alloc_semaphore("sem_" + k) for k in ("pe", "dve", "act", "pool")}
        self.cnt = {k: 0 for k in self.sem}
        self.seen = {k: {} for k in self.E}
        self.dq = {}
        self.sbuf_bytes = 0
        self.ninst = 0
        self.out_toks = []
        self.nops = 0
        self.max_ops = None

    def _deps(self, outs, ins):
        deps = {}

        def need(tok):
            if tok is None:
                return
            cur = deps.get(tok[0])
            if cur is None or cur[1] < tok[1]:
                deps[tok[0]] = tok

        for v in ins:
            for rg in v.regs:
                need(rg.w)
        for v in outs:
            for rg in v.regs:
                need(rg.w)
                for tok in rg.r.values():
                    need(tok)
        return deps

    def _wait(self, eng, deps):
        E = self.E[eng]
        own = self.sem[eng].num if eng in self.sem else None
        for sid, tok in deps.items():
            if sid == own and (eng == "pe" or not SAME_SYNC or eng in NOSYNC_ENG):
                continue
            if self.seen[eng].get(sid, 0) < tok[1]:
                if self.max_ops is not None and self.nops >= self.max_ops - int(os.environ.get('SHOWN', 3)):
                    print("  WAIT", eng, "on", tok[2].name, tok[1], "cnts", self.cnt)
                E.wait_ge(tok[2], tok[1])
                self.seen[eng][sid] = tok[1]
                self.ninst += 1

    def _mark(self, tok, outs, ins):
        for v in ins:
            for rg in v.regs:
                cur = rg.r.get(tok[0])
                if cur is None or cur[1] < tok[1]:
                    rg.r[tok[0]] = tok
        for v in outs:
            for rg in v.regs:
                rg.w = tok
                rg.r = {}

    def op(self, eng, fn, outs, ins, signal=True):
        xs = [v for v in ins if v.excl]
        if xs:
            outs = list(outs) + xs
            ins = [v for v in ins if not v.excl]
        self._wait(eng, self._deps(outs, ins))
        inst = fn(self.E[eng])
        if self.max_ops is not None and self.nops >= self.max_ops - int(os.environ.get('SHOWN', 3)):
            try:
                print("  INST", eng, inst.concise())
            except Exception as ex:
                print("  INST?", ex, inst.ins)
        self.ninst += 1
        sem = self.sem[eng]
        if signal:
            self.cnt[eng] += 1
            inst.then_inc(sem, 1)
            tok = (sem.num, self.cnt[eng], sem)
        else:
            tok = (sem.num, self.cnt[eng] + 1, sem)
        self._mark(tok, outs, ins)
        self.nops += 1
        if self.max_ops is not None and self.nops >= self.max_ops:
            self.max_ops = None
            print("STOP at op", self.nops, eng)
            raise StopBuild()

    def dma(self, q, out, in_, is_output=False):
        if q not in self.dq:
            nr = 32 if q == "pool" else 8
            self.dq[q] = {"ring": [self.nc.alloc_semaphore("dsem_%s_%d" % (q, i)) for i in range(nr)], "n": 0, "toks": [None] * nr, "nr": nr}
        Q = self.dq[q]
        i = Q["n"]
        nr = Q["nr"]
        slot = i % nr
        deps = self._deps([out], [in_])
        if Q["toks"][slot] is not None:
            t = Q["toks"][slot]
            if t[0] not in deps or deps[t[0]][1] < t[1]:
                deps[t[0]] = t
        self._wait(q, deps)
        sem = Q["ring"][slot]
        val = 16 * (i // nr + 1)
        self.E[q].dma_start(out=out.ap, in_=in_.ap).then_inc(sem, 16)
        self.ninst += 1
        tok = (sem.num, val, sem)
        Q["toks"][slot] = tok
        Q["n"] += 1
        self._mark(tok, [out], [in_])
        if is_output:
            self.out_toks.append(tok)

    def finish(self):
        deps = {}
        for tok in self.out_toks:
            if tok[0] not in deps or deps[tok[0]][1] < tok[1]:
                deps[tok[0]] = tok
        for f in ("pe", "dve", "act", "pool"):
            if self.cnt[f]:
                deps[self.sem[f].num] = (self.sem[f].num, self.cnt[f], self.sem[f])
        for q, Q in self.dq.items():
            for t in Q["toks"]:
                if t is not None and (t[0] not in deps or deps[t[0]][1] < t[1]):
                    deps[t[0]] = t
        self._wait("sp", deps)

    def barrier(self):
        for eng in ("pe", "dve", "act", "pool"):
            deps = {}
            for f in ("pe", "dve", "act", "pool"):
                if (f == eng and eng == "pe") or self.cnt[f] == 0:
                    continue
                sem = self.sem[f]
                deps[sem.num] = (sem.num, self.cnt[f], sem)
            self._wait(eng, deps)

    def mm(self, out, lhsT, rhs, start=True, stop=True, signal=True):
        signal = True
        try:
            key = (int(lhsT.ap.start_partition()), int(lhsT.ap.partition_size()))
        except Exception:
            key = None
        last = getattr(self, "_mm_key", None)
        if key != last and self.cnt["pe"] > 0 and ((key is not None and key[1] < 128) or (last is not None and last[1] < 128)):
            sem = self.sem["pe"]
            if self.seen["pe"].get(sem.num, 0) < self.cnt["pe"]:
                self.E["pe"].wait_ge(sem, self.cnt["pe"])
                self.seen["pe"][sem.num] = self.cnt["pe"]
                self.ninst += 1
        self._mm_key = key
        self.op("pe", lambda e: e.matmul(out.ap, lhsT=lhsT.ap, rhs=rhs.ap, start=start, stop=stop), [out], [lhsT, rhs], signal)

    def tr(self, out, in_, ident):
        self.op("pe", lambda e: e.transpose(out.ap, in_.ap, ident.ap), [out], [in_, ident])

    def tt(self, eng, out, a, b, op):
        self.op(eng, lambda e: e.tensor_tensor(out=out.ap, in0=a.ap, in1=b.ap, op=op), [out], [a, b])

    def ts(self, eng, out, a, s1, op0, s2=None, op1=None):
        ins = [a] + [s for s in (s1, s2) if isinstance(s, View)]
        a1 = s1.ap if isinstance(s1, View) else s1
        a2 = s2.ap if isinstance(s2, View) else s2
        if op1 is None:
            self.op(eng, lambda e: e.tensor_scalar(out=out.ap, in0=a.ap, scalar1=a1, scalar2=None, op0=op0), [out], ins)
        else:
            self.op(eng, lambda e: e.tensor_scalar(out=out.ap, in0=a.ap, scalar1=a1, scalar2=a2, op0=op0, op1=op1), [out], ins)

    def stt(self, eng, out, a, s, b, op0, op1):
        ins = [a, b] + ([s] if isinstance(s, View) else [])
        sa = s.ap if isinstance(s, View) else s
        self.op(eng, lambda e: e.scalar_tensor_tensor(out=out.ap, in0=a.ap, scalar=sa, in1=b.ap, op0=op0, op1=op1), [out], ins)

    def act(self, out, in_, func, bias=None, scale=None):
        ins = [in_] + [s for s in (bias, scale) if isinstance(s, View)]
        kw = {}
        if bias is not None:
            kw["bias"] = bias.ap if isinstance(bias, View) else bias
        if scale is not None:
            kw["scale"] = scale.ap if isinstance(scale, View) else scale
        self.op("act", lambda e: e.activation(out=out.ap, in_=in_.ap, func=func, **kw), [out], ins)

    def cp(self, eng, out, in_):
        if eng == "act":
            self.op("act", lambda e: e.copy(out=out.ap, in_=in_.ap), [out], [in_])
        else:
            self.op(eng, lambda e: e.tensor_copy(out=out.ap, in_=in_.ap), [out], [in_])

    def scan(self, out, d0, d1, init, op0, op1):
        ins = [d0, d1] + ([init] if isinstance(init, View) else [])
        ia = init.ap if isinstance(init, View) else init
        self.op("dve", lambda e: e.tensor_tensor_scan(out=out.ap, data0=d0.ap, data1=d1.ap, initial=ia, op0=op0, op1=op1), [out], ins)

    def recip(self, out, in_):
        self.op("dve", lambda e: e.reciprocal(out=out.ap, in_=in_.ap), [out], [in_])

    def memset(self, eng, out, val):
        self.op(eng, lambda e: e.memset(out.ap, val), [out], [])

    def cpred(self, out, mask, data):
        self.op("dve", lambda e: e.copy_predicated(out=out.ap, mask=mask.ap, data=data.ap), [out], [mask, data])


class Packer:
    def __init__(self):
        self.cols = []
        self.off = {}
        self.n = 0

    def add(self, name, arr):
        arr = np.asarray(arr, dtype=np.float32)
        assert arr.shape[0] == 128, (name, arr.shape)
        arr = arr.reshape(128, -1)
        self.off[name] = (self.n, arr.shape[1])
        self.cols.append(arr)
        self.n += arr.shape[1]

    def pack(self):
        return np.ascontiguousarray(np.concatenate(self.cols, axis=1))


def feat(v):
    v = np.asarray(v, dtype=np.float32)
    return np.ascontiguousarray(v.reshape(-1, 128).T)


def rep(v):
    v = np.asarray(v, dtype=np.float32).reshape(1, -1)
    return np.ascontiguousarray(np.broadcast_to(v, (128, v.shape[1])))


def wtiles(W):
    K, N = W.shape
    Np = ((N + 127) // 128) * 128
    Kp = ((K + 1023) // 1024) * 1024
    Wp = np.zeros((Kp, Np), np.float32)
    Wp[:K, :N] = W
    out = []
    for kg in range(Kp // 1024):
        blk = Wp[kg * 1024:(kg + 1) * 1024]
        out.append(blk.reshape(8, 128, Np // 128, 128).transpose(2, 1, 0, 3))
    return out


def consts_host():
    P = Packer()
    idx = np.arange(128)
    P.add("ident", np.eye(128))
    P.add("ones", np.ones((128, 128)))
    le = (idx[:, None] <= idx[None, :]).astype(np.float32)
    lt = (idx[:, None] < idx[None, :]).astype(np.float32)
    P.add("le", le)
    P.add("lt", lt)
    P.add("nle", -le)
    P.add("nlt", -lt)
    P.add("ngt", -(idx[:, None] > idx[None, :]).astype(np.float32))
    blk = (idx[:, None] // 64 == idx[None, :] // 64).astype(np.float32)
    P.add("blk64", blk)
    P.add("le64", le * blk)
    P.add("ramp", rep(np.arange(1, 129)))
    return P


ODD_ORDER0 = [12, 13, 0, 4, 8, 1, 5, 9, 2, 6, 10, 3, 7, 11] + [14 + h + 4 * s for h in range(4) for s in range(4)]
ODD_ORDER1 = [12, 13, 30, 0, 4, 8, 1, 5, 9, 2, 6, 10, 3, 7, 11] + [14 + h + 4 * s for h in range(4) for s in range(4)]


def layer_host(inp, L):
    P = Packer()
    B = Packer()
    g = lambda n: np.asarray(inp[n], dtype=np.float32)
    P.add("n_mix_pre", feat(g("norm_mix_pre")[L]))
    P.add("n_mix_post", feat(g("norm_mix_post")[L]))
    P.add("n_ffn_pre", feat(g("norm_ffn_pre")[L]))
    P.add("n_ffn_post", feat(g("norm_ffn_post")[L]))
    cw = g("ffn_conv_w")[L]
    cb = g("ffn_conv_b")[L]
    order = np.concatenate([np.concatenate([np.arange(j * 128, (j + 1) * 128), DFF + np.arange(j * 128, (j + 1) * 128)]) for j in range(NJ)])
    P.add("ffn_cw", np.stack([feat(cw[k][order]) for k in range(3)], axis=2))
    P.add("ffn_cb", feat(cb[order]))
    tiles = []
    if L % 2 == 0:
        e = L // 2
        tiles.append(wtiles(g("ev_w_in")[e])[0])
        lr, li, ldt = g("s5_lam_re")[e], g("s5_lam_im")[e], g("s5_log_dt")[e]
        st = lambda a: np.ascontiguousarray(a.reshape(8, 2, 64).transpose(1, 2, 0).reshape(128, 8))
        P.add("s5_lr_s", st(lr))
        P.add("s5_li_s", st(li))
        P.add("s5_ldt_s", st(np.repeat(ldt[:, None], 64, axis=1)))
        B.add("s5_lr_b", rep(lr.reshape(-1)))
        B.add("s5_li_b", rep(li.reshape(-1)))
        B.add("s5_ldt_b", rep(np.repeat(ldt, 64)))
        br, bi = g("s5_b_re")[e], g("s5_b_im")[e]

        def bd(b):
            o = np.zeros((128, 2, 8, 64), np.float32)
            for kt in range(2):
                for gl in range(8):
                    o[gl * 16:(gl + 1) * 16, kt, gl, :] = b[8 * kt + gl].T
            return o.reshape(128, 1024)
        B.add("s5_br_bd", bd(br))
        B.add("s5_bi_bd", bd(bi))
        cr, ci = g("s5_c_re")[e], g("s5_c_im")[e]

        def cpad(c):
            o = np.zeros((128, 8, 128), np.float32)
            for j in range(8):
                for gp in range(2):
                    gg = 2 * j + gp
                    col = (gg % 8) * 16
                    o[gp * 64:(gp + 1) * 64, j, col:col + 16] = c[gg].T
            return o.reshape(128, 1024)
        B.add("s5_cr_pad", cpad(cr))
        B.add("s5_ci_pad", cpad(ci))
        B.add("s5_wglu", g("s5_w_glu")[e].reshape(2, 128, 256).transpose(1, 0, 2))
        P.add("s5_d", feat(g("s5_d")[e]))
        P.add("s5_bglu", feat(g("s5_b_glu")[e]))
        scw = g("ssd_conv_w")[e]
        P.add("ssd_cw", np.stack([feat(scw[k]) for k in range(4)], axis=2))
        P.add("ssd_cb", feat(g("ssd_conv_b")[e]))
        dtb = np.zeros((128, 1), np.float32)
        dtb[:12, 0] = g("ssd_dt_bias")[e]
        P.add("ssd_dtb", dtb)
        P.add("ssd_alog", rep(g("ssd_a_log")[e]))
        P.add("ssd_d", feat(np.repeat(g("ssd_d")[e], 64)))
        P.add("ssd_norm", feat(g("ssd_norm")[e]))
    else:
        o = L // 2
        w_in = g("od_w_in")[o]
        if o > 0:
            w_in = np.concatenate([w_in, g("rw_w_vin")[o - 1]], axis=1)
        wt = wtiles(w_in)[0]
        tiles.append(np.stack([wt[i] for i in (ODD_ORDER1 if o > 0 else ODD_ORDER0)]))
        mu = g("rw_mu")[o]
        if o > 0:
            mu = np.concatenate([mu, g("rw_mu_v")[o - 1], np.zeros(96, np.float32)])
        else:
            mu = np.concatenate([mu, np.zeros(128, np.float32)])
        P.add("rw_mu", feat(mu))
        P.add("rw_w0", feat(g("rw_w0")[o]))
        P.add("rw_a0", feat(g("rw_a0")[o]))
        B.add("rw_wa2", np.concatenate([g("rw_w2")[o], g("rw_a2")[o]], axis=0))
        B.add("rw_g2", g("rw_g2")[o])
        v2 = np.zeros((128, 512), np.float32)
        if o > 0:
            v2[:32] = g("rw_v2")[o - 1]
            P.add("rw_v0", feat(g("rw_v0")[o - 1]))
        B.add("rw_v2", v2)
        P.add("rw_k_k", feat(g("rw_k_k")[o]))
        P.add("rw_k_a", feat(g("rw_k_a")[o]))
        P.add("rw_r_k", feat(g("rw_r_k")[o]))
        P.add("rw_ln_w", feat(g("rw_ln_w")[o]))
        P.add("rw_ln_b", feat(g("rw_ln_b")[o]))
        P.add("hg_lb_raw0", feat(g("hg_lb_raw")[0]))
        P.add("hg_lb_raw1", feat(g("hg_lb_raw")[1]))
        P.add("hg_norm", feat(g("hg_norm")[o]))
    tiles.append(wtiles(g("mix_w_out")[L])[0])
    up = wtiles(g("ffn_w_up")[L])[0]
    tiles.append(np.stack([up[j + (0 if s == 0 else NJ)] for j in range(NJ) for s in (0, 1)]))
    dn = wtiles(g("ffn_w_down")[L])
    tiles.append(np.stack([dn[kg][m] for m in range(8) for kg in range(3)]))
    ws = np.ascontiguousarray(np.concatenate(tiles, axis=0).reshape(-1, 128, 1024))
    return P, B, ws
def build_program(pk_consts, pk_layers, pk_big, n_ws, n_layers=DEPTH, n_tiles=NTILES, dbg=None):
    nc = bass.Bass("TRN2", target_bir_lowering=False)
    S = Sched(nc)
    if dbg is not None and dbg[0] == "nops":
        S.max_ops = dbg[1]
    xT = nc.dram_tensor("xT", [8, 128, TPAD], F32, kind="ExternalInput").ap()
    oT = nc.dram_tensor("oT", [8, 128, TPAD], F32, kind="ExternalOutput").ap()
    cst_d = nc.dram_tensor("consts", [128, pk_consts.n], F32, kind="ExternalInput").ap()
    pv_d = [nc.dram_tensor("pv%d" % L, [128, pk_layers[L].n], F32, kind="ExternalInput").ap() for L in range(DEPTH)]
    pb_d = [nc.dram_tensor("pb%d" % L, [128, pk_big[L].n], F32, kind="ExternalInput").ap() for L in range(DEPTH)]
    ws_d = [nc.dram_tensor("ws%d" % L, [n_ws[L], 128, 1024], F32, kind="ExternalInput").ap() for L in range(DEPTH)]
    wsb_d = [nc.dram_tensor("wsb%d" % L, [n_ws[L], 128, 1024], BF16, kind="Internal").ap() for L in range(DEPTH)]
    WCH = 4
    wsb_regs = [[Reg() for _ in range((n_ws[L] + WCH - 1) // WCH)] for L in range(DEPTH)]
    dbg_d = None
    if dbg is not None:
        dbg_d = nc.dram_tensor("dbg", [8, 128, NT], F32, kind="ExternalOutput").ap()

    uid = [0]

    def mk(name, shape, dtype, nreg=1):
        uid[0] += 1
        return Buf(S, "%s_%d" % (name, uid[0]), shape, dtype, nreg)

    CST = mk("cst", [128, pk_consts.n], F32)
    PV = [mk("pvs", [128, pk_layers[L].n], F32) for L in range(DEPTH)]
    cst_bf = mk("cst_bf", [128, 3, 128], BF16)
    PSQ = [Reg() for _ in range(32)]
    PSt = nc.alloc_psum_tensor("psum_all", [128, 8, 512], F32)

    def C(name):
        o, n = pk_consts.off[name]
        return CST(CST.t[:, o:o + n])

    def pvv(L, name, a=None, b=None, p0=0, p1=128):
        o, n = pk_layers[L].off[name]
        if a is None:
            a, b = 0, n
        return PV[L](PV[L].t[p0:p1, o + a:o + b])

    def psr(bank, a, b):
        return [PSQ[bank]]

    def ps(bank, a=0, b=512, p0=0, p1=128):
        return View(PSt[p0:p1, bank, a:b], psr(bank, a, b), True)

    H = mk("h", [128, 8, NT], F32)
    HN = mk("hn", [128, 8, NT], BF16)
    SQ = mk("sq", [128, 2, NT], BF16, nreg=2)
    RSTD = mk("rstd", [128, NT], F32)
    TMP8 = mk("tmp8", [128, 8, NT], F32, nreg=8)
    YMIX = mk("ymix", [128, 8, NT], BF16, nreg=8)
    NWS = 6
    WRING = mk("wring", [128, NWS, 1024], BF16, nreg=NWS)
    FHALO = [mk("fhalo", [128, 2 * NJ, 2], F32) for L in range(DEPTH)]
    VFIRST = mk("vfirst", [128, 4, NT], F32, nreg=4)
    EPSB = mk("epsb", [128, 2], F32)
    wctr = [0]
    sqc = [0]

    ident = C("ident")
    ones_f = C("ones")
    ident_bf = cst_bf(cst_bf.t[:, 0, :])
    ones_bf = cst_bf(cst_bf.t[:, 1, :])
    blk64_bf = cst_bf(cst_bf.t[:, 2, :])

    def fl(v, pat, **kw):
        return View(v.ap.rearrange(pat, **kw), v.regs)

    def bc(v, ap):
        return View(ap, v.regs)

    LS = {}
    for L in range(n_layers):
        st = {}
        if L % 2 == 0:
            st["bb_re"] = mk("bbre", [128, 1024], BF16)
            st["bb_im"] = mk("bbim", [128, 1024], BF16)
            st["c_re"] = mk("cre", [128, 8, 128], F32)
            st["c_imn"] = mk("cimn", [128, 8, 128], F32)
            st["wglu"] = mk("wglu", [128, 2, 256], BF16)
            st["cos"] = mk("cos", [128, 8, CH], F32)
            st["sin"] = mk("sin", [128, 8, CH], F32)
            st["m"] = mk("m", [128, 8], F32)
            st["car_re"] = mk("carre", [128, 8], F32)
            st["car_im"] = mk("carim", [128, 8], F32)
            st["a_neg"] = mk("aneg", [128, 12], F32)
            st["Sst"] = mk("sst", [128, 12, 64], F32)
            st["prevT"] = mk("prevT", [128, 12, 128], BF16)
            st["xhalo"] = mk("xhalo", [128, 10, 3], F32)
        else:
            st["wa2"] = mk("wa2", [128, 512], BF16)
            st["g2"] = mk("g2", [128, 512], BF16)
            st["v2"] = mk("v2", [128, 512], BF16)
            st["lb"] = mk("lb", [128, 4], F32)
            st["oml"] = mk("oml", [128, 4], F32)
            st["T"] = mk("rwT", [128, 4, 64], F32, nreg=4)
            st["Tbf"] = mk("rwTbf", [128, 4, 64], BF16, nreg=4)
            st["HS"] = mk("hgS", [128, 4, 128], F32, nreg=4)
            st["HSbf"] = mk("hgSbf", [128, 4, 128], BF16, nreg=4)
            st["rwhalo"] = mk("rwhalo", [128, 15], F32)
        LS[L] = st

    arena_snap = (nc.sbuf_base, nc.sbuf_top)
    persist_bytes = S.sbuf_bytes

    def stop(stage):
        if dbg is not None and dbg[0] == "stop" and dbg[1] == stage:
            print("STOP stage", stage, "nops", S.nops)
            raise StopBuild()

    def arena_reset():
        S.barrier()
        nc.sbuf_base, nc.sbuf_top = arena_snap

    for L in range(n_layers):
        for ci in range(len(wsb_regs[L])):
            a, b = ci * WCH, min(n_ws[L], (ci + 1) * WCH)
            import os
            if os.environ.get("MAXCAST") and ci >= int(os.environ["MAXCAST"]):
                continue
            S.dma("pool", View(wsb_d[L][a:b], [wsb_regs[L][ci]]), dram(ws_d[L][a:b]))
    S.dma("sp", CST(), dram(cst_d))
    S.dma("sp", H(), dram(xT[:, :, 0:NT].rearrange("d p t -> p d t")))
    for L in range(n_layers):
        S.dma("sp", PV[L](), dram(pv_d[L]))
    for i, nm in enumerate(["ident", "ones", "blk64"]):
        S.cp("dve", cst_bf(cst_bf.t[:, i, :]), C(nm))
    for L in range(n_layers):
        S.memset("pool", FHALO[L](), 0.0)
    S.memset("dve", EPSB(EPSB.t[:, 0:1]), EPS)
    S.memset("dve", EPSB(EPSB.t[:, 1:2]), GN_EPS)
    eps_v = EPSB(EPSB.t[:, 0:1])
    gneps_v = EPSB(EPSB.t[:, 1:2])

    def sincos(x, sin_out, cos_out, tmp):
        ti = View(tmp.t[:, 3, :].bitcast(I32), tmp.regs)
        y, k, f = tmp(tmp.t[:, 0, :]), tmp(tmp.t[:, 1, :]), tmp(tmp.t[:, 2, :])
        for which, outv in ((0, sin_out), (1, cos_out)):
            S.ts("dve", y, x, 1.0 / (2 * math.pi), ALU.mult, 0.5 + 0.25 * which, ALU.add)
            S.cp("dve", ti, y)
            S.cp("dve", k, ti)
            S.tt("dve", f, y, k, ALU.subtract)
            S.ts("dve", k, f, 0.0, ALU.is_lt)
            S.tt("dve", f, f, k, ALU.add)
            S.ts("dve", f, f, 1.0, ALU.min)
            S.ts("dve", f, f, 2 * math.pi, ALU.mult, -math.pi, ALU.add)
            S.act(outv, f, AF.Sin)

    PVB = mk("pvb", [128, 8, 512], F32, nreg=8)
    TB = mk("s5tmpb", [128, 12, 512], F32, nreg=12)
    TS4 = mk("s5tmp4", [128, 4, 512], F32)

    def tb(i):
        return TB(TB.t[:, i, :], i)

    for L in range(n_layers):
        st = LS[L]

        def bgload(slot, name, a, b):
            o, n = pk_big[L].off[name]
            v = PVB(PVB.t[:, slot, 0:b - a], slot)
            S.dma("sp", v, dram(pb_d[L][:, o + a:o + b]))
            return v
        if L % 2 == 0:
            for half in range(2):
                hs = slice(half * 512, (half + 1) * 512)
                a_, b_ = half * 512, (half + 1) * 512
                lr, li, ldt = bgload(0, "s5_lr_b", a_, b_), bgload(1, "s5_li_b", a_, b_), bgload(2, "s5_ldt_b", a_, b_)
                br, bi = bgload(3, "s5_br_bd", a_, b_), bgload(4, "s5_bi_bd", a_, b_)
                crp, cip = bgload(5, "s5_cr_pad", a_, b_), bgload(6, "s5_ci_pad", a_, b_)
                dt, lrdt, lidt, mag, sn, cs, abr, abi, den, fre, fim, t1 = [tb(i) for i in range(12)]
                S.act(dt, ldt, AF.Exp)
                S.tt("dve", lrdt, lr, dt, ALU.mult)
                S.tt("dve", lidt, li, dt, ALU.mult)
                S.act(mag, lrdt, AF.Exp)
                sincos(lidt, sn, cs, TS4)
                S.tt("dve", abr, mag, cs, ALU.mult)
                S.tt("dve", abi, mag, sn, ALU.mult)
                S.tt("dve", den, lr, lr, ALU.mult)
                S.tt("dve", t1, li, li, ALU.mult)
                S.tt("dve", den, den, t1, ALU.add)
                S.recip(den, den)
                S.ts("dve", abr, abr, -1.0, ALU.add)
                S.tt("dve", fre, abr, lr, ALU.mult)
                S.tt("dve", t1, abi, li, ALU.mult)
                S.tt("dve", fre, fre, t1, ALU.add)
                S.tt("dve", fre, fre, den, ALU.mult)
                S.tt("dve", fim, abi, lr, ALU.mult)
                S.tt("dve", t1, abr, li, ALU.mult)
                S.tt("dve", fim, fim, t1, ALU.subtract)
                S.tt("dve", fim, fim, den, ALU.mult)
                S.tt("dve", t1, fre, br, ALU.mult)
                S.tt("dve", dt, fim, bi, ALU.mult)
                S.tt("dve", st["bb_re"](st["bb_re"].t[:, hs]), t1, dt, ALU.subtract)
                S.tt("dve", t1, fre, bi, ALU.mult)
                S.tt("dve", dt, fim, br, ALU.mult)
                S.tt("dve", st["bb_im"](st["bb_im"].t[:, hs]), t1, dt, ALU.add)
                js = slice(half * 4, half * 4 + 4)
                S.cp("dve", st["c_re"](st["c_re"].t[:, js, :].rearrange("p a b -> p (a b)")), crp)
                S.ts("dve", st["c_imn"](st["c_imn"].t[:, js, :].rearrange("p a b -> p (a b)")), cip, -1.0, ALU.mult)
                if half == 0:
                    wg = bgload(7, "s5_wglu", 0, 512)
                    S.cp("dve", fl(st["wglu"](), "p a b -> p (a b)"), wg)
                    dts, th = TS4(TS4.t[:, 0, 0:8]), TS4(TS4.t[:, 0, 8:16])
                    S.act(dts, pvv(L, "s5_ldt_s"), AF.Exp)
                    S.tt("dve", th, pvv(L, "s5_li_s"), dts, ALU.mult)
                    S.tt("dve", dts, pvv(L, "s5_lr_s"), dts, ALU.mult)
                    S.act(st["m"](), dts, AF.Exp)
                    S.cp("dve", st["car_re"](), th)
                ang = TB(TB.t[:, 1, :].rearrange("p (a b) -> p a b", a=4), 1)
                ramp = C("ramp")
                thv = st["car_re"](st["car_re"].t[:, js])
                S.tt("dve", ang, bc(ramp, ramp.ap[:, None, :].broadcast_to([128, 4, 128])),
                     bc(thv, thv.ap[:, :, None].broadcast_to([128, 4, 128])), ALU.mult)
                sincos(tb(1), st["sin"](st["sin"].t[:, js, :].rearrange("p a b -> p (a b)")),
                       st["cos"](st["cos"].t[:, js, :].rearrange("p a b -> p (a b)")), TS4)
            S.memset("dve", st["car_re"](), 0.0)
            S.memset("dve", st["car_im"](), 0.0)
            S.act(st["a_neg"](), pvv(L, "ssd_alog"), AF.Exp)
            S.ts("dve", st["a_neg"](), st["a_neg"](), -1.0, ALU.mult)
            S.memset("pool", st["Sst"](), 0.0)
            S.memset("pool", st["prevT"](), 0.0)
            S.memset("pool", st["xhalo"](), 0.0)
        else:
            S.cp("dve", st["wa2"](), bgload(0, "rw_wa2", 0, 512))
            S.cp("dve", st["g2"](), bgload(1, "rw_g2", 0, 512))
            S.cp("dve", st["v2"](), bgload(2, "rw_v2", 0, 512))
            if L // 2 == 0:
                S.memset("dve", st["lb"](), 0.0)
            else:
                S.tt("dve", st["lb"](), pvv(L, "hg_lb_raw1"), pvv(L, "hg_lb_raw0"), ALU.subtract)
                S.act(st["lb"](), st["lb"](), AF.Sigmoid)
            S.ts("dve", st["oml"](), st["lb"](), -1.0, ALU.mult, 1.0, ALU.add)
            S.memset("pool", st["T"](), 0.0)
            S.memset("pool", st["Tbf"](), 0.0)
            S.memset("pool", st["HS"](), 0.0)
            S.memset("pool", st["HSbf"](), 0.0)
            S.memset("pool", st["rwhalo"](), 0.0)
    arena_reset()
    if dbg is not None and dbg[0] == "stop" and dbg[1] == "prologue":
        S.dma("sp", dram(oT[:, :, 0:NT].rearrange("d p t -> p d t")), H(), is_output=True)
        S.finish()
        return nc

    def wload(L, idx):
        slot = wctr[0] % NWS
        wctr[0] += 1
        S.dma("sp", WRING(WRING.t[:, slot, :], slot), View(wsb_d[L][idx], [wsb_regs[L][idx // WCH]]))
        return slot

    def wk(slot, k, m0=0, m1=128, p0=0, p1=128):
        return WRING(WRING.t[p0:p1, slot, k * 128 + m0:k * 128 + m1], slot)

    def interleave(gens):
        gens = list(gens)
        while gens:
            for g in list(gens):
                try:
                    next(g)
                except StopIteration:
                    gens.remove(g)

    psrot = [0]

    def next_bank(nb=4):
        b = psrot[0] % nb
        psrot[0] += 1
        return b

    def norm_stats(src_views, nfeat, ones=None, epsv=None, n=NT):
        m = len(src_views)
        for i, v in enumerate(src_views):
            q = sqc[0] % 2
            sqc[0] += 1
            S.act(SQ(SQ.t[:, q, 0:n], q), v, AF.Square)
            S.mm(ps(7, 0, n), ones if ones is not None else ones_bf, SQ(SQ.t[:, q, 0:n], q), start=(i == 0), stop=(i == m - 1), signal=(i == m - 1))
        r = RSTD(RSTD.t[:, 0:n])
        S.act(r, ps(7, 0, n), AF.Sqrt, bias=epsv if epsv is not None else eps_v, scale=1.0 / nfeat)
        S.recip(r, r)
        return r

    def pre_norm(L, gname):
        r = norm_stats([H(H.t[:, d, :]) for d in range(8)], D)
        for d in range(8):
            S.stt("dve", HN(HN.t[:, d, :]), H(H.t[:, d, :]), pvv(L, gname, d, d + 1), r, ALU.mult, ALU.mult)

    def post_norm_add(L, gname):
        r = norm_stats([TMP8(TMP8.t[:, d, :], d) for d in range(8)], D)
        for d in range(8):
            S.stt("dve", TMP8(TMP8.t[:, d, :], d), TMP8(TMP8.t[:, d, :], d), pvv(L, gname, d, d + 1), r, ALU.mult, ALU.mult)
            S.tt("dve", H(H.t[:, d, :]), H(H.t[:, d, :]), TMP8(TMP8.t[:, d, :], d), ALU.add)

    def inproj(L, widx, nb=4):
        slot = wload(L, widx)
        b = next_bank(nb)
        for k in range(8):
            S.mm(ps(b, 0, NT), wk(slot, k), HN(HN.t[:, k, :]), start=(k == 0), stop=(k == 7), signal=(k == 7))
        return b

    def proj8(L, widx0, rhs_buf, nk):
        nkg = (nk + 7) // 8
        wi = widx0
        for m in range(8):
            b = next_bank()
            kk = 0
            for kg in range(nkg):
                slot = wload(L, wi)
                wi += 1
                for k in range(min(8, nk - kg * 8)):
                    S.mm(ps(b, 0, NT), wk(slot, k), rhs_buf(rhs_buf.t[:, kk, :], kk), start=(kk == 0), stop=(kk == nk - 1), signal=(kk == nk - 1))
                    kk += 1
            S.cp("act", TMP8(TMP8.t[:, m, :], m), ps(b, 0, NT))
        return wi

    def rolling(factories, width):
        pending = list(factories)
        active = []
        while pending or active:
            while pending and len(active) < width:
                active.append(pending.pop(0)())
            for g in list(active):
                try:
                    next(g)
                except StopIteration:
                    active.remove(g)

    def ffn(L, widx0):
        arena_reset()
        NST = 3
        ABUF = mk("abuf", [128, NJ, NT], BF16, nreg=NJ)
        FRAW = [mk("fraw", [128, 2, 2 + NT], F32, nreg=2) for i in range(NST)]
        FACC = [mk("facc", [128, 2, NT], F32, nreg=2) for i in range(NST)]
        FTMP = [mk("ftmp", [128, NT], F32) for i in range(NST)]
        pre_norm(L, "n_ffn_pre")
        o, _ = pk_layers[L].off["ffn_cw"]
        ob, _ = pk_layers[L].off["ffn_cb"]

        def ffn_j(j):
            sl = j % NST
            fr, fa = FRAW[sl], FACC[sl]
            S.cp("pool", fr(fr.t[:, :, 0:2]), FHALO[L](FHALO[L].t[:, 2 * j:2 * j + 2, :]))
            for s in range(2):
                b = inproj(L, widx0 + 2 * j + s, nb=6)
                S.cp("act", fr(fr.t[:, s, 2:2 + NT], s), ps(b, 0, NT))
                yield
                ti = 2 * j + s
                cw = lambda kk: PV[L](PV[L].t[:, o + ti * 3 + kk:o + ti * 3 + kk + 1])
                cbv = PV[L](PV[L].t[:, ob + ti:ob + ti + 1])
                acc = fa(fa.t[:, s, :], s)
                eng = "dve"
                S.ts(eng, acc, fr(fr.t[:, s, 0:NT], s), cw(0), ALU.mult, cbv, ALU.add)
                yield
                for kk in (1, 2):
                    if eng == "dve":
                        S.stt(eng, acc, fr(fr.t[:, s, kk:kk + NT], s), cw(kk), acc, ALU.mult, ALU.add)
                    else:
                        S.ts(eng, FTMP[sl](), fr(fr.t[:, s, kk:kk + NT], s), cw(kk), ALU.mult)
                        S.tt(eng, acc, acc, FTMP[sl](), ALU.add)
                    yield
            S.cp("pool", FHALO[L](FHALO[L].t[:, 2 * j:2 * j + 2, :]), fr(fr.t[:, :, NT:NT + 2]))
            S.act(fa(fa.t[:, 0, :], 0), fa(fa.t[:, 0, :], 0), AF.Gelu_apprx_tanh)
            yield
            S.tt("dve", ABUF(ABUF.t[:, j, :], j), fa(fa.t[:, 0, :], 0), fa(fa.t[:, 1, :], 1), ALU.mult)
            yield

        rolling([(lambda j=j: ffn_j(j)) for j in range(NJ)], NST)
        widx = proj8(L, widx0 + 2 * NJ, ABUF, NJ)
        post_norm_add(L, "n_ffn_post")
        return widx

    def even_layer(L):
        st = LS[L]
        arena_reset()
        pre_norm(L, "n_mix_pre")
        stop("prenorm")
        UF = mk("uf", [128, 2, NT], F32)
        UB = mk("ub", [128, 2, NT], BF16)
        S5T = [mk("s5t", [128, 6, CH], F32) for i in range(2)]
        S5X = [mk("s5x", [128, 2, NT], F32) for i in range(2)]
        S5Y = mk("s5y", [128, 2, NT], F32)
        S5YB = mk("s5yb", [128, 2, NT], BF16)
        for m in range(2):
            b = inproj(L, m)
            S.cp("act", UF(UF.t[:, m, :]), ps(b, 0, NT))
            S.cp("dve", UB(UB.t[:, m, :]), ps(b, 0, NT))
        stop("inproj")
        for j in range(8):
            kt, jj = j // 4, j % 4
            bre, bim = next_bank(), next_bank()
            c0 = kt * 512 + jj * 128
            S.mm(ps(bre, 0, NT), st["bb_re"](st["bb_re"].t[:, c0:c0 + 128]), UB(UB.t[:, kt, :]))
            S.mm(ps(bim, 0, NT), st["bb_im"](st["bb_im"].t[:, c0:c0 + 128]), UB(UB.t[:, kt, :]))
            X = S5X[j % 2]
            cs, sn = st["cos"](st["cos"].t[:, j, :]), st["sin"](st["sin"].t[:, j, :])
            mb = bc(st["m"](), st["m"].t[:, j:j + 1].to_broadcast([128, CH]))
            for c in range(CPT):
                T = S5T[c % 2]
                pr, pi = ps(bre, c * CH, (c + 1) * CH), ps(bim, c * CH, (c + 1) * CH)
                t = lambda i: T(T.t[:, i, :])
                S.tt("dve", t(0), pr, cs, ALU.mult)
                S.tt("dve", t(1), pi, sn, ALU.mult)
                S.tt("dve", t(0), t(0), t(1), ALU.add)
                S.tt("dve", t(2), pi, cs, ALU.mult)
                S.tt("dve", t(3), pr, sn, ALU.mult)
                S.tt("dve", t(2), t(2), t(3), ALU.subtract)
                if c == 0:
                    ir, ii = st["car_re"](st["car_re"].t[:, j:j + 1]), st["car_im"](st["car_im"].t[:, j:j + 1])
                else:
                    ir, ii = X(X.t[:, 0, c * CH - 1:c * CH]), X(X.t[:, 1, c * CH - 1:c * CH])
                S.scan(t(4), mb, t(0), ir, ALU.mult, ALU.add)
                S.scan(t(5), mb, t(2), ii, ALU.mult, ALU.add)
                xr, xi = X(X.t[:, 0, c * CH:(c + 1) * CH]), X(X.t[:, 1, c * CH:(c + 1) * CH])
                S.tt("dve", t(0), t(4), cs, ALU.mult)
                S.tt("dve", t(1), t(5), sn, ALU.mult)
                S.tt("dve", xr, t(0), t(1), ALU.subtract)
                S.tt("dve", t(2), t(5), cs, ALU.mult)
                S.tt("dve", t(3), t(4), sn, ALU.mult)
                S.tt("dve", xi, t(2), t(3), ALU.add)
            S.cp("pool", st["car_re"](st["car_re"].t[:, j:j + 1]), X(X.t[:, 0, NT - 1:NT]))
            S.cp("pool", st["car_im"](st["car_im"].t[:, j:j + 1]), X(X.t[:, 1, NT - 1:NT]))
            yb = 4 + j // 4
            S.mm(ps(yb, 0, NT), st["c_re"](st["c_re"].t[:, j, :]), X(X.t[:, 0, :]), start=(jj == 0), stop=False, signal=False)
            S.mm(ps(yb, 0, NT), st["c_imn"](st["c_imn"].t[:, j, :]), X(X.t[:, 1, :]), start=False, stop=(jj == 3), signal=True)
        stop("s5scan")
        for ot in range(2):
            yv = S5Y(S5Y.t[:, ot, :])
            S.stt("dve", yv, UF(UF.t[:, ot, :]), pvv(L, "s5_d", ot, ot + 1), ps(4 + ot, 0, NT), ALU.mult, ALU.add)
            S.act(yv, yv, AF.Gelu_apprx_tanh)
            S.cp("act", S5YB(S5YB.t[:, ot, :]), yv)
        for ot in range(2):
            b = next_bank()
            for k in range(2):
                S.mm(ps(b, 0, NT), st["wglu"](st["wglu"].t[:, k, ot * 128:(ot + 1) * 128]), S5YB(S5YB.t[:, k, :]), start=(k == 0), stop=(k == 1), signal=(k == 1))
            sg = S5X[0](S5X[0].t[:, 0, :])
            S.act(sg, ps(b, 0, NT), AF.Sigmoid, bias=pvv(L, "s5_bglu", ot, ot + 1))
            S.tt("dve", YMIX(YMIX.t[:, ot, :], ot), S5Y(S5Y.t[:, ot, :]), sg, ALU.mult)
        stop("s5")
        arena_reset()
        ZS = mk("zs", [128, 6, NT], F32)
        XA = mk("xa", [128, 10, NT], F32, nreg=10)
        XAB = mk("xab", [128, 4, NT], BF16)
        DTT = mk("dtt", [128, NT], F32)
        YS = mk("ys", [128, 6, NT], F32)
        RAW = [mk("xraw", [128, 3 + NT], F32) for i in range(2)]
        XDP = mk("xdp", [128, 12, 128], BF16)
        XDD = mk("xdd", [128, 12, 64], BF16)
        XST = mk("xst", [128, 12, 64], F32)
        BTK = mk("btk", [128, 2, 128], BF16)
        SM = mk("ssdsmall", [128, 8, 12], F32, nreg=8)
        SEG = mk("seg", [128, 12, 128], F32)
        EAR = mk("ear", [128, 12, 128], F32)
        SCT = mk("sct", [128, 12, 128], BF16)
        CEX = mk("cex", [128, 12, 128], BF16)
        CBM = mk("cbm", [128, 2, 128], F32)
        S.memset("pool", XDP(), 0.0)
        ocw, _ = pk_layers[L].off["ssd_cw"]
        ocb, _ = pk_layers[L].off["ssd_cb"]
        for m in range(2, 19):
            b = inproj(L, m)
            p = ps(b, 0, NT)
            if m < 8:
                S.act(ZS(ZS.t[:, m - 2, :]), p, AF.Silu)
            elif m < 18:
                i = m - 8
                rw = RAW[i % 2]
                S.cp("pool", rw(rw.t[:, 0:3]), st["xhalo"](st["xhalo"].t[:, i, :]))
                S.cp("act", rw(rw.t[:, 3:3 + NT]), p)
                cw = lambda kk: PV[L](PV[L].t[:, ocw + i * 4 + kk:ocw + i * 4 + kk + 1])
                acc = XA(XA.t[:, i, :], i)
                S.ts("dve", acc, rw(rw.t[:, 0:NT]), cw(0), ALU.mult, PV[L](PV[L].t[:, ocb + i:ocb + i + 1]), ALU.add)
                for kk in range(1, 4):
                    S.stt("dve", acc, rw(rw.t[:, kk:kk + NT]), cw(kk), acc, ALU.mult, ALU.add)
                S.cp("pool", st["xhalo"](st["xhalo"].t[:, i, :]), rw(rw.t[:, NT:NT + 3]))
                S.act(acc, acc, AF.Silu)
                if i >= 6:
                    S.cp("act", XAB(XAB.t[:, i - 6, :]), acc)
            else:
                S.act(DTT(DTT.t[0:12, :]), ps(b, 0, NT, 0, 12), AF.Exp, bias=pvv(L, "ssd_dtb", 0, 1, 0, 12))
                S.act(DTT(DTT.t[0:12, :]), DTT(DTT.t[0:12, :]), AF.Ln, bias=pvv(L, "ssd_dtb", 0, 1, 0, 12) if False else 1.0)
        stop("ssdproj")
        a_neg = st["a_neg"]()
        le = C("le")
        sm = lambda i: SM(SM.t[:, i, :], i)
        arow_regs = psr(1, 0, 512) + psr(2, 0, 512) + psr(3, 0, 512)
        for c in range(CPT):
            cc = slice(c * CH, (c + 1) * CH)
            for i in range(6):
                bk, off = (4, i * 128) if i < 4 else (5, (i - 4) * 128)
                S.tr(ps(bk, off, off + 128), XA(XA.t[:, i, cc], i), ident)
            for g2 in range(2):
                S.tr(ps(6, g2 * 128, (g2 + 1) * 128), XA(XA.t[:, 6 + g2, cc], 6 + g2), ident)
            S.tr(ps(7, 256, 268), DTT(DTT.t[0:12, cc]), bc(ident, ident.ap[0:12, 0:12]))
            S.cp("act", XST(XST.t[:, 0:8, :].rearrange("p a b -> p (a b)")), ps(4, 0, 512))
            S.cp("act", XST(XST.t[:, 8:12, :].rearrange("p a b -> p (a b)")), ps(5, 0, 256))
            S.cp("act", fl(BTK(), "p a b -> p (a b)"), ps(6, 0, 256))
            dtk, da, acol, aend, dte, w2, edec = [sm(i) for i in range(7)]
            S.cp("dve", dtk, ps(7, 256, 268))
            S.tt("dve", da, dtk, a_neg, ALU.mult)
            S.mm(ps(7, 272, 284), le, da)
            S.cp("act", acol, ps(7, 272, 284))
            S.tt("dve", SEG(), bc(da, da.ap[:, :, None].broadcast_to([128, 12, 128])),
                 bc(le, le.ap[:, None, :].broadcast_to([128, 12, 128])), ALU.mult)
            for q in range(3):
                S.mm(ps(1 + q, 0, 512), ones_f, SEG(SEG.t[:, 4 * q:4 * q + 4, :].rearrange("p a b -> p (a b)")))
            arow = View(PSt[:, 1:4, :].rearrange("p a (b c) -> p (a b) c", c=128), arow_regs, True)
            S.act(EAR(), arow, AF.Exp)
            S.cp("dve", aend, View(PSt[:, 1:4, :].rearrange("p a (b c) -> p (a b) c", c=128)[:, :, 127], arow_regs, True))
            S.tt("dve", SEG(), arow, bc(acol, acol.ap[:, :, None].broadcast_to([128, 12, 128])), ALU.subtract)
            S.ts("dve", SEG(), SEG(), 0.0, ALU.min)
            S.act(SEG(), SEG(), AF.Exp)
            for g2 in range(2):
                S.mm(ps(6, 256 + g2 * 128, 256 + (g2 + 1) * 128), XAB(XAB.t[:, g2, cc]), XAB(XAB.t[:, 2 + g2, cc]))
            S.tt("dve", CBM(), View(PSt[:, 6, 256:512].rearrange("p (a b) -> p a b", a=2), psr(6, 256, 512), True),
                 bc(le, le.ap[:, None, :].broadcast_to([128, 2, 128])), ALU.mult)
            for g2 in range(2):
                hs = slice(6 * g2, 6 * g2 + 6)
                S.tt("dve", SCT(SCT.t[:, hs, :]), SEG(SEG.t[:, hs, :]), CBM(CBM.t[:, g2:g2 + 1, :].broadcast_to([128, 6, 128])), ALU.mult)
                S.tt("dve", CEX(CEX.t[:, hs, :]), EAR(EAR.t[:, hs, :]),
                     XA(XA.t[:, 8 + g2:9 + g2, cc].broadcast_to([128, 6, 128]), 8 + g2), ALU.mult)
            xdp5 = XDP.t[:].rearrange("p (a two) (h c) -> p a two h c", two=2, h=2)
            xst4 = XST.t[:].rearrange("p (a two) c -> p a two c", two=2)
            dtk4 = dtk.ap.rearrange("p (a two) -> p a two", two=2)
            for par in range(2):
                S.tt("dve", XDP(xdp5[:, :, par, par, :]), XST(xst4[:, :, par, :]),
                     bc(dtk, dtk4[:, :, par:par + 1].broadcast_to([128, 6, 64])), ALU.mult)
            for pj in range(6):
                yo = ps(0, pj * 128, (pj + 1) * 128) if pj < 4 else ps(7, (pj - 4) * 128, (pj - 3) * 128)
                for hh in range(2):
                    hI = 2 * pj + hh
                    S.mm(yo, XDP(XDP.t[:, hI, :]), SCT(SCT.t[:, hI, :]), start=(hh == 0), stop=False, signal=False)
                for hh in range(2):
                    hI = 2 * pj + hh
                    S.mm(yo, st["prevT"](st["prevT"].t[:, hI, :]), CEX(CEX.t[:, hI, :]), start=False, stop=(hh == 1), signal=(hh == 1))
                S.stt("dve", YS(YS.t[:, pj, cc]), XA(XA.t[:, pj, cc], pj), pvv(L, "ssd_d", pj, pj + 1), yo, ALU.mult, ALU.add)
            S.tt("dve", dte, aend, acol, ALU.subtract)
            S.act(dte, dte, AF.Exp)
            S.tt("dve", w2, dtk, dte, ALU.mult)
            S.act(edec, aend, AF.Exp)
            S.tt("dve", XDD(), XST(), bc(w2, w2.ap[:, :, None].broadcast_to([128, 12, 64])), ALU.mult)
            for g2 in range(2):
                S.mm(ps(4 + g2, 0, 384), BTK(BTK.t[:, g2, :]), XDD(XDD.t[:, 6 * g2:6 * g2 + 6, :].rearrange("p a b -> p (a b)")))
            Sst = st["Sst"]
            S.tt("dve", Sst(), Sst(), bc(edec, edec.ap[:, :, None].broadcast_to([128, 12, 64])), ALU.mult)
            for g2 in range(2):
                S.tt("dve", Sst(Sst.t[:, 6 * g2:6 * g2 + 6, :]), Sst(Sst.t[:, 6 * g2:6 * g2 + 6, :]),
                     View(PSt[:, 4 + g2, 0:384].rearrange("p (a b) -> p a b", a=6), psr(4 + g2, 0, 384), True), ALU.add)
            pt5 = st["prevT"].t[:].rearrange("p (a two) (h c) -> p a two h c", two=2, h=2)
            ss4 = Sst.t[:].rearrange("p (a two) c -> p a two c", two=2)
            for par in range(2):
                S.cp("act", st["prevT"](pt5[:, :, par, par, :]), Sst(ss4[:, :, par, :]))
        S.tt("dve", YS(), YS(), ZS(), ALU.mult)
        for g2 in range(2):
            r = norm_stats([YS(YS.t[:, 3 * g2 + i, :]) for i in range(3)], 384)
            for i in range(3):
                S.stt("dve", YMIX(YMIX.t[:, 2 + 3 * g2 + i, :], 2 + 3 * g2 + i), YS(YS.t[:, 3 * g2 + i, :]),
                      pvv(L, "ssd_norm", 3 * g2 + i, 3 * g2 + i + 1), r, ALU.mult, ALU.mult)
        return 19

    C0 = math.exp(-0.5)

    def odd_layer(L):
        st = LS[L]
        o = L // 2
        arena_reset()
        pre_norm(L, "n_mix_pre")
        widx = 0
        RH = mk("rh", [128, 4, NT], BF16, nreg=4)
        KH = mk("kh", [128, 4, NT], BF16, nreg=4)
        KHH = mk("khh", [128, 4, NT], BF16, nreg=4)
        BH = mk("bh", [128, 4, NT], BF16, nreg=4)
        VB = mk("vb", [128, 4, NT], BF16, nreg=4)
        BV = mk("bv", [128, 4, NT], F32, nreg=4)
        SG = mk("sg", [128, NT], BF16)
        GL = mk("gl", [128, 4, CPT], F32, nreg=4)
        r1_snap = (nc.sbuf_base, nc.sbuf_top)
        RAW = [mk("rraw", [128, 1 + NT], F32) for i in range(2)]
        TW = mk("tw", [128, NT], BF16)
        PVb = mk("pvb16", [128, NT], BF16)
        Dt = mk("dtmp", [128, NT], F32)
        Rr, Kk, Vv, Aa, SGW, KAP, KT_, Bb, E_, LGS, VS = [mk("r0t", [128, NT], F32) for _ in range(11)]
        RK = mk("rk", [128, NT], BF16)

        def shifted(idx, dst):
            nonlocal widx
            b = inproj(L, widx)
            widx += 1
            rw = RAW[widx % 2]
            S.cp("pool", rw(rw.t[:, 0:1]), st["rwhalo"](st["rwhalo"].t[:, idx:idx + 1]))
            S.cp("act", rw(rw.t[:, 1:1 + NT]), ps(b, 0, NT))
            S.tt("dve", Dt(), rw(rw.t[:, 0:NT]), rw(rw.t[:, 1:1 + NT]), ALU.subtract)
            S.stt("dve", dst, Dt(), pvv(L, "rw_mu", idx, idx + 1), rw(rw.t[:, 1:1 + NT]), ALU.mult, ALU.add)
            S.cp("pool", st["rwhalo"](st["rwhalo"].t[:, idx:idx + 1]), rw(rw.t[:, NT:NT + 1]))

        shifted(12, Rr())
        S.act(TW(TW.t[0:64, :]), Rr(Rr.t[0:64, :]), AF.Tanh)
        S.cp("dve", TW(TW.t[64:128, :]), Rr(Rr.t[64:128, :]))
        shifted(13, Kk())
        S.act(SG(), Kk(), AF.Sigmoid)
        if o > 0:
            shifted(14, Vv())
            S.cp("dve", PVb(PVb.t[0:32, :]), Vv(Vv.t[0:32, :]))
        for i in range(4):
            ic = slice(i * 128, (i + 1) * 128)
            shifted(i, Rr())
            shifted(4 + i, Kk())
            shifted(8 + i, Vv())
            bw, ba = next_bank(), next_bank()
            S.mm(ps(bw, 0, NT), st["wa2"](st["wa2"].t[0:64, ic]), TW(TW.t[0:64, :]))
            S.mm(ps(ba, 0, NT), st["wa2"](st["wa2"].t[64:128, ic]), TW(TW.t[64:128, :]))
            S.act(SGW(), ps(bw, 0, NT), AF.Sigmoid, bias=pvv(L, "rw_w0", i, i + 1))
            S.act(Aa(), ps(ba, 0, NT), AF.Sigmoid, bias=pvv(L, "rw_a0", i, i + 1))
            if o > 0:
                bv_ = next_bank()
                S.mm(ps(bv_, 0, NT), st["v2"](st["v2"].t[0:32, ic]), PVb(PVb.t[0:32, :]))
                S.act(VS(), ps(bv_, 0, NT), AF.Sigmoid, bias=pvv(L, "rw_v0", i, i + 1))
                S.tt("dve", Dt(), VFIRST(VFIRST.t[:, i, :], i), Vv(), ALU.subtract)
                S.tt("dve", Dt(), Dt(), VS(), ALU.mult)
                S.tt("dve", Vv(), Vv(), Dt(), ALU.add)
            else:
                S.cp("act", VFIRST(VFIRST.t[:, i, :], i), Vv())
            S.cp("act", VB(VB.t[:, i, :], i), Vv())
            S.ts("dve", KAP(), Kk(), pvv(L, "rw_k_k", i, i + 1), ALU.mult)
            q = sqc[0] % 2
            sqc[0] += 1
            S.act(SQ(SQ.t[:, q, :], q), KAP(), AF.Square)
            S.mm(ps(7, 0, NT), blk64_bf, SQ(SQ.t[:, q, :], q))
            S.act(E_(), ps(7, 0, NT), AF.Sqrt)
            S.ts("dve", E_(), E_(), 1e-12, ALU.max)
            S.recip(E_(), E_())
            S.tt("dve", KAP(), KAP(), E_(), ALU.mult)
            S.ts("dve", KT_(), Aa(), -1.0, ALU.add, pvv(L, "rw_k_a", i, i + 1), ALU.mult)
            S.stt("dve", KT_(), KT_(), 1.0, Kk(), ALU.add, ALU.mult)
            S.tt("dve", Bb(), KAP(), Aa(), ALU.mult)
            S.stt("dve", RK(), Rr(), pvv(L, "rw_r_k", i, i + 1), KT_(), ALU.mult, ALU.mult)
            bb_ = next_bank()
            S.mm(ps(bb_, 0, NT), blk64_bf, RK())
            S.tt("dve", BV(BV.t[:, i, :], i), ps(bb_, 0, NT), Vv(), ALU.mult)
            for c in range(CPT):
                cc = slice(c * CH, (c + 1) * CH)
                S.scan(LGS(LGS.t[:, cc]), ones_f, SGW(SGW.t[:, cc]), 0.0, ALU.mult, ALU.add)
            S.act(E_(), LGS(), AF.Exp, scale=-C0)
            S.tt("dve", RH(RH.t[:, i, :], i), Rr(), E_(), ALU.mult)
            S.cp("pool", GL(GL.t[:, i, :], i), bc(E_(), E_.t[:, :].rearrange("p (c t) -> p c t", c=CPT)[:, :, CH - 1]))
            S.tt("dve", Dt(), LGS(), SGW(), ALU.subtract)
            S.act(Dt(), Dt(), AF.Exp, scale=-C0)
            S.tt("dve", KH(KH.t[:, i, :], i), KAP(), Dt(), ALU.mult)
            S.act(E_(), LGS(), AF.Exp, scale=C0)
            S.tt("dve", KHH(KHH.t[:, i, :], i), KT_(), E_(), ALU.mult)
            S.tt("dve", BH(BH.t[:, i, :], i), Bb(), E_(), ALU.mult)
        stop("r0")
        S.barrier()
        nc.sbuf_base, nc.sbuf_top = r1_snap
        G = mk("g", [128, 4, NT], F32, nreg=4)
        RB = []
        for sl in range(2):
            d = {}
            for nm in ("VT", "KTK", "NBT", "UT"):
                d[nm] = mk(nm, [128, 128], BF16)
            for hh in range(2):
                for nm in ("NM", "NMT", "PP0", "PP1", "PPT0", "PPT1", "YY0", "YY1"):
                    d[nm + str(hh)] = mk(nm, [128, 128], F32)
                for nm in ("ARK", "ARB", "AKK"):
                    d[nm + str(hh)] = mk(nm, [128, 128], BF16)
                d["RR" + str(hh)] = mk("rr", [128, 64], F32)
            RB.append(d)
        OS = mk("os", [128, 8, 64], F32)
        OQ = mk("oq", [128, 8, 64], F32)
        ST8 = mk("st8", [128, 4, 8], F32, nreg=4)
        OF = [mk("of", [128, CH], F32) for _ in range(2)]
        for i in range(4):
            b = next_bank()
            S.mm(ps(b, 0, NT), st["g2"](st["g2"].t[:, i * 128:(i + 1) * 128]), SG())
            S.cp("act", G(G.t[:, i, :], i), ps(b, 0, NT))
        lt, le, nle, nlt, ngt = C("lt"), C("le"), C("nle"), C("nlt"), C("ngt")
        T, Tbf = st["T"], st["Tbf"]

        def r1_pair(c, i, sl):
            cc = slice(c * CH, (c + 1) * CH)
            Bf = RB[sl]
            hbank = (1, 2) if sl == 0 else (5, 6)
            cbank = (5, 6) if os.environ.get("CBK") else hbank
            VT, KTK, NBT, UT = Bf["VT"], Bf["KTK"], Bf["NBT"], Bf["UT"]
            S.mm(ps(0, 0, 128), VB(VB.t[:, i, cc], i), ident_bf)
            S.mm(ps(0, 128, 256), KHH(KHH.t[:, i, cc], i), ident_bf)
            S.mm(ps(0, 256, 384), BH(BH.t[:, i, cc], i), ident_bf)
            S.cp("act", VT(), ps(0, 0, 128))
            S.cp("act", KTK(), ps(0, 128, 256))
            S.ts("dve", NBT(), ps(0, 256, 384), -1.0, ALU.mult)
            yield
            hv = []
            for hh in range(2):
                pb = 64 * hh
                hv.append((pb, RH(RH.t[pb:pb + 64, i, cc], i), KH(KH.t[pb:pb + 64, i, cc], i), KHH(KHH.t[pb:pb + 64, i, cc], i),
                           BH(BH.t[pb:pb + 64, i, cc], i), Tbf(Tbf.t[pb:pb + 64, i, :], i)))
            for hh in range(2):
                pb, rh, kh, khh, bh, t0v = hv[hh]
                bN = hbank[hh]
                qa = (2 * sl + hh) * 128
                S.mm(ps(bN, 0, 128), bh, kh)
                S.mm(ps(bN, 128, 256), kh, bh)
                S.mm(ps(bN, 256, 384), khh, rh)
                S.mm(ps(bN, 384, 512), bh, rh)
                S.mm(ps(3, qa, qa + 128), khh, kh)
                yield
            for hh in range(2):
                bN = hbank[hh]
                qa = (2 * sl + hh) * 128
                h_ = str(hh)
                S.tt("dve", Bf["NM" + h_](), ps(bN, 0, 128), nlt, ALU.mult)
                S.tt("dve", Bf["NMT" + h_](), ps(bN, 128, 256), ngt, ALU.mult)
                S.tt("dve", Bf["ARK" + h_](), ps(bN, 256, 384), le, ALU.mult)
                S.tt("dve", Bf["ARB" + h_](), ps(bN, 384, 512), nle, ALU.mult)
                S.tt("dve", Bf["AKK" + h_](), ps(3, qa, qa + 128), lt, ALU.mult)
                S.tt("dve", Bf["YY0" + h_](), Bf["NM" + h_](), ident, ALU.add)
                yield
            for hh in range(2):
                pb, rh, kh, khh, bh, t0v = hv[hh]
                h_ = str(hh)
                qr = (2 * sl + hh) * 64
                RBK = int(os.environ.get("RBK", 7))
                if RBK == 3:
                    qr = 256 + hh * 64
                S.mm(ps(RBK, qr, qr + 64), kh, t0v, start=True, stop=False)
                S.mm(ps(RBK, qr, qr + 64), Bf["AKK" + h_](), VT(VT.t[:, pb:pb + 64]), start=False, stop=True)
                S.cp("act", Bf["RR" + h_](), ps(RBK, qr, qr + 64))
                yield
            cur = [(Bf["NM0"], Bf["NMT0"]), (Bf["NM1"], Bf["NMT1"])]
            for lvl in range(1, 7):
                a = str(lvl % 2)
                for hh in range(2):
                    Pm, PTm = cur[hh]
                    bC = cbank[hh]
                    h_ = str(hh)
                    if lvl < 6:
                        S.mm(ps(bC, 0, 128), PTm(), Pm())
                    S.mm(ps(bC, 128, 256), Pm(), PTm())
                    if lvl < 6:
                        S.cp("act", Bf["PP" + a + h_](), ps(bC, 0, 128))
                    S.cp("act", Bf["PPT" + a + h_](), ps(bC, 128, 256))
                    yield
                for hh in range(2):
                    bC = cbank[hh]
                    h_ = str(hh)
                    yprev, ynew = Bf["YY" + str((lvl - 1) % 2) + h_], Bf["YY" + a + h_]
                    S.mm(ps(bC, 256, 384), Bf["PPT" + a + h_](), yprev())
                    S.tt("dve", ynew(), yprev(), ps(bC, 256, 384), ALU.add)
                    cur[hh] = (Bf["PP" + a + h_], Bf["PPT" + a + h_])
                    yield
            for hh in range(2):
                pb, rh, kh, khh, bh, t0v = hv[hh]
                h_ = str(hh)
                qu = 256 + (2 * sl + hh) * 64
                RBK = int(os.environ.get("RBK", 7))
                if RBK == 3:
                    qu = 384 + hh * 64
                S.mm(ps(RBK, qu, qu + 64), Bf["YY0" + h_](), Bf["RR" + h_]())
                S.cp("act", UT(UT.t[:, pb:pb + 64]), ps(RBK, qu, qu + 64))
                yield
            for hh in range(2):
                pb, rh, kh, khh, bh, t0v = hv[hh]
                h_ = str(hh)
                oc = (2 * i + hh) * 64
                ov = ps(4, oc, oc + 64)
                S.mm(ov, rh, t0v, start=True, stop=False)
                S.mm(ov, Bf["ARK" + h_](), VT(VT.t[:, pb:pb + 64]), start=False, stop=False)
                S.mm(ov, Bf["ARB" + h_](), UT(UT.t[:, pb:pb + 64]), start=False, stop=True)
                yield
            S.mm(ps(0, 384, 512), KTK(), VT(), start=True, stop=False)
            S.mm(ps(0, 384, 512), NBT(), UT(), start=False, stop=True)
            for hh in range(2):
                pb = 64 * hh
                tv = T(T.t[pb:pb + 64, i, :], i)
                S.tt("dve", tv, tv, ps(0, 384 + pb, 448 + pb, pb, pb + 64), ALU.add)
                S.ts("dve", tv, tv, GL(GL.t[pb:pb + 64, i, c:c + 1], i), ALU.mult)
                S.cp("act", Tbf(Tbf.t[pb:pb + 64, i, :], i), tv)
            if os.environ.get("SHOWB"):
                print("PAIR END", c, i, S.nops)
            yield

        def r1_tail(c):
            cc = slice(c * CH, (c + 1) * CH)
            osf = fl(OS(), "p a b -> p (a b)")
            S.cp("act", osf, ps(4, 0, 512))
            S.act(fl(OQ(), "p a b -> p (a b)"), ps(4, 0, 512), AF.Square)
            yield
            s1, s2, s3, s4 = [ST8(ST8.t[:, k, :], k) for k in range(4)]
            S.op("dve", lambda e: e.tensor_reduce(out=s1.ap, in_=OS.t[:], axis=AX.X, op=ALU.add), [s1], [OS()])
            S.op("dve", lambda e: e.tensor_reduce(out=s2.ap, in_=OQ.t[:], axis=AX.X, op=ALU.add), [s2], [OQ()])
            S.ts("dve", s1, s1, 1.0 / 64, ALU.mult)
            S.tt("dve", s3, s1, s1, ALU.mult)
            S.stt("dve", s2, s2, 1.0 / 64, s3, ALU.mult, ALU.subtract)
            S.act(s2, s2, AF.Sqrt, bias=gneps_v)
            S.recip(s2, s2)
            yield
            S.tt("dve", OS(), OS(), bc(s1, s1.ap[:, :, None].broadcast_to([128, 8, 64])), ALU.subtract)
            S.tt("dve", OS(), OS(), bc(s2, s2.ap[:, :, None].broadcast_to([128, 8, 64])), ALU.mult)
            yield
            for i in range(4):
                bt = 5 + i % 2
                S.tr(ps(bt, 384, 512), bc(OS(), osf.ap[:, i * 128:(i + 1) * 128]), ident)
                ofv = OF[i % 2]()
                S.ts("dve", ofv, ps(bt, 384, 512), pvv(L, "rw_ln_w", i, i + 1), ALU.mult, pvv(L, "rw_ln_b", i, i + 1), ALU.add)
                S.tt("dve", ofv, ofv, BV(BV.t[:, i, cc], i), ALU.add)
                S.tt("dve", YMIX(YMIX.t[:, i, cc], i), ofv, G(G.t[:, i, cc], i), ALU.mult)
                if os.environ.get("SHOWB"):
                    print("TAIL", c, i, S.nops)
                yield

        pend = None
        for c in range(CPT):
            import os
            if os.environ.get("NOILV"):
                for i in range(4):
                    interleave([r1_pair(c, i, (i % 2) if os.environ.get("NOILV") == "2" else 0)])
                interleave([r1_tail(c)])
                continue
            gens = [r1_pair(c, 0, 0), r1_pair(c, 1, 1)]
            if pend is not None:
                gens.append(pend)
            interleave(gens)
            interleave([r1_pair(c, 2, 0), r1_pair(c, 3, 1)])
            pend = r1_tail(c)
        if pend is not None:
            interleave([pend])
        stop("r1")
        arena_reset()
        le64 = View(C("le64").ap.bitcast(U32), CST.regs)
        HS, HSbf = st["HS"], st["HSbf"]
        hb = [dict() for _ in range(2)]
        for k in range(2):
            for nm in ("Q", "LF", "K1", "I", "OG", "GC", "NG", "EC", "EX", "KD", "O"):
                hb[k][nm] = mk("hg" + nm, [128, NT], F32)
            for nm in ("QT", "KT", "QG"):
                hb[k][nm] = mk("hg" + nm, [128, NT], BF16)
            hb[k]["ITK"] = mk("hgitk", [128, 128], BF16)
            hb[k]["KDT"] = mk("hgkdt", [128, 128], BF16)
            hb[k]["ATM"] = mk("hgatm", [128, 128], BF16)
            S.memset("pool", hb[k]["ATM"](), 0.0)
        for hd in range(4):
            Bf = hb[hd % 2]
            Q, LF, K1, I_, OG, GC, NG, EC, EX, KD, O_ = [Bf[n] for n in ("Q", "LF", "K1", "I", "OG", "GC", "NG", "EC", "EX", "KD", "O")]
            QT, KT, QG, ITK, KDT, ATM = [Bf[n] for n in ("QT", "KT", "QG", "ITK", "KDT", "ATM")]
            b = inproj(L, widx); widx += 1
            S.act(Q(), ps(b, 0, NT), AF.Silu)
            b = inproj(L, widx); widx += 1
            S.act(LF(), ps(b, 0, NT), AF.Sigmoid)
            S.ts("dve", LF(), LF(), st["oml"](st["oml"].t[:, hd:hd + 1]), ALU.mult, st["lb"](st["lb"].t[:, hd:hd + 1]), ALU.add)
            S.ts("dve", K1(), LF(), -1.0, ALU.mult, 1.0, ALU.add)
            S.act(LF(), LF(), AF.Ln)
            b = inproj(L, widx); widx += 1
            S.cp("act", I_(), ps(b, 0, NT))
            b = inproj(L, widx); widx += 1
            S.act(OG(), ps(b, 0, NT), AF.Silu)
            for q in range(NQ):
                cq = slice(q * 64, (q + 1) * 64)
                S.scan(GC(GC.t[:, cq]), bc(ones_f, ones_f.ap[:, 0:64]), LF(LF.t[:, cq]), 0.0, ALU.mult, ALU.add)
            S.ts("dve", NG(), GC(), -1.0, ALU.mult)
            S.act(EC(), GC(), AF.Exp)
            S.tt("dve", QG(), Q(), EC(), ALU.mult)
            for q in range(NQ):
                cq = slice(q * 64, (q + 1) * 64)
                mid = q * 64 + 31
                end = q * 64 + 63
                S.act(EX(EX.t[:, cq]), GC(GC.t[:, cq]), AF.Exp, bias=NG(NG.t[:, mid:mid + 1]))
                S.tt("dve", QT(QT.t[:, cq]), Q(Q.t[:, cq]), EX(EX.t[:, cq]), ALU.mult)
            for q in range(NQ):
                cq = slice(q * 64, (q + 1) * 64)
                mid = q * 64 + 31
                S.act(EX(EX.t[:, cq]), NG(NG.t[:, cq]), AF.Exp, bias=GC(GC.t[:, mid:mid + 1]))
                S.tt("dve", KT(KT.t[:, cq]), K1(K1.t[:, cq]), EX(EX.t[:, cq]), ALU.mult)
            for q in range(NQ):
                cq = slice(q * 64, (q + 1) * 64)
                end = q * 64 + 63
                S.act(EX(EX.t[:, cq]), NG(NG.t[:, cq]), AF.Exp, bias=GC(GC.t[:, end:end + 1]))
                S.tt("dve", KD(KD.t[:, cq]), K1(K1.t[:, cq]), EX(EX.t[:, cq]), ALU.mult)
            for blk in range(CPT):
                cb_ = slice(blk * 128, (blk + 1) * 128)
                S.tr(ps(0, 0, 128), I_(I_.t[:, cb_]), ident)
                S.tr(ps(0, 128, 256), KD(KD.t[:, cb_]), ident)
                S.cp("act", ITK(), ps(0, 0, 128))
                S.cp("act", KDT(), ps(0, 128, 256))
                S.mm(ps(1, 0, 128), KT(KT.t[:, cb_]), QT(QT.t[:, cb_]))
                S.cpred(ATM(), le64, ps(1, 0, 128))
                ob_ = 2 + blk % 2
                S.mm(ps(ob_, 0, 128), ITK(), ATM(), start=True, stop=False, signal=False)
                for qq in range(2):
                    q = 2 * blk + qq
                    cq = slice(q * 64, (q + 1) * 64)
                    end = q * 64 + 63
                    S.mm(ps(ob_, qq * 64, qq * 64 + 64), HSbf(HSbf.t[:, hd, :], hd), QG(QG.t[:, cq]), start=False, stop=(qq == 1), signal=True)
                    S.mm(ps(4 + qq, 0, 128), KDT(KDT.t[qq * 64:qq * 64 + 64, :]), ITK(ITK.t[qq * 64:qq * 64 + 64, :]))
                    hs = HS(HS.t[:, hd, :], hd)
                    S.stt("dve", hs, hs, EC(EC.t[:, end:end + 1]), ps(4 + qq, 0, 128), ALU.mult, ALU.add)
                    S.cp("act", HSbf(HSbf.t[:, hd, :], hd), hs)
                S.cp("act", O_(O_.t[:, cb_]), ps(ob_, 0, 128))
            r = norm_stats([O_()], 128)
            S.stt("dve", O_(), O_(), pvv(L, "hg_norm", hd, hd + 1), r, ALU.mult, ALU.mult)
            S.tt("dve", YMIX(YMIX.t[:, 4 + hd, :], 4 + hd), O_(), OG(), ALU.mult)
        return widx

    def mix_out(L, widx):
        widx = proj8(L, widx, YMIX, 8)
        post_norm_add(L, "n_mix_post")
        return widx

    try:
      for ti in range(n_tiles):
        t0 = ti * NT
        S.dma("sp", H(), dram(xT[:, :, t0:t0 + NT].rearrange("d p t -> p d t")))
        for L in range(n_layers):
            stop("start")
            widx = even_layer(L) if L % 2 == 0 else odd_layer(L)
            stop("mixer")
            if dbg is not None and dbg[0] == "ymix" and dbg[1] == L and ti == 0:
                for d in range(8):
                    S.cp("dve", TMP8(TMP8.t[:, d, :], d), YMIX(YMIX.t[:, d, :], d))
                S.dma("sp", dram(dbg_d.rearrange("d p t -> p d t")), TMP8(), is_output=True)
            widx = mix_out(L, widx)
            stop("mixout")
            widx = ffn(L, widx)
            assert widx == n_ws[L], (widx, n_ws[L])
        S.dma("sp", dram(oT[:, :, t0:t0 + NT].rearrange("d p t -> p d t")), H(), is_output=True)
    except StopBuild:
        S.dma("sp", dram(oT[:, :, 0:NT].rearrange("d p t -> p d t")), H(), is_output=True)
    S.finish()
    print("program: ninst=%d persist_bytes/partition=%d" % (S.ninst, persist_bytes))
    return nc


def host_prep(inputs):
    pkc = consts_host()
    pls, pbs, wss = [], [], []
    for L in range(DEPTH):
        P, B, ws = layer_host(inputs, L)
        pls.append(P)
        pbs.append(B)
        wss.append(ws)
    return pkc, pls, pbs, wss


def make_xT(inputs):
    x = np.asarray(inputs["x"], dtype=np.float32)
    meta = np.asarray(inputs["meta"], dtype=np.float32)
    xs = []
    for b in range(NB):
        full = np.zeros((TPAD, D), np.float32)
        full[:NMETA] = meta
        full[NMETA:TREAL] = x[b]
        xs.append(np.ascontiguousarray(full.T).reshape(8, 128, TPAD))
    return xs


def make_shared(pkc, pls, pbs, wss):
    shared = {"consts": pkc.pack()}
    for L in range(DEPTH):
        shared["pv%d" % L] = pls[L].pack()
        shared["pb%d" % L] = pbs[L].pack()
        shared["ws%d" % L] = wss[L]
    return shared


def kernel(**inputs):
    pkc, pls, pbs, wss = host_prep(inputs)
    nc = build_program(pkc, pls, pbs, [w.shape[0] for w in wss])
    xs = make_xT(inputs)
    shared = make_shared(pkc, pls, pbs, wss)
    in_maps = [dict(shared, xT=xs[b]) for b in range(NB)]
    res = run_bass_kernel_spmd(nc, in_maps, core_ids=list(range(NB)))
    out = np.empty((NB, SEQ, D), np.float32)
    for b in range(NB):
        o = np.asarray(res.results[b]["oT"]).reshape(D, TPAD)
        out[b] = o[:, NMETA:TREAL].T
    return out
```

```python
import math
import numpy as np
import concourse.bass as bass
import concourse.mybir as mybir
from concourse.bass_utils import run_bass_kernel_spmd

F32 = mybir.dt.float32
BF16 = mybir.dt.bfloat16
I32 = mybir.dt.int32
U32 = mybir.dt.uint32
AF = mybir.ActivationFunctionType
ALU = mybir.AluOpType
AX = mybir.AxisListType

D = 1024
NB = 8
SEQ = 4096
NMETA = 16
TREAL = SEQ + NMETA
CH = 128
CPT = 3
NQ = 2 * CPT
NT = CH * CPT
NTILES = (TREAL + NT - 1) // NT
TPAD = NTILES * NT
DEPTH = 4
DFF = 2816
NJ = DFF // 128
EPS = 1e-6
GN_EPS = 64e-5
import os
SAME_SYNC = True
COST = {} if os.environ.get('COST') else None
PHASE = ['pro']
NOSYNC_ENG = tuple(os.environ.get('NOSYNC', '').split(',')) if os.environ.get('NOSYNC') else ()


class StopBuild(Exception):
    pass


class Reg:
    __slots__ = ("w", "r")

    def __init__(self):
        self.w = None
        self.r = {}


class View:
    __slots__ = ("ap", "regs", "excl")

    def __init__(self, ap, regs, excl=False):
        self.ap = ap
        self.regs = regs
        self.excl = excl


class Buf:
    def __init__(self, S, name, shape, dtype, nreg=1, space="sbuf"):
        nc = S.nc
        if space == "sbuf":
            self.t = nc.alloc_sbuf_tensor(name, list(shape), dtype, align_bytes=64)
            S.sbuf_bytes += int(np.prod(shape[1:])) * (2 if dtype == BF16 else 4)
        else:
            self.t = nc.alloc_psum_tensor(name, list(shape), dtype)
        self.regs = [Reg() for _ in range(nreg)]

    def __call__(self, ap=None, r=None):
        if ap is None:
            ap = self.t[:]
        if r is None:
            regs = self.regs
        elif isinstance(r, int):
            regs = [self.regs[r]]
        else:
            regs = [self.regs[i] for i in r]
        return View(ap, regs)


def dram(ap):
    return View(ap, [])


class Sched:
    def __init__(self, nc):
        self.nc = nc
        self.E = {"pe": nc.tensor, "dve": nc.vector, "act": nc.scalar, "pool": nc.gpsimd, "sp": nc.sync}
        self.sem = {k: nc.alloc_semaphore("sem_" + k) for k in ("pe", "dve", "act", "pool")}
        self.cnt = {k: 0 for k in self.sem}
        self.seen = {k: {} for k in self.E}
        self.dq = {}
        self.sbuf_bytes = 0
        self.ninst = 0
        self.out_toks = []
        self.nops = 0
        self.max_ops = None

    def _deps(self, outs, ins):
        deps = {}

        def need(tok):
            if tok is None:
                return
            cur = deps.get(tok[0])
            if cur is None or cur[1] < tok[1]:
                deps[tok[0]] = tok

        for v in ins:
            for rg in v.regs:
                need(rg.w)
        for v in outs:
            for rg in v.regs:
                need(rg.w)
                for tok in rg.r.values():
                    need(tok)
        return deps

    def _wait(self, eng, deps):
        E = self.E[eng]
        own = self.sem[eng].num if eng in self.sem else None
        for sid, tok in deps.items():
            if sid == own and (eng == "pe" or not SAME_SYNC or eng in NOSYNC_ENG):
                continue
            if self.seen[eng].get(sid, 0) < tok[1]:
                if self.max_ops is not None and self.nops >= self.max_ops - int(os.environ.get('SHOWN', 3)):
                    print("  WAIT", eng, "on", tok[2].name, tok[1], "cnts", self.cnt)
                E.wait_ge(tok[2], tok[1])
                self.seen[eng][sid] = tok[1]
                self.ninst += 1

    def _mark(self, tok, outs, ins):
        for v in ins:
            for rg in v.regs:
                cur = rg.r.get(tok[0])
                if cur is None or cur[1] < tok[1]:
                    rg.r[tok[0]] = tok
        for v in outs:
            for rg in v.regs:
                rg.w = tok
                rg.r = {}

    def op(self, eng, fn, outs, ins, signal=True):
        xs = [v for v in ins if v.excl]
        if xs:
            outs = list(outs) + xs
            ins = [v for v in ins if not v.excl]
        self._wait(eng, self._deps(outs, ins))
        if COST is not None:
            try:
                shp = outs[0].ap.shape
                n = 1
                for d_ in shp[1:]:
                    n *= int(d_)
            except Exception:
                n = 128
            if eng == "pe":
                f32 = "float32" in str(ins[0].ap.dtype)
                c = n * (4 if f32 else 1) / 2400.0 + 0.03
            elif eng == "dve":
                c = max(64, n) / 960.0 + 0.06
            elif eng == "act":
                c = max(64, n) / 1400.0 + 0.2
            else:
                c = max(64, n) / 200.0 + 0.1
            key = (PHASE[0], eng)
            COST[key] = COST.get(key, 0.0) + c
            COST[(PHASE[0], "n_" + eng)] = COST.get((PHASE[0], "n_" + eng), 0) + 1
        inst = fn(self.E[eng])
        if self.max_ops is not None and self.nops >= self.max_ops - int(os.environ.get('SHOWN', 3)):
            try:
                print("  INST", eng, inst.concise())
            except Exception as ex:
                print("  INST?", ex, inst.ins)
        self.ninst += 1
        sem = self.sem[eng]
        if signal:
            self.cnt[eng] += 1
            inst.then_inc(sem, 1)
            tok = (sem.num, self.cnt[eng], sem)
        else:
            tok = (sem.num, self.cnt[eng] + 1, sem)
        self._mark(tok, outs, ins)
        self.nops += 1
        if self.max_ops is not None and self.nops >= self.max_ops:
            self.max_ops = None
            print("STOP at op", self.nops, eng)
            raise StopBuild()

    def dma(self, q, out, in_, is_output=False):
        if q not in self.dq:
            nr = 32 if q == "pool" else 8
            self.dq[q] = {"ring": [self.nc.alloc_semaphore("dsem_%s_%d" % (q, i)) for i in range(nr)], "n": 0, "toks": [None] * nr, "nr": nr}
        Q = self.dq[q]
        i = Q["n"]
        nr = Q["nr"]
        slot = i % nr
        deps = self._deps([out], [in_])
        if Q["toks"][slot] is not None:
            t = Q["toks"][slot]
            if t[0] not in deps or deps[t[0]][1] < t[1]:
                deps[t[0]] = t
        self._wait(q, deps)
        sem = Q["ring"][slot]
        val = 16 * (i // nr + 1)
        self.E[q].dma_start(out=out.ap, in_=in_.ap).then_inc(sem, 16)
        self.ninst += 1
        tok = (sem.num, val, sem)
        Q["toks"][slot] = tok
        Q["n"] += 1
        self._mark(tok, [out], [in_])
        if is_output:
            self.out_toks.append(tok)

    def finish(self):
        deps = {}
        for tok in self.out_toks:
            if tok[0] not in deps or deps[tok[0]][1] < tok[1]:
                deps[tok[0]] = tok
        for f in ("pe", "dve", "act", "pool"):
            if self.cnt[f]:
                deps[self.sem[f].num] = (self.sem[f].num, self.cnt[f], self.sem[f])
        for q, Q in self.dq.items():
            for t in Q["toks"]:
                if t is not None and (t[0] not in deps or deps[t[0]][1] < t[1]):
                    deps[t[0]] = t
        self._wait("sp", deps)

    def barrier(self):
        for eng in ("pe", "dve", "act", "pool"):
            deps = {}
            for f in ("pe", "dve", "act", "pool"):
                if (f == eng and eng == "pe") or self.cnt[f] == 0:
                    continue
                sem = self.sem[f]
                deps[sem.num] = (sem.num, self.cnt[f], sem)
            self._wait(eng, deps)

    def mm(self, out, lhsT, rhs, start=True, stop=True, signal=True):
        signal = True
        try:
            key = (int(lhsT.ap.start_partition()), int(lhsT.ap.partition_size()))
        except Exception:
            key = None
        last = getattr(self, "_mm_key", None)
        if key != last and self.cnt["pe"] > 0 and ((key is not None and key[1] < 128) or (last is not None and last[1] < 128)):
            sem = self.sem["pe"]
            if self.seen["pe"].get(sem.num, 0) < self.cnt["pe"]:
                self.E["pe"].wait_ge(sem, self.cnt["pe"])
                self.seen["pe"][sem.num] = self.cnt["pe"]
                self.ninst += 1
        self._mm_key = key
        self.op("pe", lambda e: e.matmul(out.ap, lhsT=lhsT.ap, rhs=rhs.ap, start=start, stop=stop), [out], [lhsT, rhs], signal)

    def tr(self, out, in_, ident):
        self.op("pe", lambda e: e.transpose(out.ap, in_.ap, ident.ap), [out], [in_, ident])

    def tt(self, eng, out, a, b, op):
        self.op(eng, lambda e: e.tensor_tensor(out=out.ap, in0=a.ap, in1=b.ap, op=op), [out], [a, b])

    def ts(self, eng, out, a, s1, op0, s2=None, op1=None):
        ins = [a] + [s for s in (s1, s2) if isinstance(s, View)]
        a1 = s1.ap if isinstance(s1, View) else s1
        a2 = s2.ap if isinstance(s2, View) else s2
        if op1 is None:
            self.op(eng, lambda e: e.tensor_scalar(out=out.ap, in0=a.ap, scalar1=a1, scalar2=None, op0=op0), [out], ins)
        else:
            self.op(eng, lambda e: e.tensor_scalar(out=out.ap, in0=a.ap, scalar1=a1, scalar2=a2, op0=op0, op1=op1), [out], ins)

    def stt(self, eng, out, a, s, b, op0, op1):
        ins = [a, b] + ([s] if isinstance(s, View) else [])
        sa = s.ap if isinstance(s, View) else s
        self.op(eng, lambda e: e.scalar_tensor_tensor(out=out.ap, in0=a.ap, scalar=sa, in1=b.ap, op0=op0, op1=op1), [out], ins)

    def act(self, out, in_, func, bias=None, scale=None):
        ins = [in_] + [s for s in (bias, scale) if isinstance(s, View)]
        kw = {}
        if bias is not None:
            kw["bias"] = bias.ap if isinstance(bias, View) else bias
        if scale is not None:
            kw["scale"] = scale.ap if isinstance(scale, View) else scale
        self.op("act", lambda e: e.activation(out=out.ap, in_=in_.ap, func=func, **kw), [out], ins)

    def cp(self, eng, out, in_):
        if eng == "act":
            self.op("act", lambda e: e.copy(out=out.ap, in_=in_.ap), [out], [in_])
        else:
            self.op(eng, lambda e: e.tensor_copy(out=out.ap, in_=in_.ap), [out], [in_])

    def scan(self, out, d0, d1, init, op0, op1):
        ins = [d0, d1] + ([init] if isinstance(init, View) else [])
        ia = init.ap if isinstance(init, View) else init
        self.op("dve", lambda e: e.tensor_tensor_scan(out=out.ap, data0=d0.ap, data1=d1.ap, initial=ia, op0=op0, op1=op1), [out], ins)

    def recip(self, out, in_):
        self.op("dve", lambda e: e.reciprocal(out=out.ap, in_=in_.ap), [out], [in_])

    def memset(self, eng, out, val):
        self.op(eng, lambda e: e.memset(out.ap, val), [out], [])

    def cpred(self, out, mask, data):
        self.op("dve", lambda e: e.copy_predicated(out=out.ap, mask=mask.ap, data=data.ap), [out], [mask, data])


class Packer:
    def __init__(self):
        self.cols = []
        self.off = {}
        self.n = 0

    def add(self, name, arr):
        arr = np.asarray(arr, dtype=np.float32)
        assert arr.shape[0] == 128, (name, arr.shape)
        arr = arr.reshape(128, -1)
        self.off[name] = (self.n, arr.shape[1])
        self.cols.append(arr)
        self.n += arr.shape[1]

    def pack(self):
        return np.ascontiguousarray(np.concatenate(self.cols, axis=1))


def feat(v):
    v = np.asarray(v, dtype=np.float32)
    return np.ascontiguousarray(v.reshape(-1, 128).T)


def rep(v):
    v = np.asarray(v, dtype=np.float32).reshape(1, -1)
    return np.ascontiguousarray(np.broadcast_to(v, (128, v.shape[1])))


def wtiles(W):
    K, N = W.shape
    Np = ((N + 127) // 128) * 128
    Kp = ((K + 1023) // 1024) * 1024
    Wp = np.zeros((Kp, Np), np.float32)
    Wp[:K, :N] = W
    out = []
    for kg in range(Kp // 1024):
        blk = Wp[kg * 1024:(kg + 1) * 1024]
        out.append(blk.reshape(8, 128, Np // 128, 128).transpose(2, 1, 0, 3))
    return out


def consts_host():
    P = Packer()
    idx = np.arange(128)
    P.add("ident", np.eye(128))
    P.add("ones", np.ones((128, 128)))
    le = (idx[:, None] <= idx[None, :]).astype(np.float32)
    lt = (idx[:, None] < idx[None, :]).astype(np.float32)
    P.add("le", le)
    P.add("lt", lt)
    P.add("nle", -le)
    P.add("nlt", -lt)
    P.add("ngt", -(idx[:, None] > idx[None, :]).astype(np.float32))
    blk = (idx[:, None] // 64 == idx[None, :] // 64).astype(np.float32)
    P.add("blk64", blk)
    P.add("le64", le * blk)
    P.add("ramp", rep(np.arange(1, 129)))
    return P


ODD_ORDER0 = [12, 13, 0, 4, 8, 1, 5, 9, 2, 6, 10, 3, 7, 11] + [14 + h + 4 * s for h in range(4) for s in range(4)]
ODD_ORDER1 = [12, 13, 30, 0, 4, 8, 1, 5, 9, 2, 6, 10, 3, 7, 11] + [14 + h + 4 * s for h in range(4) for s in range(4)]


def layer_host(inp, L):
    P = Packer()
    B = Packer()
    g = lambda n: np.asarray(inp[n], dtype=np.float32)
    P.add("n_mix_pre", feat(g("norm_mix_pre")[L]))
    P.add("n_mix_post", feat(g("norm_mix_post")[L]))
    P.add("n_ffn_pre", feat(g("norm_ffn_pre")[L]))
    P.add("n_ffn_post", feat(g("norm_ffn_post")[L]))
    cw = g("ffn_conv_w")[L]
    cb = g("ffn_conv_b")[L]
    order = np.concatenate([np.concatenate([np.arange(j * 128, (j + 1) * 128), DFF + np.arange(j * 128, (j + 1) * 128)]) for j in range(NJ)])
    P.add("ffn_cw", np.stack([feat(cw[k][order]) for k in range(3)], axis=2))
    P.add("ffn_cb", feat(cb[order]))
    tiles = []
    if L % 2 == 0:
        e = L // 2
        tiles.append(wtiles(g("ev_w_in")[e])[0])
        lr, li, ldt = g("s5_lam_re")[e], g("s5_lam_im")[e], g("s5_log_dt")[e]
        st = lambda a: np.ascontiguousarray(a.reshape(8, 2, 64).transpose(1, 2, 0).reshape(128, 8))
        P.add("s5_lr_s", st(lr))
        P.add("s5_li_s", st(li))
        P.add("s5_ldt_s", st(np.repeat(ldt[:, None], 64, axis=1)))
        B.add("s5_lr_b", rep(lr.reshape(-1)))
        B.add("s5_li_b", rep(li.reshape(-1)))
        B.add("s5_ldt_b", rep(np.repeat(ldt, 64)))
        br, bi = g("s5_b_re")[e], g("s5_b_im")[e]

        def bd(b):
            o = np.zeros((128, 2, 8, 64), np.float32)
            for kt in range(2):
                for gl in range(8):
                    o[gl * 16:(gl + 1) * 16, kt, gl, :] = b[8 * kt + gl].T
            return o.reshape(128, 1024)
        B.add("s5_br_bd", bd(br))
        B.add("s5_bi_bd", bd(bi))
        cr, ci = g("s5_c_re")[e], g("s5_c_im")[e]

        def cpad(c):
            o = np.zeros((128, 8, 128), np.float32)
            for j in range(8):
                for gp in range(2):
                    gg = 2 * j + gp
                    col = (gg % 8) * 16
                    o[gp * 64:(gp + 1) * 64, j, col:col + 16] = c[gg].T
            return o.reshape(128, 1024)
        B.add("s5_cr_pad", cpad(cr))
        B.add("s5_ci_pad", cpad(ci))
        B.add("s5_wglu", g("s5_w_glu")[e].reshape(2, 128, 256).transpose(1, 0, 2))
        P.add("s5_d", feat(g("s5_d")[e]))
        P.add("s5_bglu", feat(g("s5_b_glu")[e]))
        scw = g("ssd_conv_w")[e]
        P.add("ssd_cw", np.stack([feat(scw[k]) for k in range(4)], axis=2))
        P.add("ssd_cb", feat(g("ssd_conv_b")[e]))
        dtb = np.zeros((128, 1), np.float32)
        dtb[:12, 0] = g("ssd_dt_bias")[e]
        P.add("ssd_dtb", dtb)
        P.add("ssd_alog", rep(g("ssd_a_log")[e]))
        P.add("ssd_d", feat(np.repeat(g("ssd_d")[e], 64)))
        P.add("ssd_norm", feat(g("ssd_norm")[e]))
    else:
        o = L // 2
        w_in = g("od_w_in")[o]
        if o > 0:
            w_in = np.concatenate([w_in, g("rw_w_vin")[o - 1]], axis=1)
        wt = wtiles(w_in)[0]
        tiles.append(np.stack([wt[i] for i in (ODD_ORDER1 if o > 0 else ODD_ORDER0)]))
        mu = g("rw_mu")[o]
        if o > 0:
            mu = np.concatenate([mu, g("rw_mu_v")[o - 1], np.zeros(96, np.float32)])
        else:
            mu = np.concatenate([mu, np.zeros(128, np.float32)])
        P.add("rw_mu", feat(mu))
        P.add("rw_w0", feat(g("rw_w0")[o]))
        P.add("rw_a0", feat(g("rw_a0")[o]))
        B.add("rw_wa2", np.concatenate([g("rw_w2")[o], g("rw_a2")[o]], axis=0))
        B.add("rw_g2", g("rw_g2")[o])
        v2 = np.zeros((128, 512), np.float32)
        if o > 0:
            v2[:32] = g("rw_v2")[o - 1]
            P.add("rw_v0", feat(g("rw_v0")[o - 1]))
        B.add("rw_v2", v2)
        P.add("rw_k_k", feat(g("rw_k_k")[o]))
        P.add("rw_k_a", feat(g("rw_k_a")[o]))
        P.add("rw_r_k", feat(g("rw_r_k")[o]))
        P.add("rw_ln_w", feat(g("rw_ln_w")[o]))
        P.add("rw_ln_b", feat(g("rw_ln_b")[o]))
        P.add("hg_lb_raw0", feat(g("hg_lb_raw")[0]))
        P.add("hg_lb_raw1", feat(g("hg_lb_raw")[1]))
        P.add("hg_norm", feat(g("hg_norm")[o]))
    tiles.append(wtiles(g("mix_w_out")[L])[0])
    up = wtiles(g("ffn_w_up")[L])[0]
    tiles.append(np.stack([up[j + (0 if s == 0 else NJ)] for j in range(NJ) for s in (0, 1)]))
    dn = wtiles(g("ffn_w_down")[L])
    tiles.append(np.stack([dn[kg][m] for m in range(8) for kg in range(3)]))
    ws = np.ascontiguousarray(np.concatenate(tiles, axis=0).reshape(-1, 128, 1024))
    return P, B, ws
def build_program(pk_consts, pk_layers, pk_big, n_ws, n_layers=DEPTH, n_tiles=NTILES, dbg=None):
    nc = bass.Bass("TRN2", target_bir_lowering=False)
    S = Sched(nc)
    if dbg is not None and dbg[0] == "nops":
        S.max_ops = dbg[1]
    xT = nc.dram_tensor("xT", [8, 128, TPAD], F32, kind="ExternalInput").ap()
    oT = nc.dram_tensor("oT", [8, 128, TPAD], F32, kind="ExternalOutput").ap()
    cst_d = nc.dram_tensor("consts", [128, pk_consts.n], F32, kind="ExternalInput").ap()
    pv_d = [nc.dram_tensor("pv%d" % L, [128, pk_layers[L].n], F32, kind="ExternalInput").ap() for L in range(DEPTH)]
    pb_d = [nc.dram_tensor("pb%d" % L, [128, pk_big[L].n], F32, kind="ExternalInput").ap() for L in range(DEPTH)]
    ws_d = [nc.dram_tensor("ws%d" % L, [n_ws[L], 128, 1024], F32, kind="ExternalInput").ap() for L in range(DEPTH)]
    wsb_d = [nc.dram_tensor("wsb%d" % L, [n_ws[L], 128, 1024], BF16, kind="Internal").ap() for L in range(DEPTH)]
    WCH = 4
    wsb_regs = [[Reg() for _ in range((n_ws[L] + WCH - 1) // WCH)] for L in range(DEPTH)]
    dbg_d = None
    if dbg is not None:
        dbg_d = nc.dram_tensor("dbg", [8, 128, NT], F32, kind="ExternalOutput").ap()

    uid = [0]

    def mk(name, shape, dtype, nreg=1):
        uid[0] += 1
        return Buf(S, "%s_%d" % (name, uid[0]), shape, dtype, nreg)

    CST = mk("cst", [128, pk_consts.n], F32)
    PV = [mk("pvs", [128, pk_layers[L].n], F32) for L in range(DEPTH)]
    cst_bf = mk("cst_bf", [128, 3, 128], BF16)
    PSQ = [Reg() for _ in range(32)]
    PSt = nc.alloc_psum_tensor("psum_all", [128, 8, 512], F32)

    def C(name):
        o, n = pk_consts.off[name]
        return CST(CST.t[:, o:o + n])

    def pvv(L, name, a=None, b=None, p0=0, p1=128):
        o, n = pk_layers[L].off[name]
        if a is None:
            a, b = 0, n
        return PV[L](PV[L].t[p0:p1, o + a:o + b])

    def psr(bank, a, b):
        return [PSQ[bank]]

    def ps(bank, a=0, b=512, p0=0, p1=128):
        return View(PSt[p0:p1, bank, a:b], psr(bank, a, b), True)

    H = mk("h", [128, 8, NT], F32)
    HN = mk("hn", [128, 8, NT], BF16)
    SQ = mk("sq", [128, 2, NT], BF16, nreg=2)
    RSTD = mk("rstd", [128, NT], F32)
    TMP8 = mk("tmp8", [128, 8, NT], F32, nreg=8)
    YMIX = mk("ymix", [128, 8, NT], BF16, nreg=8)
    NWS = 6
    WRING = mk("wring", [128, NWS, 1024], BF16, nreg=NWS)
    FHALO = [mk("fhalo", [128, 2 * NJ, 2], F32) for L in range(DEPTH)]
    VFIRST = mk("vfirst", [128, 4, NT], F32, nreg=4)
    EPSB = mk("epsb", [128, 2], F32)
    wctr = [0]
    sqc = [0]

    ident = C("ident")
    ones_f = C("ones")
    ident_bf = cst_bf(cst_bf.t[:, 0, :])
    ones_bf = cst_bf(cst_bf.t[:, 1, :])
    blk64_bf = cst_bf(cst_bf.t[:, 2, :])

    def fl(v, pat, **kw):
        return View(v.ap.rearrange(pat, **kw), v.regs)

    def bc(v, ap):
        return View(ap, v.regs)

    LS = {}
    for L in range(n_layers):
        st = {}
        if L % 2 == 0:
            st["bb_re"] = mk("bbre", [128, 1024], BF16)
            st["bb_im"] = mk("bbim", [128, 1024], BF16)
            st["c_re"] = mk("cre", [128, 8, 128], F32)
            st["c_imn"] = mk("cimn", [128, 8, 128], F32)
            st["wglu"] = mk("wglu", [128, 2, 256], BF16)
            st["cos"] = mk("cos", [128, 8, CH], F32)
            st["sin"] = mk("sin", [128, 8, CH], F32)
            st["m"] = mk("m", [128, 8], F32)
            st["car_re"] = mk("carre", [128, 8], F32)
            st["car_im"] = mk("carim", [128, 8], F32)
            st["a_neg"] = mk("aneg", [128, 12], F32)
            st["Sst"] = mk("sst", [128, 12, 64], F32)
            st["prevT"] = mk("prevT", [128, 12, 128], BF16)
            st["xhalo"] = mk("xhalo", [128, 10, 3], F32)
        else:
            st["wa2"] = mk("wa2", [128, 512], BF16)
            st["g2"] = mk("g2", [128, 512], BF16)
            st["v2"] = mk("v2", [128, 512], BF16)
            st["lb"] = mk("lb", [128, 4], F32)
            st["oml"] = mk("oml", [128, 4], F32)
            st["T"] = mk("rwT", [128, 4, 64], F32, nreg=4)
            st["Tbf"] = mk("rwTbf", [128, 4, 64], BF16, nreg=4)
            st["HS"] = mk("hgS", [128, 4, 128], F32, nreg=4)
            st["HSbf"] = mk("hgSbf", [128, 4, 128], BF16, nreg=4)
            st["rwhalo"] = mk("rwhalo", [128, 15], F32)
            st["omm"] = mk("omm", [128, 15], F32)
        LS[L] = st

    arena_snap = (nc.sbuf_base, nc.sbuf_top)
    persist_bytes = S.sbuf_bytes

    def stop(stage):
        if dbg is not None and dbg[0] == "stop" and dbg[1] == stage:
            print("STOP stage", stage, "nops", S.nops)
            raise StopBuild()

    def arena_reset():
        S.barrier()
        nc.sbuf_base, nc.sbuf_top = arena_snap

    for L in range(n_layers):
        for ci in range(len(wsb_regs[L])):
            a, b = ci * WCH, min(n_ws[L], (ci + 1) * WCH)
            import os
            if os.environ.get("MAXCAST") and ci >= int(os.environ["MAXCAST"]):
                continue
            S.dma("pool", View(wsb_d[L][a:b], [wsb_regs[L][ci]]), dram(ws_d[L][a:b]))
    S.dma("sp", CST(), dram(cst_d))
    S.dma("sp", H(), dram(xT[:, :, 0:NT].rearrange("d p t -> p d t")))
    for L in range(n_layers):
        S.dma("sp", PV[L](), dram(pv_d[L]))
    for i, nm in enumerate(["ident", "ones", "blk64"]):
        S.cp("dve", cst_bf(cst_bf.t[:, i, :]), C(nm))
    for L in range(n_layers):
        S.memset("pool", FHALO[L](), 0.0)
    S.memset("dve", EPSB(EPSB.t[:, 0:1]), EPS)
    S.memset("dve", EPSB(EPSB.t[:, 1:2]), GN_EPS)
    eps_v = EPSB(EPSB.t[:, 0:1])
    gneps_v = EPSB(EPSB.t[:, 1:2])

    def sincos(x, sin_out, cos_out, tmp):
        ti = View(tmp.t[:, 3, :].bitcast(I32), tmp.regs)
        y, k, f = tmp(tmp.t[:, 0, :]), tmp(tmp.t[:, 1, :]), tmp(tmp.t[:, 2, :])
        for which, outv in ((0, sin_out), (1, cos_out)):
            S.ts("dve", y, x, 1.0 / (2 * math.pi), ALU.mult, 0.5 + 0.25 * which, ALU.add)
            S.cp("dve", ti, y)
            S.cp("dve", k, ti)
            S.tt("dve", f, y, k, ALU.subtract)
            S.ts("dve", k, f, 0.0, ALU.is_lt)
            S.tt("dve", f, f, k, ALU.add)
            S.ts("dve", f, f, 1.0, ALU.min)
            S.ts("dve", f, f, 2 * math.pi, ALU.mult, -math.pi, ALU.add)
            S.act(outv, f, AF.Sin)

    PVB = mk("pvb", [128, 8, 512], F32, nreg=8)
    TB = mk("s5tmpb", [128, 12, 512], F32, nreg=12)
    TS4 = mk("s5tmp4", [128, 4, 512], F32)

    def tb(i):
        return TB(TB.t[:, i, :], i)

    for L in range(n_layers):
        st = LS[L]

        def bgload(slot, name, a, b):
            o, n = pk_big[L].off[name]
            v = PVB(PVB.t[:, slot, 0:b - a], slot)
            S.dma("sp", v, dram(pb_d[L][:, o + a:o + b]))
            return v
        if L % 2 == 0:
            for half in range(2):
                hs = slice(half * 512, (half + 1) * 512)
                a_, b_ = half * 512, (half + 1) * 512
                lr, li, ldt = bgload(0, "s5_lr_b", a_, b_), bgload(1, "s5_li_b", a_, b_), bgload(2, "s5_ldt_b", a_, b_)
                br, bi = bgload(3, "s5_br_bd", a_, b_), bgload(4, "s5_bi_bd", a_, b_)
                crp, cip = bgload(5, "s5_cr_pad", a_, b_), bgload(6, "s5_ci_pad", a_, b_)
                dt, lrdt, lidt, mag, sn, cs, abr, abi, den, fre, fim, t1 = [tb(i) for i in range(12)]
                S.act(dt, ldt, AF.Exp)
                S.tt("dve", lrdt, lr, dt, ALU.mult)
                S.tt("dve", lidt, li, dt, ALU.mult)
                S.act(mag, lrdt, AF.Exp)
                sincos(lidt, sn, cs, TS4)
                S.tt("dve", abr, mag, cs, ALU.mult)
                S.tt("dve", abi, mag, sn, ALU.mult)
                S.tt("dve", den, lr, lr, ALU.mult)
                S.tt("dve", t1, li, li, ALU.mult)
                S.tt("dve", den, den, t1, ALU.add)
                S.recip(den, den)
                S.ts("dve", abr, abr, -1.0, ALU.add)
                S.tt("dve", fre, abr, lr, ALU.mult)
                S.tt("dve", t1, abi, li, ALU.mult)
                S.tt("dve", fre, fre, t1, ALU.add)
                S.tt("dve", fre, fre, den, ALU.mult)
                S.tt("dve", fim, abi, lr, ALU.mult)
                S.tt("dve", t1, abr, li, ALU.mult)
                S.tt("dve", fim, fim, t1, ALU.subtract)
                S.tt("dve", fim, fim, den, ALU.mult)
                S.tt("dve", t1, fre, br, ALU.mult)
                S.tt("dve", dt, fim, bi, ALU.mult)
                S.tt("dve", st["bb_re"](st["bb_re"].t[:, hs]), t1, dt, ALU.subtract)
                S.tt("dve", t1, fre, bi, ALU.mult)
                S.tt("dve", dt, fim, br, ALU.mult)
                S.tt("dve", st["bb_im"](st["bb_im"].t[:, hs]), t1, dt, ALU.add)
                js = slice(half * 4, half * 4 + 4)
                S.cp("dve", st["c_re"](st["c_re"].t[:, js, :].rearrange("p a b -> p (a b)")), crp)
                S.ts("dve", st["c_imn"](st["c_imn"].t[:, js, :].rearrange("p a b -> p (a b)")), cip, -1.0, ALU.mult)
                if half == 0:
                    wg = bgload(7, "s5_wglu", 0, 512)
                    S.cp("dve", fl(st["wglu"](), "p a b -> p (a b)"), wg)
                    dts, th = TS4(TS4.t[:, 0, 0:8]), TS4(TS4.t[:, 0, 8:16])
                    S.act(dts, pvv(L, "s5_ldt_s"), AF.Exp)
                    S.tt("dve", th, pvv(L, "s5_li_s"), dts, ALU.mult)
                    S.tt("dve", dts, pvv(L, "s5_lr_s"), dts, ALU.mult)
                    S.act(st["m"](), dts, AF.Exp)
                    S.cp("dve", st["car_re"](), th)
                ang = TB(TB.t[:, 1, :].rearrange("p (a b) -> p a b", a=4), 1)
                ramp = C("ramp")
                thv = st["car_re"](st["car_re"].t[:, js])
                S.tt("dve", ang, bc(ramp, ramp.ap[:, None, :].broadcast_to([128, 4, 128])),
                     bc(thv, thv.ap[:, :, None].broadcast_to([128, 4, 128])), ALU.mult)
                sincos(tb(1), st["sin"](st["sin"].t[:, js, :].rearrange("p a b -> p (a b)")),
                       st["cos"](st["cos"].t[:, js, :].rearrange("p a b -> p (a b)")), TS4)
            S.memset("dve", st["car_re"](), 0.0)
            S.memset("dve", st["car_im"](), 0.0)
            S.act(st["a_neg"](), pvv(L, "ssd_alog"), AF.Exp)
            S.ts("dve", st["a_neg"](), st["a_neg"](), -1.0, ALU.mult)
            S.memset("pool", st["Sst"](), 0.0)
            S.memset("pool", st["prevT"](), 0.0)
            S.memset("pool", st["xhalo"](), 0.0)
        else:
            S.cp("dve", st["wa2"](), bgload(0, "rw_wa2", 0, 512))
            S.cp("dve", st["g2"](), bgload(1, "rw_g2", 0, 512))
            S.cp("dve", st["v2"](), bgload(2, "rw_v2", 0, 512))
            if L // 2 == 0:
                S.memset("dve", st["lb"](), 0.0)
            else:
                S.tt("dve", st["lb"](), pvv(L, "hg_lb_raw1"), pvv(L, "hg_lb_raw0"), ALU.subtract)
                S.act(st["lb"](), st["lb"](), AF.Sigmoid)
            S.ts("dve", st["oml"](), st["lb"](), -1.0, ALU.mult, 1.0, ALU.add)
            S.ts("dve", st["omm"](), pvv(L, "rw_mu"), -1.0, ALU.mult, 1.0, ALU.add)
            S.memset("pool", st["T"](), 0.0)
            S.memset("pool", st["Tbf"](), 0.0)
            S.memset("pool", st["HS"](), 0.0)
            S.memset("pool", st["HSbf"](), 0.0)
            S.memset("pool", st["rwhalo"](), 0.0)
    arena_reset()
    if dbg is not None and dbg[0] == "stop" and dbg[1] == "prologue":
        S.dma("sp", dram(oT[:, :, 0:NT].rearrange("d p t -> p d t")), H(), is_output=True)
        S.finish()
        return nc

    def wload(L, idx):
        slot = wctr[0] % NWS
        wctr[0] += 1
        S.dma("sp", WRING(WRING.t[:, slot, :], slot), View(wsb_d[L][idx], [wsb_regs[L][idx // WCH]]))
        return slot

    def wk(slot, k, m0=0, m1=128, p0=0, p1=128):
        return WRING(WRING.t[p0:p1, slot, k * 128 + m0:k * 128 + m1], slot)

    def interleave(gens):
        gens = list(gens)
        while gens:
            for g in list(gens):
                try:
                    next(g)
                except StopIteration:
                    gens.remove(g)

    psrot = [0]

    def next_bank(nb=4):
        b = psrot[0] % nb
        psrot[0] += 1
        return b

    def norm_stats(src_views, nfeat, ones=None, epsv=None, n=NT):
        m = len(src_views)
        for i, v in enumerate(src_views):
            q = sqc[0] % 2
            sqc[0] += 1
            S.act(SQ(SQ.t[:, q, 0:n], q), v, AF.Square)
            S.mm(ps(7, 0, n), ones if ones is not None else ones_bf, SQ(SQ.t[:, q, 0:n], q), start=(i == 0), stop=(i == m - 1), signal=(i == m - 1))
        r = RSTD(RSTD.t[:, 0:n])
        S.act(r, ps(7, 0, n), AF.Sqrt, bias=epsv if epsv is not None else eps_v, scale=1.0 / nfeat)
        S.recip(r, r)
        return r

    def pre_norm(L, gname):
        r = norm_stats([H(H.t[:, d, :]) for d in range(8)], D)
        for d in range(8):
            S.stt("dve", HN(HN.t[:, d, :]), H(H.t[:, d, :]), pvv(L, gname, d, d + 1), r, ALU.mult, ALU.mult)

    def post_norm_add(L, gname):
        r = norm_stats([TMP8(TMP8.t[:, d, :], d) for d in range(8)], D)
        for d in range(8):
            S.stt("dve", TMP8(TMP8.t[:, d, :], d), TMP8(TMP8.t[:, d, :], d), pvv(L, gname, d, d + 1), r, ALU.mult, ALU.mult)
            S.tt("dve", H(H.t[:, d, :]), H(H.t[:, d, :]), TMP8(TMP8.t[:, d, :], d), ALU.add)

    def inproj(L, widx, nb=4, banks=None):
        slot = wload(L, widx)
        b = next_bank(nb) if banks is None else banks[next_bank(len(banks))]
        for k in range(8):
            S.mm(ps(b, 0, NT), wk(slot, k), HN(HN.t[:, k, :]), start=(k == 0), stop=(k == 7), signal=(k == 7))
        return b

    def proj8(L, widx0, rhs_buf, nk):
        nkg = (nk + 7) // 8
        wi = widx0
        for m in range(8):
            b = next_bank()
            kk = 0
            for kg in range(nkg):
                slot = wload(L, wi)
                wi += 1
                for k in range(min(8, nk - kg * 8)):
                    S.mm(ps(b, 0, NT), wk(slot, k), rhs_buf(rhs_buf.t[:, kk, :], kk), start=(kk == 0), stop=(kk == nk - 1), signal=(kk == nk - 1))
                    kk += 1
            S.cp("act", TMP8(TMP8.t[:, m, :], m), ps(b, 0, NT))
        return wi

    def rolling(factories, width):
        pending = list(factories)
        active = []
        while pending or active:
            while pending and len(active) < width:
                active.append(pending.pop(0)())
            for g in list(active):
                try:
                    next(g)
                except StopIteration:
                    active.remove(g)

    def ffn(L, widx0):
        PHASE[0] = "ffn"
        arena_reset()
        NST = 3
        ABUF = mk("abuf", [128, NJ, NT], BF16, nreg=NJ)
        FRAW = [mk("fraw", [128, 2, 2 + NT], F32, nreg=2) for i in range(NST)]
        FACC = [mk("facc", [128, 2, NT], F32, nreg=2) for i in range(NST)]
        FTMP = [mk("ftmp", [128, NT], F32) for i in range(NST)]
        pre_norm(L, "n_ffn_pre")
        o, _ = pk_layers[L].off["ffn_cw"]
        ob, _ = pk_layers[L].off["ffn_cb"]

        def ffn_j(j):
            sl = j % NST
            fr, fa = FRAW[sl], FACC[sl]
            S.cp("pool", fr(fr.t[:, :, 0:2]), FHALO[L](FHALO[L].t[:, 2 * j:2 * j + 2, :]))
            for s in range(2):
                b = inproj(L, widx0 + 2 * j + s, nb=6)
                ti = 2 * j + s
                cw = lambda kk: PV[L](PV[L].t[:, o + ti * 3 + kk:o + ti * 3 + kk + 1])
                cbv = PV[L](PV[L].t[:, ob + ti:ob + ti + 1])
                acc = fa(fa.t[:, s, :], s)
                S.cp("act", fr(fr.t[:, s, 2:2 + NT], s), ps(b, 0, NT))
                S.act(acc, ps(b, 0, NT), AF.Identity, bias=cbv, scale=cw(2))
                yield
                eng = "dve"
                for kk in (0, 1):
                    if eng == "dve":
                        S.stt(eng, acc, fr(fr.t[:, s, kk:kk + NT], s), cw(kk), acc, ALU.mult, ALU.add)
                    else:
                        S.ts(eng, FTMP[sl](), fr(fr.t[:, s, kk:kk + NT], s), cw(kk), ALU.mult)
                        S.tt(eng, acc, acc, FTMP[sl](), ALU.add)
                    yield
            S.cp("pool", FHALO[L](FHALO[L].t[:, 2 * j:2 * j + 2, :]), fr(fr.t[:, :, NT:NT + 2]))
            S.act(fa(fa.t[:, 0, :], 0), fa(fa.t[:, 0, :], 0), AF.Gelu_apprx_tanh)
            yield
            S.tt("dve", ABUF(ABUF.t[:, j, :], j), fa(fa.t[:, 0, :], 0), fa(fa.t[:, 1, :], 1), ALU.mult)
            yield

        rolling([(lambda j=j: ffn_j(j)) for j in range(NJ)], NST)
        widx = proj8(L, widx0 + 2 * NJ, ABUF, NJ)
        post_norm_add(L, "n_ffn_post")
        return widx

    def even_layer(L):
        st = LS[L]
        PHASE[0] = "s5"
        arena_reset()
        pre_norm(L, "n_mix_pre")
        stop("prenorm")
        UF = mk("uf", [128, 2, NT], F32)
        UB = mk("ub", [128, 2, NT], BF16)
        NS5 = 3
        S5T = [[mk("s5t", [128, 6, CH], F32, nreg=6) for _ in range(2)] for i in range(NS5)]
        S5X = [mk("s5x", [128, 2, NT], F32, nreg=2) for i in range(NS5)]
        S5Y = mk("s5y", [128, 2, NT], F32)
        S5YB = mk("s5yb", [128, 2, NT], BF16)
        for m in range(2):
            b = inproj(L, m)
            S.cp("act", UF(UF.t[:, m, :]), ps(b, 0, NT))
            S.cp("dve", UB(UB.t[:, m, :]), ps(b, 0, NT))
        stop("inproj")

        def s5_j(j):
            sl = j % NS5
            kt, jj = j // 4, j % 4
            bre, bim = 2 * sl, 2 * sl + 1
            c0 = kt * 512 + jj * 128
            S.mm(ps(bre, 0, NT), st["bb_re"](st["bb_re"].t[:, c0:c0 + 128]), UB(UB.t[:, kt, :]))
            S.mm(ps(bim, 0, NT), st["bb_im"](st["bb_im"].t[:, c0:c0 + 128]), UB(UB.t[:, kt, :]))
            yield
            X = S5X[sl]
            cs, sn = st["cos"](st["cos"].t[:, j, :]), st["sin"](st["sin"].t[:, j, :])
            mb = bc(st["m"](), st["m"].t[:, j:j + 1].to_broadcast([128, CH]))
            for c in range(CPT):
                T = S5T[sl][c % 2]
                pr, pi = ps(bre, c * CH, (c + 1) * CH), ps(bim, c * CH, (c + 1) * CH)
                t = lambda i: T(T.t[:, i, :], i)
                S.tt("dve", t(0), pr, cs, ALU.mult)
                S.tt("dve", t(1), pi, sn, ALU.mult)
                yield
                S.tt("dve", t(0), t(0), t(1), ALU.add)
                S.tt("dve", t(2), pi, cs, ALU.mult)
                yield
                S.tt("dve", t(3), pr, sn, ALU.mult)
                S.tt("dve", t(2), t(2), t(3), ALU.subtract)
                yield
                if c == 0:
                    ir, ii = st["car_re"](st["car_re"].t[:, j:j + 1]), st["car_im"](st["car_im"].t[:, j:j + 1])
                else:
                    ir, ii = X(X.t[:, 0, c * CH - 1:c * CH], 0), X(X.t[:, 1, c * CH - 1:c * CH], 1)
                S.scan(t(4), mb, t(0), ir, ALU.mult, ALU.add)
                yield
                S.scan(t(5), mb, t(2), ii, ALU.mult, ALU.add)
                yield
                xr, xi = X(X.t[:, 0, c * CH:(c + 1) * CH], 0), X(X.t[:, 1, c * CH:(c + 1) * CH], 1)
                S.tt("dve", t(0), t(4), cs, ALU.mult)
                S.tt("dve", t(1), t(5), sn, ALU.mult)
                yield
                S.tt("dve", xr, t(0), t(1), ALU.subtract)
                S.tt("dve", t(2), t(5), cs, ALU.mult)
                yield
                S.tt("dve", t(3), t(4), sn, ALU.mult)
                S.tt("dve", xi, t(2), t(3), ALU.add)
                yield
            S.cp("pool", st["car_re"](st["car_re"].t[:, j:j + 1]), X(X.t[:, 0, NT - 1:NT], 0))
            S.cp("pool", st["car_im"](st["car_im"].t[:, j:j + 1]), X(X.t[:, 1, NT - 1:NT], 1))
            yb = 6 + j // 4
            S.mm(ps(yb, 0, NT), st["c_re"](st["c_re"].t[:, j, :]), X(X.t[:, 0, :], 0), start=(jj == 0), stop=False)
            S.mm(ps(yb, 0, NT), st["c_imn"](st["c_imn"].t[:, j, :]), X(X.t[:, 1, :], 1), start=False, stop=(jj == 3))
            yield

        rolling([(lambda j=j: s5_j(j)) for j in range(8)], NS5)
        stop("s5scan")
        for ot in range(2):
            yv = S5Y(S5Y.t[:, ot, :])
            S.stt("dve", yv, UF(UF.t[:, ot, :]), pvv(L, "s5_d", ot, ot + 1), ps(6 + ot, 0, NT), ALU.mult, ALU.add)
            S.act(yv, yv, AF.Gelu_apprx_tanh)
            S.cp("act", S5YB(S5YB.t[:, ot, :]), yv)
        for ot in range(2):
            b = next_bank()
            for k in range(2):
                S.mm(ps(b, 0, NT), st["wglu"](st["wglu"].t[:, k, ot * 128:(ot + 1) * 128]), S5YB(S5YB.t[:, k, :]), start=(k == 0), stop=(k == 1), signal=(k == 1))
            sg = S5X[0](S5X[0].t[:, ot, :], ot)
            S.act(sg, ps(b, 0, NT), AF.Sigmoid, bias=pvv(L, "s5_bglu", ot, ot + 1))
            S.tt("dve", YMIX(YMIX.t[:, ot, :], ot), S5Y(S5Y.t[:, ot, :]), sg, ALU.mult)
        stop("s5")
        PHASE[0] = "ssd"
        arena_reset()
        ZS = mk("zs", [128, 6, NT], F32)
        XA = mk("xa", [128, 10, NT], F32, nreg=10)
        XAB = mk("xab", [128, 4, NT], BF16)
        DTT = mk("dtt", [128, NT], F32)
        YS = mk("ys", [128, 6, NT], F32)
        RAW = [mk("xraw", [128, 3 + NT], F32) for i in range(2)]
        XDP = mk("xdp", [128, 12, 128], BF16)
        XDD = mk("xdd", [128, 12, 64], BF16)
        XST = mk("xst", [128, 12, 64], F32)
        BTK = mk("btk", [128, 2, 128], BF16)
        SM = mk("ssdsmall", [128, 8, 12], F32, nreg=8)
        SEG = mk("seg", [128, 12, 128], F32)
        EAR = mk("ear", [128, 12, 128], F32)
        SCT = mk("sct", [128, 12, 128], BF16)
        CEX = mk("cex", [128, 12, 128], BF16)
        CBM = mk("cbm", [128, 2, 128], F32)
        S.memset("pool", XDP(), 0.0)
        ocw, _ = pk_layers[L].off["ssd_cw"]
        ocb, _ = pk_layers[L].off["ssd_cb"]
        for m in range(2, 19):
            b = inproj(L, m)
            p = ps(b, 0, NT)
            if m < 8:
                S.act(ZS(ZS.t[:, m - 2, :]), p, AF.Silu)
            elif m < 18:
                i = m - 8
                rw = RAW[i % 2]
                S.cp("pool", rw(rw.t[:, 0:3]), st["xhalo"](st["xhalo"].t[:, i, :]))
                S.cp("act", rw(rw.t[:, 3:3 + NT]), p)
                cw = lambda kk: PV[L](PV[L].t[:, ocw + i * 4 + kk:ocw + i * 4 + kk + 1])
                acc = XA(XA.t[:, i, :], i)
                S.act(acc, p, AF.Identity, bias=PV[L](PV[L].t[:, ocb + i:ocb + i + 1]), scale=cw(3))
                for kk in range(0, 3):
                    S.stt("dve", acc, rw(rw.t[:, kk:kk + NT]), cw(kk), acc, ALU.mult, ALU.add)
                S.cp("pool", st["xhalo"](st["xhalo"].t[:, i, :]), rw(rw.t[:, NT:NT + 3]))
                S.act(acc, acc, AF.Silu)
                if i >= 6:
                    S.cp("act", XAB(XAB.t[:, i - 6, :]), acc)
            else:
                S.act(DTT(DTT.t[0:12, :]), ps(b, 0, NT, 0, 12), AF.Exp, bias=pvv(L, "ssd_dtb", 0, 1, 0, 12))
                S.act(DTT(DTT.t[0:12, :]), DTT(DTT.t[0:12, :]), AF.Ln, bias=pvv(L, "ssd_dtb", 0, 1, 0, 12) if False else 1.0)
        stop("ssdproj")
        a_neg = st["a_neg"]()
        le = C("le")
        sm = lambda i: SM(SM.t[:, i, :], i)
        arow_regs = psr(1, 0, 512) + psr(2, 0, 512) + psr(3, 0, 512)
        for c in range(CPT):
            cc = slice(c * CH, (c + 1) * CH)
            for i in range(6):
                bk, off = (4, i * 128) if i < 4 else (5, (i - 4) * 128)
                S.tr(ps(bk, off, off + 128), XA(XA.t[:, i, cc], i), ident)
            for g2 in range(2):
                S.tr(ps(6, g2 * 128, (g2 + 1) * 128), XA(XA.t[:, 6 + g2, cc], 6 + g2), ident)
            S.tr(ps(7, 256, 268), DTT(DTT.t[0:12, cc]), bc(ident, ident.ap[0:12, 0:12]))
            S.cp("act", XST(XST.t[:, 0:8, :].rearrange("p a b -> p (a b)")), ps(4, 0, 512))
            S.cp("act", XST(XST.t[:, 8:12, :].rearrange("p a b -> p (a b)")), ps(5, 0, 256))
            S.cp("act", fl(BTK(), "p a b -> p (a b)"), ps(6, 0, 256))
            dtk, da, acol, aend, dte, w2, edec = [sm(i) for i in range(7)]
            S.cp("dve", dtk, ps(7, 256, 268))
            S.tt("dve", da, dtk, a_neg, ALU.mult)
            S.mm(ps(7, 272, 284), le, da)
            S.cp("act", acol, ps(7, 272, 284))
            S.tt("dve", SEG(), bc(da, da.ap[:, :, None].broadcast_to([128, 12, 128])),
                 bc(le, le.ap[:, None, :].broadcast_to([128, 12, 128])), ALU.mult)
            for q in range(3):
                S.mm(ps(1 + q, 0, 512), ones_f, SEG(SEG.t[:, 4 * q:4 * q + 4, :].rearrange("p a b -> p (a b)")))
            arow = View(PSt[:, 1:4, :].rearrange("p a (b c) -> p (a b) c", c=128), arow_regs, True)
            S.act(EAR(), arow, AF.Exp)
            S.cp("dve", aend, View(PSt[:, 1:4, :].rearrange("p a (b c) -> p (a b) c", c=128)[:, :, 127], arow_regs, True))
            S.tt("dve", SEG(), arow, bc(acol, acol.ap[:, :, None].broadcast_to([128, 12, 128])), ALU.subtract)
            S.ts("dve", SEG(), SEG(), 0.0, ALU.min)
            S.act(SEG(), SEG(), AF.Exp)
            for g2 in range(2):
                S.mm(ps(6, 256 + g2 * 128, 256 + (g2 + 1) * 128), XAB(XAB.t[:, g2, cc]), XAB(XAB.t[:, 2 + g2, cc]))
            S.tt("dve", CBM(), View(PSt[:, 6, 256:512].rearrange("p (a b) -> p a b", a=2), psr(6, 256, 512), True),
                 bc(le, le.ap[:, None, :].broadcast_to([128, 2, 128])), ALU.mult)
            for g2 in range(2):
                hs = slice(6 * g2, 6 * g2 + 6)
                S.tt("dve", SCT(SCT.t[:, hs, :]), SEG(SEG.t[:, hs, :]), CBM(CBM.t[:, g2:g2 + 1, :].broadcast_to([128, 6, 128])), ALU.mult)
                S.tt("dve", CEX(CEX.t[:, hs, :]), EAR(EAR.t[:, hs, :]),
                     XA(XA.t[:, 8 + g2:9 + g2, cc].broadcast_to([128, 6, 128]), 8 + g2), ALU.mult)
            xdp5 = XDP.t[:].rearrange("p (a two) (h c) -> p a two h c", two=2, h=2)
            xst4 = XST.t[:].rearrange("p (a two) c -> p a two c", two=2)
            dtk4 = dtk.ap.rearrange("p (a two) -> p a two", two=2)
            for par in range(2):
                S.tt("dve", XDP(xdp5[:, :, par, par, :]), XST(xst4[:, :, par, :]),
                     bc(dtk, dtk4[:, :, par:par + 1].broadcast_to([128, 6, 64])), ALU.mult)
            for pj in range(6):
                yo = ps(0, pj * 128, (pj + 1) * 128) if pj < 4 else ps(7, (pj - 4) * 128, (pj - 3) * 128)
                for hh in range(2):
                    hI = 2 * pj + hh
                    S.mm(yo, XDP(XDP.t[:, hI, :]), SCT(SCT.t[:, hI, :]), start=(hh == 0), stop=False, signal=False)
                for hh in range(2):
                    hI = 2 * pj + hh
                    S.mm(yo, st["prevT"](st["prevT"].t[:, hI, :]), CEX(CEX.t[:, hI, :]), start=False, stop=(hh == 1), signal=(hh == 1))
                S.stt("dve", YS(YS.t[:, pj, cc]), XA(XA.t[:, pj, cc], pj), pvv(L, "ssd_d", pj, pj + 1), yo, ALU.mult, ALU.add)
            S.tt("dve", dte, aend, acol, ALU.subtract)
            S.act(dte, dte, AF.Exp)
            S.tt("dve", w2, dtk, dte, ALU.mult)
            S.act(edec, aend, AF.Exp)
            S.tt("dve", XDD(), XST(), bc(w2, w2.ap[:, :, None].broadcast_to([128, 12, 64])), ALU.mult)
            for g2 in range(2):
                S.mm(ps(4 + g2, 0, 384), BTK(BTK.t[:, g2, :]), XDD(XDD.t[:, 6 * g2:6 * g2 + 6, :].rearrange("p a b -> p (a b)")))
            Sst = st["Sst"]
            S.tt("dve", Sst(), Sst(), bc(edec, edec.ap[:, :, None].broadcast_to([128, 12, 64])), ALU.mult)
            for g2 in range(2):
                S.tt("dve", Sst(Sst.t[:, 6 * g2:6 * g2 + 6, :]), Sst(Sst.t[:, 6 * g2:6 * g2 + 6, :]),
                     View(PSt[:, 4 + g2, 0:384].rearrange("p (a b) -> p a b", a=6), psr(4 + g2, 0, 384), True), ALU.add)
            pt5 = st["prevT"].t[:].rearrange("p (a two) (h c) -> p a two h c", two=2, h=2)
            ss4 = Sst.t[:].rearrange("p (a two) c -> p a two c", two=2)
            for par in range(2):
                S.cp("act", st["prevT"](pt5[:, :, par, par, :]), Sst(ss4[:, :, par, :]))
        S.tt("dve", YS(), YS(), ZS(), ALU.mult)
        for g2 in range(2):
            r = norm_stats([YS(YS.t[:, 3 * g2 + i, :]) for i in range(3)], 384)
            for i in range(3):
                S.stt("dve", YMIX(YMIX.t[:, 2 + 3 * g2 + i, :], 2 + 3 * g2 + i), YS(YS.t[:, 3 * g2 + i, :]),
                      pvv(L, "ssd_norm", 3 * g2 + i, 3 * g2 + i + 1), r, ALU.mult, ALU.mult)
        return 19

    C0 = math.exp(-0.5)

    def odd_layer(L):
        st = LS[L]
        o = L // 2
        PHASE[0] = "r0"
        arena_reset()
        pre_norm(L, "n_mix_pre")
        widx = 0
        RH = mk("rh", [128, 4, NT], BF16, nreg=4)
        KH = mk("kh", [128, 4, NT], BF16, nreg=4)
        KHH = mk("khh", [128, 4, NT], BF16, nreg=4)
        BH = mk("bh", [128, 4, NT], BF16, nreg=4)
        VB = mk("vb", [128, 4, NT], BF16, nreg=4)
        BV = mk("bv", [128, 4, NT], F32, nreg=4)
        SG = mk("sg", [128, NT], BF16)
        GL = mk("gl", [128, 4, CPT], F32, nreg=4)
        r1_snap = (nc.sbuf_base, nc.sbuf_top)
        TW = mk("tw", [128, NT], BF16)
        PVb = mk("pvb16", [128, NT], BF16)
        NR0 = 2
        R0B = []
        for sl in range(NR0):
            d = {"RAW": mk("rraw", [128, 1 + NT], F32), "RK": mk("rk", [128, NT], BF16)}
            for nm in ("Dt", "Rr", "Kk", "Vv", "Aa", "SGW", "KAP", "KT_", "E_", "LGS") + (("VS",) if o > 0 else ()):
                d[nm] = mk("r0" + nm, [128, NT], F32)
            R0B.append(d)
        nhead = 3 if o > 0 else 2

        def shifted(widx_, idx, dst, Bf):
            b = inproj(L, widx_)
            rw, Dt = Bf["RAW"], Bf["Dt"]
            S.cp("pool", rw(rw.t[:, 0:1]), st["rwhalo"](st["rwhalo"].t[:, idx:idx + 1]))
            S.cp("act", rw(rw.t[:, 1:1 + NT]), ps(b, 0, NT))
            S.act(Dt(), ps(b, 0, NT), AF.Identity, scale=st["omm"](st["omm"].t[:, idx:idx + 1]))
            S.stt("dve", dst, rw(rw.t[:, 0:NT]), pvv(L, "rw_mu", idx, idx + 1), Dt(), ALU.mult, ALU.add)
            S.cp("pool", st["rwhalo"](st["rwhalo"].t[:, idx:idx + 1]), rw(rw.t[:, NT:NT + 1]))

        B0 = R0B[0]
        shifted(0, 12, B0["Rr"](), B0)
        S.act(TW(TW.t[0:64, :]), B0["Rr"](B0["Rr"].t[0:64, :]), AF.Tanh)
        S.cp("dve", TW(TW.t[64:128, :]), B0["Rr"](B0["Rr"].t[64:128, :]))
        shifted(1, 13, B0["Kk"](), B0)
        S.act(SG(), B0["Kk"](), AF.Sigmoid)
        if o > 0:
            shifted(2, 14, B0["Vv"](), B0)
            S.cp("dve", PVb(PVb.t[0:32, :]), B0["Vv"](B0["Vv"].t[0:32, :]))

        def r0_i(i):
            Bf = R0B[i % NR0]
            Rr, Kk, Vv, Aa, SGW, KAP, KT_, E_, LGS, Dt, RK = [Bf[n] for n in ("Rr", "Kk", "Vv", "Aa", "SGW", "KAP", "KT_", "E_", "LGS", "Dt", "RK")]
            ic = slice(i * 128, (i + 1) * 128)
            wb = nhead + 3 * i
            shifted(wb, i, Rr(), Bf)
            yield
            shifted(wb + 1, 4 + i, Kk(), Bf)
            yield
            shifted(wb + 2, 8 + i, Vv(), Bf)
            yield
            bw, ba = next_bank(), next_bank()
            S.mm(ps(bw, 0, NT), st["wa2"](st["wa2"].t[0:64, ic]), TW(TW.t[0:64, :]))
            S.mm(ps(ba, 0, NT), st["wa2"](st["wa2"].t[64:128, ic]), TW(TW.t[64:128, :]))
            S.act(SGW(), ps(bw, 0, NT), AF.Sigmoid, bias=pvv(L, "rw_w0", i, i + 1))
            S.act(Aa(), ps(ba, 0, NT), AF.Sigmoid, bias=pvv(L, "rw_a0", i, i + 1))
            yield
            if o > 0:
                VS = Bf["VS"]
                bv_ = next_bank()
                S.mm(ps(bv_, 0, NT), st["v2"](st["v2"].t[0:32, ic]), PVb(PVb.t[0:32, :]))
                S.act(VS(), ps(bv_, 0, NT), AF.Sigmoid, bias=pvv(L, "rw_v0", i, i + 1))
                S.tt("dve", Dt(), VFIRST(VFIRST.t[:, i, :], i), Vv(), ALU.subtract)
                yield
                S.tt("dve", Dt(), Dt(), VS(), ALU.mult)
                S.tt("dve", Vv(), Vv(), Dt(), ALU.add)
            else:
                S.cp("act", VFIRST(VFIRST.t[:, i, :], i), Vv())
            S.cp("act", VB(VB.t[:, i, :], i), Vv())
            yield
            S.ts("dve", KAP(), Kk(), pvv(L, "rw_k_k", i, i + 1), ALU.mult)
            q = sqc[0] % 2
            sqc[0] += 1
            S.act(SQ(SQ.t[:, q, :], q), KAP(), AF.Square)
            S.mm(ps(7, 0, NT), blk64_bf, SQ(SQ.t[:, q, :], q))
            S.act(E_(), ps(7, 0, NT), AF.Sqrt)
            yield
            S.ts("dve", E_(), E_(), 1e-12, ALU.max)
            S.recip(E_(), E_())
            S.tt("dve", KAP(), KAP(), E_(), ALU.mult)
            yield
            S.ts("dve", KT_(), Aa(), -1.0, ALU.add, pvv(L, "rw_k_a", i, i + 1), ALU.mult)
            S.stt("dve", KT_(), KT_(), 1.0, Kk(), ALU.add, ALU.mult)
            S.tt("dve", Aa(), KAP(), Aa(), ALU.mult)
            yield
            S.stt("dve", RK(), Rr(), pvv(L, "rw_r_k", i, i + 1), KT_(), ALU.mult, ALU.mult)
            bb_ = next_bank()
            S.mm(ps(bb_, 0, NT), blk64_bf, RK())
            S.tt("dve", BV(BV.t[:, i, :], i), ps(bb_, 0, NT), Vv(), ALU.mult)
            yield
            for c in range(CPT):
                cc = slice(c * CH, (c + 1) * CH)
                S.scan(LGS(LGS.t[:, cc]), ones_f, SGW(SGW.t[:, cc]), 0.0, ALU.mult, ALU.add)
            yield
            S.act(E_(), LGS(), AF.Exp, scale=-C0)
            S.tt("dve", RH(RH.t[:, i, :], i), Rr(), E_(), ALU.mult)
            S.cp("pool", GL(GL.t[:, i, :], i), bc(E_(), E_.t[:, :].rearrange("p (c t) -> p c t", c=CPT)[:, :, CH - 1]))
            S.tt("dve", Dt(), LGS(), SGW(), ALU.subtract)
            yield
            S.act(Dt(), Dt(), AF.Exp, scale=-C0)
            S.tt("dve", KH(KH.t[:, i, :], i), KAP(), Dt(), ALU.mult)
            yield
            S.act(E_(), LGS(), AF.Exp, scale=C0)
            S.tt("dve", KHH(KHH.t[:, i, :], i), KT_(), E_(), ALU.mult)
            S.tt("dve", BH(BH.t[:, i, :], i), Aa(), E_(), ALU.mult)
            yield

        rolling([(lambda i=i: r0_i(i)) for i in range(4)], NR0)
        widx = nhead + 12
        stop("r0")
        PHASE[0] = "r1"
        S.barrier()
        nc.sbuf_base, nc.sbuf_top = r1_snap
        G = mk("g", [128, 4, NT], F32, nreg=4)
        RB = []
        for sl in range(2):
            d = {}
            for nm in ("VT", "KTK", "NBT", "UT"):
                d[nm] = mk(nm, [128, 128], BF16)
            for hh in range(2):
                for nm in ("NM", "NMT", "PP0", "PP1", "PPT0", "PPT1", "YY0", "YY1"):
                    d[nm + str(hh)] = mk(nm, [128, 128], F32)
                for nm in ("ARK", "ARB", "AKK"):
                    d[nm + str(hh)] = mk(nm, [128, 128], BF16)
                d["RR" + str(hh)] = mk("rr", [128, 64], F32)
            RB.append(d)
        OS = mk("os", [128, 8, 64], F32)
        OQ = mk("oq", [128, 8, 64], F32)
        ST8 = mk("st8", [128, 4, 8], F32, nreg=4)
        OF = [mk("of", [128, CH], F32) for _ in range(2)]
        for i in range(4):
            b = next_bank()
            S.mm(ps(b, 0, NT), st["g2"](st["g2"].t[:, i * 128:(i + 1) * 128]), SG())
            S.cp("act", G(G.t[:, i, :], i), ps(b, 0, NT))
        lt, le, nle, nlt, ngt = C("lt"), C("le"), C("nle"), C("nlt"), C("ngt")
        T, Tbf = st["T"], st["Tbf"]

        def r1_pair(c, i, sl):
            cc = slice(c * CH, (c + 1) * CH)
            Bf = RB[sl]
            hbank = (1, 2) if sl == 0 else (5, 6)
            cbank = (5, 6) if os.environ.get("CBK") else hbank
            VT, KTK, NBT, UT = Bf["VT"], Bf["KTK"], Bf["NBT"], Bf["UT"]
            S.mm(ps(0, 0, 128), VB(VB.t[:, i, cc], i), ident_bf)
            S.mm(ps(0, 128, 256), KHH(KHH.t[:, i, cc], i), ident_bf)
            S.mm(ps(0, 256, 384), BH(BH.t[:, i, cc], i), ident_bf)
            S.cp("act", VT(), ps(0, 0, 128))
            S.cp("act", KTK(), ps(0, 128, 256))
            S.ts("dve", NBT(), ps(0, 256, 384), -1.0, ALU.mult)
            yield
            hv = []
            for hh in range(2):
                pb = 64 * hh
                hv.append((pb, RH(RH.t[pb:pb + 64, i, cc], i), KH(KH.t[pb:pb + 64, i, cc], i), KHH(KHH.t[pb:pb + 64, i, cc], i),
                           BH(BH.t[pb:pb + 64, i, cc], i), Tbf(Tbf.t[pb:pb + 64, i, :], i)))
            for hh in range(2):
                pb, rh, kh, khh, bh, t0v = hv[hh]
                bN = hbank[hh]
                qa = (2 * sl + hh) * 128
                S.mm(ps(bN, 0, 128), bh, kh)
                S.mm(ps(bN, 128, 256), kh, bh)
                S.mm(ps(bN, 256, 384), khh, rh)
                S.mm(ps(bN, 384, 512), bh, rh)
                S.mm(ps(3, qa, qa + 128), khh, kh)
                yield
            for hh in range(2):
                bN = hbank[hh]
                qa = (2 * sl + hh) * 128
                h_ = str(hh)
                S.tt("dve", Bf["NM" + h_](), ps(bN, 0, 128), nlt, ALU.mult)
                S.tt("dve", Bf["NMT" + h_](), ps(bN, 128, 256), ngt, ALU.mult)
                S.tt("dve", Bf["ARK" + h_](), ps(bN, 256, 384), le, ALU.mult)
                S.tt("dve", Bf["ARB" + h_](), ps(bN, 384, 512), nle, ALU.mult)
                S.tt("dve", Bf["AKK" + h_](), ps(3, qa, qa + 128), lt, ALU.mult)
                S.tt("dve", Bf["YY0" + h_](), Bf["NM" + h_](), ident, ALU.add)
                yield
            for hh in range(2):
                pb, rh, kh, khh, bh, t0v = hv[hh]
                h_ = str(hh)
                qr = (2 * sl + hh) * 64
                RBK = int(os.environ.get("RBK", 7))
                if RBK == 3:
                    qr = 256 + hh * 64
                S.mm(ps(RBK, qr, qr + 64), kh, t0v, start=True, stop=False)
                S.mm(ps(RBK, qr, qr + 64), Bf["AKK" + h_](), VT(VT.t[:, pb:pb + 64]), start=False, stop=True)
                S.cp("act", Bf["RR" + h_](), ps(RBK, qr, qr + 64))
                yield
            cur = [(Bf["NM0"], Bf["NMT0"]), (Bf["NM1"], Bf["NMT1"])]
            for lvl in range(1, 7):
                a = str(lvl % 2)
                for hh in range(2):
                    Pm, PTm = cur[hh]
                    bC = cbank[hh]
                    h_ = str(hh)
                    if lvl < 6:
                        S.mm(ps(bC, 0, 128), PTm(), Pm())
                    S.mm(ps(bC, 128, 256), Pm(), PTm())
                    if lvl < 6:
                        S.cp("act", Bf["PP" + a + h_](), ps(bC, 0, 128))
                    S.cp("act", Bf["PPT" + a + h_](), ps(bC, 128, 256))
                    yield
                for hh in range(2):
                    bC = cbank[hh]
                    h_ = str(hh)
                    yprev, ynew = Bf["YY" + str((lvl - 1) % 2) + h_], Bf["YY" + a + h_]
                    S.mm(ps(bC, 256, 384), Bf["PPT" + a + h_](), yprev())
                    S.tt("dve", ynew(), yprev(), ps(bC, 256, 384), ALU.add)
                    cur[hh] = (Bf["PP" + a + h_], Bf["PPT" + a + h_])
                    yield
            for hh in range(2):
                pb, rh, kh, khh, bh, t0v = hv[hh]
                h_ = str(hh)
                qu = 256 + (2 * sl + hh) * 64
                RBK = int(os.environ.get("RBK", 7))
                if RBK == 3:
                    qu = 384 + hh * 64
                S.mm(ps(RBK, qu, qu + 64), Bf["YY0" + h_](), Bf["RR" + h_]())
                S.cp("act", UT(UT.t[:, pb:pb + 64]), ps(RBK, qu, qu + 64))
                yield
            for hh in range(2):
                pb, rh, kh, khh, bh, t0v = hv[hh]
                h_ = str(hh)
                oc = (2 * i + hh) * 64
                ov = ps(4, oc, oc + 64)
                S.mm(ov, rh, t0v, start=True, stop=False)
                S.mm(ov, Bf["ARK" + h_](), VT(VT.t[:, pb:pb + 64]), start=False, stop=False)
                S.mm(ov, Bf["ARB" + h_](), UT(UT.t[:, pb:pb + 64]), start=False, stop=True)
                yield
            S.mm(ps(0, 384, 512), KTK(), VT(), start=True, stop=False)
            S.mm(ps(0, 384, 512), NBT(), UT(), start=False, stop=True)
            for hh in range(2):
                pb = 64 * hh
                tv = T(T.t[pb:pb + 64, i, :], i)
                S.tt("dve", tv, tv, ps(0, 384 + pb, 448 + pb, pb, pb + 64), ALU.add)
                S.ts("dve", tv, tv, GL(GL.t[pb:pb + 64, i, c:c + 1], i), ALU.mult)
                S.cp("act", Tbf(Tbf.t[pb:pb + 64, i, :], i), tv)
            if os.environ.get("SHOWB"):
                print("PAIR END", c, i, S.nops)
            yield

        def r1_tail(c):
            cc = slice(c * CH, (c + 1) * CH)
            osf = fl(OS(), "p a b -> p (a b)")
            S.cp("act", osf, ps(4, 0, 512))
            S.act(fl(OQ(), "p a b -> p (a b)"), ps(4, 0, 512), AF.Square)
            yield
            s1, s2, s3, s4 = [ST8(ST8.t[:, k, :], k) for k in range(4)]
            S.op("dve", lambda e: e.tensor_reduce(out=s1.ap, in_=OS.t[:], axis=AX.X, op=ALU.add), [s1], [OS()])
            S.op("dve", lambda e: e.tensor_reduce(out=s2.ap, in_=OQ.t[:], axis=AX.X, op=ALU.add), [s2], [OQ()])
            S.ts("dve", s1, s1, 1.0 / 64, ALU.mult)
            S.tt("dve", s3, s1, s1, ALU.mult)
            S.stt("dve", s2, s2, 1.0 / 64, s3, ALU.mult, ALU.subtract)
            S.act(s2, s2, AF.Sqrt, bias=gneps_v)
            S.recip(s2, s2)
            yield
            S.tt("dve", OS(), OS(), bc(s1, s1.ap[:, :, None].broadcast_to([128, 8, 64])), ALU.subtract)
            S.tt("dve", OS(), OS(), bc(s2, s2.ap[:, :, None].broadcast_to([128, 8, 64])), ALU.mult)
            yield
            for i in range(4):
                bt = 5 + i % 2
                S.tr(ps(bt, 384, 512), bc(OS(), osf.ap[:, i * 128:(i + 1) * 128]), ident)
                ofv = OF[i % 2]()
                S.ts("dve", ofv, ps(bt, 384, 512), pvv(L, "rw_ln_w", i, i + 1), ALU.mult, pvv(L, "rw_ln_b", i, i + 1), ALU.add)
                S.tt("dve", ofv, ofv, BV(BV.t[:, i, cc], i), ALU.add)
                S.tt("dve", YMIX(YMIX.t[:, i, cc], i), ofv, G(G.t[:, i, cc], i), ALU.mult)
                if os.environ.get("SHOWB"):
                    print("TAIL", c, i, S.nops)
                yield

        pend = None
        for c in range(CPT):
            import os
            if os.environ.get("NOILV"):
                for i in range(4):
                    interleave([r1_pair(c, i, (i % 2) if os.environ.get("NOILV") == "2" else 0)])
                interleave([r1_tail(c)])
                continue
            gens = [r1_pair(c, 0, 0), r1_pair(c, 1, 1)]
            if pend is not None:
                gens.append(pend)
            interleave(gens)
            interleave([r1_pair(c, 2, 0), r1_pair(c, 3, 1)])
            pend = r1_tail(c)
        if pend is not None:
            interleave([pend])
        stop("r1")
        PHASE[0] = "hgrn"
        arena_reset()
        le64 = View(C("le64").ap.bitcast(U32), CST.regs)
        HS, HSbf = st["HS"], st["HSbf"]
        hb = [dict() for _ in range(2)]
        for k in range(2):
            for nm in ("Q", "LF", "K1", "I", "OG", "GC", "NG", "EC", "EX", "KD", "O"):
                hb[k][nm] = mk("hg" + nm, [128, NT], F32)
            for nm in ("QT", "KT", "QG"):
                hb[k][nm] = mk("hg" + nm, [128, NT], BF16)
            hb[k]["ITK"] = mk("hgitk", [128, 128], BF16)
            hb[k]["KDT"] = mk("hgkdt", [128, 128], BF16)
            hb[k]["ATM"] = mk("hgatm", [128, 128], BF16)
            S.memset("pool", hb[k]["ATM"](), 0.0)
        hg_w0 = widx

        def hg_head(hd):
            sl = hd % 2
            Bf = hb[sl]
            Q, LF, K1, I_, OG, GC, NG, EC, EX, KD, O_ = [Bf[n] for n in ("Q", "LF", "K1", "I", "OG", "GC", "NG", "EC", "EX", "KD", "O")]
            QT, KT, QG, ITK, KDT, ATM = [Bf[n] for n in ("QT", "KT", "QG", "ITK", "KDT", "ATM")]
            bt, ba, bo = 3 * sl, 3 * sl + 1, 3 * sl + 2
            w0 = hg_w0 + 4 * hd
            b = inproj(L, w0, banks=(6, 7))
            S.act(Q(), ps(b, 0, NT), AF.Silu)
            yield
            b = inproj(L, w0 + 1, banks=(6, 7))
            S.act(LF(), ps(b, 0, NT), AF.Sigmoid)
            S.act(LF(), LF(), AF.Identity, scale=st["oml"](st["oml"].t[:, hd:hd + 1]), bias=st["lb"](st["lb"].t[:, hd:hd + 1]))
            yield
            S.ts("dve", K1(), LF(), -1.0, ALU.mult, 1.0, ALU.add)
            S.act(LF(), LF(), AF.Ln)
            yield
            b = inproj(L, w0 + 2, banks=(6, 7))
            S.cp("act", I_(), ps(b, 0, NT))
            yield
            b = inproj(L, w0 + 3, banks=(6, 7))
            S.act(OG(), ps(b, 0, NT), AF.Silu)
            yield
            for q in range(NQ):
                cq = slice(q * 64, (q + 1) * 64)
                S.scan(GC(GC.t[:, cq]), bc(ones_f, ones_f.ap[:, 0:64]), LF(LF.t[:, cq]), 0.0, ALU.mult, ALU.add)
            yield
            gc3 = GC.t[:].rearrange("p (q t) -> p q t", t=64)
            S.act(EC(), GC(), AF.Exp)
            S.tt("dve", View(NG.t[:].rearrange("p (q t) -> p q t", t=64), NG.regs), View(gc3, GC.regs),
                 View(gc3[:, :, 31:32].broadcast_to([128, NQ, 64]), GC.regs), ALU.subtract)
            S.tt("dve", View(EX.t[:].rearrange("p (q t) -> p q t", t=64), EX.regs), View(gc3, GC.regs),
                 View(gc3[:, :, 63:64].broadcast_to([128, NQ, 64]), GC.regs), ALU.subtract)
            yield
            S.tt("dve", QG(), Q(), EC(), ALU.mult)
            S.act(KD(), NG(), AF.Exp)
            S.tt("dve", QT(), Q(), KD(), ALU.mult)
            yield
            S.act(NG(), NG(), AF.Exp, scale=-1.0)
            S.tt("dve", KT(), K1(), NG(), ALU.mult)
            yield
            S.act(EX(), EX(), AF.Exp, scale=-1.0)
            S.tt("dve", KD(), K1(), EX(), ALU.mult)
            yield
            for blk in range(CPT):
                cb_ = slice(blk * 128, (blk + 1) * 128)
                S.tr(ps(bt, 0, 128), I_(I_.t[:, cb_]), ident)
                S.tr(ps(bt, 128, 256), KD(KD.t[:, cb_]), ident)
                S.cp("act", ITK(), ps(bt, 0, 128))
                S.cp("act", KDT(), ps(bt, 128, 256))
                yield
                S.mm(ps(ba, 0, 128), KT(KT.t[:, cb_]), QT(QT.t[:, cb_]))
                S.cpred(ATM(), le64, ps(ba, 0, 128))
                yield
                S.mm(ps(bo, 0, 128), ITK(), ATM(), start=True, stop=False)
                for qq in range(2):
                    q = 2 * blk + qq
                    cq = slice(q * 64, (q + 1) * 64)
                    end = q * 64 + 63
                    S.mm(ps(bo, qq * 64, qq * 64 + 64), HSbf(HSbf.t[:, hd, :], hd), QG(QG.t[:, cq]), start=False, stop=(qq == 1))
                    S.mm(ps(bt, 256, 384), KDT(KDT.t[qq * 64:qq * 64 + 64, :]), ITK(ITK.t[qq * 64:qq * 64 + 64, :]))
                    hs = HS(HS.t[:, hd, :], hd)
                    S.stt("dve", hs, hs, EC(EC.t[:, end:end + 1]), ps(bt, 256, 384), ALU.mult, ALU.add)
                    S.cp("act", HSbf(HSbf.t[:, hd, :], hd), hs)
                    yield
                S.cp("act", O_(O_.t[:, cb_]), ps(bo, 0, 128))
                yield
            r = norm_stats([O_()], 128)
            S.stt("dve", O_(), O_(), pvv(L, "hg_norm", hd, hd + 1), r, ALU.mult, ALU.mult)
            S.tt("dve", YMIX(YMIX.t[:, 4 + hd, :], 4 + hd), O_(), OG(), ALU.mult)
            yield

        rolling([(lambda hd=hd: hg_head(hd)) for hd in range(4)], 2)
        widx = hg_w0 + 16
        return widx

    def mix_out(L, widx):
        PHASE[0] = "mixout"
        widx = proj8(L, widx, YMIX, 8)
        post_norm_add(L, "n_mix_post")
        return widx

    try:
      for ti in range(n_tiles):
        t0 = ti * NT
        S.dma("sp", H(), dram(xT[:, :, t0:t0 + NT].rearrange("d p t -> p d t")))
        for L in range(n_layers):
            stop("start")
            widx = even_layer(L) if L % 2 == 0 else odd_layer(L)
            stop("mixer")
            if dbg is not None and dbg[0] == "ymix" and dbg[1] == L and ti == 0:
                for d in range(8):
                    S.cp("dve", TMP8(TMP8.t[:, d, :], d), YMIX(YMIX.t[:, d, :], d))
                S.dma("sp", dram(dbg_d.rearrange("d p t -> p d t")), TMP8(), is_output=True)
            widx = mix_out(L, widx)
            stop("mixout")
            widx = ffn(L, widx)
            assert widx == n_ws[L], (widx, n_ws[L])
        S.dma("sp", dram(oT[:, :, t0:t0 + NT].rearrange("d p t -> p d t")), H(), is_output=True)
    except StopBuild:
        S.dma("sp", dram(oT[:, :, 0:NT].rearrange("d p t -> p d t")), H(), is_output=True)
    S.finish()
    print("program: ninst=%d persist_bytes/partition=%d" % (S.ninst, persist_bytes))
    if COST is not None:
        phases = sorted(set(k[0] for k in COST))
        for ph in phases:
            print("COST %-7s" % ph, " ".join("%s=%.0fus(%d)" % (e, COST.get((ph, e), 0), COST.get((ph, "n_" + e), 0)) for e in ("pe", "dve", "act", "pool")))
    return nc


def host_prep(inputs):
    pkc = consts_host()
    pls, pbs, wss = [], [], []
    for L in range(DEPTH):
        P, B, ws = layer_host(inputs, L)
        pls.append(P)
        pbs.append(B)
        wss.append(ws)
    return pkc, pls, pbs, wss


def make_xT(inputs):
    x = np.asarray(inputs["x"], dtype=np.float32)
    meta = np.asarray(inputs["meta"], dtype=np.float32)
    xs = []
    for b in range(NB):
        full = np.zeros((TPAD, D), np.float32)
        full[:NMETA] = meta
        full[NMETA:TREAL] = x[b]
        xs.append(np.ascontiguousarray(full.T).reshape(8, 128, TPAD))
    return xs


def make_shared(pkc, pls, pbs, wss):
    shared = {"consts": pkc.pack()}
    for L in range(DEPTH):
        shared["pv%d" % L] = pls[L].pack()
        shared["pb%d" % L] = pbs[L].pack()
        shared["ws%d" % L] = wss[L]
    return shared


def kernel(**inputs):
    pkc, pls, pbs, wss = host_prep(inputs)
    nc = build_program(pkc, pls, pbs, [w.shape[0] for w in wss])
    xs = make_xT(inputs)
    shared = make_shared(pkc, pls, pbs, wss)
    in_maps = [dict(shared, xT=xs[b]) for b in range(NB)]
    res = run_bass_kernel_spmd(nc, in_maps, core_ids=list(range(NB)))
    out = np.empty((NB, SEQ, D), np.float32)
    for b in range(NB):
        o = np.asarray(res.results[b]["oT"]).reshape(D, TPAD)
        out[b] = o[:, NMETA:TREAL].T
    return out
```

```python
import math
import numpy as np
import concourse.bass as bass
import concourse.mybir as mybir
from concourse.bass_utils import run_bass_kernel_spmd

F32 = mybir.dt.float32
BF16 = mybir.dt.bfloat16
I32 = mybir.dt.int32
U32 = mybir.dt.uint32
AF = mybir.ActivationFunctionType
ALU = mybir.AluOpType
AX = mybir.AxisListType

D = 1024
NB = 8
SEQ = 4096
NMETA = 16
TREAL = SEQ + NMETA
CH = 128
CPT = 3
NQ = 2 * CPT
NT = CH * CPT
NTILES = (TREAL + NT - 1) // NT
TPAD = NTILES * NT
DEPTH = 4
DFF = 2816
NJ = DFF // 128
EPS = 1e-6
GN_EPS = 64e-5
import os
SAME_SYNC = True
COST = {} if os.environ.get('COST') else None
PHASE = ['pro']
NOSYNC_ENG = tuple(os.environ.get('NOSYNC', '').split(',')) if os.environ.get('NOSYNC') else ()


class StopBuild(Exception):
    pass


class Reg:
    __slots__ = ("w", "r")

    def __init__(self):
        self.w = None
        self.r = {}


class View:
    __slots__ = ("ap", "regs", "excl")

    def __init__(self, ap, regs, excl=False):
        self.ap = ap
        self.regs = regs
        self.excl = excl


class Buf:
    def __init__(self, S, name, shape, dtype, nreg=1, space="sbuf"):
        nc = S.nc
        if space == "sbuf":
            self.t = nc.alloc_sbuf_tensor(name, list(shape), dtype, align_bytes=64)
            S.sbuf_bytes += int(np.prod(shape[1:])) * (2 if dtype == BF16 else 4)
        else:
            self.t = nc.alloc_psum_tensor(name, list(shape), dtype)
        self.regs = [Reg() for _ in range(nreg)]

    def __call__(self, ap=None, r=None):
        if ap is None:
            ap = self.t[:]
        if r is None:
            regs = self.regs
        elif isinstance(r, int):
            regs = [self.regs[r]]
        else:
            regs = [self.regs[i] for i in r]
        return View(ap, regs)


def dram(ap):
    return View(ap, [])


class Sched:
    def __init__(self, nc):
        self.nc = nc
        self.E = {"pe": nc.tensor, "dve": nc.vector, "act": nc.scalar, "pool": nc.gpsimd, "sp": nc.sync}
        self.sem = {k: nc.alloc_semaphore("sem_" + k) for k in ("pe", "dve", "act", "pool")}
        self.cnt = {k: 0 for k in self.sem}
        self.seen = {k: {} for k in self.E}
        self.dq = {}
        self.sbuf_bytes = 0
        self.ninst = 0
        self.out_toks = []
        self.nops = 0
        self.max_ops = None

    def _deps(self, outs, ins):
        deps = {}

        def need(tok):
            if tok is None:
                return
            cur = deps.get(tok[0])
            if cur is None or cur[1] < tok[1]:
                deps[tok[0]] = tok

        for v in ins:
            for rg in v.regs:
                need(rg.w)
        for v in outs:
            for rg in v.regs:
                need(rg.w)
                for tok in rg.r.values():
                    need(tok)
        return deps

    def _wait(self, eng, deps):
        E = self.E[eng]
        own = self.sem[eng].num if eng in self.sem else None
        for sid, tok in deps.items():
            if sid == own and (eng == "pe" or not SAME_SYNC or eng in NOSYNC_ENG):
                continue
            if self.seen[eng].get(sid, 0) < tok[1]:
                if self.max_ops is not None and self.nops >= self.max_ops - int(os.environ.get('SHOWN', 3)):
                    print("  WAIT", eng, "on", tok[2].name, tok[1], "cnts", self.cnt)
                E.wait_ge(tok[2], tok[1])
                self.seen[eng][sid] = tok[1]
                self.ninst += 1

    def _mark(self, tok, outs, ins):
        for v in ins:
            for rg in v.regs:
                cur = rg.r.get(tok[0])
                if cur is None or cur[1] < tok[1]:
                    rg.r[tok[0]] = tok
        for v in outs:
            for rg in v.regs:
                rg.w = tok
                rg.r = {}

    def op(self, eng, fn, outs, ins, signal=True):
        xs = [v for v in ins if v.excl]
        if xs:
            outs = list(outs) + xs
            ins = [v for v in ins if not v.excl]
        self._wait(eng, self._deps(outs, ins))
        if COST is not None:
            try:
                shp = outs[0].ap.shape
                n = 1
                for d_ in shp[1:]:
                    n *= int(d_)
            except Exception:
                n = 128
            if eng == "pe":
                f32 = "float32" in str(ins[0].ap.dtype)
                c = n * (4 if f32 else 1) / 2400.0 + 0.03
            elif eng == "dve":
                c = max(64, n) / 960.0 + 0.06
            elif eng == "act":
                c = max(64, n) / 1400.0 + 0.2
            else:
                c = max(64, n) / 200.0 + 0.1
            key = (PHASE[0], eng)
            COST[key] = COST.get(key, 0.0) + c
            COST[(PHASE[0], "n_" + eng)] = COST.get((PHASE[0], "n_" + eng), 0) + 1
        inst = fn(self.E[eng])
        if self.max_ops is not None and self.nops >= self.max_ops - int(os.environ.get('SHOWN', 3)):
            try:
                print("  INST", eng, inst.concise())
            except Exception as ex:
                print("  INST?", ex, inst.ins)
        self.ninst += 1
        sem = self.sem[eng]
        if signal:
            self.cnt[eng] += 1
            inst.then_inc(sem, 1)
            tok = (sem.num, self.cnt[eng], sem)
        else:
            tok = (sem.num, self.cnt[eng] + 1, sem)
        self._mark(tok, outs, ins)
        self.nops += 1
        if self.max_ops is not None and self.nops >= self.max_ops:
            self.max_ops = None
            print("STOP at op", self.nops, eng)
            raise StopBuild()

    def dma(self, q, out, in_, is_output=False):
        if q not in self.dq:
            nr = 32 if q == "pool" else 8
            self.dq[q] = {"ring": [self.nc.alloc_semaphore("dsem_%s_%d" % (q, i)) for i in range(nr)], "n": 0, "toks": [None] * nr, "nr": nr}
        Q = self.dq[q]
        i = Q["n"]
        nr = Q["nr"]
        slot = i % nr
        deps = self._deps([out], [in_])
        if Q["toks"][slot] is not None:
            t = Q["toks"][slot]
            if t[0] not in deps or deps[t[0]][1] < t[1]:
                deps[t[0]] = t
        self._wait(q, deps)
        sem = Q["ring"][slot]
        val = 16 * (i // nr + 1)
        self.E[q].dma_start(out=out.ap, in_=in_.ap).then_inc(sem, 16)
        self.ninst += 1
        tok = (sem.num, val, sem)
        Q["toks"][slot] = tok
        Q["n"] += 1
        self._mark(tok, [out], [in_])
        if is_output:
            self.out_toks.append(tok)

    def finish(self):
        deps = {}
        for tok in self.out_toks:
            if tok[0] not in deps or deps[tok[0]][1] < tok[1]:
                deps[tok[0]] = tok
        for f in ("pe", "dve", "act", "pool"):
            if self.cnt[f]:
                deps[self.sem[f].num] = (self.sem[f].num, self.cnt[f], self.sem[f])
        for q, Q in self.dq.items():
            for t in Q["toks"]:
                if t is not None and (t[0] not in deps or deps[t[0]][1] < t[1]):
                    deps[t[0]] = t
        self._wait("sp", deps)

    def barrier(self):
        for eng in ("pe", "dve", "act", "pool"):
            deps = {}
            for f in ("pe", "dve", "act", "pool"):
                if (f == eng and eng == "pe") or self.cnt[f] == 0:
                    continue
                sem = self.sem[f]
                deps[sem.num] = (sem.num, self.cnt[f], sem)
            self._wait(eng, deps)

    def mm(self, out, lhsT, rhs, start=True, stop=True, signal=True):
        signal = True
        try:
            key = (int(lhsT.ap.start_partition()), int(lhsT.ap.partition_size()))
        except Exception:
            key = None
        last = getattr(self, "_mm_key", None)
        if key != last and self.cnt["pe"] > 0 and ((key is not None and key[1] < 128) or (last is not None and last[1] < 128)):
            sem = self.sem["pe"]
            if self.seen["pe"].get(sem.num, 0) < self.cnt["pe"]:
                self.E["pe"].wait_ge(sem, self.cnt["pe"])
                self.seen["pe"][sem.num] = self.cnt["pe"]
                self.ninst += 1
        self._mm_key = key
        self.op("pe", lambda e: e.matmul(out.ap, lhsT=lhsT.ap, rhs=rhs.ap, start=start, stop=stop), [out], [lhsT, rhs], signal)

    def tr(self, out, in_, ident):
        self.op("pe", lambda e: e.transpose(out.ap, in_.ap, ident.ap), [out], [in_, ident])

    def tt(self, eng, out, a, b, op):
        self.op(eng, lambda e: e.tensor_tensor(out=out.ap, in0=a.ap, in1=b.ap, op=op), [out], [a, b])

    def ts(self, eng, out, a, s1, op0, s2=None, op1=None):
        ins = [a] + [s for s in (s1, s2) if isinstance(s, View)]
        a1 = s1.ap if isinstance(s1, View) else s1
        a2 = s2.ap if isinstance(s2, View) else s2
        if op1 is None:
            self.op(eng, lambda e: e.tensor_scalar(out=out.ap, in0=a.ap, scalar1=a1, scalar2=None, op0=op0), [out], ins)
        else:
            self.op(eng, lambda e: e.tensor_scalar(out=out.ap, in0=a.ap, scalar1=a1, scalar2=a2, op0=op0, op1=op1), [out], ins)

    def stt(self, eng, out, a, s, b, op0, op1):
        ins = [a, b] + ([s] if isinstance(s, View) else [])
        sa = s.ap if isinstance(s, View) else s
        self.op(eng, lambda e: e.scalar_tensor_tensor(out=out.ap, in0=a.ap, scalar=sa, in1=b.ap, op0=op0, op1=op1), [out], ins)

    def act(self, out, in_, func, bias=None, scale=None):
        ins = [in_] + [s for s in (bias, scale) if isinstance(s, View)]
        kw = {}
        if bias is not None:
            kw["bias"] = bias.ap if isinstance(bias, View) else bias
        if scale is not None:
            kw["scale"] = scale.ap if isinstance(scale, View) else scale
        self.op("act", lambda e: e.activation(out=out.ap, in_=in_.ap, func=func, **kw), [out], ins)

    def cp(self, eng, out, in_):
        if eng == "act":
            self.op("act", lambda e: e.copy(out=out.ap, in_=in_.ap), [out], [in_])
        else:
            self.op(eng, lambda e: e.tensor_copy(out=out.ap, in_=in_.ap), [out], [in_])

    def scan(self, out, d0, d1, init, op0, op1):
        ins = [d0, d1] + ([init] if isinstance(init, View) else [])
        ia = init.ap if isinstance(init, View) else init
        self.op("dve", lambda e: e.tensor_tensor_scan(out=out.ap, data0=d0.ap, data1=d1.ap, initial=ia, op0=op0, op1=op1), [out], ins)

    def recip(self, out, in_):
        self.op("dve", lambda e: e.reciprocal(out=out.ap, in_=in_.ap), [out], [in_])

    def memset(self, eng, out, val):
        self.op(eng, lambda e: e.memset(out.ap, val), [out], [])

    def cpred(self, out, mask, data):
        self.op("dve", lambda e: e.copy_predicated(out=out.ap, mask=mask.ap, data=data.ap), [out], [mask, data])


class Packer:
    def __init__(self):
        self.cols = []
        self.off = {}
        self.n = 0

    def add(self, name, arr):
        arr = np.asarray(arr, dtype=np.float32)
        assert arr.shape[0] == 128, (name, arr.shape)
        arr = arr.reshape(128, -1)
        self.off[name] = (self.n, arr.shape[1])
        self.cols.append(arr)
        self.n += arr.shape[1]

    def pack(self):
        return np.ascontiguousarray(np.concatenate(self.cols, axis=1))


def feat(v):
    v = np.asarray(v, dtype=np.float32)
    return np.ascontiguousarray(v.reshape(-1, 128).T)


def rep(v):
    v = np.asarray(v, dtype=np.float32).reshape(1, -1)
    return np.ascontiguousarray(np.broadcast_to(v, (128, v.shape[1])))


def wtiles(W):
    K, N = W.shape
    Np = ((N + 127) // 128) * 128
    Kp = ((K + 1023) // 1024) * 1024
    Wp = np.zeros((Kp, Np), np.float32)
    Wp[:K, :N] = W
    out = []
    for kg in range(Kp // 1024):
        blk = Wp[kg * 1024:(kg + 1) * 1024]
        out.append(blk.reshape(8, 128, Np // 128, 128).transpose(2, 1, 0, 3))
    return out


def consts_host():
    P = Packer()
    idx = np.arange(128)
    P.add("ident", np.eye(128))
    P.add("ones", np.ones((128, 128)))
    le = (idx[:, None] <= idx[None, :]).astype(np.float32)
    lt = (idx[:, None] < idx[None, :]).astype(np.float32)
    P.add("le", le)
    P.add("lt", lt)
    P.add("nle", -le)
    P.add("nlt", -lt)
    P.add("ngt", -(idx[:, None] > idx[None, :]).astype(np.float32))
    blk = (idx[:, None] // 64 == idx[None, :] // 64).astype(np.float32)
    P.add("blk64", blk)
    P.add("le64", le * blk)
    P.add("ramp", rep(np.arange(1, 129)))
    return P


ODD_ORDER0 = [12, 13, 0, 4, 8, 1, 5, 9, 2, 6, 10, 3, 7, 11] + [14 + h + 4 * s for h in range(4) for s in range(4)]
ODD_ORDER1 = [12, 13, 30, 0, 4, 8, 1, 5, 9, 2, 6, 10, 3, 7, 11] + [14 + h + 4 * s for h in range(4) for s in range(4)]


def layer_host(inp, L):
    P = Packer()
    B = Packer()
    g = lambda n: np.asarray(inp[n], dtype=np.float32)
    P.add("n_mix_pre", feat(g("norm_mix_pre")[L]))
    P.add("n_mix_post", feat(g("norm_mix_post")[L]))
    P.add("n_ffn_pre", feat(g("norm_ffn_pre")[L]))
    P.add("n_ffn_post", feat(g("norm_ffn_post")[L]))
    cw = g("ffn_conv_w")[L]
    cb = g("ffn_conv_b")[L]
    order = np.concatenate([np.concatenate([np.arange(j * 128, (j + 1) * 128), DFF + np.arange(j * 128, (j + 1) * 128)]) for j in range(NJ)])
    P.add("ffn_cw", np.stack([feat(cw[k][order]) for k in range(3)], axis=2))
    P.add("ffn_cb", feat(cb[order]))
    tiles = []
    if L % 2 == 0:
        e = L // 2
        tiles.append(wtiles(g("ev_w_in")[e])[0])
        lr, li, ldt = g("s5_lam_re")[e], g("s5_lam_im")[e], g("s5_log_dt")[e]
        st = lambda a: np.ascontiguousarray(a.reshape(8, 2, 64).transpose(1, 2, 0).reshape(128, 8))
        P.add("s5_lr_s", st(lr))
        P.add("s5_li_s", st(li))
        P.add("s5_ldt_s", st(np.repeat(ldt[:, None], 64, axis=1)))
        B.add("s5_lr_b", rep(lr.reshape(-1)))
        B.add("s5_li_b", rep(li.reshape(-1)))
        B.add("s5_ldt_b", rep(np.repeat(ldt, 64)))
        br, bi = g("s5_b_re")[e], g("s5_b_im")[e]

        def bd(b):
            o = np.zeros((128, 2, 8, 64), np.float32)
            for kt in range(2):
                for gl in range(8):
                    o[gl * 16:(gl + 1) * 16, kt, gl, :] = b[8 * kt + gl].T
            return o.reshape(128, 1024)
        B.add("s5_br_bd", bd(br))
        B.add("s5_bi_bd", bd(bi))
        cr, ci = g("s5_c_re")[e], g("s5_c_im")[e]

        def cpad(c):
            o = np.zeros((128, 8, 128), np.float32)
            for j in range(8):
                for gp in range(2):
                    gg = 2 * j + gp
                    col = (gg % 8) * 16
                    o[gp * 64:(gp + 1) * 64, j, col:col + 16] = c[gg].T
            return o.reshape(128, 1024)
        B.add("s5_cr_pad", cpad(cr))
        B.add("s5_ci_pad", cpad(ci))
        B.add("s5_wglu", g("s5_w_glu")[e].reshape(2, 128, 256).transpose(1, 0, 2))
        P.add("s5_d", feat(g("s5_d")[e]))
        P.add("s5_bglu", feat(g("s5_b_glu")[e]))
        scw = g("ssd_conv_w")[e]
        P.add("ssd_cw", np.stack([feat(scw[k]) for k in range(4)], axis=2))
        P.add("ssd_cb", feat(g("ssd_conv_b")[e]))
        dtb = np.zeros((128, 1), np.float32)
        dtb[:12, 0] = g("ssd_dt_bias")[e]
        P.add("ssd_dtb", dtb)
        P.add("ssd_alog", rep(g("ssd_a_log")[e]))
        P.add("ssd_d", feat(np.repeat(g("ssd_d")[e], 64)))
        P.add("ssd_norm", feat(g("ssd_norm")[e]))
    else:
        o = L // 2
        w_in = g("od_w_in")[o]
        if o > 0:
            w_in = np.concatenate([w_in, g("rw_w_vin")[o - 1]], axis=1)
        wt = wtiles(w_in)[0]
        tiles.append(np.stack([wt[i] for i in (ODD_ORDER1 if o > 0 else ODD_ORDER0)]))
        mu = g("rw_mu")[o]
        if o > 0:
            mu = np.concatenate([mu, g("rw_mu_v")[o - 1], np.zeros(96, np.float32)])
        else:
            mu = np.concatenate([mu, np.zeros(128, np.float32)])
        P.add("rw_mu", feat(mu))
        P.add("rw_w0", feat(g("rw_w0")[o]))
        P.add("rw_a0", feat(g("rw_a0")[o]))
        B.add("rw_wa2", np.concatenate([g("rw_w2")[o], g("rw_a2")[o]], axis=0))
        B.add("rw_g2", g("rw_g2")[o])
        v2 = np.zeros((128, 512), np.float32)
        if o > 0:
            v2[:32] = g("rw_v2")[o - 1]
            P.add("rw_v0", feat(g("rw_v0")[o - 1]))
        B.add("rw_v2", v2)
        P.add("rw_k_k", feat(g("rw_k_k")[o]))
        P.add("rw_k_a", feat(g("rw_k_a")[o]))
        P.add("rw_r_k", feat(g("rw_r_k")[o]))
        P.add("rw_ln_w", feat(g("rw_ln_w")[o]))
        P.add("rw_ln_b", feat(g("rw_ln_b")[o]))
        P.add("hg_lb_raw0", feat(g("hg_lb_raw")[0]))
        P.add("hg_lb_raw1", feat(g("hg_lb_raw")[1]))
        P.add("hg_norm", feat(g("hg_norm")[o]))
    tiles.append(wtiles(g("mix_w_out")[L])[0])
    up = wtiles(g("ffn_w_up")[L])[0]
    tiles.append(np.stack([up[j + (0 if s == 0 else NJ)] for j in range(NJ) for s in (0, 1)]))
    dn = wtiles(g("ffn_w_down")[L])
    tiles.append(np.stack([dn[kg][m] for m in range(8) for kg in range(3)]))
    ws = np.ascontiguousarray(np.concatenate(tiles, axis=0).reshape(-1, 128, 1024))
    return P, B, ws
def build_program(pk_consts, pk_layers, pk_big, n_ws, n_layers=DEPTH, n_tiles=NTILES, dbg=None):
    nc = bass.Bass("TRN2", target_bir_lowering=False)
    S = Sched(nc)
    if dbg is not None and dbg[0] == "nops":
        S.max_ops = dbg[1]
    xT = nc.dram_tensor("xT", [8, 128, TPAD], F32, kind="ExternalInput").ap()
    oT = nc.dram_tensor("oT", [8, 128, TPAD], F32, kind="ExternalOutput").ap()
    cst_d = nc.dram_tensor("consts", [128, pk_consts.n], F32, kind="ExternalInput").ap()
    pv_d = [nc.dram_tensor("pv%d" % L, [128, pk_layers[L].n], F32, kind="ExternalInput").ap() for L in range(DEPTH)]
    pb_d = [nc.dram_tensor("pb%d" % L, [128, pk_big[L].n], F32, kind="ExternalInput").ap() for L in range(DEPTH)]
    ws_d = [nc.dram_tensor("ws%d" % L, [n_ws[L], 128, 1024], F32, kind="ExternalInput").ap() for L in range(DEPTH)]
    wsb_d = [nc.dram_tensor("wsb%d" % L, [n_ws[L], 128, 1024], BF16, kind="Internal").ap() for L in range(DEPTH)]
    WCH = 4
    wsb_regs = [[Reg() for _ in range((n_ws[L] + WCH - 1) // WCH)] for L in range(DEPTH)]
    dbg_d = None
    if dbg is not None:
        dbg_d = nc.dram_tensor("dbg", [8, 128, NT], F32, kind="ExternalOutput").ap()

    uid = [0]

    def mk(name, shape, dtype, nreg=1):
        uid[0] += 1
        return Buf(S, "%s_%d" % (name, uid[0]), shape, dtype, nreg)

    CST = mk("cst", [128, pk_consts.n], F32)
    PV = [mk("pvs", [128, pk_layers[L].n], F32) for L in range(DEPTH)]
    cst_bf = mk("cst_bf", [128, 3, 128], BF16)
    PSQ = [Reg() for _ in range(32)]
    PSt = nc.alloc_psum_tensor("psum_all", [128, 8, 512], F32)

    def C(name):
        o, n = pk_consts.off[name]
        return CST(CST.t[:, o:o + n])

    def pvv(L, name, a=None, b=None, p0=0, p1=128):
        o, n = pk_layers[L].off[name]
        if a is None:
            a, b = 0, n
        return PV[L](PV[L].t[p0:p1, o + a:o + b])

    def psr(bank, a, b):
        return [PSQ[bank]]

    def ps(bank, a=0, b=512, p0=0, p1=128):
        return View(PSt[p0:p1, bank, a:b], psr(bank, a, b), True)

    H = mk("h", [128, 8, NT], F32)
    HN = mk("hn", [128, 8, NT], BF16)
    SQ = mk("sq", [128, 2, NT], BF16, nreg=2)
    RSTD = mk("rstd", [128, NT], F32)
    TMP8 = mk("tmp8", [128, 8, NT], F32, nreg=8)
    YMIX = mk("ymix", [128, 8, NT], BF16, nreg=8)
    NWS = 6
    WRING = mk("wring", [128, NWS, 1024], BF16, nreg=NWS)
    FHALO = [mk("fhalo", [128, 2 * NJ, 2], F32) for L in range(DEPTH)]
    VFIRST = mk("vfirst", [128, 4, NT], F32, nreg=4)
    EPSB = mk("epsb", [128, 2], F32)
    wctr = [0]
    sqc = [0]

    ident = C("ident")
    ones_f = C("ones")
    ident_bf = cst_bf(cst_bf.t[:, 0, :])
    ones_bf = cst_bf(cst_bf.t[:, 1, :])
    blk64_bf = cst_bf(cst_bf.t[:, 2, :])

    def fl(v, pat, **kw):
        return View(v.ap.rearrange(pat, **kw), v.regs)

    def bc(v, ap):
        return View(ap, v.regs)

    LS = {}
    for L in range(n_layers):
        st = {}
        if L % 2 == 0:
            st["bb_re"] = mk("bbre", [128, 1024], BF16)
            st["bb_im"] = mk("bbim", [128, 1024], BF16)
            st["c_re"] = mk("cre", [128, 8, 128], F32)
            st["c_imn"] = mk("cimn", [128, 8, 128], F32)
            st["wglu"] = mk("wglu", [128, 2, 256], BF16)
            st["cos"] = mk("cos", [128, 8, CH], F32)
            st["sin"] = mk("sin", [128, 8, CH], F32)
            st["m"] = mk("m", [128, 8], F32)
            st["car_re"] = mk("carre", [128, 8], F32)
            st["car_im"] = mk("carim", [128, 8], F32)
            st["a_neg"] = mk("aneg", [128, 12], F32)
            st["Sst"] = mk("sst", [128, 12, 64], F32)
            st["prevT"] = mk("prevT", [128, 12, 128], BF16)
            st["xhalo"] = mk("xhalo", [128, 10, 3], F32)
        else:
            st["wa2"] = mk("wa2", [128, 512], BF16)
            st["g2"] = mk("g2", [128, 512], BF16)
            st["v2"] = mk("v2", [128, 512], BF16)
            st["lb"] = mk("lb", [128, 4], F32)
            st["oml"] = mk("oml", [128, 4], F32)
            st["T"] = mk("rwT", [128, 4, 64], F32, nreg=4)
            st["Tbf"] = mk("rwTbf", [128, 4, 64], BF16, nreg=4)
            st["HS"] = mk("hgS", [128, 4, 128], F32, nreg=4)
            st["HSbf"] = mk("hgSbf", [128, 4, 128], BF16, nreg=4)
            st["rwhalo"] = mk("rwhalo", [128, 15], F32)
            st["omm"] = mk("omm", [128, 15], F32)
        LS[L] = st

    arena_snap = (nc.sbuf_base, nc.sbuf_top)
    persist_bytes = S.sbuf_bytes

    def stop(stage):
        if dbg is not None and dbg[0] == "stop" and dbg[1] == stage:
            print("STOP stage", stage, "nops", S.nops)
            raise StopBuild()

    def arena_reset():
        S.barrier()
        nc.sbuf_base, nc.sbuf_top = arena_snap

    for L in range(n_layers):
        for ci in range(len(wsb_regs[L])):
            a, b = ci * WCH, min(n_ws[L], (ci + 1) * WCH)
            if os.environ.get("MAXCAST") and ci >= int(os.environ["MAXCAST"]):
                continue
            S.dma("pool", View(wsb_d[L][a:b], [wsb_regs[L][ci]]), dram(ws_d[L][a:b]))
    S.dma("sp", CST(), dram(cst_d))
    S.dma("sp", H(), dram(xT[:, :, 0:NT].rearrange("d p t -> p d t")))
    for L in range(n_layers):
        S.dma("sp", PV[L](), dram(pv_d[L]))
    for i, nm in enumerate(["ident", "ones", "blk64"]):
        S.cp("dve", cst_bf(cst_bf.t[:, i, :]), C(nm))
    for L in range(n_layers):
        S.memset("pool", FHALO[L](), 0.0)
    S.memset("dve", EPSB(EPSB.t[:, 0:1]), EPS)
    S.memset("dve", EPSB(EPSB.t[:, 1:2]), GN_EPS)
    eps_v = EPSB(EPSB.t[:, 0:1])
    gneps_v = EPSB(EPSB.t[:, 1:2])

    def sincos(x, sin_out, cos_out, tmp):
        ti = View(tmp.t[:, 3, :].bitcast(I32), tmp.regs)
        y, k, f = tmp(tmp.t[:, 0, :]), tmp(tmp.t[:, 1, :]), tmp(tmp.t[:, 2, :])
        for which, outv in ((0, sin_out), (1, cos_out)):
            S.ts("dve", y, x, 1.0 / (2 * math.pi), ALU.mult, 0.5 + 0.25 * which, ALU.add)
            S.cp("dve", ti, y)
            S.cp("dve", k, ti)
            S.tt("dve", f, y, k, ALU.subtract)
            S.ts("dve", k, f, 0.0, ALU.is_lt)
            S.tt("dve", f, f, k, ALU.add)
            S.ts("dve", f, f, 1.0, ALU.min)
            S.ts("dve", f, f, 2 * math.pi, ALU.mult, -math.pi, ALU.add)
            S.act(outv, f, AF.Sin)

    PVB = mk("pvb", [128, 8, 512], F32, nreg=8)
    TB = mk("s5tmpb", [128, 12, 512], F32, nreg=12)
    TS4 = mk("s5tmp4", [128, 4, 512], F32)

    def tb(i):
        return TB(TB.t[:, i, :], i)

    for L in range(n_layers):
        st = LS[L]

        def bgload(slot, name, a, b):
            o, n = pk_big[L].off[name]
            v = PVB(PVB.t[:, slot, 0:b - a], slot)
            S.dma("sp", v, dram(pb_d[L][:, o + a:o + b]))
            return v
        if L % 2 == 0:
            for half in range(2):
                hs = slice(half * 512, (half + 1) * 512)
                a_, b_ = half * 512, (half + 1) * 512
                lr, li, ldt = bgload(0, "s5_lr_b", a_, b_), bgload(1, "s5_li_b", a_, b_), bgload(2, "s5_ldt_b", a_, b_)
                br, bi = bgload(3, "s5_br_bd", a_, b_), bgload(4, "s5_bi_bd", a_, b_)
                crp, cip = bgload(5, "s5_cr_pad", a_, b_), bgload(6, "s5_ci_pad", a_, b_)
                dt, lrdt, lidt, mag, sn, cs, abr, abi, den, fre, fim, t1 = [tb(i) for i in range(12)]
                S.act(dt, ldt, AF.Exp)
                S.tt("dve", lrdt, lr, dt, ALU.mult)
                S.tt("dve", lidt, li, dt, ALU.mult)
                S.act(mag, lrdt, AF.Exp)
                sincos(lidt, sn, cs, TS4)
                S.tt("dve", abr, mag, cs, ALU.mult)
                S.tt("dve", abi, mag, sn, ALU.mult)
                S.tt("dve", den, lr, lr, ALU.mult)
                S.tt("dve", t1, li, li, ALU.mult)
                S.tt("dve", den, den, t1, ALU.add)
                S.recip(den, den)
                S.ts("dve", abr, abr, -1.0, ALU.add)
                S.tt("dve", fre, abr, lr, ALU.mult)
                S.tt("dve", t1, abi, li, ALU.mult)
                S.tt("dve", fre, fre, t1, ALU.add)
                S.tt("dve", fre, fre, den, ALU.mult)
                S.tt("dve", fim, abi, lr, ALU.mult)
                S.tt("dve", t1, abr, li, ALU.mult)
                S.tt("dve", fim, fim, t1, ALU.subtract)
                S.tt("dve", fim, fim, den, ALU.mult)
                S.tt("dve", t1, fre, br, ALU.mult)
                S.tt("dve", dt, fim, bi, ALU.mult)
                S.tt("dve", st["bb_re"](st["bb_re"].t[:, hs]), t1, dt, ALU.subtract)
                S.tt("dve", t1, fre, bi, ALU.mult)
                S.tt("dve", dt, fim, br, ALU.mult)
                S.tt("dve", st["bb_im"](st["bb_im"].t[:, hs]), t1, dt, ALU.add)
                js = slice(half * 4, half * 4 + 4)
                S.cp("dve", st["c_re"](st["c_re"].t[:, js, :].rearrange("p a b -> p (a b)")), crp)
                S.ts("dve", st["c_imn"](st["c_imn"].t[:, js, :].rearrange("p a b -> p (a b)")), cip, -1.0, ALU.mult)
                if half == 0:
                    wg = bgload(7, "s5_wglu", 0, 512)
                    S.cp("dve", fl(st["wglu"](), "p a b -> p (a b)"), wg)
                    dts, th = TS4(TS4.t[:, 0, 0:8]), TS4(TS4.t[:, 0, 8:16])
                    S.act(dts, pvv(L, "s5_ldt_s"), AF.Exp)
                    S.tt("dve", th, pvv(L, "s5_li_s"), dts, ALU.mult)
                    S.tt("dve", dts, pvv(L, "s5_lr_s"), dts, ALU.mult)
                    S.act(st["m"](), dts, AF.Exp)
                    S.cp("dve", st["car_re"](), th)
                ang = TB(TB.t[:, 1, :].rearrange("p (a b) -> p a b", a=4), 1)
                ramp = C("ramp")
                thv = st["car_re"](st["car_re"].t[:, js])
                S.tt("dve", ang, bc(ramp, ramp.ap[:, None, :].broadcast_to([128, 4, 128])),
                     bc(thv, thv.ap[:, :, None].broadcast_to([128, 4, 128])), ALU.mult)
                sincos(tb(1), st["sin"](st["sin"].t[:, js, :].rearrange("p a b -> p (a b)")),
                       st["cos"](st["cos"].t[:, js, :].rearrange("p a b -> p (a b)")), TS4)
            S.memset("dve", st["car_re"](), 0.0)
            S.memset("dve", st["car_im"](), 0.0)
            S.act(st["a_neg"](), pvv(L, "ssd_alog"), AF.Exp)
            S.ts("dve", st["a_neg"](), st["a_neg"](), -1.0, ALU.mult)
            S.memset("pool", st["Sst"](), 0.0)
            S.memset("pool", st["prevT"](), 0.0)
            S.memset("pool", st["xhalo"](), 0.0)
        else:
            S.cp("dve", st["wa2"](), bgload(0, "rw_wa2", 0, 512))
            S.cp("dve", st["g2"](), bgload(1, "rw_g2", 0, 512))
            S.cp("dve", st["v2"](), bgload(2, "rw_v2", 0, 512))
            if L // 2 == 0:
                S.memset("dve", st["lb"](), 0.0)
            else:
                S.tt("dve", st["lb"](), pvv(L, "hg_lb_raw1"), pvv(L, "hg_lb_raw0"), ALU.subtract)
                S.act(st["lb"](), st["lb"](), AF.Sigmoid)
            S.ts("dve", st["oml"](), st["lb"](), -1.0, ALU.mult, 1.0, ALU.add)
            S.ts("dve", st["omm"](), pvv(L, "rw_mu"), -1.0, ALU.mult, 1.0, ALU.add)
            S.memset("pool", st["T"](), 0.0)
            S.memset("pool", st["Tbf"](), 0.0)
            S.memset("pool", st["HS"](), 0.0)
            S.memset("pool", st["HSbf"](), 0.0)
            S.memset("pool", st["rwhalo"](), 0.0)
    arena_reset()
    if dbg is not None and dbg[0] == "stop" and dbg[1] == "prologue":
        S.dma("sp", dram(oT[:, :, 0:NT].rearrange("d p t -> p d t")), H(), is_output=True)
        S.finish()
        return nc

    def wload(L, idx):
        slot = wctr[0] % NWS
        wctr[0] += 1
        if os.environ.get("SKIPW") and wctr[0] > NWS:
            return slot
        S.dma("sp", WRING(WRING.t[:, slot, :], slot), View(wsb_d[L][idx], [wsb_regs[L][idx // WCH]]))
        return slot

    def wk(slot, k, m0=0, m1=128, p0=0, p1=128):
        return WRING(WRING.t[p0:p1, slot, k * 128 + m0:k * 128 + m1], slot)

    def interleave(gens):
        gens = list(gens)
        while gens:
            for g in list(gens):
                try:
                    next(g)
                except StopIteration:
                    gens.remove(g)

    psrot = [0]

    def next_bank(nb=4):
        b = psrot[0] % nb
        psrot[0] += 1
        return b

    def norm_stats(src_views, nfeat, ones=None, epsv=None, n=NT):
        m = len(src_views)
        for i, v in enumerate(src_views):
            q = sqc[0] % 2
            sqc[0] += 1
            S.act(SQ(SQ.t[:, q, 0:n], q), v, AF.Square)
            S.mm(ps(7, 0, n), ones if ones is not None else ones_bf, SQ(SQ.t[:, q, 0:n], q), start=(i == 0), stop=(i == m - 1), signal=(i == m - 1))
        r = RSTD(RSTD.t[:, 0:n])
        S.act(r, ps(7, 0, n), AF.Sqrt, bias=epsv if epsv is not None else eps_v, scale=1.0 / nfeat)
        S.recip(r, r)
        return r

    def pre_norm(L, gname):
        r = norm_stats([H(H.t[:, d, :]) for d in range(8)], D)
        for d in range(8):
            S.stt("dve", HN(HN.t[:, d, :]), H(H.t[:, d, :]), pvv(L, gname, d, d + 1), r, ALU.mult, ALU.mult)

    def post_norm_add(L, gname):
        r = norm_stats([TMP8(TMP8.t[:, d, :], d) for d in range(8)], D)
        for d in range(8):
            S.stt("dve", TMP8(TMP8.t[:, d, :], d), TMP8(TMP8.t[:, d, :], d), pvv(L, gname, d, d + 1), r, ALU.mult, ALU.mult)
            S.tt("dve", H(H.t[:, d, :]), H(H.t[:, d, :]), TMP8(TMP8.t[:, d, :], d), ALU.add)

    def inproj(L, widx, nb=4, banks=None):
        slot = wload(L, widx)
        b = next_bank(nb) if banks is None else banks[next_bank(len(banks))]
        for k in range(8):
            S.mm(ps(b, 0, NT), wk(slot, k), HN(HN.t[:, k, :]), start=(k == 0), stop=(k == 7), signal=(k == 7))
        return b

    def proj8(L, widx0, rhs_buf, nk):
        nkg = (nk + 7) // 8
        wi = widx0
        for m in range(8):
            b = next_bank()
            kk = 0
            for kg in range(nkg):
                slot = wload(L, wi)
                wi += 1
                for k in range(min(8, nk - kg * 8)):
                    S.mm(ps(b, 0, NT), wk(slot, k), rhs_buf(rhs_buf.t[:, kk, :], kk), start=(kk == 0), stop=(kk == nk - 1), signal=(kk == nk - 1))
                    kk += 1
            S.cp("act", TMP8(TMP8.t[:, m, :], m), ps(b, 0, NT))
        return wi

    def rolling(factories, width, stagger=0):
        pending = list(factories)
        active = []
        since = stagger
        while pending or active:
            while pending and len(active) < width and (since >= stagger or not active):
                active.append(pending.pop(0)())
                since = 0
            since += 1
            for g in list(active):
                try:
                    next(g)
                except StopIteration:
                    active.remove(g)

    def rolling_groups(groups):
        pend = [list(f) for f, _ in groups]
        act = [[] for _ in groups]
        while any(pend) or any(act):
            for gi, (_, w) in enumerate(groups):
                while pend[gi] and len(act[gi]) < w:
                    act[gi].append(pend[gi].pop(0)())
            for gi in range(len(groups)):
                for g in list(act[gi]):
                    try:
                        next(g)
                    except StopIteration:
                        act[gi].remove(g)

    def ffn(L, widx0):
        PHASE[0] = "ffn"
        arena_reset()
        NST = 4
        ABUF = mk("abuf", [128, NJ, NT], BF16, nreg=NJ)
        FRAW = [mk("fraw", [128, 2, 2 + NT], F32, nreg=2) for i in range(NST)]
        FACC = [mk("facc", [128, 2, NT], F32, nreg=2) for i in range(NST)]
        pre_norm(L, "n_ffn_pre")
        o, _ = pk_layers[L].off["ffn_cw"]
        ob, _ = pk_layers[L].off["ffn_cb"]

        def ffn_j(j):
            sl = j % NST
            fr, fa = FRAW[sl], FACC[sl]
            S.cp("pool", fr(fr.t[:, :, 0:2]), FHALO[L](FHALO[L].t[:, 2 * j:2 * j + 2, :]))
            for s in range(2):
                b = inproj(L, widx0 + 2 * j + s, nb=6)
                ti = 2 * j + s
                cw = lambda kk: PV[L](PV[L].t[:, o + ti * 3 + kk:o + ti * 3 + kk + 1])
                cbv = PV[L](PV[L].t[:, ob + ti:ob + ti + 1])
                acc = fa(fa.t[:, s, :], s)
                S.cp("act", fr(fr.t[:, s, 2:2 + NT], s), ps(b, 0, NT))
                S.act(acc, ps(b, 0, NT), AF.Identity, bias=cbv, scale=cw(2))
                yield
                eng = "dve"
                for kk in (0, 1):
                    if eng == "dve":
                        S.stt(eng, acc, fr(fr.t[:, s, kk:kk + NT], s), cw(kk), acc, ALU.mult, ALU.add)
                    else:
                        S.ts(eng, FTMP[sl](), fr(fr.t[:, s, kk:kk + NT], s), cw(kk), ALU.mult)
                        S.tt(eng, acc, acc, FTMP[sl](), ALU.add)
                    yield
            S.cp("pool", FHALO[L](FHALO[L].t[:, 2 * j:2 * j + 2, :]), fr(fr.t[:, :, NT:NT + 2]))
            S.act(fa(fa.t[:, 0, :], 0), fa(fa.t[:, 0, :], 0), AF.Gelu_apprx_tanh)
            yield
            S.tt("dve", ABUF(ABUF.t[:, j, :], j), fa(fa.t[:, 0, :], 0), fa(fa.t[:, 1, :], 1), ALU.mult)
            yield

        rolling([(lambda j=j: ffn_j(j)) for j in range(NJ)], NST, int(os.environ.get('STG_FFN', 0)))
        widx = proj8(L, widx0 + 2 * NJ, ABUF, NJ)
        post_norm_add(L, "n_ffn_post")
        return widx

    def even_layer(L):
        st = LS[L]
        PHASE[0] = "s5"
        arena_reset()
        pre_norm(L, "n_mix_pre")
        stop("prenorm")
        ZS = mk("zs", [128, 6, NT], F32, nreg=6)
        XA = mk("xa", [128, 10, NT], F32, nreg=10)
        XAB = mk("xab", [128, 4, NT], BF16, nreg=4)
        DTT = mk("dtt", [128, NT], F32)
        RAW = [mk("xraw", [128, 3 + NT], F32) for i in range(2)]
        ssd_snap = (nc.sbuf_base, nc.sbuf_top)
        ocw, _ = pk_layers[L].off["ssd_cw"]
        ocb, _ = pk_layers[L].off["ssd_cb"]

        def ssd_inproj():
            for m in range(2, 19):
                b = inproj(L, m, banks=(4, 5))
                p = ps(b, 0, NT)
                if m < 8:
                    S.act(ZS(ZS.t[:, m - 2, :], m - 2), p, AF.Silu)
                    yield
                elif m < 18:
                    i = m - 8
                    rw = RAW[i % 2]
                    S.cp("pool", rw(rw.t[:, 0:3]), st["xhalo"](st["xhalo"].t[:, i, :]))
                    S.cp("act", rw(rw.t[:, 3:3 + NT]), p)
                    cw = lambda kk: PV[L](PV[L].t[:, ocw + i * 4 + kk:ocw + i * 4 + kk + 1])
                    acc = XA(XA.t[:, i, :], i)
                    S.act(acc, p, AF.Identity, bias=PV[L](PV[L].t[:, ocb + i:ocb + i + 1]), scale=cw(3))
                    yield
                    for kk in range(0, 3):
                        S.stt("dve", acc, rw(rw.t[:, kk:kk + NT]), cw(kk), acc, ALU.mult, ALU.add)
                        yield
                    S.cp("pool", st["xhalo"](st["xhalo"].t[:, i, :]), rw(rw.t[:, NT:NT + 3]))
                    S.act(acc, acc, AF.Silu)
                    if i >= 6:
                        S.cp("act", XAB(XAB.t[:, i - 6, :], i - 6), acc)
                    yield
                else:
                    S.act(DTT(DTT.t[0:12, :]), ps(b, 0, NT, 0, 12), AF.Exp, bias=pvv(L, "ssd_dtb", 0, 1, 0, 12))
                    S.act(DTT(DTT.t[0:12, :]), DTT(DTT.t[0:12, :]), AF.Ln, bias=1.0)
                    yield

        UF = mk("uf", [128, 2, NT], F32)
        UB = mk("ub", [128, 2, NT], BF16)
        NS5 = 2
        S5T = [[mk("s5t", [128, 6, CH], F32, nreg=6) for _ in range(2)] for i in range(NS5)]
        S5X = [mk("s5x", [128, 2, NT], F32, nreg=2) for i in range(NS5)]
        S5Y = mk("s5y", [128, 2, NT], F32)
        S5YB = mk("s5yb", [128, 2, NT], BF16)
        for m in range(2):
            b = inproj(L, m)
            S.cp("act", UF(UF.t[:, m, :]), ps(b, 0, NT))
            S.cp("dve", UB(UB.t[:, m, :]), ps(b, 0, NT))
        stop("inproj")

        def s5_j(j):
            sl = j % NS5
            kt, jj = j // 4, j % 4
            bre, bim = 2 * sl, 2 * sl + 1
            c0 = kt * 512 + jj * 128
            S.mm(ps(bre, 0, NT), st["bb_re"](st["bb_re"].t[:, c0:c0 + 128]), UB(UB.t[:, kt, :]))
            S.mm(ps(bim, 0, NT), st["bb_im"](st["bb_im"].t[:, c0:c0 + 128]), UB(UB.t[:, kt, :]))
            yield
            X = S5X[sl]
            cs, sn = st["cos"](st["cos"].t[:, j, :]), st["sin"](st["sin"].t[:, j, :])
            mb = bc(st["m"](), st["m"].t[:, j:j + 1].to_broadcast([128, CH]))
            for c in range(CPT):
                T = S5T[sl][c % 2]
                pr, pi = ps(bre, c * CH, (c + 1) * CH), ps(bim, c * CH, (c + 1) * CH)
                t = lambda i: T(T.t[:, i, :], i)
                S.tt("dve", t(0), pr, cs, ALU.mult)
                S.tt("dve", t(1), pi, sn, ALU.mult)
                yield
                S.tt("dve", t(0), t(0), t(1), ALU.add)
                S.tt("dve", t(2), pi, cs, ALU.mult)
                yield
                S.tt("dve", t(3), pr, sn, ALU.mult)
                S.tt("dve", t(2), t(2), t(3), ALU.subtract)
                yield
                if c == 0:
                    ir, ii = st["car_re"](st["car_re"].t[:, j:j + 1]), st["car_im"](st["car_im"].t[:, j:j + 1])
                else:
                    ir, ii = X(X.t[:, 0, c * CH - 1:c * CH], 0), X(X.t[:, 1, c * CH - 1:c * CH], 1)
                S.scan(t(4), mb, t(0), ir, ALU.mult, ALU.add)
                yield
                S.scan(t(5), mb, t(2), ii, ALU.mult, ALU.add)
                yield
                xr, xi = X(X.t[:, 0, c * CH:(c + 1) * CH], 0), X(X.t[:, 1, c * CH:(c + 1) * CH], 1)
                S.tt("dve", t(0), t(4), cs, ALU.mult)
                S.tt("dve", t(1), t(5), sn, ALU.mult)
                yield
                S.tt("dve", xr, t(0), t(1), ALU.subtract)
                S.tt("dve", t(2), t(5), cs, ALU.mult)
                yield
                S.tt("dve", t(3), t(4), sn, ALU.mult)
                S.tt("dve", xi, t(2), t(3), ALU.add)
                yield
            S.cp("pool", st["car_re"](st["car_re"].t[:, j:j + 1]), X(X.t[:, 0, NT - 1:NT], 0))
            S.cp("pool", st["car_im"](st["car_im"].t[:, j:j + 1]), X(X.t[:, 1, NT - 1:NT], 1))
            yb = 6 + j // 4
            S.mm(ps(yb, 0, NT), st["c_re"](st["c_re"].t[:, j, :]), X(X.t[:, 0, :], 0), start=(jj == 0), stop=False)
            S.mm(ps(yb, 0, NT), st["c_imn"](st["c_imn"].t[:, j, :]), X(X.t[:, 1, :], 1), start=False, stop=(jj == 3))
            yield

        rolling_groups([([ssd_inproj], 1), ([(lambda j=j: s5_j(j)) for j in range(8)], NS5)])
        stop("s5scan")
        for ot in range(2):
            yv = S5Y(S5Y.t[:, ot, :])
            S.stt("dve", yv, UF(UF.t[:, ot, :]), pvv(L, "s5_d", ot, ot + 1), ps(6 + ot, 0, NT), ALU.mult, ALU.add)
            S.act(yv, yv, AF.Gelu_apprx_tanh)
            S.cp("act", S5YB(S5YB.t[:, ot, :]), yv)
        for ot in range(2):
            b = next_bank()
            for k in range(2):
                S.mm(ps(b, 0, NT), st["wglu"](st["wglu"].t[:, k, ot * 128:(ot + 1) * 128]), S5YB(S5YB.t[:, k, :]), start=(k == 0), stop=(k == 1), signal=(k == 1))
            sg = S5X[0](S5X[0].t[:, ot, :], ot)
            S.act(sg, ps(b, 0, NT), AF.Sigmoid, bias=pvv(L, "s5_bglu", ot, ot + 1))
            S.tt("dve", YMIX(YMIX.t[:, ot, :], ot), S5Y(S5Y.t[:, ot, :]), sg, ALU.mult)
        stop("s5")
        PHASE[0] = "ssd"
        S.barrier()
        nc.sbuf_base, nc.sbuf_top = ssd_snap
        YS = mk("ys", [128, 6, NT], F32)
        XDP = mk("xdp", [128, 12, 128], BF16)
        XDD = mk("xdd", [128, 12, 64], BF16)
        XST = mk("xst", [128, 12, 64], F32)
        BTK = mk("btk", [128, 2, 128], BF16)
        SM = mk("ssdsmall", [128, 8, 12], F32, nreg=8)
        SEG = mk("seg", [128, 12, 128], F32)
        EAR = mk("ear", [128, 12, 128], F32)
        SCT = mk("sct", [128, 12, 128], BF16)
        CEX = mk("cex", [128, 12, 128], BF16)
        CBM = mk("cbm", [128, 2, 128], F32)
        S.memset("pool", XDP(), 0.0)
        stop("ssdproj")
        a_neg = st["a_neg"]()
        le = C("le")
        sm = lambda i: SM(SM.t[:, i, :], i)
        arow_regs = psr(1, 0, 512) + psr(2, 0, 512) + psr(3, 0, 512)
        for c in range(CPT):
            cc = slice(c * CH, (c + 1) * CH)
            for i in range(6):
                bk, off = (4, i * 128) if i < 4 else (5, (i - 4) * 128)
                S.tr(ps(bk, off, off + 128), XA(XA.t[:, i, cc], i), ident)
            for g2 in range(2):
                S.tr(ps(6, g2 * 128, (g2 + 1) * 128), XA(XA.t[:, 6 + g2, cc], 6 + g2), ident)
            S.tr(ps(7, 256, 268), DTT(DTT.t[0:12, cc]), bc(ident, ident.ap[0:12, 0:12]))
            S.cp("act", XST(XST.t[:, 0:8, :].rearrange("p a b -> p (a b)")), ps(4, 0, 512))
            S.cp("act", XST(XST.t[:, 8:12, :].rearrange("p a b -> p (a b)")), ps(5, 0, 256))
            S.cp("act", fl(BTK(), "p a b -> p (a b)"), ps(6, 0, 256))
            dtk, da, acol, aend, dte, w2, edec = [sm(i) for i in range(7)]
            S.cp("dve", dtk, ps(7, 256, 268))
            S.tt("dve", da, dtk, a_neg, ALU.mult)
            S.mm(ps(7, 272, 284), le, da)
            S.cp("act", acol, ps(7, 272, 284))
            S.tt("dve", SEG(), bc(da, da.ap[:, :, None].broadcast_to([128, 12, 128])),
                 bc(le, le.ap[:, None, :].broadcast_to([128, 12, 128])), ALU.mult)
            for q in range(3):
                S.mm(ps(1 + q, 0, 512), ones_f, SEG(SEG.t[:, 4 * q:4 * q + 4, :].rearrange("p a b -> p (a b)")))
            arow = View(PSt[:, 1:4, :].rearrange("p a (b c) -> p (a b) c", c=128), arow_regs, True)
            S.act(EAR(), arow, AF.Exp)
            S.cp("dve", aend, View(PSt[:, 1:4, :].rearrange("p a (b c) -> p (a b) c", c=128)[:, :, 127], arow_regs, True))
            S.tt("dve", SEG(), arow, bc(acol, acol.ap[:, :, None].broadcast_to([128, 12, 128])), ALU.subtract)
            S.act(SEG(), SEG(), AF.Exp)
            for g2 in range(2):
                S.mm(ps(6, 256 + g2 * 128, 256 + (g2 + 1) * 128), XAB(XAB.t[:, g2, cc]), XAB(XAB.t[:, 2 + g2, cc]))
            S.tt("dve", CBM(), View(PSt[:, 6, 256:512].rearrange("p (a b) -> p a b", a=2), psr(6, 256, 512), True),
                 bc(le, le.ap[:, None, :].broadcast_to([128, 2, 128])), ALU.mult)
            for g2 in range(2):
                hs = slice(6 * g2, 6 * g2 + 6)
                S.stt("dve", SCT(SCT.t[:, hs, :]), SEG(SEG.t[:, hs, :]), 1.0, CBM(CBM.t[:, g2:g2 + 1, :].broadcast_to([128, 6, 128])), ALU.min, ALU.mult)
                S.tt("dve", CEX(CEX.t[:, hs, :]), EAR(EAR.t[:, hs, :]),
                     XA(XA.t[:, 8 + g2:9 + g2, cc].broadcast_to([128, 6, 128]), 8 + g2), ALU.mult)
            xdp5 = XDP.t[:].rearrange("p (a two) (h c) -> p a two h c", two=2, h=2)
            xst4 = XST.t[:].rearrange("p (a two) c -> p a two c", two=2)
            dtk4 = dtk.ap.rearrange("p (a two) -> p a two", two=2)
            for par in range(2):
                S.tt("dve", XDP(xdp5[:, :, par, par, :]), XST(xst4[:, :, par, :]),
                     bc(dtk, dtk4[:, :, par:par + 1].broadcast_to([128, 6, 64])), ALU.mult)
            for pj in range(6):
                yo = ps(0, pj * 128, (pj + 1) * 128) if pj < 4 else ps(7, (pj - 4) * 128, (pj - 3) * 128)
                for hh in range(2):
                    hI = 2 * pj + hh
                    S.mm(yo, XDP(XDP.t[:, hI, :]), SCT(SCT.t[:, hI, :]), start=(hh == 0), stop=False, signal=False)
                for hh in range(2):
                    hI = 2 * pj + hh
                    S.mm(yo, st["prevT"](st["prevT"].t[:, hI, :]), CEX(CEX.t[:, hI, :]), start=False, stop=(hh == 1), signal=(hh == 1))
                S.stt("dve", YS(YS.t[:, pj, cc]), XA(XA.t[:, pj, cc], pj), pvv(L, "ssd_d", pj, pj + 1), yo, ALU.mult, ALU.add)
            S.tt("dve", dte, aend, acol, ALU.subtract)
            S.act(dte, dte, AF.Exp)
            S.tt("dve", w2, dtk, dte, ALU.mult)
            S.act(edec, aend, AF.Exp)
            S.tt("dve", XDD(), XST(), bc(w2, w2.ap[:, :, None].broadcast_to([128, 12, 64])), ALU.mult)
            for g2 in range(2):
                S.mm(ps(4 + g2, 0, 384), BTK(BTK.t[:, g2, :]), XDD(XDD.t[:, 6 * g2:6 * g2 + 6, :].rearrange("p a b -> p (a b)")))
            Sst = st["Sst"]
            S.tt("dve", Sst(), Sst(), bc(edec, edec.ap[:, :, None].broadcast_to([128, 12, 64])), ALU.mult)
            for g2 in range(2):
                S.tt("dve", Sst(Sst.t[:, 6 * g2:6 * g2 + 6, :]), Sst(Sst.t[:, 6 * g2:6 * g2 + 6, :]),
                     View(PSt[:, 4 + g2, 0:384].rearrange("p (a b) -> p a b", a=6), psr(4 + g2, 0, 384), True), ALU.add)
            pt5 = st["prevT"].t[:].rearrange("p (a two) (h c) -> p a two h c", two=2, h=2)
            ss4 = Sst.t[:].rearrange("p (a two) c -> p a two c", two=2)
            for par in range(2):
                S.cp("act", st["prevT"](pt5[:, :, par, par, :]), Sst(ss4[:, :, par, :]))
        S.tt("dve", YS(), YS(), ZS(), ALU.mult)
        for g2 in range(2):
            r = norm_stats([YS(YS.t[:, 3 * g2 + i, :]) for i in range(3)], 384)
            for i in range(3):
                S.stt("dve", YMIX(YMIX.t[:, 2 + 3 * g2 + i, :], 2 + 3 * g2 + i), YS(YS.t[:, 3 * g2 + i, :]),
                      pvv(L, "ssd_norm", 3 * g2 + i, 3 * g2 + i + 1), r, ALU.mult, ALU.mult)
        return 19

    C0 = math.exp(-0.5)

    def odd_layer(L):
        st = LS[L]
        o = L // 2
        PHASE[0] = "r0"
        arena_reset()
        pre_norm(L, "n_mix_pre")
        widx = 0
        RH = mk("rh", [128, 4, NT], BF16, nreg=4)
        KH = mk("kh", [128, 4, NT], BF16, nreg=4)
        KHH = mk("khh", [128, 4, NT], BF16, nreg=4)
        BH = mk("bh", [128, 4, NT], BF16, nreg=4)
        VB = mk("vb", [128, 4, NT], BF16, nreg=4)
        BV = mk("bv", [128, 4, NT], F32, nreg=4)
        SG = mk("sg", [128, NT], BF16)
        GL = mk("gl", [128, 4, CPT], F32, nreg=4)
        r1_snap = (nc.sbuf_base, nc.sbuf_top)
        TW = mk("tw", [128, NT], BF16)
        PVb = mk("pvb16", [128, NT], BF16)
        NR0 = 2
        R0B = []
        for sl in range(NR0):
            d = {"RAW": mk("rraw", [128, 1 + NT], F32), "RK": mk("rk", [128, NT], BF16)}
            for nm in ("Dt", "Rr", "Kk", "Vv", "Aa", "SGW", "KAP", "KT_", "E_", "LGS") + (("VS",) if o > 0 else ()):
                d[nm] = mk("r0" + nm, [128, NT], F32)
            R0B.append(d)
        nhead = 3 if o > 0 else 2

        def shifted(widx_, idx, dst, Bf):
            b = inproj(L, widx_)
            rw, Dt = Bf["RAW"], Bf["Dt"]
            S.cp("pool", rw(rw.t[:, 0:1]), st["rwhalo"](st["rwhalo"].t[:, idx:idx + 1]))
            S.cp("act", rw(rw.t[:, 1:1 + NT]), ps(b, 0, NT))
            S.act(Dt(), ps(b, 0, NT), AF.Identity, scale=st["omm"](st["omm"].t[:, idx:idx + 1]))
            S.stt("dve", dst, rw(rw.t[:, 0:NT]), pvv(L, "rw_mu", idx, idx + 1), Dt(), ALU.mult, ALU.add)
            S.cp("pool", st["rwhalo"](st["rwhalo"].t[:, idx:idx + 1]), rw(rw.t[:, NT:NT + 1]))

        B0 = R0B[0]
        shifted(0, 12, B0["Rr"](), B0)
        S.act(TW(TW.t[0:64, :]), B0["Rr"](B0["Rr"].t[0:64, :]), AF.Tanh)
        S.cp("dve", TW(TW.t[64:128, :]), B0["Rr"](B0["Rr"].t[64:128, :]))
        shifted(1, 13, B0["Kk"](), B0)
        S.act(SG(), B0["Kk"](), AF.Sigmoid)
        if o > 0:
            shifted(2, 14, B0["Vv"](), B0)
            S.cp("dve", PVb(PVb.t[0:32, :]), B0["Vv"](B0["Vv"].t[0:32, :]))

        def r0_i(i):
            Bf = R0B[i % NR0]
            Rr, Kk, Vv, Aa, SGW, KAP, KT_, E_, LGS, Dt, RK = [Bf[n] for n in ("Rr", "Kk", "Vv", "Aa", "SGW", "KAP", "KT_", "E_", "LGS", "Dt", "RK")]
            ic = slice(i * 128, (i + 1) * 128)
            wb = nhead + 3 * i
            shifted(wb, i, Rr(), Bf)
            yield
            shifted(wb + 1, 4 + i, Kk(), Bf)
            yield
            shifted(wb + 2, 8 + i, Vv(), Bf)
            yield
            bw, ba = next_bank(), next_bank()
            S.mm(ps(bw, 0, NT), st["wa2"](st["wa2"].t[0:64, ic]), TW(TW.t[0:64, :]))
            S.mm(ps(ba, 0, NT), st["wa2"](st["wa2"].t[64:128, ic]), TW(TW.t[64:128, :]))
            S.act(SGW(), ps(bw, 0, NT), AF.Sigmoid, bias=pvv(L, "rw_w0", i, i + 1))
            S.act(Aa(), ps(ba, 0, NT), AF.Sigmoid, bias=pvv(L, "rw_a0", i, i + 1))
            yield
            if o > 0:
                VS = Bf["VS"]
                bv_ = next_bank()
                S.mm(ps(bv_, 0, NT), st["v2"](st["v2"].t[0:32, ic]), PVb(PVb.t[0:32, :]))
                S.act(VS(), ps(bv_, 0, NT), AF.Sigmoid, bias=pvv(L, "rw_v0", i, i + 1))
                S.tt("dve", Dt(), VFIRST(VFIRST.t[:, i, :], i), Vv(), ALU.subtract)
                yield
                S.tt("dve", Dt(), Dt(), VS(), ALU.mult)
                S.tt("dve", Vv(), Vv(), Dt(), ALU.add)
            else:
                S.cp("act", VFIRST(VFIRST.t[:, i, :], i), Vv())
            S.cp("act", VB(VB.t[:, i, :], i), Vv())
            yield
            S.ts("dve", KAP(), Kk(), pvv(L, "rw_k_k", i, i + 1), ALU.mult)
            q = sqc[0] % 2
            sqc[0] += 1
            S.act(SQ(SQ.t[:, q, :], q), KAP(), AF.Square)
            S.mm(ps(7, 0, NT), blk64_bf, SQ(SQ.t[:, q, :], q))
            S.act(E_(), ps(7, 0, NT), AF.Sqrt)
            yield
            S.ts("dve", E_(), E_(), 1e-12, ALU.max)
            S.recip(E_(), E_())
            S.tt("dve", KAP(), KAP(), E_(), ALU.mult)
            yield
            S.ts("dve", KT_(), Aa(), -1.0, ALU.add, pvv(L, "rw_k_a", i, i + 1), ALU.mult)
            S.stt("dve", KT_(), KT_(), 1.0, Kk(), ALU.add, ALU.mult)
            S.tt("dve", Aa(), KAP(), Aa(), ALU.mult)
            yield
            S.stt("dve", RK(), Rr(), pvv(L, "rw_r_k", i, i + 1), KT_(), ALU.mult, ALU.mult)
            bb_ = next_bank()
            S.mm(ps(bb_, 0, NT), blk64_bf, RK())
            S.tt("dve", BV(BV.t[:, i, :], i), ps(bb_, 0, NT), Vv(), ALU.mult)
            yield
            for c in range(CPT):
                cc = slice(c * CH, (c + 1) * CH)
                S.scan(LGS(LGS.t[:, cc]), ones_f, SGW(SGW.t[:, cc]), 0.0, ALU.mult, ALU.add)
            yield
            S.act(E_(), LGS(), AF.Exp, scale=-C0)
            S.tt("dve", RH(RH.t[:, i, :], i), Rr(), E_(), ALU.mult)
            S.cp("pool", GL(GL.t[:, i, :], i), bc(E_(), E_.t[:, :].rearrange("p (c t) -> p c t", c=CPT)[:, :, CH - 1]))
            S.tt("dve", Dt(), LGS(), SGW(), ALU.subtract)
            yield
            S.act(Dt(), Dt(), AF.Exp, scale=-C0)
            S.tt("dve", KH(KH.t[:, i, :], i), KAP(), Dt(), ALU.mult)
            yield
            S.act(E_(), LGS(), AF.Exp, scale=C0)
            S.tt("dve", KHH(KHH.t[:, i, :], i), KT_(), E_(), ALU.mult)
            S.tt("dve", BH(BH.t[:, i, :], i), Aa(), E_(), ALU.mult)
            yield

        rolling([(lambda i=i: r0_i(i)) for i in range(4)], NR0, int(os.environ.get('STG_R0', 0)))
        widx = nhead + 12
        stop("r0")
        PHASE[0] = "r1"
        S.barrier()
        nc.sbuf_base, nc.sbuf_top = r1_snap
        G = mk("g", [128, 4, NT], F32, nreg=4)
        RB = []
        for sl in range(2):
            d = {}
            for nm in ("VT", "KTK", "NBT", "UT"):
                d[nm] = mk(nm, [128, 128], BF16)
            for hh in range(2):
                for nm in ("NM", "NMT", "PP0", "PP1", "PPT0", "PPT1", "YY0", "YY1"):
                    d[nm + str(hh)] = mk(nm, [128, 128], F32)
                for nm in ("ARK", "ARB", "AKK"):
                    d[nm + str(hh)] = mk(nm, [128, 128], BF16)
                d["RR" + str(hh)] = mk("rr", [128, 64], F32)
            RB.append(d)
        OS = mk("os", [128, 8, 64], F32)
        OQ = mk("oq", [128, 8, 64], F32)
        ST8 = mk("st8", [128, 4, 8], F32, nreg=4)
        OF = [mk("of", [128, CH], F32) for _ in range(2)]
        for i in range(4):
            b = next_bank()
            S.mm(ps(b, 0, NT), st["g2"](st["g2"].t[:, i * 128:(i + 1) * 128]), SG())
            S.cp("act", G(G.t[:, i, :], i), ps(b, 0, NT))
        lt, le, nle, nlt, ngt = C("lt"), C("le"), C("nle"), C("nlt"), C("ngt")
        T, Tbf = st["T"], st["Tbf"]

        def r1_pair(c, i, sl):
            cc = slice(c * CH, (c + 1) * CH)
            Bf = RB[sl]
            hbank = (1, 2) if sl == 0 else (5, 6)
            cbank = (5, 6) if os.environ.get("CBK") else hbank
            VT, KTK, NBT, UT = Bf["VT"], Bf["KTK"], Bf["NBT"], Bf["UT"]
            S.mm(ps(0, 0, 128), VB(VB.t[:, i, cc], i), ident_bf)
            S.mm(ps(0, 128, 256), KHH(KHH.t[:, i, cc], i), ident_bf)
            S.mm(ps(0, 256, 384), BH(BH.t[:, i, cc], i), ident_bf)
            S.cp("act", VT(), ps(0, 0, 128))
            S.cp("act", KTK(), ps(0, 128, 256))
            S.ts("dve", NBT(), ps(0, 256, 384), -1.0, ALU.mult)
            yield
            hv = []
            for hh in range(2):
                pb = 64 * hh
                hv.append((pb, RH(RH.t[pb:pb + 64, i, cc], i), KH(KH.t[pb:pb + 64, i, cc], i), KHH(KHH.t[pb:pb + 64, i, cc], i),
                           BH(BH.t[pb:pb + 64, i, cc], i), Tbf(Tbf.t[pb:pb + 64, i, :], i)))
            for hh in range(2):
                pb, rh, kh, khh, bh, t0v = hv[hh]
                bN = hbank[hh]
                qa = (2 * sl + hh) * 128
                S.mm(ps(bN, 0, 128), bh, kh)
                S.mm(ps(bN, 128, 256), kh, bh)
                S.mm(ps(bN, 256, 384), khh, rh)
                S.mm(ps(bN, 384, 512), bh, rh)
                S.mm(ps(3, qa, qa + 128), khh, kh)
                yield
            for hh in range(2):
                bN = hbank[hh]
                qa = (2 * sl + hh) * 128
                h_ = str(hh)
                S.tt("dve", Bf["NM" + h_](), ps(bN, 0, 128), nlt, ALU.mult)
                S.tt("dve", Bf["NMT" + h_](), ps(bN, 128, 256), ngt, ALU.mult)
                S.tt("dve", Bf["ARK" + h_](), ps(bN, 256, 384), le, ALU.mult)
                S.tt("dve", Bf["ARB" + h_](), ps(bN, 384, 512), nle, ALU.mult)
                S.tt("dve", Bf["AKK" + h_](), ps(3, qa, qa + 128), lt, ALU.mult)
                S.tt("dve", Bf["YY0" + h_](), Bf["NM" + h_](), ident, ALU.add)
                yield
            for hh in range(2):
                pb, rh, kh, khh, bh, t0v = hv[hh]
                h_ = str(hh)
                qr = (2 * sl + hh) * 64
                RBK = int(os.environ.get("RBK", 7))
                if RBK == 3:
                    qr = 256 + hh * 64
                S.mm(ps(RBK, qr, qr + 64), kh, t0v, start=True, stop=False)
                S.mm(ps(RBK, qr, qr + 64), Bf["AKK" + h_](), VT(VT.t[:, pb:pb + 64]), start=False, stop=True)
                S.cp("act", Bf["RR" + h_](), ps(RBK, qr, qr + 64))
                yield
            cur = [(Bf["NM0"], Bf["NMT0"]), (Bf["NM1"], Bf["NMT1"])]
            for lvl in range(1, 7):
                a = str(lvl % 2)
                for hh in range(2):
                    Pm, PTm = cur[hh]
                    bC = cbank[hh]
                    h_ = str(hh)
                    if lvl < 6:
                        S.mm(ps(bC, 0, 128), PTm(), Pm())
                    S.mm(ps(bC, 128, 256), Pm(), PTm())
                    if lvl < 6:
                        S.cp("act", Bf["PP" + a + h_](), ps(bC, 0, 128))
                    S.cp("act", Bf["PPT" + a + h_](), ps(bC, 128, 256))
                    yield
                for hh in range(2):
                    bC = cbank[hh]
                    h_ = str(hh)
                    yprev, ynew = Bf["YY" + str((lvl - 1) % 2) + h_], Bf["YY" + a + h_]
                    S.mm(ps(bC, 256, 384), Bf["PPT" + a + h_](), yprev())
                    S.tt("dve", ynew(), yprev(), ps(bC, 256, 384), ALU.add)
                    cur[hh] = (Bf["PP" + a + h_], Bf["PPT" + a + h_])
                    yield
            for hh in range(2):
                pb, rh, kh, khh, bh, t0v = hv[hh]
                h_ = str(hh)
                qu = 256 + (2 * sl + hh) * 64
                RBK = int(os.environ.get("RBK", 7))
                if RBK == 3:
                    qu = 384 + hh * 64
                S.mm(ps(RBK, qu, qu + 64), Bf["YY0" + h_](), Bf["RR" + h_]())
                S.cp("act", UT(UT.t[:, pb:pb + 64]), ps(RBK, qu, qu + 64))
                yield
            for hh in range(2):
                pb, rh, kh, khh, bh, t0v = hv[hh]
                h_ = str(hh)
                oc = (2 * i + hh) * 64
                ov = ps(4, oc, oc + 64)
                S.mm(ov, rh, t0v, start=True, stop=False)
                S.mm(ov, Bf["ARK" + h_](), VT(VT.t[:, pb:pb + 64]), start=False, stop=False)
                S.mm(ov, Bf["ARB" + h_](), UT(UT.t[:, pb:pb + 64]), start=False, stop=True)
                yield
            S.mm(ps(0, 384, 512), KTK(), VT(), start=True, stop=False)
            S.mm(ps(0, 384, 512), NBT(), UT(), start=False, stop=True)
            for hh in range(2):
                pb = 64 * hh
                tv = T(T.t[pb:pb + 64, i, :], i)
                S.tt("dve", tv, tv, ps(0, 384 + pb, 448 + pb, pb, pb + 64), ALU.add)
                S.ts("dve", tv, tv, GL(GL.t[pb:pb + 64, i, c:c + 1], i), ALU.mult)
                S.cp("act", Tbf(Tbf.t[pb:pb + 64, i, :], i), tv)
            if os.environ.get("SHOWB"):
                print("PAIR END", c, i, S.nops)
            yield

        def r1_tail(c):
            cc = slice(c * CH, (c + 1) * CH)
            osf = fl(OS(), "p a b -> p (a b)")
            S.cp("act", osf, ps(4, 0, 512))
            S.act(fl(OQ(), "p a b -> p (a b)"), ps(4, 0, 512), AF.Square)
            yield
            s1, s2, s3, s4 = [ST8(ST8.t[:, k, :], k) for k in range(4)]
            S.op("dve", lambda e: e.tensor_reduce(out=s1.ap, in_=OS.t[:], axis=AX.X, op=ALU.add), [s1], [OS()])
            S.op("dve", lambda e: e.tensor_reduce(out=s2.ap, in_=OQ.t[:], axis=AX.X, op=ALU.add), [s2], [OQ()])
            S.ts("dve", s1, s1, 1.0 / 64, ALU.mult)
            S.tt("dve", s3, s1, s1, ALU.mult)
            S.stt("dve", s2, s2, 1.0 / 64, s3, ALU.mult, ALU.subtract)
            S.act(s2, s2, AF.Sqrt, bias=gneps_v)
            S.recip(s2, s2)
            yield
            S.tt("dve", OS(), OS(), bc(s1, s1.ap[:, :, None].broadcast_to([128, 8, 64])), ALU.subtract)
            S.tt("dve", OS(), OS(), bc(s2, s2.ap[:, :, None].broadcast_to([128, 8, 64])), ALU.mult)
            yield
            for i in range(4):
                bt = 5 + i % 2
                S.tr(ps(bt, 384, 512), bc(OS(), osf.ap[:, i * 128:(i + 1) * 128]), ident)
                ofv = OF[i % 2]()
                S.ts("dve", ofv, ps(bt, 384, 512), pvv(L, "rw_ln_w", i, i + 1), ALU.mult, pvv(L, "rw_ln_b", i, i + 1), ALU.add)
                S.tt("dve", ofv, ofv, BV(BV.t[:, i, cc], i), ALU.add)
                S.tt("dve", YMIX(YMIX.t[:, i, cc], i), ofv, G(G.t[:, i, cc], i), ALU.mult)
                if os.environ.get("SHOWB"):
                    print("TAIL", c, i, S.nops)
                yield

        pend = None
        for c in range(CPT):
            if os.environ.get("NOILV"):
                for i in range(4):
                    interleave([r1_pair(c, i, (i % 2) if os.environ.get("NOILV") == "2" else 0)])
                interleave([r1_tail(c)])
                continue
            gens = [r1_pair(c, 0, 0), r1_pair(c, 1, 1)]
            if pend is not None:
                gens.append(pend)
            interleave(gens)
            interleave([r1_pair(c, 2, 0), r1_pair(c, 3, 1)])
            pend = r1_tail(c)
        if pend is not None:
            interleave([pend])
        stop("r1")
        PHASE[0] = "hgrn"
        arena_reset()
        le64 = View(C("le64").ap.bitcast(U32), CST.regs)
        HS, HSbf = st["HS"], st["HSbf"]
        hb = [dict() for _ in range(2)]
        for k in range(2):
            for nm in ("Q", "LF", "K1", "I", "OG", "GC", "NG", "EC", "EX", "KD", "O"):
                hb[k][nm] = mk("hg" + nm, [128, NT], F32)
            for nm in ("QT", "KT", "QG"):
                hb[k][nm] = mk("hg" + nm, [128, NT], BF16)
            hb[k]["ITK"] = mk("hgitk", [128, 128], BF16)
            hb[k]["KDT"] = mk("hgkdt", [128, 128], BF16)
            hb[k]["ATM"] = mk("hgatm", [128, 128], BF16)
            S.memset("pool", hb[k]["ATM"](), 0.0)
        hg_w0 = widx

        def hg_head(hd):
            sl = hd % 2
            Bf = hb[sl]
            Q, LF, K1, I_, OG, GC, NG, EC, EX, KD, O_ = [Bf[n] for n in ("Q", "LF", "K1", "I", "OG", "GC", "NG", "EC", "EX", "KD", "O")]
            QT, KT, QG, ITK, KDT, ATM = [Bf[n] for n in ("QT", "KT", "QG", "ITK", "KDT", "ATM")]
            bt, ba, bo = 3 * sl, 3 * sl + 1, 3 * sl + 2
            w0 = hg_w0 + 4 * hd
            b = inproj(L, w0, banks=(6, 7))
            S.act(Q(), ps(b, 0, NT), AF.Silu)
            yield
            b = inproj(L, w0 + 1, banks=(6, 7))
            S.act(LF(), ps(b, 0, NT), AF.Sigmoid)
            S.act(LF(), LF(), AF.Identity, scale=st["oml"](st["oml"].t[:, hd:hd + 1]), bias=st["lb"](st["lb"].t[:, hd:hd + 1]))
            yield
            S.ts("dve", K1(), LF(), -1.0, ALU.mult, 1.0, ALU.add)
            S.act(LF(), LF(), AF.Ln)
            yield
            b = inproj(L, w0 + 2, banks=(6, 7))
            S.cp("act", I_(), ps(b, 0, NT))
            yield
            b = inproj(L, w0 + 3, banks=(6, 7))
            S.act(OG(), ps(b, 0, NT), AF.Silu)
            yield
            for q in range(NQ):
                cq = slice(q * 64, (q + 1) * 64)
                S.scan(GC(GC.t[:, cq]), bc(ones_f, ones_f.ap[:, 0:64]), LF(LF.t[:, cq]), 0.0, ALU.mult, ALU.add)
            yield
            gc3 = GC.t[:].rearrange("p (q t) -> p q t", t=64)
            S.act(EC(), GC(), AF.Exp)
            S.tt("dve", View(NG.t[:].rearrange("p (q t) -> p q t", t=64), NG.regs), View(gc3, GC.regs),
                 View(gc3[:, :, 31:32].broadcast_to([128, NQ, 64]), GC.regs), ALU.subtract)
            S.tt("dve", View(EX.t[:].rearrange("p (q t) -> p q t", t=64), EX.regs), View(gc3, GC.regs),
                 View(gc3[:, :, 63:64].broadcast_to([128, NQ, 64]), GC.regs), ALU.subtract)
            yield
            S.tt("dve", QG(), Q(), EC(), ALU.mult)
            S.act(KD(), NG(), AF.Exp)
            S.tt("dve", QT(), Q(), KD(), ALU.mult)
            yield
            S.act(NG(), NG(), AF.Exp, scale=-1.0)
            S.tt("dve", KT(), K1(), NG(), ALU.mult)
            yield
            S.act(EX(), EX(), AF.Exp, scale=-1.0)
            S.tt("dve", KD(), K1(), EX(), ALU.mult)
            yield
            for blk in range(CPT):
                cb_ = slice(blk * 128, (blk + 1) * 128)
                S.tr(ps(bt, 0, 128), I_(I_.t[:, cb_]), ident)
                S.tr(ps(bt, 128, 256), KD(KD.t[:, cb_]), ident)
                S.cp("act", ITK(), ps(bt, 0, 128))
                S.cp("act", KDT(), ps(bt, 128, 256))
                yield
                S.mm(ps(ba, 0, 128), KT(KT.t[:, cb_]), QT(QT.t[:, cb_]))
                S.cpred(ATM(), le64, ps(ba, 0, 128))
                yield
                S.mm(ps(bo, 0, 128), ITK(), ATM(), start=True, stop=False)
                for qq in range(2):
                    q = 2 * blk + qq
                    cq = slice(q * 64, (q + 1) * 64)
                    end = q * 64 + 63
                    S.mm(ps(bo, qq * 64, qq * 64 + 64), HSbf(HSbf.t[:, hd, :], hd), QG(QG.t[:, cq]), start=False, stop=(qq == 1))
                    S.mm(ps(bt, 256, 384), KDT(KDT.t[qq * 64:qq * 64 + 64, :]), ITK(ITK.t[qq * 64:qq * 64 + 64, :]))
                    hs = HS(HS.t[:, hd, :], hd)
                    S.stt("dve", hs, hs, EC(EC.t[:, end:end + 1]), ps(bt, 256, 384), ALU.mult, ALU.add)
                    S.cp("act", HSbf(HSbf.t[:, hd, :], hd), hs)
                    yield
                S.cp("act", O_(O_.t[:, cb_]), ps(bo, 0, 128))
                yield
            r = norm_stats([O_()], 128)
            S.stt("dve", O_(), O_(), pvv(L, "hg_norm", hd, hd + 1), r, ALU.mult, ALU.mult)
            S.tt("dve", YMIX(YMIX.t[:, 4 + hd, :], 4 + hd), O_(), OG(), ALU.mult)
            yield

        rolling([(lambda hd=hd: hg_head(hd)) for hd in range(4)], 2, int(os.environ.get('STG_HG', 0)))
        widx = hg_w0 + 16
        return widx

    def mix_out(L, widx):
        PHASE[0] = "mixout"
        widx = proj8(L, widx, YMIX, 8)
        post_norm_add(L, "n_mix_post")
        return widx

    try:
      for ti in range(n_tiles):
        t0 = ti * NT
        S.dma("sp", H(), dram(xT[:, :, t0:t0 + NT].rearrange("d p t -> p d t")))
        for L in range(n_layers):
            stop("start")
            widx = even_layer(L) if L % 2 == 0 else odd_layer(L)
            stop("mixer")
            if dbg is not None and dbg[0] == "ymix" and dbg[1] == L and ti == 0:
                for d in range(8):
                    S.cp("dve", TMP8(TMP8.t[:, d, :], d), YMIX(YMIX.t[:, d, :], d))
                S.dma("sp", dram(dbg_d.rearrange("d p t -> p d t")), TMP8(), is_output=True)
            widx = mix_out(L, widx)
            stop("mixout")
            widx = ffn(L, widx)
            assert widx == n_ws[L], (widx, n_ws[L])
        S.dma("sp", dram(oT[:, :, t0:t0 + NT].rearrange("d p t -> p d t")), H(), is_output=True)
    except StopBuild:
        S.dma("sp", dram(oT[:, :, 0:NT].rearrange("d p t -> p d t")), H(), is_output=True)
    S.finish()
    print("program: ninst=%d persist_bytes/partition=%d" % (S.ninst, persist_bytes))
    if COST is not None:
        phases = sorted(set(k[0] for k in COST))
        for ph in phases:
            print("COST %-7s" % ph, " ".join("%s=%.0fus(%d)" % (e, COST.get((ph, e), 0), COST.get((ph, "n_" + e), 0)) for e in ("pe", "dve", "act", "pool")))
    return nc


def host_prep(inputs):
    pkc = consts_host()
    pls, pbs, wss = [], [], []
    for L in range(DEPTH):
        P, B, ws = layer_host(inputs, L)
        pls.append(P)
        pbs.append(B)
        wss.append(ws)
    return pkc, pls, pbs, wss


def make_xT(inputs):
    x = np.asarray(inputs["x"], dtype=np.float32)
    meta = np.asarray(inputs["meta"], dtype=np.float32)
    xs = []
    for b in range(NB):
        full = np.zeros((TPAD, D), np.float32)
        full[:NMETA] = meta
        full[NMETA:TREAL] = x[b]
        xs.append(np.ascontiguousarray(full.T).reshape(8, 128, TPAD))
    return xs


def make_shared(pkc, pls, pbs, wss):
    shared = {"consts": pkc.pack()}
    for L in range(DEPTH):
        shared["pv%d" % L] = pls[L].pack()
        shared["pb%d" % L] = pbs[L].pack()
        shared["ws%d" % L] = wss[L]
    return shared


def kernel(**inputs):
    pkc, pls, pbs, wss = host_prep(inputs)
    nc = build_program(pkc, pls, pbs, [w.shape[0] for w in wss])
    xs = make_xT(inputs)
    shared = make_shared(pkc, pls, pbs, wss)
    in_maps = [dict(shared, xT=xs[b]) for b in range(NB)]
    res = run_bass_kernel_spmd(nc, in_maps, core_ids=list(range(NB)))
    out = np.empty((NB, SEQ, D), np.float32)
    for b in range(NB):
        o = np.asarray(res.results[b]["oT"]).reshape(D, TPAD)
        out[b] = o[:, NMETA:TREAL].T
    return out
```

```python
import math
import numpy as np
import concourse.bass as bass
import concourse.mybir as mybir
from concourse.bass_utils import run_bass_kernel_spmd

F32 = mybir.dt.float32
BF16 = mybir.dt.bfloat16
I32 = mybir.dt.int32
U32 = mybir.dt.uint32
AF = mybir.ActivationFunctionType
ALU = mybir.AluOpType
AX = mybir.AxisListType

D = 1024
NB = 8
SEQ = 4096
NMETA = 16
TREAL = SEQ + NMETA
CH = 128
CPT = 3
NQ = 2 * CPT
NT = CH * CPT
NTILES = (TREAL + NT - 1) // NT
TPAD = NTILES * NT
DEPTH = 4
DFF = 2816
NJ = DFF // 128
EPS = 1e-6
GN_EPS = 64e-5
import os
SAME_SYNC = True
COST = {} if os.environ.get('COST') else None
PHASE = ['pro']
NOSYNC_ENG = tuple(os.environ.get('NOSYNC', '').split(',')) if os.environ.get('NOSYNC') else ()


class StopBuild(Exception):
    pass


class Reg:
    __slots__ = ("w", "r")

    def __init__(self):
        self.w = None
        self.r = {}


class View:
    __slots__ = ("ap", "regs", "excl")

    def __init__(self, ap, regs, excl=False):
        self.ap = ap
        self.regs = regs
        self.excl = excl


class Buf:
    def __init__(self, S, name, shape, dtype, nreg=1, space="sbuf"):
        nc = S.nc
        if space == "sbuf":
            self.t = nc.alloc_sbuf_tensor(name, list(shape), dtype, align_bytes=64)
            S.sbuf_bytes += int(np.prod(shape[1:])) * (2 if dtype == BF16 else 4)
        else:
            self.t = nc.alloc_psum_tensor(name, list(shape), dtype)
        self.regs = [Reg() for _ in range(nreg)]

    def __call__(self, ap=None, r=None):
        if ap is None:
            ap = self.t[:]
        if r is None:
            regs = self.regs
        elif isinstance(r, int):
            regs = [self.regs[r]]
        else:
            regs = [self.regs[i] for i in r]
        return View(ap, regs)


def dram(ap):
    return View(ap, [])


class Sched:
    def __init__(self, nc):
        self.nc = nc
        self.E = {"pe": nc.tensor, "dve": nc.vector, "act": nc.scalar, "pool": nc.gpsimd, "sp": nc.sync}
        self.sem = {k: nc.alloc_semaphore("sem_" + k) for k in ("pe", "dve", "act", "pool")}
        self.cnt = {k: 0 for k in self.sem}
        self.seen = {k: {} for k in self.E}
        self.dq = {}
        self.sbuf_bytes = 0
        self.ninst = 0
        self.out_toks = []
        self.nops = 0
        self.max_ops = None

    def _deps(self, outs, ins):
        deps = {}

        def need(tok):
            if tok is None:
                return
            cur = deps.get(tok[0])
            if cur is None or cur[1] < tok[1]:
                deps[tok[0]] = tok

        for v in ins:
            for rg in v.regs:
                need(rg.w)
        for v in outs:
            for rg in v.regs:
                need(rg.w)
                for tok in rg.r.values():
                    need(tok)
        return deps

    def _wait(self, eng, deps):
        E = self.E[eng]
        own = self.sem[eng].num if eng in self.sem else None
        for sid, tok in deps.items():
            if sid == own and (eng == "pe" or not SAME_SYNC or eng in NOSYNC_ENG):
                continue
            if self.seen[eng].get(sid, 0) < tok[1]:
                if self.max_ops is not None and self.nops >= self.max_ops - int(os.environ.get('SHOWN', 3)):
                    print("  WAIT", eng, "on", tok[2].name, tok[1], "cnts", self.cnt)
                E.wait_ge(tok[2], tok[1])
                self.seen[eng][sid] = tok[1]
                self.ninst += 1

    def _mark(self, tok, outs, ins):
        for v in ins:
            for rg in v.regs:
                cur = rg.r.get(tok[0])
                if cur is None or cur[1] < tok[1]:
                    rg.r[tok[0]] = tok
        for v in outs:
            for rg in v.regs:
                rg.w = tok
                rg.r = {}

    def op(self, eng, fn, outs, ins, signal=True):
        xs = [v for v in ins if v.excl]
        if xs:
            outs = list(outs) + xs
            ins = [v for v in ins if not v.excl]
        self._wait(eng, self._deps(outs, ins))
        if COST is not None:
            try:
                shp = outs[0].ap.shape
                n = 1
                for d_ in shp[1:]:
                    n *= int(d_)
            except Exception:
                n = 128
            if eng == "pe":
                f32 = "float32" in str(ins[0].ap.dtype)
                c = n * (4 if f32 else 1) / 2400.0 + 0.03
            elif eng == "dve":
                c = max(64, n) / 960.0 + 0.06
            elif eng == "act":
                c = max(64, n) / 1400.0 + 0.2
            else:
                c = max(64, n) / 200.0 + 0.1
            key = (PHASE[0], eng)
            COST[key] = COST.get(key, 0.0) + c
            COST[(PHASE[0], "n_" + eng)] = COST.get((PHASE[0], "n_" + eng), 0) + 1
        inst = fn(self.E[eng])
        if self.max_ops is not None and self.nops >= self.max_ops - int(os.environ.get('SHOWN', 3)):
            try:
                print("  INST", eng, inst.concise())
            except Exception as ex:
                print("  INST?", ex, inst.ins)
        self.ninst += 1
        sem = self.sem[eng]
        if signal:
            self.cnt[eng] += 1
            inst.then_inc(sem, 1)
            tok = (sem.num, self.cnt[eng], sem)
        else:
            tok = (sem.num, self.cnt[eng] + 1, sem)
        self._mark(tok, outs, ins)
        self.nops += 1
        if self.max_ops is not None and self.nops >= self.max_ops:
            self.max_ops = None
            print("STOP at op", self.nops, eng)
            raise StopBuild()

    def dma(self, q, out, in_, is_output=False):
        if q not in self.dq:
            nr = 32 if q == "pool" else 8
            self.dq[q] = {"ring": [self.nc.alloc_semaphore("dsem_%s_%d" % (q, i)) for i in range(nr)], "n": 0, "toks": [None] * nr, "nr": nr}
        Q = self.dq[q]
        i = Q["n"]
        nr = Q["nr"]
        slot = i % nr
        deps = self._deps([out], [in_])
        if Q["toks"][slot] is not None:
            t = Q["toks"][slot]
            if t[0] not in deps or deps[t[0]][1] < t[1]:
                deps[t[0]] = t
        self._wait(q, deps)
        sem = Q["ring"][slot]
        val = 16 * (i // nr + 1)
        self.E[q].dma_start(out=out.ap, in_=in_.ap).then_inc(sem, 16)
        self.ninst += 1
        tok = (sem.num, val, sem)
        Q["toks"][slot] = tok
        Q["n"] += 1
        self._mark(tok, [out], [in_])
        if is_output:
            self.out_toks.append(tok)

    def finish(self):
        deps = {}
        for tok in self.out_toks:
            if tok[0] not in deps or deps[tok[0]][1] < tok[1]:
                deps[tok[0]] = tok
        for f in ("pe", "dve", "act", "pool"):
            if self.cnt[f]:
                deps[self.sem[f].num] = (self.sem[f].num, self.cnt[f], self.sem[f])
        for q, Q in self.dq.items():
            for t in Q["toks"]:
                if t is not None and (t[0] not in deps or deps[t[0]][1] < t[1]):
                    deps[t[0]] = t
        self._wait("sp", deps)

    def barrier(self):
        for eng in ("pe", "dve", "act", "pool"):
            deps = {}
            for f in ("pe", "dve", "act", "pool"):
                if (f == eng and eng == "pe") or self.cnt[f] == 0:
                    continue
                sem = self.sem[f]
                deps[sem.num] = (sem.num, self.cnt[f], sem)
            self._wait(eng, deps)

    def mm(self, out, lhsT, rhs, start=True, stop=True, signal=True):
        signal = True
        try:
            key = (int(lhsT.ap.start_partition()), int(lhsT.ap.partition_size()))
        except Exception:
            key = None
        last = getattr(self, "_mm_key", None)
        if key != last and self.cnt["pe"] > 0 and ((key is not None and key[1] < 128) or (last is not None and last[1] < 128)):
            sem = self.sem["pe"]
            if self.seen["pe"].get(sem.num, 0) < self.cnt["pe"]:
                self.E["pe"].wait_ge(sem, self.cnt["pe"])
                self.seen["pe"][sem.num] = self.cnt["pe"]
                self.ninst += 1
        self._mm_key = key
        self.op("pe", lambda e: e.matmul(out.ap, lhsT=lhsT.ap, rhs=rhs.ap, start=start, stop=stop), [out], [lhsT, rhs], signal)

    def tr(self, out, in_, ident):
        self.op("pe", lambda e: e.transpose(out.ap, in_.ap, ident.ap), [out], [in_, ident])

    def tt(self, eng, out, a, b, op):
        self.op(eng, lambda e: e.tensor_tensor(out=out.ap, in0=a.ap, in1=b.ap, op=op), [out], [a, b])

    def ts(self, eng, out, a, s1, op0, s2=None, op1=None):
        ins = [a] + [s for s in (s1, s2) if isinstance(s, View)]
        a1 = s1.ap if isinstance(s1, View) else s1
        a2 = s2.ap if isinstance(s2, View) else s2
        if op1 is None:
            self.op(eng, lambda e: e.tensor_scalar(out=out.ap, in0=a.ap, scalar1=a1, scalar2=None, op0=op0), [out], ins)
        else:
            self.op(eng, lambda e: e.tensor_scalar(out=out.ap, in0=a.ap, scalar1=a1, scalar2=a2, op0=op0, op1=op1), [out], ins)

    def stt(self, eng, out, a, s, b, op0, op1):
        ins = [a, b] + ([s] if isinstance(s, View) else [])
        sa = s.ap if isinstance(s, View) else s
        self.op(eng, lambda e: e.scalar_tensor_tensor(out=out.ap, in0=a.ap, scalar=sa, in1=b.ap, op0=op0, op1=op1), [out], ins)

    def act(self, out, in_, func, bias=None, scale=None):
        ins = [in_] + [s for s in (bias, scale) if isinstance(s, View)]
        kw = {}
        if bias is not None:
            kw["bias"] = bias.ap if isinstance(bias, View) else bias
        if scale is not None:
            kw["scale"] = scale.ap if isinstance(scale, View) else scale
        self.op("act", lambda e: e.activation(out=out.ap, in_=in_.ap, func=func, **kw), [out], ins)

    def cp(self, eng, out, in_):
        if eng == "act":
            self.op("act", lambda e: e.copy(out=out.ap, in_=in_.ap), [out], [in_])
        else:
            self.op(eng, lambda e: e.tensor_copy(out=out.ap, in_=in_.ap), [out], [in_])

    def scan(self, out, d0, d1, init, op0, op1):
        ins = [d0, d1] + ([init] if isinstance(init, View) else [])
        ia = init.ap if isinstance(init, View) else init
        self.op("dve", lambda e: e.tensor_tensor_scan(out=out.ap, data0=d0.ap, data1=d1.ap, initial=ia, op0=op0, op1=op1), [out], ins)

    def recip(self, out, in_):
        self.op("dve", lambda e: e.reciprocal(out=out.ap, in_=in_.ap), [out], [in_])

    def memset(self, eng, out, val):
        self.op(eng, lambda e: e.memset(out.ap, val), [out], [])

    def cpred(self, out, mask, data):
        self.op("dve", lambda e: e.copy_predicated(out=out.ap, mask=mask.ap, data=data.ap), [out], [mask, data])


class Packer:
    def __init__(self):
        self.cols = []
        self.off = {}
        self.n = 0

    def add(self, name, arr):
        arr = np.asarray(arr, dtype=np.float32)
        assert arr.shape[0] == 128, (name, arr.shape)
        arr = arr.reshape(128, -1)
        self.off[name] = (self.n, arr.shape[1])
        self.cols.append(arr)
        self.n += arr.shape[1]

    def pack(self):
        return np.ascontiguousarray(np.concatenate(self.cols, axis=1))


def feat(v):
    v = np.asarray(v, dtype=np.float32)
    return np.ascontiguousarray(v.reshape(-1, 128).T)


def rep(v):
    v = np.asarray(v, dtype=np.float32).reshape(1, -1)
    return np.ascontiguousarray(np.broadcast_to(v, (128, v.shape[1])))


def wtiles(W):
    K, N = W.shape
    Np = ((N + 127) // 128) * 128
    Kp = ((K + 1023) // 1024) * 1024
    Wp = np.zeros((Kp, Np), np.float32)
    Wp[:K, :N] = W
    out = []
    for kg in range(Kp // 1024):
        blk = Wp[kg * 1024:(kg + 1) * 1024]
        out.append(blk.reshape(8, 128, Np // 128, 128).transpose(2, 1, 0, 3))
    return out


def consts_host():
    P = Packer()
    idx = np.arange(128)
    P.add("ident", np.eye(128))
    P.add("ones", np.ones((128, 128)))
    le = (idx[:, None] <= idx[None, :]).astype(np.float32)
    lt = (idx[:, None] < idx[None, :]).astype(np.float32)
    P.add("le", le)
    P.add("lt", lt)
    P.add("nle", -le)
    P.add("nlt", -lt)
    P.add("ngt", -(idx[:, None] > idx[None, :]).astype(np.float32))
    blk = (idx[:, None] // 64 == idx[None, :] // 64).astype(np.float32)
    P.add("blk64", blk)
    P.add("le64", le * blk)
    P.add("ramp", rep(np.arange(1, 129)))
    return P


ODD_ORDER0 = [12, 13, 0, 4, 8, 1, 5, 9, 2, 6, 10, 3, 7, 11] + [14 + h + 4 * s for h in range(4) for s in range(4)]
ODD_ORDER1 = [12, 13, 30, 0, 4, 8, 1, 5, 9, 2, 6, 10, 3, 7, 11] + [14 + h + 4 * s for h in range(4) for s in range(4)]


def layer_host(inp, L):
    P = Packer()
    B = Packer()
    g = lambda n: np.asarray(inp[n], dtype=np.float32)
    P.add("n_mix_pre", feat(g("norm_mix_pre")[L]))
    P.add("n_mix_post", feat(g("norm_mix_post")[L]))
    P.add("n_ffn_pre", feat(g("norm_ffn_pre")[L]))
    P.add("n_ffn_post", feat(g("norm_ffn_post")[L]))
    cw = g("ffn_conv_w")[L]
    cb = g("ffn_conv_b")[L]
    order = np.concatenate([np.concatenate([np.arange(j * 128, (j + 1) * 128), DFF + np.arange(j * 128, (j + 1) * 128)]) for j in range(NJ)])
    P.add("ffn_cw", np.stack([feat(cw[k][order]) for k in range(3)], axis=2))
    P.add("ffn_cb", feat(cb[order]))
    tiles = []
    if L % 2 == 0:
        e = L // 2
        tiles.append(wtiles(g("ev_w_in")[e])[0])
        lr, li, ldt = g("s5_lam_re")[e], g("s5_lam_im")[e], g("s5_log_dt")[e]
        st = lambda a: np.ascontiguousarray(a.reshape(8, 2, 64).transpose(1, 2, 0).reshape(128, 8))
        P.add("s5_lr_s", st(lr))
        P.add("s5_li_s", st(li))
        P.add("s5_ldt_s", st(np.repeat(ldt[:, None], 64, axis=1)))
        B.add("s5_lr_b", rep(lr.reshape(-1)))
        B.add("s5_li_b", rep(li.reshape(-1)))
        B.add("s5_ldt_b", rep(np.repeat(ldt, 64)))
        br, bi = g("s5_b_re")[e], g("s5_b_im")[e]

        def bd(b):
            o = np.zeros((128, 2, 8, 64), np.float32)
            for kt in range(2):
                for gl in range(8):
                    o[gl * 16:(gl + 1) * 16, kt, gl, :] = b[8 * kt + gl].T
            return o.reshape(128, 1024)
        B.add("s5_br_bd", bd(br))
        B.add("s5_bi_bd", bd(bi))
        cr, ci = g("s5_c_re")[e], g("s5_c_im")[e]

        def cpad(c):
            o = np.zeros((128, 8, 128), np.float32)
            for j in range(8):
                for gp in range(2):
                    gg = 2 * j + gp
                    col = (gg % 8) * 16
                    o[gp * 64:(gp + 1) * 64, j, col:col + 16] = c[gg].T
            return o.reshape(128, 1024)
        B.add("s5_cr_pad", cpad(cr))
        B.add("s5_ci_pad", cpad(ci))
        B.add("s5_wglu", g("s5_w_glu")[e].reshape(2, 128, 256).transpose(1, 0, 2))
        P.add("s5_d", feat(g("s5_d")[e]))
        P.add("s5_bglu", feat(g("s5_b_glu")[e]))
        scw = g("ssd_conv_w")[e]
        P.add("ssd_cw", np.stack([feat(scw[k]) for k in range(4)], axis=2))
        P.add("ssd_cb", feat(g("ssd_conv_b")[e]))
        dtb = np.zeros((128, 1), np.float32)
        dtb[:12, 0] = g("ssd_dt_bias")[e]
        P.add("ssd_dtb", dtb)
        P.add("ssd_alog", rep(g("ssd_a_log")[e]))
        P.add("ssd_d", feat(np.repeat(g("ssd_d")[e], 64)))
        P.add("ssd_norm", feat(g("ssd_norm")[e]))
    else:
        o = L // 2
        w_in = g("od_w_in")[o]
        if o > 0:
            w_in = np.concatenate([w_in, g("rw_w_vin")[o - 1]], axis=1)
        wt = wtiles(w_in)[0]
        tiles.append(np.stack([wt[i] for i in (ODD_ORDER1 if o > 0 else ODD_ORDER0)]))
        mu = g("rw_mu")[o]
        if o > 0:
            mu = np.concatenate([mu, g("rw_mu_v")[o - 1], np.zeros(96, np.float32)])
        else:
            mu = np.concatenate([mu, np.zeros(128, np.float32)])
        P.add("rw_mu", feat(mu))
        P.add("rw_w0", feat(g("rw_w0")[o]))
        P.add("rw_a0", feat(g("rw_a0")[o]))
        B.add("rw_wa2", np.concatenate([g("rw_w2")[o], g("rw_a2")[o]], axis=0))
        B.add("rw_g2", g("rw_g2")[o])
        v2 = np.zeros((128, 512), np.float32)
        if o > 0:
            v2[:32] = g("rw_v2")[o - 1]
            P.add("rw_v0", feat(g("rw_v0")[o - 1]))
        B.add("rw_v2", v2)
        P.add("rw_k_k", feat(g("rw_k_k")[o]))
        P.add("rw_k_a", feat(g("rw_k_a")[o]))
        P.add("rw_r_k", feat(g("rw_r_k")[o]))
        P.add("rw_ln_w", feat(g("rw_ln_w")[o]))
        P.add("rw_ln_b", feat(g("rw_ln_b")[o]))
        P.add("hg_lb_raw0", feat(g("hg_lb_raw")[0]))
        P.add("hg_lb_raw1", feat(g("hg_lb_raw")[1]))
        P.add("hg_norm", feat(g("hg_norm")[o]))
    tiles.append(wtiles(g("mix_w_out")[L])[0])
    up = wtiles(g("ffn_w_up")[L])[0]
    tiles.append(np.stack([up[j + (0 if s == 0 else NJ)] for j in range(NJ) for s in (0, 1)]))
    dn = wtiles(g("ffn_w_down")[L])
    tiles.append(np.stack([dn[kg][m] for m in range(8) for kg in range(3)]))
    ws = np.ascontiguousarray(np.concatenate(tiles, axis=0).reshape(-1, 128, 1024))
    return P, B, ws
def build_program(pk_consts, pk_layers, pk_big, n_ws, n_layers=DEPTH, n_tiles=NTILES, dbg=None):
    nc = bass.Bass("TRN2", target_bir_lowering=False)
    S = Sched(nc)
    if dbg is not None and dbg[0] == "nops":
        S.max_ops = dbg[1]
    xT = nc.dram_tensor("xT", [8, 128, TPAD], F32, kind="ExternalInput").ap()
    oT = nc.dram_tensor("oT", [8, 128, TPAD], F32, kind="ExternalOutput").ap()
    cst_d = nc.dram_tensor("consts", [128, pk_consts.n], F32, kind="ExternalInput").ap()
    pv_d = [nc.dram_tensor("pv%d" % L, [128, pk_layers[L].n], F32, kind="ExternalInput").ap() for L in range(DEPTH)]
    pb_d = [nc.dram_tensor("pb%d" % L, [128, pk_big[L].n], F32, kind="ExternalInput").ap() for L in range(DEPTH)]
    ws_d = [nc.dram_tensor("ws%d" % L, [n_ws[L], 128, 1024], F32, kind="ExternalInput").ap() for L in range(DEPTH)]
    wsb_d = [nc.dram_tensor("wsb%d" % L, [n_ws[L], 128, 1024], BF16, kind="Internal").ap() for L in range(DEPTH)]
    WCH = 4
    wsb_regs = [[Reg() for _ in range((n_ws[L] + WCH - 1) // WCH)] for L in range(DEPTH)]
    dbg_d = None
    if dbg is not None:
        dbg_d = nc.dram_tensor("dbg", [8, 128, NT], F32, kind="ExternalOutput").ap()

    uid = [0]

    def mk(name, shape, dtype, nreg=1):
        uid[0] += 1
        return Buf(S, "%s_%d" % (name, uid[0]), shape, dtype, nreg)

    CST = mk("cst", [128, pk_consts.n], F32)
    PV = [mk("pvs", [128, pk_layers[L].n], F32) for L in range(DEPTH)]
    cst_bf = mk("cst_bf", [128, 3, 128], BF16)
    PSQ = [Reg() for _ in range(32)]
    PSt = nc.alloc_psum_tensor("psum_all", [128, 8, 512], F32)

    def C(name):
        o, n = pk_consts.off[name]
        return CST(CST.t[:, o:o + n])

    def pvv(L, name, a=None, b=None, p0=0, p1=128):
        o, n = pk_layers[L].off[name]
        if a is None:
            a, b = 0, n
        return PV[L](PV[L].t[p0:p1, o + a:o + b])

    def psr(bank, a, b):
        return [PSQ[bank]]

    def ps(bank, a=0, b=512, p0=0, p1=128):
        return View(PSt[p0:p1, bank, a:b], psr(bank, a, b), True)

    H = mk("h", [128, 8, NT], F32)
    HN = mk("hn", [128, 8, NT], BF16)
    SQ = mk("sq", [128, 2, NT], BF16, nreg=2)
    RSTD = mk("rstd", [128, NT], F32)
    TM = [None]
    YMIX = mk("ymix", [128, 8, NT], BF16, nreg=8)
    NWS = 6
    WRING = mk("wring", [128, NWS, 1024], BF16, nreg=NWS)
    FHALO = [mk("fhalo", [128, 2 * NJ, 2], F32) for L in range(DEPTH)]
    VFIRST = mk("vfirst", [128, 4, NT], F32, nreg=4)
    EPSB = mk("epsb", [128, 2], F32)
    wctr = [0]
    sqc = [0]

    ident = C("ident")
    ones_f = C("ones")
    ident_bf = cst_bf(cst_bf.t[:, 0, :])
    ones_bf = cst_bf(cst_bf.t[:, 1, :])
    blk64_bf = cst_bf(cst_bf.t[:, 2, :])

    def fl(v, pat, **kw):
        return View(v.ap.rearrange(pat, **kw), v.regs)

    def bc(v, ap):
        return View(ap, v.regs)

    LS = {}
    for L in range(n_layers):
        st = {}
        if L % 2 == 0:
            st["bb_re"] = mk("bbre", [128, 1024], BF16)
            st["bb_im"] = mk("bbim", [128, 1024], BF16)
            st["c_re"] = mk("cre", [128, 8, 128], F32)
            st["c_imn"] = mk("cimn", [128, 8, 128], F32)
            st["wglu"] = mk("wglu", [128, 2, 256], BF16)
            st["cos"] = mk("cos", [128, 8, CH], F32)
            st["sin"] = mk("sin", [128, 8, CH], F32)
            st["m"] = mk("m", [128, 8], F32)
            st["car_re"] = mk("carre", [128, 8], F32)
            st["car_im"] = mk("carim", [128, 8], F32)
            st["a_neg"] = mk("aneg", [128, 12], F32)
            st["Sst"] = mk("sst", [128, 12, 64], F32)
            st["prevT"] = mk("prevT", [128, 12, 128], BF16)
            st["xhalo"] = mk("xhalo", [128, 10, 3], F32)
        else:
            st["wa2"] = mk("wa2", [128, 512], BF16)
            st["g2"] = mk("g2", [128, 512], BF16)
            st["v2"] = mk("v2", [128, 512], BF16)
            st["lb"] = mk("lb", [128, 4], F32)
            st["oml"] = mk("oml", [128, 4], F32)
            st["T"] = mk("rwT", [128, 4, 64], F32, nreg=4)
            st["Tbf"] = mk("rwTbf", [128, 4, 64], BF16, nreg=4)
            st["HS"] = mk("hgS", [128, 4, 128], F32, nreg=4)
            st["HSbf"] = mk("hgSbf", [128, 4, 128], BF16, nreg=4)
            st["rwhalo"] = mk("rwhalo", [128, 15], F32)
            st["omm"] = mk("omm", [128, 15], F32)
        LS[L] = st

    arena_snap = (nc.sbuf_base, nc.sbuf_top)
    persist_bytes = S.sbuf_bytes

    def stop(stage):
        if dbg is not None and dbg[0] == "stop" and dbg[1] == stage:
            print("STOP stage", stage, "nops", S.nops)
            raise StopBuild()

    def arena_reset():
        S.barrier()
        nc.sbuf_base, nc.sbuf_top = arena_snap

    for L in range(n_layers):
        for ci in range(len(wsb_regs[L])):
            a, b = ci * WCH, min(n_ws[L], (ci + 1) * WCH)
            if os.environ.get("MAXCAST") and ci >= int(os.environ["MAXCAST"]):
                continue
            S.dma("pool", View(wsb_d[L][a:b], [wsb_regs[L][ci]]), dram(ws_d[L][a:b]))
    S.dma("sp", CST(), dram(cst_d))
    S.dma("sp", H(), dram(xT[:, :, 0:NT].rearrange("d p t -> p d t")))
    for L in range(n_layers):
        S.dma("sp", PV[L](), dram(pv_d[L]))
    for i, nm in enumerate(["ident", "ones", "blk64"]):
        S.cp("dve", cst_bf(cst_bf.t[:, i, :]), C(nm))
    for L in range(n_layers):
        S.memset("pool", FHALO[L](), 0.0)
    S.memset("dve", EPSB(EPSB.t[:, 0:1]), EPS)
    S.memset("dve", EPSB(EPSB.t[:, 1:2]), GN_EPS)
    eps_v = EPSB(EPSB.t[:, 0:1])
    gneps_v = EPSB(EPSB.t[:, 1:2])

    def sincos(x, sin_out, cos_out, tmp):
        ti = View(tmp.t[:, 3, :].bitcast(I32), tmp.regs)
        y, k, f = tmp(tmp.t[:, 0, :]), tmp(tmp.t[:, 1, :]), tmp(tmp.t[:, 2, :])
        for which, outv in ((0, sin_out), (1, cos_out)):
            S.ts("dve", y, x, 1.0 / (2 * math.pi), ALU.mult, 0.5 + 0.25 * which, ALU.add)
            S.cp("dve", ti, y)
            S.cp("dve", k, ti)
            S.tt("dve", f, y, k, ALU.subtract)
            S.ts("dve", k, f, 0.0, ALU.is_lt)
            S.tt("dve", f, f, k, ALU.add)
            S.ts("dve", f, f, 1.0, ALU.min)
            S.ts("dve", f, f, 2 * math.pi, ALU.mult, -math.pi, ALU.add)
            S.act(outv, f, AF.Sin)

    PVB = mk("pvb", [128, 8, 512], F32, nreg=8)
    TB = mk("s5tmpb", [128, 12, 512], F32, nreg=12)
    TS4 = mk("s5tmp4", [128, 4, 512], F32)

    def tb(i):
        return TB(TB.t[:, i, :], i)

    for L in range(n_layers):
        st = LS[L]

        def bgload(slot, name, a, b):
            o, n = pk_big[L].off[name]
            v = PVB(PVB.t[:, slot, 0:b - a], slot)
            S.dma("sp", v, dram(pb_d[L][:, o + a:o + b]))
            return v
        if L % 2 == 0:
            for half in range(2):
                hs = slice(half * 512, (half + 1) * 512)
                a_, b_ = half * 512, (half + 1) * 512
                lr, li, ldt = bgload(0, "s5_lr_b", a_, b_), bgload(1, "s5_li_b", a_, b_), bgload(2, "s5_ldt_b", a_, b_)
                br, bi = bgload(3, "s5_br_bd", a_, b_), bgload(4, "s5_bi_bd", a_, b_)
                crp, cip = bgload(5, "s5_cr_pad", a_, b_), bgload(6, "s5_ci_pad", a_, b_)
                dt, lrdt, lidt, mag, sn, cs, abr, abi, den, fre, fim, t1 = [tb(i) for i in range(12)]
                S.act(dt, ldt, AF.Exp)
                S.tt("dve", lrdt, lr, dt, ALU.mult)
                S.tt("dve", lidt, li, dt, ALU.mult)
                S.act(mag, lrdt, AF.Exp)
                sincos(lidt, sn, cs, TS4)
                S.tt("dve", abr, mag, cs, ALU.mult)
                S.tt("dve", abi, mag, sn, ALU.mult)
                S.tt("dve", den, lr, lr, ALU.mult)
                S.tt("dve", t1, li, li, ALU.mult)
                S.tt("dve", den, den, t1, ALU.add)
                S.recip(den, den)
                S.ts("dve", abr, abr, -1.0, ALU.add)
                S.tt("dve", fre, abr, lr, ALU.mult)
                S.tt("dve", t1, abi, li, ALU.mult)
                S.tt("dve", fre, fre, t1, ALU.add)
                S.tt("dve", fre, fre, den, ALU.mult)
                S.tt("dve", fim, abi, lr, ALU.mult)
                S.tt("dve", t1, abr, li, ALU.mult)
                S.tt("dve", fim, fim, t1, ALU.subtract)
                S.tt("dve", fim, fim, den, ALU.mult)
                S.tt("dve", t1, fre, br, ALU.mult)
                S.tt("dve", dt, fim, bi, ALU.mult)
                S.tt("dve", st["bb_re"](st["bb_re"].t[:, hs]), t1, dt, ALU.subtract)
                S.tt("dve", t1, fre, bi, ALU.mult)
                S.tt("dve", dt, fim, br, ALU.mult)
                S.tt("dve", st["bb_im"](st["bb_im"].t[:, hs]), t1, dt, ALU.add)
                js = slice(half * 4, half * 4 + 4)
                S.cp("dve", st["c_re"](st["c_re"].t[:, js, :].rearrange("p a b -> p (a b)")), crp)
                S.ts("dve", st["c_imn"](st["c_imn"].t[:, js, :].rearrange("p a b -> p (a b)")), cip, -1.0, ALU.mult)
                if half == 0:
                    wg = bgload(7, "s5_wglu", 0, 512)
                    S.cp("dve", fl(st["wglu"](), "p a b -> p (a b)"), wg)
                    dts, th = TS4(TS4.t[:, 0, 0:8]), TS4(TS4.t[:, 0, 8:16])
                    S.act(dts, pvv(L, "s5_ldt_s"), AF.Exp)
                    S.tt("dve", th, pvv(L, "s5_li_s"), dts, ALU.mult)
                    S.tt("dve", dts, pvv(L, "s5_lr_s"), dts, ALU.mult)
                    S.act(st["m"](), dts, AF.Exp)
                    S.cp("dve", st["car_re"](), th)
                ang = TB(TB.t[:, 1, :].rearrange("p (a b) -> p a b", a=4), 1)
                ramp = C("ramp")
                thv = st["car_re"](st["car_re"].t[:, js])
                S.tt("dve", ang, bc(ramp, ramp.ap[:, None, :].broadcast_to([128, 4, 128])),
                     bc(thv, thv.ap[:, :, None].broadcast_to([128, 4, 128])), ALU.mult)
                sincos(tb(1), st["sin"](st["sin"].t[:, js, :].rearrange("p a b -> p (a b)")),
                       st["cos"](st["cos"].t[:, js, :].rearrange("p a b -> p (a b)")), TS4)
            S.memset("dve", st["car_re"](), 0.0)
            S.memset("dve", st["car_im"](), 0.0)
            S.act(st["a_neg"](), pvv(L, "ssd_alog"), AF.Exp)
            S.ts("dve", st["a_neg"](), st["a_neg"](), -1.0, ALU.mult)
            S.memset("pool", st["Sst"](), 0.0)
            S.memset("pool", st["prevT"](), 0.0)
            S.memset("pool", st["xhalo"](), 0.0)
        else:
            S.cp("dve", st["wa2"](), bgload(0, "rw_wa2", 0, 512))
            S.cp("dve", st["g2"](), bgload(1, "rw_g2", 0, 512))
            S.cp("dve", st["v2"](), bgload(2, "rw_v2", 0, 512))
            if L // 2 == 0:
                S.memset("dve", st["lb"](), 0.0)
            else:
                S.tt("dve", st["lb"](), pvv(L, "hg_lb_raw1"), pvv(L, "hg_lb_raw0"), ALU.subtract)
                S.act(st["lb"](), st["lb"](), AF.Sigmoid)
            S.ts("dve", st["oml"](), st["lb"](), -1.0, ALU.mult, 1.0, ALU.add)
            S.ts("dve", st["omm"](), pvv(L, "rw_mu"), -1.0, ALU.mult, 1.0, ALU.add)
            S.memset("pool", st["T"](), 0.0)
            S.memset("pool", st["Tbf"](), 0.0)
            S.memset("pool", st["HS"](), 0.0)
            S.memset("pool", st["HSbf"](), 0.0)
            S.memset("pool", st["rwhalo"](), 0.0)
    arena_reset()
    if dbg is not None and dbg[0] == "stop" and dbg[1] == "prologue":
        S.dma("sp", dram(oT[:, :, 0:NT].rearrange("d p t -> p d t")), H(), is_output=True)
        S.finish()
        return nc

    def wload(L, idx):
        slot = wctr[0] % NWS
        wctr[0] += 1
        if os.environ.get("SKIPW") and wctr[0] > NWS:
            return slot
        S.dma("sp", WRING(WRING.t[:, slot, :], slot), View(wsb_d[L][idx], [wsb_regs[L][idx // WCH]]))
        return slot

    def wk(slot, k, m0=0, m1=128, p0=0, p1=128):
        return WRING(WRING.t[p0:p1, slot, k * 128 + m0:k * 128 + m1], slot)

    def interleave(gens):
        gens = list(gens)
        while gens:
            for g in list(gens):
                try:
                    next(g)
                except StopIteration:
                    gens.remove(g)

    psrot = [0]

    def next_bank(nb=4):
        b = psrot[0] % nb
        psrot[0] += 1
        return b

    def norm_stats(src_views, nfeat, ones=None, epsv=None, n=NT):
        m = len(src_views)
        for i, v in enumerate(src_views):
            q = sqc[0] % 2
            sqc[0] += 1
            S.act(SQ(SQ.t[:, q, 0:n], q), v, AF.Square)
            S.mm(ps(7, 0, n), ones if ones is not None else ones_bf, SQ(SQ.t[:, q, 0:n], q), start=(i == 0), stop=(i == m - 1), signal=(i == m - 1))
        r = RSTD(RSTD.t[:, 0:n])
        S.act(r, ps(7, 0, n), AF.Sqrt, bias=epsv if epsv is not None else eps_v, scale=1.0 / nfeat)
        S.recip(r, r)
        return r

    def pre_norm(L, gname):
        r = norm_stats([H(H.t[:, d, :]) for d in range(8)], D)
        for d in range(8):
            S.stt("dve", HN(HN.t[:, d, :]), H(H.t[:, d, :]), pvv(L, gname, d, d + 1), r, ALU.mult, ALU.mult)

    def post_norm_add(L, gname):
        TMP8 = TM[0]
        r = norm_stats([TMP8(TMP8.t[:, d, :], d) for d in range(8)], D)
        for d in range(8):
            S.stt("dve", TMP8(TMP8.t[:, d, :], d), TMP8(TMP8.t[:, d, :], d), pvv(L, gname, d, d + 1), r, ALU.mult, ALU.mult)
            S.tt("dve", H(H.t[:, d, :]), H(H.t[:, d, :]), TMP8(TMP8.t[:, d, :], d), ALU.add)

    def inproj(L, widx, nb=4, banks=None):
        slot = wload(L, widx)
        b = next_bank(nb) if banks is None else banks[next_bank(len(banks))]
        for k in range(8):
            S.mm(ps(b, 0, NT), wk(slot, k), HN(HN.t[:, k, :]), start=(k == 0), stop=(k == 7), signal=(k == 7))
        return b

    def proj8(L, widx0, rhs_buf, nk):
        nkg = (nk + 7) // 8
        wi = widx0
        for m in range(8):
            b = next_bank()
            kk = 0
            for kg in range(nkg):
                slot = wload(L, wi)
                wi += 1
                for k in range(min(8, nk - kg * 8)):
                    S.mm(ps(b, 0, NT), wk(slot, k), rhs_buf(rhs_buf.t[:, kk, :], kk), start=(kk == 0), stop=(kk == nk - 1), signal=(kk == nk - 1))
                    kk += 1
            S.cp("act", TM[0](TM[0].t[:, m, :], m), ps(b, 0, NT))
        return wi

    def rolling(factories, width, stagger=0):
        pending = list(factories)
        active = []
        since = stagger
        while pending or active:
            while pending and len(active) < width and (since >= stagger or not active):
                active.append(pending.pop(0)())
                since = 0
            since += 1
            for g in list(active):
                try:
                    next(g)
                except StopIteration:
                    active.remove(g)

    def rolling_groups(groups):
        pend = [list(f) for f, _ in groups]
        act = [[] for _ in groups]
        while any(pend) or any(act):
            for gi, (_, w) in enumerate(groups):
                while pend[gi] and len(act[gi]) < w:
                    act[gi].append(pend[gi].pop(0)())
            for gi in range(len(groups)):
                for g in list(act[gi]):
                    try:
                        next(g)
                    except StopIteration:
                        act[gi].remove(g)

    def ffn(L, widx0):
        PHASE[0] = "ffn"
        arena_reset()
        NST = 4
        TM[0] = mk("tmp8", [128, 8, NT], F32, nreg=8)
        ABUF = mk("abuf", [128, NJ, NT], BF16, nreg=NJ)
        FRAW = [mk("fraw", [128, 2, 2 + NT], F32, nreg=2) for i in range(NST)]
        FACC = [mk("facc", [128, 2, NT], F32, nreg=2) for i in range(NST)]
        pre_norm(L, "n_ffn_pre")
        o, _ = pk_layers[L].off["ffn_cw"]
        ob, _ = pk_layers[L].off["ffn_cb"]

        def ffn_j(j):
            sl = j % NST
            fr, fa = FRAW[sl], FACC[sl]
            S.cp("pool", fr(fr.t[:, :, 0:2]), FHALO[L](FHALO[L].t[:, 2 * j:2 * j + 2, :]))
            for s in range(2):
                b = inproj(L, widx0 + 2 * j + s, nb=6)
                ti = 2 * j + s
                cw = lambda kk: PV[L](PV[L].t[:, o + ti * 3 + kk:o + ti * 3 + kk + 1])
                cbv = PV[L](PV[L].t[:, ob + ti:ob + ti + 1])
                acc = fa(fa.t[:, s, :], s)
                S.cp("act", fr(fr.t[:, s, 2:2 + NT], s), ps(b, 0, NT))
                S.act(acc, ps(b, 0, NT), AF.Identity, bias=cbv, scale=cw(2))
                yield
                eng = "dve"
                for kk in (0, 1):
                    if eng == "dve":
                        S.stt(eng, acc, fr(fr.t[:, s, kk:kk + NT], s), cw(kk), acc, ALU.mult, ALU.add)
                    else:
                        S.ts(eng, FTMP[sl](), fr(fr.t[:, s, kk:kk + NT], s), cw(kk), ALU.mult)
                        S.tt(eng, acc, acc, FTMP[sl](), ALU.add)
                    yield
            S.cp("pool", FHALO[L](FHALO[L].t[:, 2 * j:2 * j + 2, :]), fr(fr.t[:, :, NT:NT + 2]))
            S.act(fa(fa.t[:, 0, :], 0), fa(fa.t[:, 0, :], 0), AF.Gelu_apprx_tanh)
            yield
            S.tt("dve", ABUF(ABUF.t[:, j, :], j), fa(fa.t[:, 0, :], 0), fa(fa.t[:, 1, :], 1), ALU.mult)
            yield

        rolling([(lambda j=j: ffn_j(j)) for j in range(NJ)], NST, int(os.environ.get('STG_FFN', 0)))
        widx = proj8(L, widx0 + 2 * NJ, ABUF, NJ)
        post_norm_add(L, "n_ffn_post")
        return widx

    def even_layer(L):
        st = LS[L]
        PHASE[0] = "s5"
        arena_reset()
        pre_norm(L, "n_mix_pre")
        stop("prenorm")
        ZS = mk("zs", [128, 6, NT], F32, nreg=6)
        XA = mk("xa", [128, 10, NT], F32, nreg=10)
        XAB = mk("xab", [128, 4, NT], BF16, nreg=4)
        DTT = mk("dtt", [128, NT], F32)
        RAW = [mk("xraw", [128, 3 + NT], F32) for i in range(2)]
        ssd_snap = (nc.sbuf_base, nc.sbuf_top)
        ocw, _ = pk_layers[L].off["ssd_cw"]
        ocb, _ = pk_layers[L].off["ssd_cb"]

        def ssd_inproj():
            for m in range(2, 19):
                b = inproj(L, m, banks=(4, 5))
                p = ps(b, 0, NT)
                if m < 8:
                    S.act(ZS(ZS.t[:, m - 2, :], m - 2), p, AF.Silu)
                    yield
                elif m < 18:
                    i = m - 8
                    rw = RAW[i % 2]
                    S.cp("pool", rw(rw.t[:, 0:3]), st["xhalo"](st["xhalo"].t[:, i, :]))
                    S.cp("act", rw(rw.t[:, 3:3 + NT]), p)
                    cw = lambda kk: PV[L](PV[L].t[:, ocw + i * 4 + kk:ocw + i * 4 + kk + 1])
                    acc = XA(XA.t[:, i, :], i)
                    S.act(acc, p, AF.Identity, bias=PV[L](PV[L].t[:, ocb + i:ocb + i + 1]), scale=cw(3))
                    yield
                    for kk in range(0, 3):
                        S.stt("dve", acc, rw(rw.t[:, kk:kk + NT]), cw(kk), acc, ALU.mult, ALU.add)
                        yield
                    S.cp("pool", st["xhalo"](st["xhalo"].t[:, i, :]), rw(rw.t[:, NT:NT + 3]))
                    S.act(acc, acc, AF.Silu)
                    if i >= 6:
                        S.cp("act", XAB(XAB.t[:, i - 6, :], i - 6), acc)
                    yield
                else:
                    S.act(DTT(DTT.t[0:12, :]), ps(b, 0, NT, 0, 12), AF.Exp, bias=pvv(L, "ssd_dtb", 0, 1, 0, 12))
                    S.act(DTT(DTT.t[0:12, :]), DTT(DTT.t[0:12, :]), AF.Ln, bias=1.0)
                    yield

        UF = mk("uf", [128, 2, NT], F32)
        UB = mk("ub", [128, 2, NT], BF16)
        NS5 = 2
        S5T = [[mk("s5t", [128, 6, CH], F32, nreg=6) for _ in range(2)] for i in range(NS5)]
        S5X = [mk("s5x", [128, 2, NT], F32, nreg=2) for i in range(NS5)]
        S5Y = mk("s5y", [128, 2, NT], F32)
        S5YB = mk("s5yb", [128, 2, NT], BF16)
        for m in range(2):
            b = inproj(L, m)
            S.cp("act", UF(UF.t[:, m, :]), ps(b, 0, NT))
            S.cp("dve", UB(UB.t[:, m, :]), ps(b, 0, NT))
        stop("inproj")

        def s5_j(j):
            sl = j % NS5
            kt, jj = j // 4, j % 4
            bre, bim = 2 * sl, 2 * sl + 1
            c0 = kt * 512 + jj * 128
            S.mm(ps(bre, 0, NT), st["bb_re"](st["bb_re"].t[:, c0:c0 + 128]), UB(UB.t[:, kt, :]))
            S.mm(ps(bim, 0, NT), st["bb_im"](st["bb_im"].t[:, c0:c0 + 128]), UB(UB.t[:, kt, :]))
            yield
            X = S5X[sl]
            cs, sn = st["cos"](st["cos"].t[:, j, :]), st["sin"](st["sin"].t[:, j, :])
            mb = bc(st["m"](), st["m"].t[:, j:j + 1].to_broadcast([128, CH]))
            for c in range(CPT):
                T = S5T[sl][c % 2]
                pr, pi = ps(bre, c * CH, (c + 1) * CH), ps(bim, c * CH, (c + 1) * CH)
                t = lambda i: T(T.t[:, i, :], i)
                S.tt("dve", t(0), pr, cs, ALU.mult)
                S.tt("dve", t(1), pi, sn, ALU.mult)
                yield
                S.tt("dve", t(0), t(0), t(1), ALU.add)
                S.tt("dve", t(2), pi, cs, ALU.mult)
                yield
                S.tt("dve", t(3), pr, sn, ALU.mult)
                S.tt("dve", t(2), t(2), t(3), ALU.subtract)
                yield
                if c == 0:
                    ir, ii = st["car_re"](st["car_re"].t[:, j:j + 1]), st["car_im"](st["car_im"].t[:, j:j + 1])
                else:
                    ir, ii = X(X.t[:, 0, c * CH - 1:c * CH], 0), X(X.t[:, 1, c * CH - 1:c * CH], 1)
                S.scan(t(4), mb, t(0), ir, ALU.mult, ALU.add)
                yield
                S.scan(t(5), mb, t(2), ii, ALU.mult, ALU.add)
                yield
                xr, xi = X(X.t[:, 0, c * CH:(c + 1) * CH], 0), X(X.t[:, 1, c * CH:(c + 1) * CH], 1)
                S.tt("dve", t(0), t(4), cs, ALU.mult)
                S.tt("dve", t(1), t(5), sn, ALU.mult)
                yield
                S.tt("dve", xr, t(0), t(1), ALU.subtract)
                S.tt("dve", t(2), t(5), cs, ALU.mult)
                yield
                S.tt("dve", t(3), t(4), sn, ALU.mult)
                S.tt("dve", xi, t(2), t(3), ALU.add)
                yield
            S.cp("pool", st["car_re"](st["car_re"].t[:, j:j + 1]), X(X.t[:, 0, NT - 1:NT], 0))
            S.cp("pool", st["car_im"](st["car_im"].t[:, j:j + 1]), X(X.t[:, 1, NT - 1:NT], 1))
            yb = 6 + j // 4
            S.mm(ps(yb, 0, NT), st["c_re"](st["c_re"].t[:, j, :]), X(X.t[:, 0, :], 0), start=(jj == 0), stop=False)
            S.mm(ps(yb, 0, NT), st["c_imn"](st["c_imn"].t[:, j, :]), X(X.t[:, 1, :], 1), start=False, stop=(jj == 3))
            yield

        rolling_groups([([ssd_inproj], 1), ([(lambda j=j: s5_j(j)) for j in range(8)], NS5)])
        stop("s5scan")
        for ot in range(2):
            yv = S5Y(S5Y.t[:, ot, :])
            S.stt("dve", yv, UF(UF.t[:, ot, :]), pvv(L, "s5_d", ot, ot + 1), ps(6 + ot, 0, NT), ALU.mult, ALU.add)
            S.act(yv, yv, AF.Gelu_apprx_tanh)
            S.cp("act", S5YB(S5YB.t[:, ot, :]), yv)
        for ot in range(2):
            b = next_bank()
            for k in range(2):
                S.mm(ps(b, 0, NT), st["wglu"](st["wglu"].t[:, k, ot * 128:(ot + 1) * 128]), S5YB(S5YB.t[:, k, :]), start=(k == 0), stop=(k == 1), signal=(k == 1))
            sg = S5X[0](S5X[0].t[:, ot, :], ot)
            S.act(sg, ps(b, 0, NT), AF.Sigmoid, bias=pvv(L, "s5_bglu", ot, ot + 1))
            S.tt("dve", YMIX(YMIX.t[:, ot, :], ot), S5Y(S5Y.t[:, ot, :]), sg, ALU.mult)
        stop("s5")
        PHASE[0] = "ssd"
        S.barrier()
        nc.sbuf_base, nc.sbuf_top = ssd_snap
        YS = mk("ys", [128, 6, NT], F32)
        XDP = mk("xdp", [128, 12, 128], BF16)
        XDD = mk("xdd", [128, 12, 64], BF16)
        XST = mk("xst", [128, 12, 64], F32)
        BTK = mk("btk", [128, 2, 128], BF16)
        SM = mk("ssdsmall", [128, 8, 12], F32, nreg=8)
        SEG = mk("seg", [128, 12, 128], F32)
        EAR = mk("ear", [128, 12, 128], F32)
        SCT = mk("sct", [128, 12, 128], BF16)
        CEX = mk("cex", [128, 12, 128], BF16)
        CBM = mk("cbm", [128, 2, 128], F32)
        S.memset("pool", XDP(), 0.0)
        stop("ssdproj")
        a_neg = st["a_neg"]()
        le = C("le")
        sm = lambda i: SM(SM.t[:, i, :], i)
        arow_regs = psr(1, 0, 512) + psr(2, 0, 512) + psr(3, 0, 512)
        for c in range(CPT):
            cc = slice(c * CH, (c + 1) * CH)
            for i in range(6):
                bk, off = (4, i * 128) if i < 4 else (5, (i - 4) * 128)
                S.tr(ps(bk, off, off + 128), XA(XA.t[:, i, cc], i), ident)
            for g2 in range(2):
                S.tr(ps(6, g2 * 128, (g2 + 1) * 128), XA(XA.t[:, 6 + g2, cc], 6 + g2), ident)
            S.tr(ps(7, 256, 268), DTT(DTT.t[0:12, cc]), bc(ident, ident.ap[0:12, 0:12]))
            S.cp("act", XST(XST.t[:, 0:8, :].rearrange("p a b -> p (a b)")), ps(4, 0, 512))
            S.cp("act", XST(XST.t[:, 8:12, :].rearrange("p a b -> p (a b)")), ps(5, 0, 256))
            S.cp("act", fl(BTK(), "p a b -> p (a b)"), ps(6, 0, 256))
            dtk, da, acol, aend, dte, w2, edec = [sm(i) for i in range(7)]
            S.cp("dve", dtk, ps(7, 256, 268))
            S.tt("dve", da, dtk, a_neg, ALU.mult)
            S.mm(ps(7, 272, 284), le, da)
            S.cp("act", acol, ps(7, 272, 284))
            S.tt("dve", SEG(), bc(da, da.ap[:, :, None].broadcast_to([128, 12, 128])),
                 bc(le, le.ap[:, None, :].broadcast_to([128, 12, 128])), ALU.mult)
            for q in range(3):
                S.mm(ps(1 + q, 0, 512), ones_f, SEG(SEG.t[:, 4 * q:4 * q + 4, :].rearrange("p a b -> p (a b)")))
            arow = View(PSt[:, 1:4, :].rearrange("p a (b c) -> p (a b) c", c=128), arow_regs, True)
            S.act(EAR(), arow, AF.Exp)
            S.cp("dve", aend, View(PSt[:, 1:4, :].rearrange("p a (b c) -> p (a b) c", c=128)[:, :, 127], arow_regs, True))
            S.tt("dve", SEG(), arow, bc(acol, acol.ap[:, :, None].broadcast_to([128, 12, 128])), ALU.subtract)
            S.act(SEG(), SEG(), AF.Exp)
            for g2 in range(2):
                S.mm(ps(6, 256 + g2 * 128, 256 + (g2 + 1) * 128), XAB(XAB.t[:, g2, cc]), XAB(XAB.t[:, 2 + g2, cc]))
            S.tt("dve", CBM(), View(PSt[:, 6, 256:512].rearrange("p (a b) -> p a b", a=2), psr(6, 256, 512), True),
                 bc(le, le.ap[:, None, :].broadcast_to([128, 2, 128])), ALU.mult)
            for g2 in range(2):
                hs = slice(6 * g2, 6 * g2 + 6)
                S.stt("dve", SCT(SCT.t[:, hs, :]), SEG(SEG.t[:, hs, :]), 1.0, CBM(CBM.t[:, g2:g2 + 1, :].broadcast_to([128, 6, 128])), ALU.min, ALU.mult)
                S.tt("dve", CEX(CEX.t[:, hs, :]), EAR(EAR.t[:, hs, :]),
                     XA(XA.t[:, 8 + g2:9 + g2, cc].broadcast_to([128, 6, 128]), 8 + g2), ALU.mult)
            xdp5 = XDP.t[:].rearrange("p (a two) (h c) -> p a two h c", two=2, h=2)
            xst4 = XST.t[:].rearrange("p (a two) c -> p a two c", two=2)
            dtk4 = dtk.ap.rearrange("p (a two) -> p a two", two=2)
            for par in range(2):
                S.tt("dve", XDP(xdp5[:, :, par, par, :]), XST(xst4[:, :, par, :]),
                     bc(dtk, dtk4[:, :, par:par + 1].broadcast_to([128, 6, 64])), ALU.mult)
            for pj in range(6):
                yo = ps(0, pj * 128, (pj + 1) * 128) if pj < 4 else ps(7, (pj - 4) * 128, (pj - 3) * 128)
                for hh in range(2):
                    hI = 2 * pj + hh
                    S.mm(yo, XDP(XDP.t[:, hI, :]), SCT(SCT.t[:, hI, :]), start=(hh == 0), stop=False, signal=False)
                for hh in range(2):
                    hI = 2 * pj + hh
                    S.mm(yo, st["prevT"](st["prevT"].t[:, hI, :]), CEX(CEX.t[:, hI, :]), start=False, stop=(hh == 1), signal=(hh == 1))
                S.stt("dve", YS(YS.t[:, pj, cc]), XA(XA.t[:, pj, cc], pj), pvv(L, "ssd_d", pj, pj + 1), yo, ALU.mult, ALU.add)
            S.tt("dve", dte, aend, acol, ALU.subtract)
            S.act(dte, dte, AF.Exp)
            S.tt("dve", w2, dtk, dte, ALU.mult)
            S.act(edec, aend, AF.Exp)
            S.tt("dve", XDD(), XST(), bc(w2, w2.ap[:, :, None].broadcast_to([128, 12, 64])), ALU.mult)
            for g2 in range(2):
                S.mm(ps(4 + g2, 0, 384), BTK(BTK.t[:, g2, :]), XDD(XDD.t[:, 6 * g2:6 * g2 + 6, :].rearrange("p a b -> p (a b)")))
            Sst = st["Sst"]
            S.tt("dve", Sst(), Sst(), bc(edec, edec.ap[:, :, None].broadcast_to([128, 12, 64])), ALU.mult)
            for g2 in range(2):
                S.tt("dve", Sst(Sst.t[:, 6 * g2:6 * g2 + 6, :]), Sst(Sst.t[:, 6 * g2:6 * g2 + 6, :]),
                     View(PSt[:, 4 + g2, 0:384].rearrange("p (a b) -> p a b", a=6), psr(4 + g2, 0, 384), True), ALU.add)
            pt5 = st["prevT"].t[:].rearrange("p (a two) (h c) -> p a two h c", two=2, h=2)
            ss4 = Sst.t[:].rearrange("p (a two) c -> p a two c", two=2)
            for par in range(2):
                S.cp("act", st["prevT"](pt5[:, :, par, par, :]), Sst(ss4[:, :, par, :]))
        S.tt("dve", YS(), YS(), ZS(), ALU.mult)
        for g2 in range(2):
            r = norm_stats([YS(YS.t[:, 3 * g2 + i, :]) for i in range(3)], 384)
            for i in range(3):
                S.stt("dve", YMIX(YMIX.t[:, 2 + 3 * g2 + i, :], 2 + 3 * g2 + i), YS(YS.t[:, 3 * g2 + i, :]),
                      pvv(L, "ssd_norm", 3 * g2 + i, 3 * g2 + i + 1), r, ALU.mult, ALU.mult)
        return 19

    C0 = math.exp(-0.5)

    def odd_layer(L):
        st = LS[L]
        o = L // 2
        PHASE[0] = "r0"
        arena_reset()
        pre_norm(L, "n_mix_pre")
        widx = 0
        RH = mk("rh", [128, 4, NT], BF16, nreg=4)
        KH = mk("kh", [128, 4, NT], BF16, nreg=4)
        KHH = mk("khh", [128, 4, NT], BF16, nreg=4)
        BH = mk("bh", [128, 4, NT], BF16, nreg=4)
        VB = mk("vb", [128, 4, NT], BF16, nreg=4)
        BV = mk("bv", [128, 4, NT], F32, nreg=4)
        SG = mk("sg", [128, NT], BF16)
        GL = mk("gl", [128, 4, CPT], F32, nreg=4)
        r1_snap = (nc.sbuf_base, nc.sbuf_top)
        TW = mk("tw", [128, NT], BF16)
        PVb = mk("pvb16", [128, NT], BF16)
        NR0 = 2
        R0B = []
        for sl in range(NR0):
            d = {"RAW": mk("rraw", [128, 1 + NT], F32), "RK": mk("rk", [128, NT], BF16)}
            for nm in ("Dt", "Rr", "Kk", "Vv", "Aa", "SGW", "KAP", "KT_", "E_", "LGS") + (("VS",) if o > 0 else ()):
                d[nm] = mk("r0" + nm, [128, NT], F32)
            R0B.append(d)
        nhead = 3 if o > 0 else 2

        def shifted(widx_, idx, dst, Bf):
            b = inproj(L, widx_)
            rw, Dt = Bf["RAW"], Bf["Dt"]
            S.cp("pool", rw(rw.t[:, 0:1]), st["rwhalo"](st["rwhalo"].t[:, idx:idx + 1]))
            S.cp("act", rw(rw.t[:, 1:1 + NT]), ps(b, 0, NT))
            S.act(Dt(), ps(b, 0, NT), AF.Identity, scale=st["omm"](st["omm"].t[:, idx:idx + 1]))
            S.stt("dve", dst, rw(rw.t[:, 0:NT]), pvv(L, "rw_mu", idx, idx + 1), Dt(), ALU.mult, ALU.add)
            S.cp("pool", st["rwhalo"](st["rwhalo"].t[:, idx:idx + 1]), rw(rw.t[:, NT:NT + 1]))

        B0 = R0B[0]
        shifted(0, 12, B0["Rr"](), B0)
        S.act(TW(TW.t[0:64, :]), B0["Rr"](B0["Rr"].t[0:64, :]), AF.Tanh)
        S.cp("dve", TW(TW.t[64:128, :]), B0["Rr"](B0["Rr"].t[64:128, :]))
        shifted(1, 13, B0["Kk"](), B0)
        S.act(SG(), B0["Kk"](), AF.Sigmoid)
        if o > 0:
            shifted(2, 14, B0["Vv"](), B0)
            S.cp("dve", PVb(PVb.t[0:32, :]), B0["Vv"](B0["Vv"].t[0:32, :]))

        def r0_i(i):
            Bf = R0B[i % NR0]
            Rr, Kk, Vv, Aa, SGW, KAP, KT_, E_, LGS, Dt, RK = [Bf[n] for n in ("Rr", "Kk", "Vv", "Aa", "SGW", "KAP", "KT_", "E_", "LGS", "Dt", "RK")]
            ic = slice(i * 128, (i + 1) * 128)
            wb = nhead + 3 * i
            shifted(wb, i, Rr(), Bf)
            yield
            shifted(wb + 1, 4 + i, Kk(), Bf)
            yield
            shifted(wb + 2, 8 + i, Vv(), Bf)
            yield
            bw, ba = next_bank(), next_bank()
            S.mm(ps(bw, 0, NT), st["wa2"](st["wa2"].t[0:64, ic]), TW(TW.t[0:64, :]))
            S.mm(ps(ba, 0, NT), st["wa2"](st["wa2"].t[64:128, ic]), TW(TW.t[64:128, :]))
            S.act(SGW(), ps(bw, 0, NT), AF.Sigmoid, bias=pvv(L, "rw_w0", i, i + 1))
            S.act(Aa(), ps(ba, 0, NT), AF.Sigmoid, bias=pvv(L, "rw_a0", i, i + 1))
            yield
            if o > 0:
                VS = Bf["VS"]
                bv_ = next_bank()
                S.mm(ps(bv_, 0, NT), st["v2"](st["v2"].t[0:32, ic]), PVb(PVb.t[0:32, :]))
                S.act(VS(), ps(bv_, 0, NT), AF.Sigmoid, bias=pvv(L, "rw_v0", i, i + 1))
                S.tt("dve", Dt(), VFIRST(VFIRST.t[:, i, :], i), Vv(), ALU.subtract)
                yield
                S.tt("dve", Dt(), Dt(), VS(), ALU.mult)
                S.tt("dve", Vv(), Vv(), Dt(), ALU.add)
            else:
                S.cp("act", VFIRST(VFIRST.t[:, i, :], i), Vv())
            S.cp("act", VB(VB.t[:, i, :], i), Vv())
            yield
            S.ts("dve", KAP(), Kk(), pvv(L, "rw_k_k", i, i + 1), ALU.mult)
            q = sqc[0] % 2
            sqc[0] += 1
            S.act(SQ(SQ.t[:, q, :], q), KAP(), AF.Square)
            S.mm(ps(7, 0, NT), blk64_bf, SQ(SQ.t[:, q, :], q))
            S.act(E_(), ps(7, 0, NT), AF.Sqrt)
            yield
            S.ts("dve", E_(), E_(), 1e-12, ALU.max)
            S.recip(E_(), E_())
            S.tt("dve", KAP(), KAP(), E_(), ALU.mult)
            yield
            S.ts("dve", KT_(), Aa(), -1.0, ALU.add, pvv(L, "rw_k_a", i, i + 1), ALU.mult)
            S.stt("dve", KT_(), KT_(), 1.0, Kk(), ALU.add, ALU.mult)
            S.tt("dve", Aa(), KAP(), Aa(), ALU.mult)
            yield
            S.stt("dve", RK(), Rr(), pvv(L, "rw_r_k", i, i + 1), KT_(), ALU.mult, ALU.mult)
            bb_ = next_bank()
            S.mm(ps(bb_, 0, NT), blk64_bf, RK())
            S.tt("dve", BV(BV.t[:, i, :], i), ps(bb_, 0, NT), Vv(), ALU.mult)
            yield
            for c in range(CPT):
                cc = slice(c * CH, (c + 1) * CH)
                S.scan(LGS(LGS.t[:, cc]), ones_f, SGW(SGW.t[:, cc]), 0.0, ALU.mult, ALU.add)
            yield
            S.act(E_(), LGS(), AF.Exp, scale=-C0)
            S.tt("dve", RH(RH.t[:, i, :], i), Rr(), E_(), ALU.mult)
            S.cp("pool", GL(GL.t[:, i, :], i), bc(E_(), E_.t[:, :].rearrange("p (c t) -> p c t", c=CPT)[:, :, CH - 1]))
            S.tt("dve", Dt(), LGS(), SGW(), ALU.subtract)
            yield
            S.act(Dt(), Dt(), AF.Exp, scale=-C0)
            S.tt("dve", KH(KH.t[:, i, :], i), KAP(), Dt(), ALU.mult)
            yield
            S.act(E_(), LGS(), AF.Exp, scale=C0)
            S.tt("dve", KHH(KHH.t[:, i, :], i), KT_(), E_(), ALU.mult)
            S.tt("dve", BH(BH.t[:, i, :], i), Aa(), E_(), ALU.mult)
            yield

        rolling([(lambda i=i: r0_i(i)) for i in range(4)], NR0, int(os.environ.get('STG_R0', 0)))
        widx = nhead + 12
        stop("r0")
        PHASE[0] = "r1"
        S.barrier()
        nc.sbuf_base, nc.sbuf_top = r1_snap
        G = mk("g", [128, 4, NT], F32, nreg=4)
        RB = []
        for sl in range(2):
            d = {}
            for nm in ("VT", "KTK", "NBT", "UT"):
                d[nm] = mk(nm, [128, 128], BF16)
            for hh in range(2):
                for nm in ("NM", "NMT", "PP0", "PP1", "PPT0", "PPT1", "YY0", "YY1"):
                    d[nm + str(hh)] = mk(nm, [128, 128], F32)
                for nm in ("ARK", "ARB", "AKK"):
                    d[nm + str(hh)] = mk(nm, [128, 128], BF16)
                d["RR" + str(hh)] = mk("rr", [128, 64], F32)
            RB.append(d)
        OS = mk("os", [128, 8, 64], F32)
        OQ = mk("oq", [128, 8, 64], F32)
        ST8 = mk("st8", [128, 4, 8], F32, nreg=4)
        OF = [mk("of", [128, CH], F32) for _ in range(2)]
        for i in range(4):
            b = next_bank()
            S.mm(ps(b, 0, NT), st["g2"](st["g2"].t[:, i * 128:(i + 1) * 128]), SG())
            S.cp("act", G(G.t[:, i, :], i), ps(b, 0, NT))
        lt, le, nle, nlt, ngt = C("lt"), C("le"), C("nle"), C("nlt"), C("ngt")
        T, Tbf = st["T"], st["Tbf"]

        def r1_pair(c, i, sl):
            cc = slice(c * CH, (c + 1) * CH)
            Bf = RB[sl]
            hbank = (1, 2) if sl == 0 else (5, 6)
            cbank = (5, 6) if os.environ.get("CBK") else hbank
            VT, KTK, NBT, UT = Bf["VT"], Bf["KTK"], Bf["NBT"], Bf["UT"]
            S.mm(ps(0, 0, 128), VB(VB.t[:, i, cc], i), ident_bf)
            S.mm(ps(0, 128, 256), KHH(KHH.t[:, i, cc], i), ident_bf)
            S.mm(ps(0, 256, 384), BH(BH.t[:, i, cc], i), ident_bf)
            S.cp("act", VT(), ps(0, 0, 128))
            S.cp("act", KTK(), ps(0, 128, 256))
            S.ts("dve", NBT(), ps(0, 256, 384), -1.0, ALU.mult)
            yield
            hv = []
            for hh in range(2):
                pb = 64 * hh
                hv.append((pb, RH(RH.t[pb:pb + 64, i, cc], i), KH(KH.t[pb:pb + 64, i, cc], i), KHH(KHH.t[pb:pb + 64, i, cc], i),
                           BH(BH.t[pb:pb + 64, i, cc], i), Tbf(Tbf.t[pb:pb + 64, i, :], i)))
            for hh in range(2):
                pb, rh, kh, khh, bh, t0v = hv[hh]
                bN = hbank[hh]
                qa = (2 * sl + hh) * 128
                S.mm(ps(bN, 0, 128), bh, kh)
                S.mm(ps(bN, 128, 256), kh, bh)
                S.mm(ps(bN, 256, 384), khh, rh)
                S.mm(ps(bN, 384, 512), bh, rh)
                yield
            for hh in range(2):
                bN = hbank[hh]
                qa = (2 * sl + hh) * 128
                h_ = str(hh)
                S.tt("dve", Bf["NM" + h_](), ps(bN, 0, 128), nlt, ALU.mult)
                S.tt("dve", Bf["NMT" + h_](), ps(bN, 128, 256), ngt, ALU.mult)
                S.tt("dve", Bf["ARK" + h_](), ps(bN, 256, 384), le, ALU.mult)
                S.tt("dve", Bf["ARB" + h_](), ps(bN, 384, 512), nle, ALU.mult)
                S.tt("dve", Bf["YY0" + h_](), Bf["NM" + h_](), ident, ALU.add)
                yield
            for hh in range(2):
                pb, rh, kh, khh, bh, t0v = hv[hh]
                bN = hbank[hh]
                h_ = str(hh)
                S.mm(ps(bN, 0, 128), khh, kh)
                S.tt("dve", Bf["AKK" + h_](), ps(bN, 0, 128), lt, ALU.mult)
                yield
            for hh in range(2):
                pb, rh, kh, khh, bh, t0v = hv[hh]
                h_ = str(hh)
                RBK, qr = hbank[hh], 384
                S.mm(ps(RBK, qr, qr + 64), kh, t0v, start=True, stop=False)
                S.mm(ps(RBK, qr, qr + 64), Bf["AKK" + h_](), VT(VT.t[:, pb:pb + 64]), start=False, stop=True)
                S.cp("act", Bf["RR" + h_](), ps(RBK, qr, qr + 64))
                yield
            cur = [(Bf["NM0"], Bf["NMT0"]), (Bf["NM1"], Bf["NMT1"])]
            for lvl in range(1, 7):
                a = str(lvl % 2)
                for hh in range(2):
                    Pm, PTm = cur[hh]
                    bC = cbank[hh]
                    h_ = str(hh)
                    if lvl < 6:
                        S.mm(ps(bC, 0, 128), PTm(), Pm())
                    S.mm(ps(bC, 128, 256), Pm(), PTm())
                    if lvl < 6:
                        S.cp("act", Bf["PP" + a + h_](), ps(bC, 0, 128))
                    S.cp("act", Bf["PPT" + a + h_](), ps(bC, 128, 256))
                    yield
                for hh in range(2):
                    bC = cbank[hh]
                    h_ = str(hh)
                    yprev, ynew = Bf["YY" + str((lvl - 1) % 2) + h_], Bf["YY" + a + h_]
                    S.mm(ps(bC, 256, 384), Bf["PPT" + a + h_](), yprev())
                    S.tt("dve", ynew(), yprev(), ps(bC, 256, 384), ALU.add)
                    cur[hh] = (Bf["PP" + a + h_], Bf["PPT" + a + h_])
                    yield
            for hh in range(2):
                pb, rh, kh, khh, bh, t0v = hv[hh]
                h_ = str(hh)
                RBK, qu = hbank[hh], 448
                S.mm(ps(RBK, qu, qu + 64), Bf["YY0" + h_](), Bf["RR" + h_]())
                S.cp("act", UT(UT.t[:, pb:pb + 64]), ps(RBK, qu, qu + 64))
                yield
            for hh in range(2):
                pb, rh, kh, khh, bh, t0v = hv[hh]
                h_ = str(hh)
                oc = (2 * i + hh) * 64
                ov = ps(4, oc, oc + 64)
                S.mm(ov, rh, t0v, start=True, stop=False)
                S.mm(ov, Bf["ARK" + h_](), VT(VT.t[:, pb:pb + 64]), start=False, stop=False)
                S.mm(ov, Bf["ARB" + h_](), UT(UT.t[:, pb:pb + 64]), start=False, stop=True)
                yield
            S.mm(ps(0, 384, 512), KTK(), VT(), start=True, stop=False)
            S.mm(ps(0, 384, 512), NBT(), UT(), start=False, stop=True)
            for hh in range(2):
                pb = 64 * hh
                tv = T(T.t[pb:pb + 64, i, :], i)
                S.tt("dve", tv, tv, ps(0, 384 + pb, 448 + pb, pb, pb + 64), ALU.add)
                S.ts("dve", tv, tv, GL(GL.t[pb:pb + 64, i, c:c + 1], i), ALU.mult)
                S.cp("act", Tbf(Tbf.t[pb:pb + 64, i, :], i), tv)
            if os.environ.get("SHOWB"):
                print("PAIR END", c, i, S.nops)
            yield

        def r1_tail(c):
            cc = slice(c * CH, (c + 1) * CH)
            osf = fl(OS(), "p a b -> p (a b)")
            S.cp("act", osf, ps(4, 0, 512))
            S.act(fl(OQ(), "p a b -> p (a b)"), ps(4, 0, 512), AF.Square)
            yield
            s1, s2, s3, s4 = [ST8(ST8.t[:, k, :], k) for k in range(4)]
            S.op("dve", lambda e: e.tensor_reduce(out=s1.ap, in_=OS.t[:], axis=AX.X, op=ALU.add), [s1], [OS()])
            S.op("dve", lambda e: e.tensor_reduce(out=s2.ap, in_=OQ.t[:], axis=AX.X, op=ALU.add), [s2], [OQ()])
            S.ts("dve", s1, s1, 1.0 / 64, ALU.mult)
            S.tt("dve", s3, s1, s1, ALU.mult)
            S.stt("dve", s2, s2, 1.0 / 64, s3, ALU.mult, ALU.subtract)
            S.act(s2, s2, AF.Sqrt, bias=gneps_v)
            S.recip(s2, s2)
            yield
            S.tt("dve", OS(), OS(), bc(s1, s1.ap[:, :, None].broadcast_to([128, 8, 64])), ALU.subtract)
            S.tt("dve", OS(), OS(), bc(s2, s2.ap[:, :, None].broadcast_to([128, 8, 64])), ALU.mult)
            yield
            for i in range(4):
                bt = 5 + i % 2
                S.tr(ps(bt, 384, 512), bc(OS(), osf.ap[:, i * 128:(i + 1) * 128]), ident)
                ofv = OF[i % 2]()
                S.ts("dve", ofv, ps(bt, 384, 512), pvv(L, "rw_ln_w", i, i + 1), ALU.mult, pvv(L, "rw_ln_b", i, i + 1), ALU.add)
                S.tt("dve", ofv, ofv, BV(BV.t[:, i, cc], i), ALU.add)
                S.tt("dve", YMIX(YMIX.t[:, i, cc], i), ofv, G(G.t[:, i, cc], i), ALU.mult)
                if os.environ.get("SHOWB"):
                    print("TAIL", c, i, S.nops)
                yield

        le64 = View(C("le64").ap.bitcast(U32), CST.regs)
        HS, HSbf = st["HS"], st["HSbf"]
        hb = [dict() for _ in range(1)]
        for k in range(1):
            for nm in ("Q", "LF", "K1", "I", "OG", "GC", "NG", "EC", "EX", "KD", "O"):
                hb[k][nm] = mk("hg" + nm, [128, NT], F32)
            for nm in ("QT", "KT", "QG"):
                hb[k][nm] = mk("hg" + nm, [128, NT], BF16)
            hb[k]["ITK"] = mk("hgitk", [128, 128], BF16)
            hb[k]["KDT"] = mk("hgkdt", [128, 128], BF16)
            hb[k]["ATM"] = mk("hgatm", [128, 128], BF16)
            S.memset("pool", hb[k]["ATM"](), 0.0)
        hg_w0 = nhead + 12

        def hg_head(hd):
            sl = 0
            Bf = hb[sl]
            Q, LF, K1, I_, OG, GC, NG, EC, EX, KD, O_ = [Bf[n] for n in ("Q", "LF", "K1", "I", "OG", "GC", "NG", "EC", "EX", "KD", "O")]
            QT, KT, QG, ITK, KDT, ATM = [Bf[n] for n in ("QT", "KT", "QG", "ITK", "KDT", "ATM")]
            bt, bo = 3, 7
            w0 = hg_w0 + 4 * hd
            b = inproj(L, w0, banks=(3,))
            S.act(Q(), ps(b, 0, NT), AF.Silu)
            yield
            b = inproj(L, w0 + 1, banks=(3,))
            S.act(LF(), ps(b, 0, NT), AF.Sigmoid)
            S.act(LF(), LF(), AF.Identity, scale=st["oml"](st["oml"].t[:, hd:hd + 1]), bias=st["lb"](st["lb"].t[:, hd:hd + 1]))
            yield
            S.ts("dve", K1(), LF(), -1.0, ALU.mult, 1.0, ALU.add)
            S.act(LF(), LF(), AF.Ln)
            yield
            b = inproj(L, w0 + 2, banks=(3,))
            S.cp("act", I_(), ps(b, 0, NT))
            yield
            b = inproj(L, w0 + 3, banks=(3,))
            S.act(OG(), ps(b, 0, NT), AF.Silu)
            yield
            for q in range(NQ):
                cq = slice(q * 64, (q + 1) * 64)
                S.scan(GC(GC.t[:, cq]), bc(ones_f, ones_f.ap[:, 0:64]), LF(LF.t[:, cq]), 0.0, ALU.mult, ALU.add)
            yield
            gc3 = GC.t[:].rearrange("p (q t) -> p q t", t=64)
            S.act(EC(), GC(), AF.Exp)
            S.tt("dve", View(NG.t[:].rearrange("p (q t) -> p q t", t=64), NG.regs), View(gc3, GC.regs),
                 View(gc3[:, :, 31:32].broadcast_to([128, NQ, 64]), GC.regs), ALU.subtract)
            S.tt("dve", View(EX.t[:].rearrange("p (q t) -> p q t", t=64), EX.regs), View(gc3, GC.regs),
                 View(gc3[:, :, 63:64].broadcast_to([128, NQ, 64]), GC.regs), ALU.subtract)
            yield
            S.tt("dve", QG(), Q(), EC(), ALU.mult)
            S.act(KD(), NG(), AF.Exp)
            S.tt("dve", QT(), Q(), KD(), ALU.mult)
            yield
            S.act(NG(), NG(), AF.Exp, scale=-1.0)
            S.tt("dve", KT(), K1(), NG(), ALU.mult)
            yield
            S.act(EX(), EX(), AF.Exp, scale=-1.0)
            S.tt("dve", KD(), K1(), EX(), ALU.mult)
            yield
            for blk in range(CPT):
                cb_ = slice(blk * 128, (blk + 1) * 128)
                S.tr(ps(bt, 0, 128), I_(I_.t[:, cb_]), ident)
                S.tr(ps(bt, 128, 256), KD(KD.t[:, cb_]), ident)
                S.cp("act", ITK(), ps(bt, 0, 128))
                S.cp("act", KDT(), ps(bt, 128, 256))
                yield
                S.mm(ps(bt, 384, 512), KT(KT.t[:, cb_]), QT(QT.t[:, cb_]))
                S.cpred(ATM(), le64, ps(bt, 384, 512))
                yield
                S.mm(ps(bo, 0, 128), ITK(), ATM(), start=True, stop=False)
                for qq in range(2):
                    q = 2 * blk + qq
                    cq = slice(q * 64, (q + 1) * 64)
                    end = q * 64 + 63
                    S.mm(ps(bo, qq * 64, qq * 64 + 64), HSbf(HSbf.t[:, hd, :], hd), QG(QG.t[:, cq]), start=False, stop=(qq == 1))
                    S.mm(ps(bt, 256, 384), KDT(KDT.t[qq * 64:qq * 64 + 64, :]), ITK(ITK.t[qq * 64:qq * 64 + 64, :]))
                    hs = HS(HS.t[:, hd, :], hd)
                    S.stt("dve", hs, hs, EC(EC.t[:, end:end + 1]), ps(bt, 256, 384), ALU.mult, ALU.add)
                    S.cp("act", HSbf(HSbf.t[:, hd, :], hd), hs)
                    yield
                S.cp("act", O_(O_.t[:, cb_]), ps(bo, 0, 128))
                yield
            r = norm_stats([O_()], 128)
            S.stt("dve", O_(), O_(), pvv(L, "hg_norm", hd, hd + 1), r, ALU.mult, ALU.mult)
            S.tt("dve", YMIX(YMIX.t[:, 4 + hd, :], 4 + hd), O_(), OG(), ALU.mult)
            yield


        def hg_all():
            for hd in range(4):
                yield from hg_head(hd)

        def r1_all():
            pend = None
            for c in range(CPT):
                for grp in ((0, 1), (2, 3)):
                    act_ = [r1_pair(c, grp[0], 0), r1_pair(c, grp[1], 1)]
                    if pend is not None:
                        act_.append(pend)
                        pend = None
                    while act_:
                        for g in list(act_):
                            try:
                                next(g)
                            except StopIteration:
                                act_.remove(g)
                        yield
                pend = r1_tail(c)
            for _ in pend:
                yield

        g1, g2 = r1_all(), hg_all()
        alive1 = alive2 = True
        HGK = int(os.environ.get("HGK", 1))
        while alive1 or alive2:
            if alive1:
                try:
                    next(g1)
                except StopIteration:
                    alive1 = False
            for _ in range(HGK if alive1 else 4):
                if alive2:
                    try:
                        next(g2)
                    except StopIteration:
                        alive2 = False
        stop("r1")
        widx = nhead + 12 + 16
        return widx

    def mix_out(L, widx):
        PHASE[0] = "mixout"
        arena_reset()
        TM[0] = mk("tmp8", [128, 8, NT], F32, nreg=8)
        widx = proj8(L, widx, YMIX, 8)
        post_norm_add(L, "n_mix_post")
        return widx

    try:
      for ti in range(n_tiles):
        t0 = ti * NT
        S.dma("sp", H(), dram(xT[:, :, t0:t0 + NT].rearrange("d p t -> p d t")))
        for L in range(n_layers):
            stop("start")
            widx = even_layer(L) if L % 2 == 0 else odd_layer(L)
            stop("mixer")
            if dbg is not None and dbg[0] == "ymix" and dbg[1] == L and ti == 0:
                for d in range(8):
                    pass
                raise NotImplementedError("dbg ymix dump removed")
            widx = mix_out(L, widx)
            stop("mixout")
            widx = ffn(L, widx)
            assert widx == n_ws[L], (widx, n_ws[L])
        S.dma("sp", dram(oT[:, :, t0:t0 + NT].rearrange("d p t -> p d t")), H(), is_output=True)
    except StopBuild:
        S.dma("sp", dram(oT[:, :, 0:NT].rearrange("d p t -> p d t")), H(), is_output=True)
    S.finish()
    print("program: ninst=%d persist_bytes/partition=%d" % (S.ninst, persist_bytes))
    if COST is not None:
        phases = sorted(set(k[0] for k in COST))
        for ph in phases:
            print("COST %-7s" % ph, " ".join("%s=%.0fus(%d)" % (e, COST.get((ph, e), 0), COST.get((ph, "n_" + e), 0)) for e in ("pe", "dve", "act", "pool")))
    return nc


def host_prep(inputs):
    pkc = consts_host()
    pls, pbs, wss = [], [], []
    for L in range(DEPTH):
        P, B, ws = layer_host(inputs, L)
        pls.append(P)
        pbs.append(B)
        wss.append(ws)
    return pkc, pls, pbs, wss


def make_xT(inputs):
    x = np.asarray(inputs["x"], dtype=np.float32)
    meta = np.asarray(inputs["meta"], dtype=np.float32)
    xs = []
    for b in range(NB):
        full = np.zeros((TPAD, D), np.float32)
        full[:NMETA] = meta
        full[NMETA:TREAL] = x[b]
        xs.append(np.ascontiguousarray(full.T).reshape(8, 128, TPAD))
    return xs


def make_shared(pkc, pls, pbs, wss):
    shared = {"consts": pkc.pack()}
    for L in range(DEPTH):
        shared["pv%d" % L] = pls[L].pack()
        shared["pb%d" % L] = pbs[L].pack()
        shared["ws%d" % L] = wss[L]
    return shared


def kernel(**inputs):
    pkc, pls, pbs, wss = host_prep(inputs)
    nc = build_program(pkc, pls, pbs, [w.shape[0] for w in wss])
    xs = make_xT(inputs)
    shared = make_shared(pkc, pls, pbs, wss)
    in_maps = [dict(shared, xT=xs[b]) for b in range(NB)]
    res = run_bass_kernel_spmd(nc, in_maps, core_ids=list(range(NB)))
    out = np.empty((NB, SEQ, D), np.float32)
    for b in range(NB):
        o = np.asarray(res.results[b]["oT"]).reshape(D, TPAD)
        out[b] = o[:, NMETA:TREAL].T
    return out
```
